# Optimizing a Trainium2 kernel written in Bass

```python
import math
import jax, jax.numpy as jnp
from jax import lax
import numpy as np

D_MODEL = 1024
BATCH = 4
SEQ = 4096
DEPTH = 1
DEC_BATCH = 32
DEC_SEQ = 8
PAST_LEN = 16384
PAGE_SIZE = 128

A_HEADS = 8
A_KV_GROUPS = 2
A_HPG = A_HEADS // A_KV_GROUPS
A_HEAD_DIM = 64
A_WIDTH = A_HEADS * A_HEAD_DIM
A_KV_WIDTH = A_KV_GROUPS * A_HEAD_DIM
CMP_BLOCK = 32
CMP_STRIDE = 16
SLC_BLOCK = 64
N_SELECT = 16
WINDOW = 512
Q_BLOCK = 128
ROPE_THETA = 10000.0
FORCE_SCORE = 1e4
NEG_INF = -1e30
B_HEADS = 4
B_KEY_DIM = 128
B_VAL_DIM = 128
B_KWIDTH = B_HEADS * B_KEY_DIM
B_WIDTH = B_HEADS * B_VAL_DIM
HGRN_CHUNK = 64
D_FF = 2816
FFN_CONV = 3
EPS = 1e-6
IN_WIDTHS = (A_WIDTH, 2 * A_KV_WIDTH, 2 * A_KV_WIDTH, 2 * A_KV_WIDTH, 3 * A_HEADS,
             B_KWIDTH, B_KWIDTH, B_WIDTH, B_WIDTH, 2 * D_MODEL)
D_IN = sum(IN_WIDTHS)

kernel_name = 'nsa_hgrn2_gated_merge_convffn_step'


def rms_norm(x, g):
    xf = x.astype(jnp.float32)
    y = xf * lax.rsqrt(jnp.mean(xf * xf, axis=-1, keepdims=True) + EPS)
    return (y * g.astype(jnp.float32)).astype(x.dtype)


def rope(x, pos):
    half = x.shape[-1] // 2
    inv = ROPE_THETA ** (-jnp.arange(half, dtype=jnp.float32) / half)
    ang = pos.astype(jnp.float32)[:, None] * inv[None, :]
    cos = jnp.cos(ang)[None, :, None, :]
    sin = jnp.sin(ang)[None, :, None, :]
    xf = x.astype(jnp.float32)
    x1, x2 = xf[..., :half], xf[..., half:]
    return jnp.concatenate([x1 * cos - x2 * sin, x2 * cos + x1 * sin], axis=-1).astype(x.dtype)


def masked_softmax(s, mask):
    s = jnp.where(mask, s.astype(jnp.float32), NEG_INF)
    p = jnp.exp(s - jnp.max(s, axis=-1, keepdims=True)) * mask
    return p / jnp.maximum(jnp.sum(p, axis=-1, keepdims=True), 1e-30)


def pad_time(rows):
    T = rows.shape[1]
    tp = -(-T // SLC_BLOCK) * SLC_BLOCK
    return jnp.pad(rows, ((0, 0), (0, tp - T), (0, 0), (0, 0), (0, 0)))


def mixer_features(h, pos, lb, w_in, q_norm_g, k_norm_g):
    B, T, _ = h.shape
    z = h @ w_in
    offs = np.cumsum(IN_WIDTHS)[:-1].tolist()
    q_a, kv_c, kv_s, kv_w, gate_a, q_b, f_b, i_b, g_b, mg = jnp.split(z, offs, axis=-1)
    q = rope(rms_norm(q_a.reshape(B, T, A_HEADS, A_HEAD_DIM), q_norm_g), pos)

    def kv_rows(kv, i):
        kv = kv.reshape(B, T, 2, A_KV_GROUPS, A_HEAD_DIM)
        k = rope(rms_norm(kv[:, :, 0], k_norm_g[i]), pos)
        return jnp.stack([k, kv[:, :, 1]], axis=2)

    gates = jax.nn.sigmoid(gate_a).reshape(B, T, A_HEADS, 3)
    fz = f_b.astype(jnp.float32)
    lb = lb.astype(jnp.float32)
    logf = jnp.log(lb + (1.0 - lb) * jax.nn.sigmoid(fz)).reshape(B, T, B_HEADS, B_KEY_DIM)
    kb = ((1.0 - lb) * jax.nn.sigmoid(-fz)).reshape(B, T, B_HEADS, B_KEY_DIM)
    qb = jax.nn.silu(q_b.astype(jnp.float32)).reshape(B, T, B_HEADS, B_KEY_DIM)
    vb = i_b.astype(jnp.float32).reshape(B, T, B_HEADS, B_VAL_DIM)
    gb = jax.nn.silu(g_b)
    ma, mb = jnp.split(jax.nn.sigmoid(mg), 2, axis=-1)
    return (q, kv_rows(kv_c, 0), kv_rows(kv_s, 1), kv_rows(kv_w, 2), gates, qb, kb, vb, logf, gb, ma, mb)


def compress(rows, pos_emb, w1, w2):
    B, T = rows.shape[:2]
    R = CMP_BLOCK // CMP_STRIDE
    n_chunk = T // CMP_STRIDE
    n_cmp = n_chunk - R + 1
    ch = rows.reshape(B, n_chunk, CMP_STRIDE, 2, A_KV_GROUPS, A_HEAD_DIM)
    ch = ch.transpose(0, 1, 3, 4, 2, 5).reshape(B, n_chunk, 2, A_KV_GROUPS, CMP_STRIDE * A_HEAD_DIM)
    w1r = w1.reshape(2, R, CMP_STRIDE * A_HEAD_DIM, A_HEAD_DIM)
    pre = jnp.einsum('jf,jfe->je', pos_emb.reshape(2, CMP_BLOCK * A_HEAD_DIM), w1)[None, None, :, None, :]
    for r in range(R):
        pre = pre + jnp.einsum('bcjgf,jfe->bcjge', ch, w1r[:, r])[:, r:r + n_cmp]
    out = jnp.einsum('bcjge,jed->bcjgd', jax.nn.silu(pre), w2)
    c_end = jnp.arange(n_cmp, dtype=jnp.int32) * CMP_STRIDE + (CMP_BLOCK - 1)
    return out[:, :, 0], out[:, :, 1], c_end


def select_blocks(rows):
    B, T = rows.shape[:2]
    blk = rows.reshape(B, T // SLC_BLOCK, SLC_BLOCK, 2, A_KV_GROUPS, A_HEAD_DIM)
    return blk.transpose(0, 4, 1, 2, 3, 5)


def nsa_core(q, q_pos, gates, kc, vc, c_end, kv_blk, kv_win, w_pos):
    B, Tq = q.shape[:2]
    NC = kc.shape[1]
    NS = kv_blk.shape[2]
    qg = q.reshape(B, Tq, A_KV_GROUPS, A_HPG, A_HEAD_DIM) * (A_HEAD_DIM ** -0.5)
    m_c = (c_end[None, :] <= q_pos[:, None])[None, :, None, None, :]
    p_c = masked_softmax(jnp.einsum('btghd,bcgd->btghc', qg, kc), m_c)
    o_c = jnp.einsum('btghc,bcgd->btghd', p_c.astype(vc.dtype), vc)
    c_start = jnp.arange(NC, dtype=jnp.int32) * CMP_STRIDE
    s_start = jnp.arange(NS, dtype=jnp.int32) * SLC_BLOCK
    cover = ((c_start[:, None] < s_start[None, :] + SLC_BLOCK)
             & (c_start[:, None] + CMP_BLOCK > s_start[None, :])).astype(jnp.float32)
    imp = jnp.einsum('btghc,cs->btgs', p_c, cover)
    blk = jnp.arange(NS, dtype=jnp.int32)[None, :]
    cur = (q_pos // SLC_BLOCK)[:, None]
    forced = (blk == 0) | (blk == cur) | (blk == cur - 1)
    valid = s_start[None, :] <= q_pos[:, None]
    score = jnp.where(forced[None, :, None, :], FORCE_SCORE,
                      jnp.where(valid[None, :, None, :], imp, -1.0))
    _, idx = lax.top_k(score, min(N_SELECT, NS))
    n = idx.shape[-1]
    bi = jnp.arange(B)[:, None, None, None]
    gi = jnp.arange(A_KV_GROUPS)[None, None, :, None]
    sel = kv_blk[bi, gi, idx]
    sel = sel.reshape(B, Tq, A_KV_GROUPS, n * SLC_BLOCK, 2, A_HEAD_DIM)
    sel_pos = (idx[..., None] * SLC_BLOCK + jnp.arange(SLC_BLOCK, dtype=jnp.int32)).reshape(B, Tq, A_KV_GROUPS, n * SLC_BLOCK)
    m_s = (sel_pos <= q_pos[None, :, None, None])[:, :, :, None, :]
    p_s = masked_softmax(jnp.einsum('btghd,btgkd->btghk', qg, sel[..., 0, :]), m_s)
    o_s = jnp.einsum('btghk,btgkd->btghd', p_s.astype(sel.dtype), sel[..., 1, :])
    dpos = q_pos[:, None] - w_pos[None, :]
    m_w = ((dpos >= 0) & (dpos < WINDOW) & (w_pos[None, :] >= 0))[None, :, None, None, :]
    p_w = masked_softmax(jnp.einsum('btghd,bkgd->btghk', qg, kv_win[:, :, 0]), m_w)
    o_w = jnp.einsum('btghk,bkgd->btghd', p_w.astype(kv_win.dtype), kv_win[:, :, 1])
    g = gates.reshape(B, Tq, A_KV_GROUPS, A_HPG, 3)
    o = g[..., 0:1] * o_c + g[..., 1:2] * o_s + g[..., 2:3] * o_w
    return o.reshape(B, Tq, A_WIDTH)


def hgrn2_recurrence(q, k, v, logf, s0):
    B, T, H, DK = q.shape
    C = math.gcd(T, HGRN_CHUNK)
    n = T // C

    def to_chunks(a):
        return jnp.swapaxes(a.reshape(B, n, C, *a.shape[2:]), 0, 1)

    tri = jnp.tril(jnp.ones((C, C), dtype=bool))[None, :, :, None, None]

    def step(S, inp):
        qc, kc, vc, lf = inp
        b = jnp.cumsum(lf, axis=1)
        o_inter = jnp.einsum('bthk,bhkv->bthv', qc * jnp.exp(b), S)
        d = b[:, :, None] - b[:, None, :]
        decay = jnp.where(tri, jnp.exp(jnp.where(tri, d, 0.0)), 0.0)
        A = jnp.einsum('bthk,btshk,bshk->btsh', qc, decay, kc)
        o_intra = jnp.einsum('btsh,bshv->bthv', A, vc)
        b_last = b[:, -1]
        S_new = jnp.exp(b_last)[..., None] * S + jnp.einsum('bshk,bshv->bhkv', kc * jnp.exp(b_last[:, None] - b), vc)
        return S_new, o_inter + o_intra

    S, o = lax.scan(step, s0, (to_chunks(q), to_chunks(k), to_chunks(v), to_chunks(logf)))
    return jnp.swapaxes(o, 0, 1).reshape(B, T, H, v.shape[-1]), S


def hgrn_readout(o, gb, g_norm):
    B, T = o.shape[:2]
    return rms_norm(o, g_norm).reshape(B, T, B_WIDTH).astype(gb.dtype) * gb


def merge_and_ffn(x, o_a, o_b, ma, mb, w_branch, w_out, ffn_norm_g, ffn_w_in, ffn_conv_w, ffn_conv_b, ffn_w_out, conv_buf):
    m = ma * (o_a.astype(x.dtype) @ w_branch[:A_WIDTH]) + mb * (o_b @ w_branch[A_WIDTH:])
    x = x + m @ w_out
    h = rms_norm(x, ffn_norm_g)
    a, b = jnp.split(h @ ffn_w_in, 2, axis=-1)
    T = a.shape[1]
    a_ext = jnp.concatenate([conv_buf.astype(a.dtype), a], axis=1)
    a_conv = ffn_conv_b + sum(a_ext[:, j:j + T] * ffn_conv_w[j] for j in range(FFN_CONV))
    y = x + (jax.nn.silu(a_conv) * b) @ ffn_w_out
    return y, a_ext[:, -(FFN_CONV - 1):]


def trunk_layer(xp, xs, cmp_pool, slc_pool, page_table, win_buf, hgrn_state, conv_buf, lb,
                attn_norm_g, w_in, q_norm_g, k_norm_g, cmp_pos_emb, cmp_w1, cmp_w2, hgrn_norm_g,
                w_branch, w_out, ffn_norm_g, ffn_w_in, ffn_conv_w, ffn_conv_b, ffn_w_out):
    B, T = xp.shape[:2]
    DB, TS = xs.shape[:2]
    pos_p = jnp.arange(T, dtype=jnp.int32)
    pos_s = PAST_LEN + jnp.arange(TS, dtype=jnp.int32)
    (q_p, kvc_p, kvs_p, kvw_p, gt_p, qb_p, kb_p, vb_p, lf_p, gb_p, ma_p, mb_p) = mixer_features(
        rms_norm(xp, attn_norm_g), pos_p, lb, w_in, q_norm_g, k_norm_g)
    (q_s, kvc_s, kvs_s, kvw_s, gt_s, qb_s, kb_s, vb_s, lf_s, gb_s, ma_s, mb_s) = mixer_features(
        rms_norm(xs, attn_norm_g), pos_s, lb, w_in, q_norm_g, k_norm_g)

    kc_p, vc_p, cend_p = compress(pad_time(kvc_p), cmp_pos_emb, cmp_w1, cmp_w2)
    blk_p = select_blocks(pad_time(kvs_p))
    kvw_pad = jnp.pad(kvw_p, ((0, 0), (WINDOW, 0), (0, 0), (0, 0), (0, 0)))

    def prompt_block(i):
        s0 = i * Q_BLOCK
        return nsa_core(
            lax.dynamic_slice_in_dim(q_p, s0, Q_BLOCK, axis=1),
            s0 + jnp.arange(Q_BLOCK, dtype=jnp.int32),
            lax.dynamic_slice_in_dim(gt_p, s0, Q_BLOCK, axis=1),
            kc_p, vc_p, cend_p, blk_p,
            lax.dynamic_slice_in_dim(kvw_pad, s0, WINDOW + Q_BLOCK, axis=1),
            s0 - WINDOW + jnp.arange(WINDOW + Q_BLOCK, dtype=jnp.int32))

    oa_p = lax.map(prompt_block, jnp.arange(T // Q_BLOCK, dtype=jnp.int32))
    oa_p = jnp.swapaxes(oa_p, 0, 1).reshape(B, T, A_WIDTH)

    past_c = cmp_pool[page_table].reshape(DB, -1, 2, A_KV_GROUPS, A_HEAD_DIM)
    past_s = slc_pool[page_table].reshape(DB, -1, 2, A_KV_GROUPS, A_HEAD_DIM)
    kc_s, vc_s, cend_s = compress(pad_time(jnp.concatenate([past_c, kvc_s.astype(past_c.dtype)], axis=1)),
                                  cmp_pos_emb, cmp_w1, cmp_w2)
    blk_s = select_blocks(pad_time(jnp.concatenate([past_s, kvs_s.astype(past_s.dtype)], axis=1)))
    wb = win_buf.shape[1]
    win_s = jnp.concatenate([win_buf, kvw_s.astype(win_buf.dtype)], axis=1)
    oa_s = nsa_core(q_s, pos_s, gt_s, kc_s, vc_s, cend_s, blk_s, win_s,
                    PAST_LEN - wb + jnp.arange(wb + TS, dtype=jnp.int32))

    o_p, S_p = hgrn2_recurrence(qb_p, kb_p, vb_p, lf_p, jnp.zeros((B, B_HEADS, B_KEY_DIM, B_VAL_DIM), jnp.float32))
    o_s, S_s = hgrn2_recurrence(qb_s, kb_s, vb_s, lf_s, hgrn_state.astype(jnp.float32))
    ob_p = hgrn_readout(o_p, gb_p, hgrn_norm_g)
    ob_s = hgrn_readout(o_s, gb_s, hgrn_norm_g)

    yp, conv_p = merge_and_ffn(xp, oa_p, ob_p, ma_p, mb_p, w_branch, w_out, ffn_norm_g, ffn_w_in,
                               ffn_conv_w, ffn_conv_b, ffn_w_out, jnp.zeros((B, FFN_CONV - 1, D_FF), xp.dtype))
    ys, conv_s = merge_and_ffn(xs, oa_s, ob_s, ma_s, mb_s, w_branch, w_out, ffn_norm_g, ffn_w_in,
                               ffn_conv_w, ffn_conv_b, ffn_w_out, conv_buf)
    return (yp, ys, kvc_p, kvc_s, kvs_p, kvs_s, kvw_p[:, -min(WINDOW, T):], win_s[:, -wb:],
            S_p, S_s, conv_p, conv_s)


def setup_inputs(seed: int = 0) -> dict:
    key = jax.random.key(seed)
    ks = jax.random.split(key, 24)
    f32 = jnp.float32
    n_pages = PAST_LEN // PAGE_SIZE
    n_pool = (DEC_BATCH * n_pages * 5) // 4
    win_buf = min(WINDOW, PAST_LEN)

    def nrm(k, shape, s):
        return s * jax.random.normal(k, shape, f32)

    page_table = jax.random.permutation(ks[4], n_pool)[:DEC_BATCH * n_pages].reshape(DEC_BATCH, n_pages).astype(jnp.int32)
    return {
        'x_prompt': nrm(ks[0], (BATCH, SEQ, D_MODEL), 1.0),
        'x_sample': nrm(ks[1], (DEC_BATCH, DEC_SEQ, D_MODEL), 1.0),
        'cache_cmp_kv': nrm(ks[2], (DEPTH, n_pool, PAGE_SIZE, 2, A_KV_GROUPS, A_HEAD_DIM), 1.0),
        'cache_slc_kv': nrm(ks[3], (DEPTH, n_pool, PAGE_SIZE, 2, A_KV_GROUPS, A_HEAD_DIM), 1.0),
        'page_table': page_table,
        'state_win_kv': nrm(ks[5], (DEPTH, DEC_BATCH, win_buf, 2, A_KV_GROUPS, A_HEAD_DIM), 1.0),
        'state_hgrn': nrm(ks[6], (DEPTH, DEC_BATCH, B_HEADS, B_KEY_DIM, B_VAL_DIM), 0.5),
        'state_ffn_conv': nrm(ks[7], (DEPTH, DEC_BATCH, FFN_CONV - 1, D_FF), 1.0),
        'attn_norm_g': 1.0 + nrm(ks[8], (DEPTH, D_MODEL), 0.02),
        'w_in': nrm(ks[9], (DEPTH, D_MODEL, D_IN), D_MODEL ** -0.5),
        'q_norm_g': 1.0 + nrm(ks[10], (DEPTH, A_HEAD_DIM), 0.02),
        'k_norm_g': 1.0 + nrm(ks[11], (DEPTH, 3, A_HEAD_DIM), 0.02),
        'cmp_pos_emb': nrm(ks[12], (DEPTH, 2, CMP_BLOCK, A_HEAD_DIM), 0.1),
        'cmp_w1': nrm(ks[13], (DEPTH, 2, CMP_BLOCK * A_HEAD_DIM, A_HEAD_DIM), (CMP_BLOCK * A_HEAD_DIM) ** -0.5),
        'cmp_w2': nrm(ks[14], (DEPTH, 2, A_HEAD_DIM, A_HEAD_DIM), A_HEAD_DIM ** -0.5),
        'hgrn_lb_logits': nrm(ks[15], (DEPTH + 1, B_KWIDTH), 0.5),
        'hgrn_norm_g': 1.0 + nrm(ks[16], (DEPTH, B_VAL_DIM), 0.02),
        'w_branch': nrm(ks[17], (DEPTH, A_WIDTH + B_WIDTH, D_MODEL), A_WIDTH ** -0.5),
        'w_out': nrm(ks[18], (DEPTH, D_MODEL, D_MODEL), D_MODEL ** -0.5),
        'ffn_norm_g': 1.0 + nrm(ks[19], (DEPTH, D_MODEL), 0.02),
        'ffn_w_in': nrm(ks[20], (DEPTH, D_MODEL, 2 * D_FF), D_MODEL ** -0.5),
        'ffn_conv_w': nrm(ks[21], (DEPTH, FFN_CONV, D_FF), FFN_CONV ** -0.5),
        'ffn_conv_b': nrm(ks[22], (DEPTH, D_FF), 0.01),
        'ffn_w_out': nrm(ks[23], (DEPTH, D_FF, D_MODEL), D_FF ** -0.5),
    }


def reference(x_prompt, x_sample, cache_cmp_kv, cache_slc_kv, page_table, state_win_kv, state_hgrn,
              state_ffn_conv, attn_norm_g, w_in, q_norm_g, k_norm_g, cmp_pos_emb, cmp_w1, cmp_w2,
              hgrn_lb_logits, hgrn_norm_g, w_branch, w_out, ffn_norm_g, ffn_w_in, ffn_conv_w,
              ffn_conv_b, ffn_w_out):
    lbs = jnp.cumsum(jax.nn.softmax(hgrn_lb_logits.astype(jnp.float32), axis=0), axis=0)
    yp, ys = x_prompt, x_sample
    cols = [[] for _ in range(10)]
    for l in range(DEPTH):
        yp, ys, *st = trunk_layer(
            yp, ys, cache_cmp_kv[l], cache_slc_kv[l], page_table, state_win_kv[l], state_hgrn[l],
            state_ffn_conv[l], lbs[l], attn_norm_g[l], w_in[l], q_norm_g[l], k_norm_g[l],
            cmp_pos_emb[l], cmp_w1[l], cmp_w2[l], hgrn_norm_g[l], w_branch[l], w_out[l],
            ffn_norm_g[l], ffn_w_in[l], ffn_conv_w[l], ffn_conv_b[l], ffn_w_out[l])
        for c, s in zip(cols, st):
            c.append(s)
    (cmp_kv_prompt, cmp_kv_sample, slc_kv_prompt, slc_kv_sample, win_kv_prompt, win_kv_sample,
     hgrn_prompt, hgrn_sample, ffn_conv_prompt, ffn_conv_sample) = [jnp.stack(c) for c in cols]
    return (yp, ys, cmp_kv_prompt, cmp_kv_sample, slc_kv_prompt, slc_kv_sample, win_kv_prompt,
            win_kv_sample, hgrn_prompt, hgrn_sample, ffn_conv_prompt, ffn_conv_sample)
```

```python
import numpy as np
import concourse.bass as bass
import concourse.mybir as mybir
from concourse.bass_utils import run_bass_kernel_spmd
from contextlib import ExitStack

F32 = mybir.dt.float32
BF16 = mybir.dt.bfloat16
I32 = mybir.dt.int32
AF = mybir.ActivationFunctionType
ALU = mybir.AluOpType
AX = mybir.AxisListType

NDS = 40
EPS = 1e-6
D = 1024
NCOL1 = 3352
NPRE = 16
NOWN = 16
NT = NPRE + NOWN


class Buf:
    def __init__(self, name, t):
        self.name = name
        self.t = t
        self.w = {}
        self.r = {}
        self.ps = False

    def __getitem__(self, idx):
        return self.t[idx]


class KB:
    def __init__(self):
        self.nc = bass.Bass("TRN2", target_bir_lowering=False)
        nc = self.nc
        self.es = ExitStack()
        self.engs = {"pe": nc.tensor, "act": nc.scalar, "dve": nc.vector, "pool": nc.gpsimd, "sp": nc.sync}
        self.esem = {}
        self.ecnt = {}
        for e in ("pe", "act", "dve", "pool"):
            self.esem[e] = self.es.enter_context(nc.semaphore("sem_" + e))
            self.ecnt[e] = 0
        self.dsems = [self.es.enter_context(nc.semaphore("sem_d%d" % i)) for i in range(NDS)]
        self.dval = [0] * NDS
        self.dnext = 0
        self.waited = {e: {} for e in self.engs}
        self.nbuf = 0
        self.n_ins = 0
        self.n_wait = 0
        self.stk = [self.es]

    def push(self):
        self.stk.append(ExitStack())

    def barrier(self):
        for e in self.engs:
            for f in ("pe", "act", "dve", "pool"):
                if f != e:
                    self._wait(e, ("e", f), self.ecnt[f])
            for i in range(NDS):
                self._wait(e, ("d", i), self.dval[i])

    def pop(self):
        self.barrier()
        self.stk.pop().close()

    def sb(self, shape, dt, name=None):
        self.nbuf += 1
        name = (name or "sb") + "_%d" % self.nbuf
        return Buf(name, self.stk[-1].enter_context(self.nc.sbuf_tensor(name, list(shape), dt)))

    def ps(self, shape, dt, name=None):
        self.nbuf += 1
        name = name or "ps%d" % self.nbuf
        name = name + "_%d" % self.nbuf
        b = Buf(name, self.stk[-1].enter_context(self.nc.psum_tensor(name, list(shape), dt)))
        b.ps = True
        return b

    def dram(self, name, shape, dt, kind=None):
        if kind is None:
            t = self.nc.dram_tensor(name, list(shape), dt)
        else:
            t = self.nc.dram_tensor(name, list(shape), dt, kind=kind)
        return Buf(name, t.ap())

    def _sem(self, key):
        return self.esem[key[1]] if key[0] == "e" else self.dsems[key[1]]

    def _wait(self, eng, key, val):
        if val <= 0:
            return
        wd = self.waited[eng]
        if wd.get(key, 0) >= val:
            return
        self.engs[eng].wait_ge(self._sem(key), val)
        wd[key] = val
        self.n_wait += 1

    def _sync(self, eng, reads, writes):
        need = {}
        for b in reads:
            for k, v in b.w.items():
                if need.get(k, 0) < v:
                    need[k] = v
        for b in writes:
            for k, v in b.w.items():
                if need.get(k, 0) < v:
                    need[k] = v
            for k, v in b.r.items():
                if need.get(k, 0) < v:
                    need[k] = v
        for k, v in need.items():
            self._wait(eng, k, v)

    def op(self, eng, fn, reads=(), writes=(), inc=True):
        self._sync(eng, reads, writes)
        ins = fn(self.engs[eng])
        self.n_ins += 1
        if inc:
            self.ecnt[eng] += 1
            n = self.ecnt[eng]
            ins.then_inc(self.esem[eng], 1)
        else:
            n = self.ecnt[eng] + 1
        key = ("e", eng)
        for b in reads:
            d = b.w if b.ps else b.r
            if d.get(key, 0) < n:
                d[key] = n
        for b in writes:
            if b.w.get(key, 0) < n:
                b.w[key] = n
        return ins

    def dma(self, q, out_ap, in_ap, reads=(), writes=(), **kw):
        i = self.dnext
        self.dnext = (i + 1) % NDS
        self._wait(q, ("d", i), self.dval[i])
        self._sync(q, reads, writes)
        ins = self.engs[q].dma_start(out=out_ap, in_=in_ap, **kw)
        self.n_ins += 1
        self.dval[i] += 16
        ins.then_inc(self.dsems[i], 16)
        key = ("d", i)
        for b in reads:
            b.r[key] = self.dval[i]
        for b in writes:
            b.w[key] = self.dval[i]
        return ins

    def finish(self):
        for i in range(NDS):
            self._wait("sp", ("d", i), self.dval[i])
        for e in ("pe", "act", "dve", "pool"):
            self._wait("sp", ("e", e), self.ecnt[e])

    def mm(self, out_ap, pairs, reads, writes):
        n = len(pairs)
        for i, (l, r) in enumerate(pairs):
            self.op("pe", lambda e, l=l, r=r, i=i: e.matmul(out_ap, lhsT=l, rhs=r, start=(i == 0), stop=(i == n - 1)),
                    reads=reads, writes=writes, inc=True)

    def act(self, out_ap, in_ap, func, reads, writes, **kw):
        return self.op("act", lambda e: e.activation(out=out_ap, in_=in_ap, func=func, **kw), reads=reads, writes=writes)

    def tt(self, eng, out_ap, a, b, op, reads, writes):
        return self.op(eng, lambda e: e.tensor_tensor(out=out_ap, in0=a, in1=b, op=op), reads=reads, writes=writes)

    def ts(self, eng, out_ap, a, s1, s2, op0, op1, reads, writes):
        if op1 is None:
            return self.op(eng, lambda e: e.tensor_scalar(out=out_ap, in0=a, scalar1=s1, scalar2=None, op0=op0),
                           reads=reads, writes=writes)
        return self.op(eng, lambda e: e.tensor_scalar(out=out_ap, in0=a, scalar1=s1, scalar2=s2, op0=op0, op1=op1),
                       reads=reads, writes=writes)


class _Stop(Exception):
    pass


def build_program(ntp=NT, with_sample=True, stop=0, npool=5120):
    k = KB()
    try:
        _build(k, ntp, with_sample, stop, npool)
    except _Stop:
        pass
    k.finish()
    return k


def _build(k, ntp, with_sample, stop, npool):
    def ck(n):
        if stop == n:
            raise _Stop()
    nc = k.nc
    P = 128
    xloc = k.dram("xloc", [NT * 128, D], F32, "ExternalInput")
    xs = k.dram("xs", [32, D], F32, "ExternalInput")
    cs_tab = k.dram("cs_tab", [NT * 128 + 32, 64], F32, "ExternalInput")
    w_in = k.dram("w_in", [D, 5400], F32, "ExternalInput")
    vecs = k.dram("vecs", [1, 1024 + 64 + 192 + 1024 + 128], F32, "ExternalInput")
    cmask = k.dram("cmask", [128, 6 * 128], F32, "ExternalInput")
    cind = k.dram("cind", [128, 8], F32, "ExternalInput")
    st_win = k.dram("st_win", [4, 512, 256], F32, "ExternalInput")
    st_hgrn = k.dram("st_hgrn", [4, 4, 128, 128], F32, "ExternalInput")
    ccolmask = k.dram("ccolmask", [128, 6 * 128], F32, "ExternalInput")
    scr_qk = k.dram("scr_qk", [NT * 128 + 32, 1280], BF16)
    scr_gate = k.dram("scr_gate", [(NOWN + 1) * 128 + 32, 24], F32)
    scr_hT = k.dram("scr_hT", [NOWN + 2, 128, 1024], BF16)
    scr_oab = k.dram("scr_oab", [(NOWN + 1) * 128 + 32, 1024], BF16)

    o_kv = k.dram("o_kv", [3, NOWN * 128, 256], F32, "ExternalOutput")
    o_kvs = k.dram("o_kvs", [2, 32, 256], F32, "ExternalOutput")
    o_win_s = k.dram("o_win_s", [4, 512, 256], F32, "ExternalOutput")
    o_hg_p = k.dram("o_hg_p", [4, 128, 128], F32, "ExternalOutput")
    o_hg_s = k.dram("o_hg_s", [4, 4, 128, 128], F32, "ExternalOutput")

    k.push()
    wsb = k.sb([P, 8, NCOL1], BF16, "wsb")
    for kc in range(8):
        k.dma("pool", wsb[:, kc, :], w_in[kc * 128:(kc + 1) * 128, 0:NCOL1], writes=[wsb])
    g_bc = k.sb([P, D], F32, "g_bc")
    k.dma("sp", g_bc[:], vecs[0:1, 0:1024].partition_broadcast(P), writes=[g_bc])
    gain = k.sb([P, 14, 64], F32, "gain")
    for h in range(8):
        k.dma("sp", gain[:, h, :], vecs[0:1, 1024:1088].partition_broadcast(P), writes=[gain])
    for i in range(3):
        for g in range(2):
            k.dma("sp", gain[:, 8 + 2 * i + g, :], vecs[0:1, 1088 + 64 * i:1088 + 64 * i + 64].partition_broadcast(P), writes=[gain])
    k.ts("dve", gain[:, 0:8, :], gain[:, 0:8, :], 0.125, None, ALU.mult, None, [gain], [gain])
    lgt = k.sb([P, 2, 512], F32, "lgt")
    k.dma("sp", lgt[:, 0, :], vecs[0:1, 1280:1792].partition_broadcast(P), writes=[lgt])
    k.dma("sp", lgt[:, 1, :], vecs[0:1, 1792:2304].partition_broadcast(P), writes=[lgt])
    lb = k.sb([P, 512], F32, "lb")
    oml = k.sb([P, 512], F32, "oml")
    k.tt("dve", lb[:], lgt[:, 0, :], lgt[:, 1, :], ALU.subtract, [lgt], [lb])
    k.act(lb[:], lb[:], AF.Sigmoid, [lb], [lb])
    k.ts("dve", oml[:], lb[:], -1.0, 1.0, ALU.mult, ALU.add, [lb], [oml])
    gn_bc = k.sb([P, 4, 128], F32, "gn_bc")
    for h in range(4):
        k.dma("sp", gn_bc[:, h, :], vecs[0:1, 2304:2432].partition_broadcast(P), writes=[gn_bc])
    cm = k.sb([P, 6, 128], F32, "cm")
    k.dma("sp", cm[:], cmask[:].rearrange("p (a b) -> p a b", b=128), writes=[cm])
    ci = k.sb([P, 8], F32, "ci")
    k.dma("sp", ci[:], cind[:], writes=[ci])
    ident_bf = k.sb([P, 128], BF16, "ident_bf")
    k.op("dve", lambda e: e.tensor_copy(out=ident_bf[:], in_=cm[:, 0, :]), [cm], [ident_bf])
    ones_col = k.sb([P, 1], F32, "ones_col")
    k.op("dve", lambda e: e.memset(ones_col[:], 1.0), [], [ones_col])

    ck(1)
    xt = [k.sb([P, D], F32, "xt%d" % i) for i in range(2)]
    junk = k.sb([P, D], BF16, "junk")
    xn = k.sb([P, D], BF16, "xn")
    hT = k.sb([P, 8, 128], BF16, "hT")
    st4 = k.sb([P, 8], F32, "st4")
    R = k.sb([P, 14, 64], F32, "R")
    R2 = k.sb([P, 14, 64], F32, "R2")
    T1 = k.sb([P, 14, 32], F32, "T1")
    T2 = k.sb([P, 14, 32], F32, "T2")
    st14 = k.sb([P, 16], F32, "st14")
    cs = k.sb([P, 64], F32, "cs")
    kvo = k.sb([P, 3, 256], F32, "kvo")
    qkv_bf = k.sb([P, 1280], BF16, "qkv_bf")
    gate_sb = k.sb([P, 24], F32, "gate_sb")
    qb = k.sb([P, 512], F32, "qb")
    u_sb = k.sb([P, 512], F32, "u_sb")
    logf = k.sb([P, 512], F32, "logf")
    kb = k.sb([P, 512], F32, "kb")
    vb = k.sb([P, 512], BF16, "vb")
    ggb = k.sb([P, 512], BF16, "ggb")
    ex = k.sb([P, 512], F32, "ex")
    qe = k.sb([P, 512], BF16, "qe")
    ke = k.sb([P, 512], BF16, "ke")
    kd = k.sb([P, 512], BF16, "kd")
    kdm = k.sb([P, 512], BF16, "kdm")
    dec = k.sb([P, 16], F32, "dec")
    S32 = [k.sb([P, 4, 128], F32, "S32_%d" % i) for i in range(5)]
    Sbf = [k.sb([P, 4, 128], BF16, "Sbf_%d" % i) for i in range(5)]

    psT = k.ps([P, 8, 128], BF16, "psT")
    psZ = [k.ps([P, 512], F32, "psZ%d" % i) for i in range(2)]
    psH = [k.ps([P, 512], F32, "psH%d" % i) for i in range(2)]
    psS = k.ps([P, 4, 128], F32, "psS")
    psA = k.ps([P, 4, 128], F32, "psA")
    psO = k.ps([P, 4, 128], F32, "psO")
    qkT = k.sb([P, 8, 128], BF16, "qkT")
    qeTm = [k.sb([P, 4, 128], BF16, "qeTm%d" % i) for i in range(4)]
    AmT = k.sb([P, 4, 128], BF16, "AmT")
    osq = k.sb([P, 512], F32, "osq")
    st8 = k.sb([P, 8], F32, "st8")
    oab = k.sb([P, 1024], BF16, "oab")
    ccol = k.sb([P, 6, 128], F32, "ccol")
    k.dma("sp", ccol[:], ccolmask[:].rearrange("p (a b) -> p a b", b=128), writes=[ccol])
    k.op("pool", lambda e: e.memset(oab[:], 0.0), [], [oab])

    k.op("dve", lambda e: e.memset(S32[0][:], 0.0), [], [S32[0]])
    k.op("pool", lambda e: e.memset(Sbf[0][:], 0.0), [], [Sbf[0]])
    if with_sample:
        for b in range(4):
            k.dma("sp", S32[1 + b][:], st_hgrn[b].rearrange("h k v -> k h v"), writes=[S32[1 + b]])
            k.op("act", lambda e, b=b: e.copy(out=Sbf[1 + b][:], in_=S32[1 + b][:]), [S32[1 + b]], [Sbf[1 + b]])
        for b in range(4):
            k.dma("sp", o_win_s[b, 0:504, :], st_win[b, 8:512, :], writes=[o_win_s])

    ck(2)
    zi = [0]

    def next_z():
        zi[0] ^= 1
        return psZ[zi[0]]

    def proj(c0, c1):
        z = next_z()
        k.mm(z[:, 0:c1 - c0], [(hT[:, kc, :], wsb[:, kc, c0:c1]) for kc in range(8)], [hT, wsb], [z])
        return z

    def do_tile(lt, rows, xsrc, csrc, full, kv_dst, sample):
        x = xt[lt % 2]
        rs = slice(0, rows)
        k.dma("sp", x[rs, :], xsrc, writes=[x])
        k.dma("sp", cs[rs, :], csrc, writes=[cs])
        ck(31)
        k.act(junk[rs, :], x[rs, :], AF.Square, [x], [junk, st4], accum_out=st4[rs, 0:1])
        k.ts("dve", st4[rs, 1:2], st4[rs, 0:1], 1.0 / D, EPS, ALU.mult, ALU.add, [st4], [st4])
        k.act(st4[rs, 2:3], st4[rs, 1:2], AF.Sqrt, [st4], [st4])
        k.op("dve", lambda e: e.reciprocal(out=st4[rs, 3:4], in_=st4[rs, 2:3]), [st4], [st4])
        k.op("dve", lambda e: e.scalar_tensor_tensor(out=xn[rs, :], in0=x[rs, :], scalar=st4[rs, 3:4], in1=g_bc[rs, :],
                                                     op0=ALU.mult, op1=ALU.mult), [x, st4, g_bc], [xn])
        ck(32)
        for kc in range(8):
            k.op("pe", lambda e, kc=kc: e.transpose(psT[:, kc, rs], xn[rs, kc * 128:(kc + 1) * 128], ident_bf[rs, rs]),
                 [xn, ident_bf], [psT], inc=True)
        ck(33)
        k.op("act", lambda e: e.copy(out=hT[:, :, rs], in_=psT[:, :, rs]), [psT], [hT])
        if full:
            ti = (NOWN + 1) if sample else (lt - (NPRE - 1))
            k.dma("pool", scr_hT[ti, :, :].rearrange("p (a b) -> p a b", b=128)[:, :, rs], hT[:, :, rs], reads=[hT], writes=[scr_hT])
        ck(3)

        def projr(c0, c1):
            z = next_z()
            k.mm(z[rs, 0:c1 - c0], [(hT[:, kc, rs], wsb[:, kc, c0:c1]) for kc in range(8)], [hT, wsb], [z])
            return z

        if full:
            z = projr(0, 512)
            ck(41)
            k.op("act", lambda e: e.copy(out=R[rs, 0:8, :], in_=z[rs, 0:512].rearrange("p (h d) -> p h d", d=64)), [z], [R])
        ck(42)
        z = projr(512, 1024)
        zv = z[rs, 0:512].rearrange("p (a j c) -> p a j c", a=2, j=2)
        k.op("act", lambda e: e.copy(out=R[rs, 8:12, :].rearrange("p (a g) d -> p a (g d)", a=2), in_=zv[:, :, 0, :]), [z], [R])
        k.op("dve", lambda e: e.tensor_copy(out=kvo[rs, 0:2, 128:256], in_=zv[:, :, 1, :]), [z], [kvo])
        ck(43)
        z = projr(1024, 1304)
        k.op("act", lambda e: e.copy(out=R[rs, 12:14, :].rearrange("p g d -> p (g d)"), in_=z[rs, 0:128]), [z], [R])
        k.op("dve", lambda e: e.tensor_copy(out=kvo[rs, 2, 128:256], in_=z[rs, 128:256]), [z], [kvo])
        if full:
            k.act(gate_sb[rs, :], z[rs, 256:280], AF.Sigmoid, [z], [gate_sb])
        ck(4)
        h0 = 0 if full else 8
        nh = 14 - h0
        k.tt("dve", R2[rs, h0:14, :], R[rs, h0:14, :], R[rs, h0:14, :], ALU.mult, [R], [R2])
        k.op("dve", lambda e: e.tensor_reduce(out=st14[rs, h0:14], in_=R2[rs, h0:14, :], axis=AX.X, op=ALU.add), [R2], [st14])
        k.ts("dve", st14[rs, h0:14], st14[rs, h0:14], 1.0 / 64, EPS, ALU.mult, ALU.add, [st14], [st14])
        k.act(st14[rs, h0:14], st14[rs, h0:14], AF.Sqrt, [st14], [st14])
        k.op("dve", lambda e: e.reciprocal(out=st14[rs, h0:14], in_=st14[rs, h0:14]), [st14], [st14])
        k.tt("pool", R2[rs, h0:14, :], R[rs, h0:14, :], gain[rs, h0:14, :], ALU.mult, [R, gain], [R2])
        k.tt("dve", R2[rs, h0:14, :], R2[rs, h0:14, :], st14[rs, h0:14].unsqueeze(2).to_broadcast([rows, nh, 64]), ALU.mult,
             [R2, st14], [R2])
        cosb = cs[rs, 0:32].unsqueeze(1).to_broadcast([rows, nh, 32])
        sinb = cs[rs, 32:64].unsqueeze(1).to_broadcast([rows, nh, 32])
        x1 = R2[rs, h0:14, 0:32]
        x2 = R2[rs, h0:14, 32:64]
        k.tt("dve", T1[rs, h0:14, :], x1, cosb, ALU.mult, [R2, cs], [T1])
        k.tt("pool", T2[rs, h0:14, :], x2, sinb, ALU.mult, [R2, cs], [T2])
        k.tt("dve", R[rs, h0:14, 0:32], T1[rs, h0:14, :], T2[rs, h0:14, :], ALU.subtract, [T1, T2], [R])
        k.tt("dve", T1[rs, h0:14, :], x2, cosb, ALU.mult, [R2, cs], [T1])
        k.tt("pool", T2[rs, h0:14, :], x1, sinb, ALU.mult, [R2, cs], [T2])
        k.tt("dve", R[rs, h0:14, 32:64], T1[rs, h0:14, :], T2[rs, h0:14, :], ALU.add, [T1, T2], [R])
        k.op("act", lambda e: e.copy(out=kvo[rs, :, 0:128], in_=R[rs, 8:14, :].rearrange("p (a g) d -> p a (g d)", a=3)), [R], [kvo])
        if kv_dst is not None:
            for i, dst in enumerate(kv_dst):
                if dst is not None:
                    k.dma("pool", dst[0], kvo[rs, i, :], reads=[kvo], writes=[dst[1]])
        if True:
            k.op("act", lambda e: e.copy(out=qkv_bf[rs, 0:512].rearrange("p (hh g d) -> p g hh d", hh=4, g=2),
                                         in_=R[rs, 0:8, :].rearrange("p (g hh) d -> p g hh d", g=2)), [R], [qkv_bf])
            k.op("act", lambda e: e.copy(out=qkv_bf[rs, 512:896].rearrange("p (h d) -> p h d", d=64), in_=R[rs, 8:14, :]), [R], [qkv_bf])
            k.op("dve", lambda e: e.tensor_copy(out=qkv_bf[rs, 896:1280].rearrange("p (a c) -> p a c", a=3), in_=kvo[rs, :, 128:256]), [kvo], [qkv_bf])
            k.dma("pool", scr_qk[lt * 128:lt * 128 + rows, :], qkv_bf[rs, :], reads=[qkv_bf], writes=[scr_qk])
            if full:
                gi_ = (NOWN + 1) if sample else (lt - (NPRE - 1))
                k.dma("pool", scr_gate[gi_ * 128:gi_ * 128 + rows, :], gate_sb[rs, :], reads=[gate_sb], writes=[scr_gate])
        ck(5)
        if full:
            z = projr(1304, 1816)
            k.act(qb[rs, :], z[rs, :], AF.Silu, [z], [qb])
        z = projr(1816, 2328)
        k.act(u_sb[rs, :], z[rs, :], AF.Sigmoid, [z], [u_sb])
        k.tt("dve", u_sb[rs, :], u_sb[rs, :], oml[rs, :], ALU.mult, [u_sb, oml], [u_sb])
        k.tt("dve", u_sb[rs, :], u_sb[rs, :], lb[rs, :], ALU.add, [u_sb, lb], [u_sb])
        k.act(logf[rs, :], u_sb[rs, :], AF.Ln, [u_sb], [logf])
        k.ts("pool", kb[rs, :], u_sb[rs, :], -1.0, 1.0, ALU.mult, ALU.add, [u_sb], [kb])
        z = projr(2328, 2840)
        k.op("act", lambda e: e.copy(out=vb[rs, :], in_=z[rs, :]), [z], [vb])
        if full:
            z = projr(2840, 3352)
            k.act(ex[rs, :], z[rs, :], AF.Silu, [z], [ex])
            k.tt("dve", ggb[rs, :], ex[rs, :], gn_bc[rs, :, :].rearrange("p h v -> p (h v)"), ALU.mult, [ex, gn_bc], [ggb])
        ck(6)
        mi = 3 if sample else 1
        nch = 4 if sample else 2
        i0 = 2 if sample else 0
        zD = psH[0]
        k.mm(zD[rs, :], [(cm[rs, mi + 1, rs], logf[rs, :])], [cm, logf], [zD])
        k.act(ex[rs, :], zD[rs, :], AF.Exp, [zD], [ex])
        k.tt("dve", kd[rs, :], kb[rs, :], ex[rs, :], ALU.mult, [kb, ex], [kd])
        if full:
            zB = psH[1]
            k.mm(zB[rs, :], [(cm[rs, mi, rs], logf[rs, :])], [cm, logf], [zB])
            k.act(ex[rs, :], zB[rs, :], AF.Exp, [zB], [ex])
            k.tt("dve", qe[rs, :], qb[rs, :], ex[rs, :], ALU.mult, [qb, ex], [qe])
            k.act(ex[rs, :], zB[rs, :], AF.Exp, [zB], [ex], scale=-1.0)
            k.tt("dve", ke[rs, :], kb[rs, :], ex[rs, :], ALU.mult, [kb, ex], [ke])
            for h in range(4):
                k.op("pe", lambda e, h=h: e.transpose(psT[:, h, rs], qe[rs, h * 128:(h + 1) * 128], ident_bf[rs, rs]), [qe, ident_bf], [psT])
                k.op("pe", lambda e, h=h: e.transpose(psT[:, 4 + h, rs], ke[rs, h * 128:(h + 1) * 128], ident_bf[rs, rs]), [ke, ident_bf], [psT])
            k.op("act", lambda e: e.copy(out=qkT[:, :, rs], in_=psT[:, :, rs]), [psT], [qkT])
            ccb = 2 if sample else 0
            for c in range(nch):
                k.tt("dve", qeTm[c][:, :, rs], qkT[:, 0:4, rs], ccol[:, ccb + c, rs].unsqueeze(1).to_broadcast([P, 4, rows]), ALU.mult,
                     [qkT, ccol], [qeTm[c]])
            for h in range(4):
                k.mm(psA[rs, h, rs], [(qkT[:, 4 + h, rs], qkT[:, h, rs])], [qkT], [psA])
            k.tt("dve", AmT[rs, :, rs], psA[rs, :, rs], cm[rs, mi, rs].unsqueeze(1).to_broadcast([rows, 4, rows]), ALU.mult,
                 [psA, cm], [AmT])
        first_o = [True]
        for c in range(nch):
            st = S32[0] if not sample else S32[1 + c]
            sbf = Sbf[0] if not sample else Sbf[1 + c]
            ind = ci[rs, i0 + c:i0 + c + 1]
            if full:
                for h in range(4):
                    fo = first_o[0]
                    first_o[0] = False
                    k.op("pe", lambda e, h=h, fo=fo, c=c, sbf=sbf: e.matmul(psO[rs, h, :], lhsT=qeTm[c][:, h, rs], rhs=sbf[:, h, :], start=fo, stop=False,
                                                                       skip_group_check=True), [qeTm[c], sbf], [psO])
            for h in range(4):
                k.mm(psS[:, h, 0:1], [(logf[rs, h * 128:(h + 1) * 128], ind)], [logf, ci], [psS])
            k.act(dec[:, 4 * c:4 * c + 4], psS[:, :, 0], AF.Exp, [psS], [dec])
            k.ts("pool", kdm[rs, :], kd[rs, :], ind, None, ALU.mult, None, [kd, ci], [kdm])
            for h in range(4):
                k.mm(psS[:, h, :], [(kdm[rs, h * 128:(h + 1) * 128], vb[rs, h * 128:(h + 1) * 128])], [kdm, vb], [psS])
            for h in range(4):
                k.op("dve", lambda e, h=h: e.scalar_tensor_tensor(out=st[:, h, :], in0=st[:, h, :], scalar=dec[:, 4 * c + h:4 * c + h + 1],
                                                                  in1=psS[:, h, :], op0=ALU.mult, op1=ALU.add),
                     [st, dec, psS], [st])
            k.op("act", lambda e: e.copy(out=sbf[:], in_=st[:]), [st], [sbf])
        if full:
            for h in range(4):
                k.op("pe", lambda e, h=h: e.matmul(psO[rs, h, :], lhsT=AmT[rs, h, rs], rhs=vb[rs, h * 128:(h + 1) * 128], start=False, stop=True,
                                                   skip_group_check=True), [AmT, vb], [psO])
            k.act(osq[rs, :], psO[rs, :, :].rearrange("p h v -> p (h v)"), AF.Square, [psO], [osq])
            k.op("dve", lambda e: e.tensor_reduce(out=st8[rs, 0:4], in_=osq[rs, :].rearrange("p (h v) -> p h v", h=4), axis=AX.X, op=ALU.add), [osq], [st8])
            k.ts("dve", st8[rs, 0:4], st8[rs, 0:4], 1.0 / 128, EPS, ALU.mult, ALU.add, [st8], [st8])
            k.act(st8[rs, 0:4], st8[rs, 0:4], AF.Sqrt, [st8], [st8])
            k.op("dve", lambda e: e.reciprocal(out=st8[rs, 4:8], in_=st8[rs, 0:4]), [st8], [st8])
            for h in range(4):
                k.op("dve", lambda e, h=h: e.scalar_tensor_tensor(out=oab[rs, 512 + h * 128:512 + (h + 1) * 128], in0=psO[rs, h, :], scalar=st8[rs, 4 + h:5 + h],
                                                                  in1=ggb[rs, h * 128:(h + 1) * 128], op0=ALU.mult, op1=ALU.mult),
                     [psO, st8, ggb], [oab])
            ck(7)
            ti = (NOWN + 1) if sample else (lt - (NPRE - 1))
            k.dma("pool", scr_oab[ti * 128:ti * 128 + rows, 512:1024], oab[rs, 512:1024], reads=[oab], writes=[scr_oab])
            if sample and npool == 0:
                k.dma("pool", scr_oab[ti * 128:ti * 128 + rows, 0:512], oab[rs, 0:512], reads=[oab], writes=[scr_oab])

    for lt in range(NT - ntp, NT):
        own = lt >= NPRE
        full = lt >= NPRE - 1
        kv_dst = None
        if own:
            r0 = (lt - NPRE) * 128
            kv_dst = [(o_kv[i, r0:r0 + 128, :], o_kv) for i in range(3)]
        do_tile(lt, 128, xloc[lt * 128:(lt + 1) * 128, :], cs_tab[lt * 128:(lt + 1) * 128, :], full, kv_dst, False)
    k.dma("sp", o_hg_p[:].rearrange("h k v -> k h v"), S32[0][:], reads=[S32[0]], writes=[o_hg_p])
    if with_sample:
        kv_dst = [(o_kvs[0, :, :], o_kvs), (o_kvs[1, :, :], o_kvs), None]
        do_tile(NT, 32, xs[:, :], cs_tab[NT * 128:NT * 128 + 32, :], True, kv_dst, True)
        for b in range(4):
            k.dma("sp", o_win_s[b, 504:512, :], kvo[8 * b:8 * b + 8, 2, :], reads=[kvo], writes=[o_win_s])
            k.dma("sp", o_hg_s[b].rearrange("h k v -> k h v"), S32[1 + b][:], reads=[S32[1 + b]], writes=[o_hg_s])
    k.pop()
    ck(10)
    pass1b(k, ck, ntp, locals())
    ck(20)
    env = dict(locals())
    env["identf_d"] = Buf("identf", k.nc_identf)
    env.update(k.shared)
    if with_sample and npool > 0:
        pass1c(k, ck, env, npool)
    ck(25)
    env["scr_x1"] = k.dram("scr_x1", [(NOWN + 1) * 128 + 32, D], F32)
    pass2(k, ck, env)
    ck(30)
    pass3(k, ck, env)


def pass1b(k, ck, ntp, env):
    P = 128
    scr_qk, scr_gate, scr_oab = env["scr_qk"], env["scr_gate"], env["scr_oab"]
    w1bd_d = k.dram("w1bd", [128, 64 * 128], F32, "ExternalInput")
    w2bd_d = k.dram("w2bd", [128, 2 * 128], F32, "ExternalInput")
    posvec_d = k.dram("posvec", [128, 64], F32, "ExternalInput")
    cover_d = k.dram("cover", [128, 2 * 62], F32, "ExternalInput")
    cmpB_d = k.dram("cmpB", [NOWN + 1, 128, 2 * 128], F32, "ExternalInput")
    selc_d = k.dram("selc", [NOWN + 1, 128, 2 * 64], F32, "ExternalInput")
    efull_d = k.dram("efull", [64, 4096], F32, "ExternalInput")
    triB_d = k.dram("triB", [128, 2 * 128], F32, "ExternalInput")
    pfx_d = k.dram("pfx", [1, 128], F32, "ExternalInput")
    identf_d = k.dram("identf", [128, 128], F32, "ExternalInput")
    k.nc_identf = identf_d.t
    k.shared = dict(w1bd_d=w1bd_d, w2bd_d=w2bd_d, posvec_d=posvec_d)

    k.push()
    W1 = k.sb([P, 64, 128], BF16, "W1")
    for a in range(4):
        k.dma("pool", W1[:, a * 16:(a + 1) * 16, :], w1bd_d[:, a * 2048:(a + 1) * 2048].rearrange("p (a b) -> p a b", b=128), writes=[W1])
    W2 = k.sb([P, 2, 128], BF16, "W2")
    k.dma("pool", W2[:], w2bd_d[:].rearrange("p (a b) -> p a b", b=128), writes=[W2])
    posv = k.sb([P, 64], BF16, "posv")
    k.dma("pool", posv[:], posvec_d[:], writes=[posv])
    efull = k.sb([64, 4096], BF16, "efull")
    k.dma("pool", efull[:], efull_d[:], writes=[efull])
    triB = k.sb([P, 2, 4, 128], BF16, "triB")
    for hh in range(4):
        k.dma("pool", triB[:, :, hh, :], triB_d[:].rearrange("p (a b) -> p a b", b=128), writes=[triB])
    pfx = k.sb([1, 128], BF16, "pfx")
    k.dma("pool", pfx[:], pfx_d[:], writes=[pfx])
    ones_row = k.sb([1, 512], BF16, "ones_row")
    k.op("dve", lambda e: e.memset(ones_row[:], 1.0), [], [ones_row])
    identf = k.sb([P, 128], F32, "identf")
    k.dma("sp", identf[:], identf_d[:], writes=[identf])
    ident_bf = k.sb([P, 128], BF16, "ident_bf2")
    k.op("dve", lambda e: e.tensor_copy(out=ident_bf[:], in_=identf[:]), [identf], [ident_bf])
    KS = k.sb([P, 2, NT * 128], BF16, "KS")
    KC2 = k.sb([P, 2, 256], BF16, "KC2")
    VX = k.sb([P, 2, NT, 2, 65], BF16, "VX")
    kvccT = k.sb([P, 2, 256], BF16, "kvccT")
    VcX = k.sb([P, 2, 2, 127], BF16, "VcX")
    k.op("pool", lambda e: e.memset(KC2[:], 0.0), [], [KC2])
    k.op("pool", lambda e: e.memset(kvccT[:], 0.0), [], [kvccT])
    k.op("dve", lambda e: e.memset(VX[:, :, :, :, 64:65], 1.0), [], [VX])
    k.op("dve", lambda e: e.memset(VcX[:, :, :, 0:64], 0.0), [], [VcX])
    k.op("dve", lambda e: e.memset(VcX[:, :, :, 64:65], 1.0), [], [VcX])
    for g in range(2):
        k.dma("pool", VcX[:, :, g, 65:127], cover_d[:].rearrange("p (a b) -> p a b", b=62), writes=[VcX])
    qkv = [k.sb([P, 1280], BF16, "qkv%d" % i) for i in range(2)]
    qT = k.sb([P, 4, 128], BF16, "qT")
    spre = k.sb([P, 2, 8], BF16, "spre")
    posW = k.sb([P, 2], F32, "posW")
    cmpB = k.sb([P, 2, 4, 128], BF16, "cmpB_t")
    selc = k.sb([P, 2, 64], F32, "selc_t")
    gate = k.sb([P, 24], F32, "gate_t")
    PT = [k.sb([P, 512], BF16, "PT%d" % i) for i in range(2)]
    ocs = k.sb([P, 2, 4, 127], F32, "ocs")
    osw = k.sb([P, 2, 4, 65], F32, "osw")
    rc = k.sb([P, 8], F32, "rc")
    coef = k.sb([P, 8], F32, "coef")
    imp = k.sb([P, 2, 64], F32, "imp")
    sc2 = k.sb([P, 64], F32, "sc2")
    top = k.sb([P, 16], F32, "top")
    nsel = k.sb([P, 2, 64], F32, "nsel")
    selT = k.sb([64, 2, 4, 128], BF16, "selT")
    oacc = k.sb([P, 8, 64], F32, "oacc")
    oa_bf = k.sb([P, 512], BF16, "oa_bf")
    psT = k.ps([P, 8, 128], BF16, "psTb")
    psZ = [k.ps([P, 512], F32, "psZb%d" % i) for i in range(2)]
    psV = [k.ps([P, 512], F32, "psVb%d" % i) for i in range(2)]
    psC = k.ps([P, 512], F32, "psCb")

    for j in range(2):
        for idx in range(32):
            k.op("pe", lambda e, j=j, idx=idx: e.matmul(psC[:, j:j + 1], lhsT=W1[:, j * 32 + idx, :], rhs=posv[:, j * 32 + idx:j * 32 + idx + 1],
                                                        start=(idx == 0), stop=(idx == 31)), [W1, posv], [psC])
    k.op("act", lambda e: e.copy(out=posW[:], in_=psC[:, 0:2]), [psC], [posW])
    ck(11)
    zi = [0]
    pi = [0]

    def attn_block(g, k_lhsT, biases, v_rhs, acc_ap, ncol, first):
        zi[0] ^= 1
        S = psZ[zi[0]]
        pairs = [(k_lhsT, qT[g * 64:(g + 1) * 64, :, :].rearrange("p a b -> p (a b)"))] + biases
        n = len(pairs)
        for i, (l, r) in enumerate(pairs):
            k.op("pe", lambda e, l=l, r=r, i=i: e.matmul(S[:, :], lhsT=l, rhs=r, start=(i == 0), stop=(i == n - 1)),
                 [KS, kvccT, qT, efull, selT, ident_bf, triB, cmpB, pfx, ones_row], [S])
        pi[0] ^= 1
        pt = PT[pi[0]]
        k.act(pt[:], S[:], AF.Exp, [S], [pt])
        for hh in range(4):
            k.op("pe", lambda e, hh=hh: e.matmul(acc_ap[:, hh, 0:ncol], lhsT=pt[:, hh * 128:(hh + 1) * 128], rhs=v_rhs,
                                                 start=(first and hh == 0), stop=False, skip_group_check=True), [pt, VX, VcX], [acc_buf[0]])

    acc_buf = [None]

    for lt in range(NT - ntp, NT):
        full = lt >= NPRE - 1
        i_own = lt - (NPRE - 1)
        qk = qkv[lt % 2]
        k.dma("sp", qk[:], scr_qk[lt * 128:(lt + 1) * 128, :], reads=[scr_qk], writes=[qk])
        v14 = qk[:, 0:896].rearrange("p (h d) -> p h d", d=64)
        if full:
            for hh in range(4):
                k.op("pe", lambda e, hh=hh: e.transpose(psT[:, hh, :], qk[:, hh * 128:(hh + 1) * 128], ident_bf[:]), [qk, ident_bf], [psT])
        k.op("pe", lambda e: e.transpose(psT[:, 4, :], qk[:, 512:640], ident_bf[:]), [qk, ident_bf], [psT])
        k.op("pe", lambda e: e.transpose(psT[:, 5, :], qk[:, 640:768], ident_bf[:]), [qk, ident_bf], [psT])
        k.op("pe", lambda e: e.transpose(psT[:, 6, :], qk[:, 768:896], ident_bf[:]), [qk, ident_bf], [psT])
        k.op("pe", lambda e: e.transpose(psT[:, 7, :], qk[:, 896:1024], ident_bf[:]), [qk, ident_bf], [psT])
        if full:
            k.op("act", lambda e: e.copy(out=qT[:], in_=psT[:, 0:4, :]), [psT], [qT])
        k.op("act", lambda e: e.copy(out=KS[:, :, lt * 128:(lt + 1) * 128], in_=psT[:, 5:7, :]), [psT], [KS])
        k.op("dve", lambda e: e.tensor_copy(out=KC2[:, :, 0:128], in_=KC2[:, :, 128:256]), [KC2], [KC2])
        k.op("dve", lambda e: e.tensor_copy(out=KC2[:, :, 128:256], in_=psT[:, 4:8:3, :]), [psT], [KC2])
        k.op("pool", lambda e: e.tensor_copy(out=VX[:, :, lt, :, 0:64], in_=qk[:, 1024:1280].rearrange("p (a g d) -> p a g d", a=2, g=2)), [qk], [VX])
        c0 = max(8 * lt - 1, 0)
        nb = 8 * lt + 7 - c0
        for j in range(2):
            n = 0
            for r in range(2):
                for i16 in range(16):
                    st_col = 16 * (c0 + r) + i16 - 128 * lt + 128
                    k.op("pe", lambda e, j=j, r=r, i16=i16, st_col=st_col, n=n: e.matmul(
                        psC[:, 8 * j:8 * j + nb], lhsT=W1[:, j * 32 + r * 16 + i16, :], rhs=KC2[:, j, st_col:st_col + 16 * (nb - 1) + 1:16],
                        start=(n == 0), stop=(n == 31)), [W1, KC2], [psC])
                    n += 1
            k.act(spre[:, j, 0:nb], psC[:, 8 * j:8 * j + nb], AF.Silu, [psC, posW], [spre], bias=posW[:, j:j + 1])
        for j in range(2):
            k.mm(psC[:, 16 + 8 * j:16 + 8 * j + nb], [(W2[:, j, :], spre[:, j, 0:nb])], [W2, spre], [psC])
        k.op("act", lambda e: e.copy(out=kvccT[:, :, c0:c0 + nb], in_=psC[:, 16:32].rearrange("p (j c) -> p j c", j=2)[:, :, 0:nb]), [psC], [kvccT])
        if not full:
            continue
        ck(12)
        k.dma("sp", selc[:], selc_d[i_own].rearrange("p (a b) -> p a b", b=64), writes=[selc])
        k.dma("sp", gate[:], scr_gate[i_own * 128:(i_own + 1) * 128, :], reads=[scr_gate], writes=[gate])
        for hh in range(4):
            k.dma("pool", cmpB[:, :, hh, :], cmpB_d[i_own].rearrange("p (a b) -> p a b", b=128), writes=[cmpB])
        nchv = 1 if lt <= 15 else 2
        for ch in range(nchv):
            k.op("pe", lambda e, ch=ch: e.transpose(psT[:, ch, :], kvccT[:, 1, ch * 128:(ch + 1) * 128], ident_bf[:]), [kvccT, ident_bf], [psT])
        k.op("act", lambda e: e.copy(out=VcX[:, 0:nchv, :, 0:64], in_=psT[:, 0:nchv, :].rearrange("p c (g d) -> p c g d", g=2)), [psT], [VcX])
        for g in range(2):
            acc_buf[0] = psV[g]
            acc = psV[g][:, 0:508].rearrange("p (h c) -> p h c", c=127)
            for ch in range(nchv):
                attn_block(g, kvccT[g * 64:(g + 1) * 64, 0, ch * 128:(ch + 1) * 128],
                           [(ident_bf[:], cmpB[:, ch, :, :].rearrange("p a b -> p (a b)"))],
                           VcX[:, ch, g, :], acc, 127, ch == 0)
            k.op("act", lambda e, g=g, acc=acc: e.copy(out=ocs[:, g, :, :], in_=acc), [psV[g]], [ocs])
        ck(13)
        k.ts("dve", rc[:], ocs[:, :, :, 64].rearrange("p g h -> p (g h)"), 1e-30, None, ALU.max, None, [ocs], [rc])
        k.op("dve", lambda e: e.reciprocal(out=rc[:], in_=rc[:]), [rc], [rc])
        gv = gate[:].rearrange("p (h b) -> p h b", b=3)
        k.tt("dve", coef[:], rc[:], gv[:, :, 0], ALU.mult, [rc, gate], [coef])
        for h in range(8):
            k.ts("dve", oacc[:, h, :], ocs[:, h // 4, h % 4, 0:64], coef[:, h:h + 1], None, ALU.mult, None, [ocs, coef], [oacc])
        for g in range(2):
            k.ts("dve", imp[:, g, 0:62], ocs[:, g, 0, 65:127], rc[:, 4 * g:4 * g + 1], None, ALU.mult, None, [ocs, rc], [imp])
            for hh in range(1, 4):
                k.op("dve", lambda e, g=g, hh=hh: e.scalar_tensor_tensor(out=imp[:, g, 0:62], in0=ocs[:, g, hh, 65:127], scalar=rc[:, 4 * g + hh:4 * g + hh + 1],
                                                                         in1=imp[:, g, 0:62], op0=ALU.mult, op1=ALU.add), [ocs, rc, imp], [imp])
        k.op("dve", lambda e: e.memset(imp[:, :, 62:64], 0.0), [], [imp])
        for g in range(2):
            k.tt("dve", imp[:, g, :], imp[:, g, :], selc[:, 0, :], ALU.mult, [imp, selc], [imp])
            k.tt("dve", imp[:, g, :], imp[:, g, :], selc[:, 1, :], ALU.add, [imp, selc], [imp])
            k.op("dve", lambda e, g=g: e.max(out=top[:, 0:8], in_=imp[:, g, :]), [imp], [top])
            k.op("dve", lambda e, g=g: e.match_replace(out=sc2[:], in_to_replace=top[:, 0:8], in_values=imp[:, g, :], imm_value=-1e30), [imp, top], [sc2])
            k.op("dve", lambda e: e.max(out=top[:, 8:16], in_=sc2[:]), [sc2], [top])
            k.ts("dve", top[:, 15:16], top[:, 15:16], -0.5, None, ALU.max, None, [top], [top])
            k.ts("dve", nsel[:, g, :], imp[:, g, :], top[:, 15:16], None, ALU.is_ge, None, [imp, top], [nsel])
            k.ts("dve", nsel[:, g, :], nsel[:, g, :], 30000.0, -30000.0, ALU.mult, ALU.add, [nsel], [nsel])
            k.mm(psC[0:64, 64 + 128 * g:64 + 128 * (g + 1)], [(nsel[:, g, :], identf[:])], [nsel, identf], [psC])
        for hh in range(4):
            k.op("act", lambda e, hh=hh: e.copy(out=selT[:, :, hh, :], in_=psC[0:64, 64:320].rearrange("p (g t) -> p g t", g=2)), [psC], [selT])
        ck(14)
        for g in range(2):
            acc_buf[0] = psV[g]
            acc = psV[g][:, 0:260].rearrange("p (h c) -> p h c", c=65)
            for kb in range(0, lt + 1):
                biases = [(efull[:, kb * 128:(kb + 1) * 128], selT[:, g, :, :].rearrange("p a b -> p (a b)"))]
                if kb == lt:
                    biases.append((ident_bf[:], triB[:, 0, :, :].rearrange("p a b -> p (a b)")))
                attn_block(g, KS[g * 64:(g + 1) * 64, 0, kb * 128:(kb + 1) * 128], biases, VX[:, 0, kb, g, :], acc, 65, kb == 0)
            k.op("act", lambda e, g=g, acc=acc: e.copy(out=osw[:, g, :, :], in_=acc), [psV[g]], [osw])
        for br in (1, 2):
            if br == 2:
                for g in range(2):
                    acc_buf[0] = psV[g]
                    acc = psV[g][:, 0:260].rearrange("p (h c) -> p h c", c=65)
                    kb0 = max(lt - 4, 0)
                    for kb in range(kb0, lt + 1):
                        biases = []
                        if kb == lt:
                            biases.append((ident_bf[:], triB[:, 0, :, :].rearrange("p a b -> p (a b)")))
                        if kb == lt - 4:
                            biases.append((ident_bf[:], triB[:, 1, :, :].rearrange("p a b -> p (a b)")))
                        if kb < NPRE:
                            biases.append((pfx[:], ones_row[:]))
                        attn_block(g, KS[g * 64:(g + 1) * 64, 1, kb * 128:(kb + 1) * 128], biases, VX[:, 1, kb, g, :], acc, 65, kb == kb0)
                    k.op("act", lambda e, g=g, acc=acc: e.copy(out=osw[:, g, :, :], in_=acc), [psV[g]], [osw])
            k.ts("dve", rc[:], osw[:, :, :, 64].rearrange("p g h -> p (g h)"), 1e-30, None, ALU.max, None, [osw], [rc])
            k.op("dve", lambda e: e.reciprocal(out=rc[:], in_=rc[:]), [rc], [rc])
            k.tt("dve", coef[:], rc[:], gv[:, :, br], ALU.mult, [rc, gate], [coef])
            for h in range(8):
                k.op("dve", lambda e, h=h: e.scalar_tensor_tensor(out=oacc[:, h, :], in0=osw[:, h // 4, h % 4, 0:64], scalar=coef[:, h:h + 1],
                                                                  in1=oacc[:, h, :], op0=ALU.mult, op1=ALU.add), [osw, coef, oacc], [oacc])
        k.op("act", lambda e: e.copy(out=oa_bf[:], in_=oacc[:].rearrange("p h d -> p (h d)")), [oacc], [oa_bf])
        k.dma("pool", scr_oab[i_own * 128:(i_own + 1) * 128, 0:512], oa_bf[:], reads=[oa_bf], writes=[scr_oab])
    k.pop()


def pass2(k, ck, env):
    P = 128
    xloc, xs, w_in, scr_hT, scr_oab = env["xloc"], env["xs"], env["w_in"], env["scr_hT"], env["scr_oab"]
    w_br_d = k.dram("w_branch", [D, D], F32, "ExternalInput")
    w_out_d = k.dram("w_out", [D, D], F32, "ExternalInput")
    identf_d = env["identf_d"]
    scr_x1 = env["scr_x1"]
    k.push()
    wmg = k.sb([P, 8, 2048], BF16, "wmg")
    wbr = k.sb([P, 8, 1024], BF16, "wbr")
    wo = k.sb([P, 8, 1024], BF16, "wo")
    for kc in range(8):
        k.dma("pool", wmg[:, kc, :], w_in[kc * 128:(kc + 1) * 128, NCOL1:5400], writes=[wmg])
        k.dma("pool", wbr[:, kc, :], w_br_d[kc * 128:(kc + 1) * 128, :], writes=[wbr])
        k.dma("pool", wo[:, kc, :], w_out_d[kc * 128:(kc + 1) * 128, :], writes=[wo])
    ident_bf = k.sb([P, 128], BF16, "ident_bf3")
    k.dma("pool", ident_bf[:], identf_d[:], writes=[ident_bf])
    xt = [k.sb([P, D], F32, "x2_%d" % i) for i in range(2)]
    hT = [k.sb([P, 8, 128], BF16, "hT2_%d" % i) for i in range(2)]
    oab = [k.sb([P, 1024], BF16, "oab2_%d" % i) for i in range(2)]
    mab = k.sb([P, 2048], BF16, "mab")
    oT = k.sb([P, 8, 128], BF16, "oT")
    m1 = k.sb([P, 1024], F32, "m1")
    m2 = k.sb([P, 1024], F32, "m2")
    mbf = k.sb([P, 1024], BF16, "mbf")
    mT = k.sb([P, 8, 128], BF16, "mT")
    x1 = k.sb([P, 1024], F32, "x1")
    psT = k.ps([P, 8, 128], BF16, "psT2")
    psZ = [k.ps([P, 512], F32, "psZ2_%d" % i) for i in range(3)]
    zi = [0]

    def nz():
        zi[0] = (zi[0] + 1) % 3
        return psZ[zi[0]]

    for ti in range(NOWN + 2):
        sample = ti == NOWN + 1
        rows = 32 if sample else 128
        rs = slice(0, rows)
        x = xt[ti % 2]
        h = hT[ti % 2]
        ob = oab[ti % 2]
        xsrc = xs[:, :] if sample else xloc[(NPRE - 1 + ti) * 128:(NPRE + ti) * 128, :]
        k.dma("sp", x[rs, :], xsrc, writes=[x])
        k.dma("sp", h[:, :, rs], scr_hT[ti, :, :].rearrange("p (a b) -> p a b", b=128)[:, :, rs], reads=[scr_hT], writes=[h])
        k.dma("sp", ob[rs, :], scr_oab[ti * 128:ti * 128 + rows, :], reads=[scr_oab], writes=[ob])
        for c in range(4):
            z = nz()
            k.mm(z[rs, :], [(h[:, kc, rs], wmg[:, kc, c * 512:(c + 1) * 512]) for kc in range(8)], [h, wmg], [z])
            k.act(mab[rs, c * 512:(c + 1) * 512], z[rs, :], AF.Sigmoid, [z], [mab])
        for kc in range(8):
            k.op("pe", lambda e, kc=kc: e.transpose(psT[:, kc, rs], ob[rs, kc * 128:(kc + 1) * 128], ident_bf[rs, rs]), [ob, ident_bf], [psT])
        k.op("act", lambda e: e.copy(out=oT[:, :, rs], in_=psT[:, :, rs]), [psT], [oT])
        for c in range(2):
            z = nz()
            k.mm(z[rs, :], [(oT[:, kc, rs], wbr[:, kc, c * 512:(c + 1) * 512]) for kc in range(4)], [oT, wbr], [z])
            k.tt("dve", m1[rs, c * 512:(c + 1) * 512], z[rs, :], mab[rs, c * 512:(c + 1) * 512], ALU.mult, [z, mab], [m1])
            z = nz()
            k.mm(z[rs, :], [(oT[:, kc, rs], wbr[:, kc, c * 512:(c + 1) * 512]) for kc in range(4, 8)], [oT, wbr], [z])
            k.tt("dve", m2[rs, c * 512:(c + 1) * 512], z[rs, :], mab[rs, 1024 + c * 512:1024 + (c + 1) * 512], ALU.mult, [z, mab], [m2])
        k.tt("pool", mbf[rs, :], m1[rs, :], m2[rs, :], ALU.add, [m1, m2], [mbf])
        for kc in range(8):
            k.op("pe", lambda e, kc=kc: e.transpose(psT[:, kc, rs], mbf[rs, kc * 128:(kc + 1) * 128], ident_bf[rs, rs]), [mbf, ident_bf], [psT])
        k.op("act", lambda e: e.copy(out=mT[:, :, rs], in_=psT[:, :, rs]), [psT], [mT])
        for c in range(2):
            z = nz()
            k.mm(z[rs, :], [(mT[:, kc, rs], wo[:, kc, c * 512:(c + 1) * 512]) for kc in range(8)], [mT, wo], [z])
            k.tt("dve", x1[rs, c * 512:(c + 1) * 512], z[rs, :], x[rs, c * 512:(c + 1) * 512], ALU.add, [z, x], [x1])
        k.dma("pool", scr_x1[ti * 128:ti * 128 + rows, :], x1[rs, :], reads=[x1], writes=[scr_x1])
    k.pop()


def pass3(k, ck, env):
    P = 128
    scr_x1, identf_d = env["scr_x1"], env["identf_d"]
    f1_d = k.dram("ffn_w_in", [D, 5632], F32, "ExternalInput")
    f2_d = k.dram("ffn_w_out", [2816, D], F32, "ExternalInput")
    fvec_d = k.dram("fvec", [1, 1024], F32, "ExternalInput")
    cw_d = k.dram("convw", [128, 22 * 4], F32, "ExternalInput")
    cst_d = k.dram("convst", [128, 22 * 8], F32, "ExternalInput")
    o_y = k.dram("o_y", [NOWN * 128, D], F32, "ExternalOutput")
    o_ys = k.dram("o_ys", [32, D], F32, "ExternalOutput")
    o_cp = k.dram("o_cp", [128, 22 * 2], F32, "ExternalOutput")
    o_cs = k.dram("o_cs", [128, 22 * 8], F32, "ExternalOutput")
    k.push()
    wf1 = k.sb([P, 8, 5632], BF16, "wf1")
    wf2 = k.sb([P, 22, 1024], BF16, "wf2")
    for kc in range(8):
        k.dma("pool", wf1[:, kc, :], f1_d[kc * 128:(kc + 1) * 128, :], writes=[wf1])
    for fc in range(22):
        k.dma("pool", wf2[:, fc, :], f2_d[fc * 128:(fc + 1) * 128, :], writes=[wf2])
    ident_bf = k.sb([P, 128], BF16, "ident_bf4")
    k.dma("pool", ident_bf[:], identf_d[:], writes=[ident_bf])
    g2 = k.sb([P, D], F32, "g2")
    k.dma("sp", g2[:], fvec_d[0:1, :].partition_broadcast(P), writes=[g2])
    cw = k.sb([P, 22, 4], F32, "cw")
    k.dma("sp", cw[:], cw_d[:].rearrange("p (a b) -> p a b", b=4), writes=[cw])
    x1 = [k.sb([P, D], F32, "x3_%d" % i) for i in range(2)]
    junk = k.sb([P, D], BF16, "junk3")
    h2 = k.sb([P, D], BF16, "h2")
    h2T = k.sb([P, 8, 128], BF16, "h2T")
    st4 = k.sb([P, 8], F32, "st4_3")
    aT = k.sb([P, 22, 130], F32, "aT")
    t1 = k.sb([P, 11, 128], F32, "t1")
    t2 = k.sb([P, 11, 128], F32, "t2")
    gT = k.sb([P, 22, 128], BF16, "gT")
    y = k.sb([P, D], F32, "y")
    psT = k.ps([P, 8, 128], BF16, "psT3")
    psZ = [k.ps([P, 512], F32, "psZ3_%d" % i) for i in range(3)]
    zi = [0]

    def nz():
        zi[0] = (zi[0] + 1) % 3
        return psZ[zi[0]]

    k.op("dve", lambda e: e.memset(aT[:], 0.0), [], [aT])
    for ti in range(NOWN + 2):
        sample = ti == NOWN + 1
        rows = 32 if sample else 128
        rs = slice(0, rows)
        nbat, T = (4, 8) if sample else (1, 128)
        x = x1[ti % 2]
        k.dma("sp", x[rs, :], scr_x1[ti * 128:ti * 128 + rows, :], reads=[scr_x1], writes=[x])
        k.act(junk[rs, :], x[rs, :], AF.Square, [x], [junk, st4], accum_out=st4[rs, 0:1])
        k.ts("dve", st4[rs, 1:2], st4[rs, 0:1], 1.0 / D, EPS, ALU.mult, ALU.add, [st4], [st4])
        k.act(st4[rs, 2:3], st4[rs, 1:2], AF.Sqrt, [st4], [st4])
        k.op("dve", lambda e: e.reciprocal(out=st4[rs, 3:4], in_=st4[rs, 2:3]), [st4], [st4])
        k.op("dve", lambda e: e.scalar_tensor_tensor(out=h2[rs, :], in0=x[rs, :], scalar=st4[rs, 3:4], in1=g2[rs, :],
                                                     op0=ALU.mult, op1=ALU.mult), [x, st4, g2], [h2])
        for kc in range(8):
            k.op("pe", lambda e, kc=kc: e.transpose(psT[:, kc, rs], h2[rs, kc * 128:(kc + 1) * 128], ident_bf[rs, rs]), [h2, ident_bf], [psT])
        k.op("act", lambda e: e.copy(out=h2T[:, :, rs], in_=psT[:, :, rs]), [psT], [h2T])
        av = aT[:, :, 0:nbat * (T + 2)].rearrange("p f (b t) -> p f b t", b=nbat)
        if sample:
            for b in range(4):
                k.dma("sp", av[:, :, b, 0:2], cst_d[:].rearrange("p (f b j) -> p f b j", f=22, b=4)[:, :, b, :], writes=[aT])
        for f0 in range(0, 22, 4):
            n = min(4, 22 - f0)
            z = nz()
            for j in range(n):
                fc = f0 + j
                k.mm(z[:, j * 128:j * 128 + rows], [(wf1[:, kc, fc * 128:(fc + 1) * 128], h2T[:, kc, rs]) for kc in range(8)], [wf1, h2T], [z])
            k.op("act", lambda e, f0=f0, n=n, z=z: e.copy(out=av[:, f0:f0 + n, :, 2:2 + T],
                                                       in_=z[:, 0:n * 128].rearrange("p (f t) -> p f t", t=128)[:, :, 0:rows].rearrange("p f (b t) -> p f b t", b=nbat)),
                 [z], [aT])
        for hf in range(2):
            fs = slice(hf * 11, hf * 11 + 11)
            tv1 = t1[:, :, 0:rows].rearrange("p f (b t) -> p f b t", b=nbat)
            tv2 = t2[:, :, 0:rows].rearrange("p f (b t) -> p f b t", b=nbat)

            def wb(j):
                return cw[:, fs, j:j + 1].unsqueeze(3).to_broadcast([P, 11, nbat, T])
            k.tt("dve", tv1, av[:, fs, :, 2:2 + T], wb(2), ALU.mult, [aT, cw], [t1])
            k.tt("pool", tv2, av[:, fs, :, 1:1 + T], wb(1), ALU.mult, [aT, cw], [t2])
            k.tt("dve", tv1, tv1, tv2, ALU.add, [t1, t2], [t1])
            k.tt("pool", tv2, av[:, fs, :, 0:T], wb(0), ALU.mult, [aT, cw], [t2])
            k.tt("dve", tv1, tv1, tv2, ALU.add, [t1, t2], [t1])
            k.tt("dve", tv1, tv1, wb(3), ALU.add, [t1, cw], [t1])
            k.act(t1[:, :, 0:rows], t1[:, :, 0:rows], AF.Silu, [t1], [t1])
            for f0 in range(hf * 11, hf * 11 + 11, 4):
                n = min(4, hf * 11 + 11 - f0)
                z = nz()
                for j in range(n):
                    fc = f0 + j
                    k.mm(z[:, j * 128:j * 128 + rows], [(wf1[:, kc, 2816 + fc * 128:2816 + (fc + 1) * 128], h2T[:, kc, rs]) for kc in range(8)], [wf1, h2T], [z])
                k.tt("dve", gT[:, f0:f0 + n, 0:rows], t1[:, f0 - hf * 11:f0 - hf * 11 + n, 0:rows],
                     z[:, 0:n * 128].rearrange("p (f t) -> p f t", t=128)[:, :, 0:rows], ALU.mult, [t1, z], [gT])
        for c in range(2):
            z = nz()
            k.mm(z[rs, :], [(gT[:, fc, rs], wf2[:, fc, c * 512:(c + 1) * 512]) for fc in range(22)], [gT, wf2], [z])
            k.tt("dve", y[rs, c * 512:(c + 1) * 512], z[rs, :], x[rs, c * 512:(c + 1) * 512], ALU.add, [z, x], [y])
        if sample:
            k.dma("pool", o_ys[:, :], y[rs, :], reads=[y], writes=[o_ys])
            for b in range(4):
                k.dma("pool", o_cs[:].rearrange("p (f b j) -> p f b j", f=22, b=4)[:, :, b, :], av[:, :, b, T:T + 2], reads=[aT], writes=[o_cs])
        else:
            if ti >= 1:
                k.dma("pool", o_y[(ti - 1) * 128:ti * 128, :], y[rs, :], reads=[y], writes=[o_y])
            if ti == NOWN:
                k.dma("pool", o_cp[:].rearrange("p (f j) -> p f j", j=2), aT[:, :, 128:130], reads=[aT], writes=[o_cp])
            k.op("pool", lambda e: e.tensor_copy(out=aT[:, :, 0:2], in_=aT[:, :, 128:130]), [aT], [aT])
    k.pop()


def pass1c(k, ck, env, npool):
    P = 128
    nc = k.nc
    scr_qk, scr_gate, scr_oab = env["scr_qk"], env["scr_gate"], env["scr_oab"]
    st_win = env["st_win"]
    identf_d = env["identf_d"]
    cpool = k.dram("cache_cmp", [npool, 128, 256], F32, "ExternalInput")
    spool = k.dram("cache_slc", [npool, 128, 256], F32, "ExternalInput")
    ptab_d = k.dram("ptab", [4, 128], I32, "ExternalInput")
    selcS_d = k.dram("selcS", [8, 2 * 384], F32, "ExternalInput")
    selm_d = k.dram("selm", [32, 8 + 16 * 32], F32, "ExternalInput")
    rep_d = k.dram("rep", [8, 32], F32, "ExternalInput")
    ef_d = k.dram("efull128", [128, 8192], F32, "ExternalInput")
    triS_d = k.dram("triBs", [128, 2 * 32], F32, "ExternalInput")
    w1bd_d, w2bd_d, posvec_d = env["w1bd_d"], env["w2bd_d"], env["posvec_d"]
    k.push()
    W1 = k.sb([P, 64, 128], BF16, "W1c")
    for a in range(4):
        k.dma("pool", W1[:, a * 16:(a + 1) * 16, :], w1bd_d[:, a * 2048:(a + 1) * 2048].rearrange("p (a b) -> p a b", b=128), writes=[W1])
    W2 = k.sb([P, 2, 128], BF16, "W2c")
    k.dma("pool", W2[:], w2bd_d[:].rearrange("p (a b) -> p a b", b=128), writes=[W2])
    posv = k.sb([P, 64], BF16, "posvc")
    k.dma("pool", posv[:], posvec_d[:], writes=[posv])
    ef = k.sb([P, 8192], BF16, "ef128")
    k.dma("pool", ef[:], ef_d[:], writes=[ef])
    triS = k.sb([P, 2, 32], BF16, "triS")
    k.dma("pool", triS[:], triS_d[:].rearrange("p (a b) -> p a b", b=32), writes=[triS])
    identf = k.sb([P, 128], F32, "identfc")
    k.dma("sp", identf[:], identf_d[:], writes=[identf])
    ident_bf = k.sb([P, 128], BF16, "identbc")
    k.op("dve", lambda e: e.tensor_copy(out=ident_bf[:], in_=identf[:]), [identf], [ident_bf])
    selcS = k.sb([8, 2, 384], F32, "selcS")
    k.dma("sp", selcS[:], selcS_d[:].rearrange("p (a b) -> p a b", b=384), writes=[selcS])
    selm = k.sb([32, 8 + 512], F32, "selm")
    k.dma("sp", selm[:], selm_d[:], writes=[selm])
    rep = k.sb([8, 32], F32, "rep")
    k.dma("sp", rep[:], rep_d[:], writes=[rep])
    iot_d = k.dram("iot", [128, 1], F32, "ExternalInput")
    ptb = k.sb([P, 512], I32, "ptb")
    for b in range(4):
        k.dma("sp", ptb[:, b * 128:(b + 1) * 128], ptab_d[b:b + 1, :].partition_broadcast(P), writes=[ptb])
    io = k.sb([P, 1], F32, "iotc")
    k.dma("sp", io[:], iot_d[:], writes=[io])
    idxf = k.sb([P, 512], F32, "idxf")
    pidx = k.sb([P, 512], I32, "pidx")
    k.op("dve", lambda e: e.tensor_copy(out=idxf[:], in_=ptb[:]), [ptb], [idxf])
    k.op("dve", lambda e: e.tensor_scalar(out=idxf[:], in0=idxf[:], scalar1=128.0, scalar2=io[:, 0:1], op0=ALU.mult, op1=ALU.add), [idxf, io], [idxf])
    k.op("dve", lambda e: e.tensor_copy(out=pidx[:], in_=idxf[:]), [idxf], [pidx])
    KCb = k.sb([P, 2, 16384], BF16, "KCb")
    KSb = k.sb([P, 16384 + 128], BF16, "KSb")
    VXb = k.sb([P, 129, 2, 65], BF16, "VXb")
    KWb = k.sb([P, 640], BF16, "KWb")
    VWb = k.sb([P, 5, 2, 65], BF16, "VWb")
    kvc = k.sb([P, 2, 1024], BF16, "kvc")
    vcb = k.sb([P, 8, 128], BF16, "vcb")
    k.op("pool", lambda e: e.memset(KSb[:, 16384:16512], 0.0), [], [KSb])
    k.op("pool", lambda e: e.memset(KWb[:, 512:640], 0.0), [], [KWb])
    k.op("pool", lambda e: e.memset(VXb[:, :, :, 64:65], 1.0), [], [VXb])
    k.op("pool", lambda e: e.memset(VXb[:, 128, :, 0:64], 0.0), [], [VXb])
    k.op("pool", lambda e: e.memset(VWb[:, :, :, 64:65], 1.0), [], [VWb])
    k.op("pool", lambda e: e.memset(VWb[:, 4, :, 0:64], 0.0), [], [VWb])
    k.op("pool", lambda e: e.memset(kvc[:], 0.0), [], [kvc])
    pg = [k.sb([P, 2, 256], F32, "pg%d" % i) for i in range(2)]
    qs = k.sb([8, 1280], BF16, "qs")
    qTs = k.sb([P, 4, 8], BF16, "qTs")
    spre = k.sb([P, 2, 128], BF16, "sprec")
    posW = k.sb([P, 2], F32, "posWc")
    pS = k.sb([32, 1024], F32, "pS")
    pbf = k.sb([32, 1024], BF16, "pbf")
    pTs = k.sb([P, 8, 32], BF16, "pTs")
    st = k.sb([32, 8], F32, "stc")
    impr = k.sb([32, 256], F32, "impr")
    sc = k.sb([8, 2, 384], F32, "scc")
    sc2 = k.sb([8, 384], F32, "sc2c")
    top = k.sb([8, 16], F32, "topc")
    selTs = k.sb([P, 3, 2, 32], BF16, "selTs")
    PTs = [k.sb([P, 32], BF16, "PTs%d" % i) for i in range(2)]
    obr = k.sb([32, 3, 2, 65], F32, "obr")
    gate_r = k.sb([32, 2, 3], F32, "gate_r")
    rcs = k.sb([32, 8], F32, "rcs")
    ofin = k.sb([32, 2, 64], F32, "ofin")
    oas = k.sb([32, 512], BF16, "oas")
    psT = k.ps([P, 8, 128], BF16, "psTc")
    psF = k.ps([P, 4, 128], F32, "psFc")
    psS = k.ps([P, 1024], F32, "psSc")
    psC = k.ps([P, 512], F32, "psCc")
    psA = k.ps([P, 512], F32, "psAc")
    psO = k.ps([P, 512], F32, "psOc")
    for j in range(2):
        for idx in range(32):
            k.op("pe", lambda e, j=j, idx=idx: e.matmul(psC[:, j:j + 1], lhsT=W1[:, j * 32 + idx, :], rhs=posv[:, j * 32 + idx:j * 32 + idx + 1],
                                                        start=(idx == 0), stop=(idx == 31)), [W1, posv], [psC])
    k.op("act", lambda e: e.copy(out=posW[:], in_=psC[:, 0:2]), [psC], [posW])
    k.op("dve", lambda e: e.memset(pS[:], 0.0), [], [pS])
    first_o = [True]
    rn = [0]
    for b in range(4):
        r0 = NT * 128 + 8 * b
        k.dma("sp", qs[:], scr_qk[r0:r0 + 8, :], reads=[scr_qk], writes=[qs])
        for hh in range(4):
            k.dma("sp", gate_r[hh * 8:hh * 8 + 8, :, :],
                  scr_gate[NOWN * 128 + 128 + 8 * b:NOWN * 128 + 128 + 8 * b + 8, :].rearrange("p (g x) -> p g x", g=2)[:, :, hh * 3:hh * 3 + 3],
                  reads=[scr_gate], writes=[gate_r])
        for hh in range(4):
            k.op("pe", lambda e, hh=hh: e.transpose(psT[:, hh, 0:8], qs[:, hh * 128:(hh + 1) * 128], ident_bf[0:8, 0:8]), [qs, ident_bf], [psT])
        k.op("pe", lambda e: e.transpose(psT[:, 4, 0:8], qs[:, 640:768], ident_bf[0:8, 0:8]), [qs, ident_bf], [psT])
        k.op("pe", lambda e: e.transpose(psT[:, 5, 0:8], qs[:, 768:896], ident_bf[0:8, 0:8]), [qs, ident_bf], [psT])
        k.op("act", lambda e: e.copy(out=qTs[:], in_=psT[:, 0:4, 0:8]), [psT], [qTs])
        k.op("act", lambda e: e.copy(out=KSb[:, 16384:16392], in_=psT[:, 4, 0:8]), [psT], [KSb])
        k.op("act", lambda e: e.copy(out=KWb[:, 512:520], in_=psT[:, 5, 0:8]), [psT], [KWb])
        k.dma("sp", VXb[0:8, 128, :, 0:64], scr_qk[r0:r0 + 8, 1024:1152].rearrange("p (g d) -> p g d", g=2), reads=[scr_qk], writes=[VXb])
        k.dma("sp", VWb[0:8, 4, :, 0:64], scr_qk[r0:r0 + 8, 1152:1280].rearrange("p (g d) -> p g d", g=2), reads=[scr_qk], writes=[VWb])
        qflat = qTs[:].rearrange("p a b -> p (a b)")
        for p in range(128):
            t_pg = pg[p % 2]
            for which, pool_d in ((0, cpool), (1, spool)):
                k._sync("pool", [pidx], [t_pg])
                di = k.dnext
                k.dnext = (k.dnext + 1) % NDS
                k._wait("pool", ("d", di), k.dval[di])
                ins = nc.gpsimd.indirect_dma_start(out=t_pg[:, which, :], out_offset=None, in_=pool_d[:].rearrange("n r f -> (n r) f"),
                                                   in_offset=bass.IndirectOffsetOnAxis(ap=pidx[:, b * 128 + p:b * 128 + p + 1], axis=0))
                k.n_ins += 1
                k.dval[di] += 16
                ins.then_inc(k.dsems[di], 16)
                t_pg.w[("d", di)] = k.dval[di]
            k.op("pe", lambda e, t_pg=t_pg: e.transpose(psF[:, 0, :], t_pg[:, 0, 0:128], identf[:]), [t_pg, identf], [psF])
            k.op("pe", lambda e, t_pg=t_pg: e.transpose(psF[:, 1, :], t_pg[:, 0, 128:256], identf[:]), [t_pg, identf], [psF])
            k.op("pe", lambda e, t_pg=t_pg: e.transpose(psF[:, 2, :], t_pg[:, 1, 0:128], identf[:]), [t_pg, identf], [psF])
            k.op("act", lambda e, p=p: e.copy(out=KCb[:, :, p * 128:(p + 1) * 128], in_=psF[:, 0:2, :]), [psF], [KCb])
            k.op("dve", lambda e, p=p: e.tensor_copy(out=KSb[:, p * 128:(p + 1) * 128], in_=psF[:, 2, :]), [psF], [KSb])
            k.op("pool", lambda e, p=p, t_pg=t_pg: e.tensor_copy(out=VXb[:, p, :, 0:64], in_=t_pg[:, 1, 128:256].rearrange("p (g d) -> p g d", g=2)), [t_pg], [VXb])
        for i in range(4):
            t_pg = pg[i % 2]
            k.dma("sp", t_pg[:, 0, :], st_win[b, i * 128:(i + 1) * 128, :], writes=[t_pg])
            k.op("pe", lambda e, t_pg=t_pg: e.transpose(psF[:, 3, :], t_pg[:, 0, 0:128], identf[:]), [t_pg, identf], [psF])
            k.op("act", lambda e, i=i: e.copy(out=KWb[:, i * 128:(i + 1) * 128], in_=psF[:, 3, :]), [psF], [KWb])
            k.op("pool", lambda e, i=i, t_pg=t_pg: e.tensor_copy(out=VWb[:, i, :, 0:64], in_=t_pg[:, 0, 128:256].rearrange("p (g d) -> p g d", g=2)), [t_pg], [VWb])
        for gi in range(8):
            c0 = max(128 * gi - 1, 0)
            nb = 128 * gi + 127 - c0
            for j in range(2):
                n = 0
                for r in range(2):
                    for i16 in range(16):
                        s0 = 16 * (c0 + r) + i16
                        k.op("pe", lambda e, j=j, r=r, i16=i16, s0=s0, n=n, nb=nb: e.matmul(
                            psC[:, 128 * j:128 * j + nb], lhsT=W1[:, j * 32 + r * 16 + i16, :], rhs=KCb[:, j, s0:s0 + 16 * (nb - 1) + 1:16],
                            start=(n == 0), stop=(n == 31)), [W1, KCb], [psC])
                        n += 1
                k.act(spre[:, j, 0:nb], psC[:, 128 * j:128 * j + nb], AF.Silu, [psC, posW], [spre], bias=posW[:, j:j + 1])
            for j in range(2):
                k.mm(psC[:, 256 + 128 * j:256 + 128 * j + nb], [(W2[:, j, :], spre[:, j, 0:nb])], [W2, spre], [psC])
            k.op("act", lambda e, c0=c0, nb=nb: e.copy(out=kvc[:, :, c0:c0 + nb], in_=psC[:, 256:512].rearrange("p (j c) -> p j c", j=2)[:, :, 0:nb]), [psC], [kvc])
        for ch in range(8):
            k.op("pe", lambda e, ch=ch: e.transpose(psT[:, ch, :], kvc[:, 1, ch * 128:(ch + 1) * 128], ident_bf[:]), [kvc, ident_bf], [psT])
        k.op("act", lambda e: e.copy(out=vcb[:], in_=psT[:]), [psT], [vcb])
        for g in range(2):
            gs = slice(g * 64, (g + 1) * 64)
            for hf in range(2):
                k.mm(psS[0:32, hf * 512:(hf + 1) * 512], [(qflat[gs, :], kvc[gs, 0, hf * 512:(hf + 1) * 512])], [qTs, kvc], [psS])
            k.op("dve", lambda e: e.tensor_reduce(out=st[:, 0:1], in_=psS[0:32, 0:1023], axis=AX.X, op=ALU.max), [psS], [st])
            k.ts("dve", st[:, 1:2], st[:, 0:1], -1.0, None, ALU.mult, None, [st], [st])
            k.act(pS[:, 0:1023], psS[0:32, 0:1023], AF.Exp, [psS, st], [pS, st], bias=st[:, 1:2], accum_out=st[:, 2:3])
            k.op("dve", lambda e: e.reciprocal(out=st[:, 3:4], in_=st[:, 2:3]), [st], [st])
            k.ts("dve", pS[:, 0:1023], pS[:, 0:1023], st[:, 3:4], None, ALU.mult, None, [pS, st], [pS])
            k.op("act", lambda e: e.copy(out=pbf[:], in_=pS[:]), [pS], [pbf])
            k.op("dve", lambda e: e.tensor_reduce(out=impr[:], in_=pS[:].rearrange("p (s f) -> p s f", f=4), axis=AX.X, op=ALU.add), [pS], [impr])
            k.tt("dve", impr[:, 1:256], impr[:, 1:256], pS[:, 3:1020:4], ALU.add, [impr, pS], [impr])
            k.mm(psA[0:8, 0:256], [(selm[:, 0:8], impr[:])], [selm, impr], [psA])
            k.op("dve", lambda e, g=g: e.memset(sc[:, g, :], 0.0), [], [sc])
            k.op("act", lambda e, g=g: e.copy(out=sc[:, g, 0:256], in_=psA[0:8, 0:256]), [psA], [sc])
            for ch in range(8):
                k.op("pe", lambda e, ch=ch: e.transpose(psT[:, ch, 0:32], pbf[:, ch * 128:(ch + 1) * 128], ident_bf[0:32, 0:32]), [pbf, ident_bf], [psT])
            k.op("act", lambda e: e.copy(out=pTs[:], in_=psT[:, :, 0:32]), [psT], [pTs])
            k.mm(psA[0:32, 256:320], [(pTs[:, ch, :], vcb[:, ch, g * 64:(g + 1) * 64]) for ch in range(8)], [pTs, vcb], [psA])
            k.op("act", lambda e, g=g: e.copy(out=obr[:, 0, g, 0:64], in_=psA[0:32, 256:320]), [psA], [obr])
            k.op("dve", lambda e, g=g: e.memset(obr[:, 0, g, 64:65], 1.0), [], [obr])
            k.tt("dve", sc[:, g, :], sc[:, g, :], selcS[:, 0, :], ALU.mult, [sc, selcS], [sc])
            k.tt("dve", sc[:, g, :], sc[:, g, :], selcS[:, 1, :], ALU.add, [sc, selcS], [sc])
            k.op("dve", lambda e, g=g: e.max(out=top[:, 0:8], in_=sc[:, g, :]), [sc], [top])
            k.op("dve", lambda e, g=g: e.match_replace(out=sc2[:], in_to_replace=top[:, 0:8], in_values=sc[:, g, :], imm_value=-1e30), [sc, top], [sc2])
            k.op("dve", lambda e: e.max(out=top[:, 8:16], in_=sc2[:]), [sc2], [top])
            k.ts("dve", top[:, 15:16], top[:, 15:16], -0.5, None, ALU.max, None, [top], [top])
            k.ts("dve", sc[:, g, :], sc[:, g, :], top[:, 15:16], None, ALU.is_ge, None, [sc, top], [sc])
            k.ts("dve", sc[:, g, :], sc[:, g, :], 30000.0, -30000.0, ALU.mult, ALU.add, [sc], [sc])
            for c3 in range(3):
                k.mm(psA[:, 320 + 32 * c3:352 + 32 * c3], [(sc[:, g, c3 * 128:(c3 + 1) * 128], rep[:])], [sc, rep], [psA])
            k.op("act", lambda e, g=g: e.copy(out=selTs[:, :, g, :], in_=psA[:, 320:416].rearrange("p (c t) -> p c t", c=3)), [psA], [selTs])
        ck(141 + b)
        pi = 0
        for br in (1, 2):
            for g in range(2):
                gs = slice(g * 64, (g + 1) * 64)
                nkb = 129 if br == 1 else 5
                for kb in range(nkb):
                    if br == 1:
                        pairs = [(KSb[gs, kb * 128:(kb + 1) * 128], qflat[gs, :]),
                                 (ef[:, (kb % 64) * 128:(kb % 64) * 128 + 128], selTs[:, (2 * kb) // 128, g, :])]
                        if kb == 128:
                            pairs.append((ident_bf[:], triS[:, 0, :]))
                        vr = VXb[:, kb, g, :]
                    else:
                        pairs = [(KWb[gs, kb * 128:(kb + 1) * 128], qflat[gs, :])]
                        if kb == 0:
                            pairs.append((ident_bf[:], triS[:, 1, :]))
                        if kb == 4:
                            pairs.append((ident_bf[:], triS[:, 0, :]))
                        vr = VWb[:, kb, g, :]
                    S = psS[:, (kb % 2) * 512:(kb % 2) * 512 + 32]
                    n = len(pairs)
                    for i, (l, r) in enumerate(pairs):
                        k.op("pe", lambda e, l=l, r=r, i=i, S=S, n=n: e.matmul(S, lhsT=l, rhs=r, start=(i == 0), stop=(i == n - 1)),
                             [KSb, KWb, qTs, ef, selTs, ident_bf, triS], [psS])
                    pi ^= 1
                    pt = PTs[pi]
                    k.act(pt[:], S, AF.Exp, [psS], [pt])
                    k.op("pe", lambda e, pt=pt, vr=vr, kb=kb, nkb=nkb, g=g: e.matmul(psA[0:32, 416 + 0:416 + 65], lhsT=pt[:], rhs=vr, start=(kb == 0), stop=(kb == nkb - 1)),
                         [pt, VXb, VWb], [psA])
                k.op("act", lambda e, br=br, g=g: e.copy(out=obr[:, br, g, :], in_=psA[0:32, 416:481]), [psA], [obr])
        k.ts("dve", rcs[:, 0:6], obr[:, :, :, 64].rearrange("p a g -> p (a g)"), 1e-30, None, ALU.max, None, [obr], [rcs])
        k.op("dve", lambda e: e.reciprocal(out=rcs[:, 0:6], in_=rcs[:, 0:6]), [rcs], [rcs])
        for g in range(2):
            for br in range(3):
                k.tt("dve", st[:, 4:5], rcs[:, br * 2 + g:br * 2 + g + 1], gate_r[:, g, br:br + 1], ALU.mult, [rcs, gate_r], [st])
                if br == 0:
                    k.ts("dve", ofin[:, g, :], obr[:, 0, g, 0:64], st[:, 4:5], None, ALU.mult, None, [obr, st], [ofin])
                else:
                    k.op("dve", lambda e, g=g, br=br: e.scalar_tensor_tensor(out=ofin[:, g, :], in0=obr[:, br, g, 0:64], scalar=st[:, 4:5], in1=ofin[:, g, :],
                                                                             op0=ALU.mult, op1=ALU.add), [obr, st, ofin], [ofin])
            for hh in range(4):
                fo = first_o[0]
                first_o[0] = False
                k.op("pe", lambda e, g=g, hh=hh, fo=fo, b=b: e.matmul(psO[0:32, (g * 4 + hh) * 64:(g * 4 + hh + 1) * 64],
                                                                   lhsT=selm[:, 8 + (hh * 4 + b) * 32:8 + (hh * 4 + b + 1) * 32], rhs=ofin[:, g, :],
                                                                   start=fo, stop=False, skip_group_check=True), [selm, ofin], [psO])
    k.op("act", lambda e: e.copy(out=oas[:], in_=psO[0:32, :]), [psO], [oas])
    k.dma("pool", scr_oab[(NOWN + 1) * 128:(NOWN + 1) * 128 + 32, 0:512], oas[:], reads=[oas], writes=[scr_oab])
    k.pop()


def _consts():
    cm = np.zeros((128, 6, 128), np.float32)
    cm[:, 0, :] = np.eye(128, dtype=np.float32)
    s = np.arange(128)[:, None]
    t = np.arange(128)[None, :]
    same_p = (s // 64) == (t // 64)
    cm[:, 1, :] = ((s <= t) & same_p)
    cm[:, 2, :] = ((s > t) & same_p)
    same_s = ((s // 8) == (t // 8)) & (s < 32) & (t < 32)
    cm[:, 3, :] = ((s <= t) & same_s)
    cm[:, 4, :] = ((s > t) & same_s)
    cc = np.zeros((128, 6, 128), np.float32)
    cc[:, 0, 0:64] = 1
    cc[:, 1, 64:128] = 1
    for b in range(4):
        cc[:, 2 + b, 8 * b:8 * b + 8] = 1
    ci = np.zeros((128, 8), np.float32)
    ci[0:64, 0] = 1
    ci[64:128, 1] = 1
    for b in range(4):
        ci[8 * b:8 * b + 8, 2 + b] = 1
    return cm.reshape(128, 768), ci, cc.reshape(128, 768)


def _rope_tab(pos):
    inv = (10000.0 ** (-np.arange(32, dtype=np.float32) / np.float32(32))).astype(np.float32)
    ang = pos.astype(np.float32)[:, None] * inv[None, :]
    return np.concatenate([np.cos(ang), np.sin(ang)], axis=1).astype(np.float32)


def _nsa_consts(inp, half):
    w1 = inp["cmp_w1"][0]
    w2 = inp["cmp_w2"][0]
    pe = inp["cmp_pos_emb"][0]
    w1bd = np.zeros((2, 64, 64, 2, 64), np.float32)
    w1r = w1.reshape(2, 32, 64, 64)
    for g in range(2):
        w1bd[g, :, :, g, :] = w1r.transpose(2, 0, 1, 3).reshape(64, 64, 64)
    w1bd = w1bd.reshape(128, 64 * 128)
    w2bd = np.zeros((2, 64, 2, 2, 64), np.float32)
    for g in range(2):
        w2bd[g, :, :, g, :] = w2.transpose(1, 0, 2)
    w2bd = w2bd.reshape(128, 256)
    posvec = np.tile(pe.transpose(2, 0, 1).reshape(64, 64), (2, 1)).astype(np.float32)
    c = np.arange(256)[:, None]
    sidx = np.arange(62)[None, :]
    cov = ((c >= 4 * sidx - 1) & (c <= 4 * sidx + 3)).astype(np.float32)
    cover = cov.reshape(2, 128, 62).transpose(1, 0, 2).reshape(128, 124)
    NEG = -30000.0
    n_t = NOWN + 1
    cmpB = np.zeros((n_t, 128, 2, 128), np.float32)
    selc = np.zeros((n_t, 128, 2, 64), np.float32)
    cl = np.arange(128)[:, None]
    t = np.arange(128)[None, :]
    soff = 32 if half == 0 else 0
    coff = 128 if half == 0 else 0
    for i in range(n_t):
        lt = NPRE - 1 + i
        for ch in range(2):
            cc = ch * 128 + cl
            vis = (16 * cc + 31 <= 128 * lt + t) & (cc - coff >= 0) & (cc <= 254)
            cmpB[i, :, ch, :] = np.where(vis, 0.0, NEG)
        l = 128 * lt + np.arange(128)[:, None]
        s = np.arange(64)[None, :]
        sg = s - soff
        cur_g = l // 64 - soff
        valid = (64 * s <= l) & (sg >= 0)
        forced = (sg >= 0) & ((sg == 0) | (sg == cur_g) | (sg == cur_g - 1)) & (cur_g >= 0)
        selc[i, :, 0, :] = (valid & ~forced)
        selc[i, :, 1, :] = np.where(forced, 1e4, np.where(valid, 0.0, -1.0))
    key = np.arange(4096)[None, :]
    efull = (key // 64 == np.arange(64)[:, None]).astype(np.float32)
    kk = np.arange(128)[:, None]
    triB = np.zeros((128, 2, 128), np.float32)
    triB[:, 0, :] = np.where(kk > t, NEG, 0.0)
    triB[:, 1, :] = np.where(kk <= t, NEG, 0.0)
    pfx = np.full((1, 128), NEG if half == 0 else 0.0, np.float32)
    return dict(w1bd=w1bd, w2bd=w2bd, posvec=posvec, cover=cover, cmpB=cmpB.reshape(n_t, 128, 256), selc=selc.reshape(n_t, 128, 128),
                efull=efull, triB=triB.reshape(128, 256), pfx=pfx, identf=np.eye(128, dtype=np.float32))


def _sample_consts():
    NEG = -30000.0
    selcS = np.zeros((8, 2, 384), np.float32)
    s = np.arange(384)
    valid = s <= 256
    forced = (s == 0) | (s == 255) | (s == 256)
    selcS[:, 0, :] = (valid & ~forced)[None, :]
    selcS[:, 1, :] = np.where(forced, 1e4, np.where(valid, 0.0, -1.0))[None, :]
    selm = np.zeros((32, 8 + 512), np.float32)
    rep = np.zeros((8, 32), np.float32)
    for hh in range(4):
        for t in range(8):
            selm[hh * 8 + t, t] = 1
            rep[t, hh * 8 + t] = 1
            for b in range(4):
                selm[hh * 8 + t, 8 + (hh * 4 + b) * 32 + b * 8 + t] = 1
    key = np.arange(8192)[None, :]
    ef = (key // 64 == np.arange(128)[:, None]).astype(np.float32)
    r = np.arange(128)[:, None]
    tt = (np.arange(32) % 8)[None, :]
    tri = np.zeros((128, 2, 32), np.float32)
    tri[:, 0, :] = np.where(r > tt, NEG, 0.0)
    tri[:, 1, :] = np.where(r <= tt, NEG, 0.0)
    return dict(selcS=selcS.reshape(8, 768), selm=selm, rep=rep, efull128=ef, triBs=tri.reshape(128, 64),
                iot=np.arange(128, dtype=np.float32)[:, None])


_CACHE = {}


def _get_program(key=(NT, True, 0, 5120)):
    if key not in _CACHE:
        _CACHE[key] = build_program(*key)
    return _CACHE[key]


def make_in_maps(inp, cores, pools=None):
    cmask, cind, ccolmask = _consts()
    vecs = np.concatenate([inp["attn_norm_g"][0], inp["q_norm_g"][0], inp["k_norm_g"][0].reshape(-1),
                           inp["hgrn_lb_logits"].reshape(-1), inp["hgrn_norm_g"][0]]).astype(np.float32)[None, :]
    w_in = np.ascontiguousarray(inp["w_in"][0])
    maps = []
    for c in cores:
        b, half = c // 2, c % 2
        c0 = half * 2048
        xloc = np.zeros((NT * 128, D), np.float32)
        if half == 0:
            xloc[2048:] = inp["x_prompt"][b, 0:2048]
        else:
            xloc[:] = inp["x_prompt"][b, 0:4096]
        pos = np.arange(NT * 128) + c0 - 2048
        cs_tab = np.concatenate([_rope_tab(pos), _rope_tab(16384 + (np.arange(32) % 8))], axis=0)
        maps.append(dict(
            xloc=xloc, xs=np.ascontiguousarray(inp["x_sample"][4 * c:4 * c + 4].reshape(32, D)), cs_tab=cs_tab,
            w_in=w_in, vecs=vecs, cmask=cmask, cind=cind, ccolmask=ccolmask,
            st_win=np.ascontiguousarray(inp["state_win_kv"][0, 4 * c:4 * c + 4].reshape(4, 512, 256)),
            st_hgrn=np.ascontiguousarray(inp["state_hgrn"][0, 4 * c:4 * c + 4]),
        ))
        maps[-1].update(_nsa_consts(inp, half))
        cwv = np.concatenate([inp["ffn_conv_w"][0], inp["ffn_conv_b"]], axis=0)
        convw = np.ascontiguousarray(cwv.reshape(4, 22, 128).transpose(2, 1, 0)).reshape(128, 88)
        cst = inp["state_ffn_conv"][0, 4 * c:4 * c + 4]
        convst = np.ascontiguousarray(cst.reshape(4, 2, 22, 128).transpose(3, 2, 0, 1)).reshape(128, 176)
        maps[-1].update(w_branch=np.ascontiguousarray(inp["w_branch"][0]), w_out=np.ascontiguousarray(inp["w_out"][0]),
                        ffn_w_in=np.ascontiguousarray(inp["ffn_w_in"][0]), ffn_w_out=np.ascontiguousarray(inp["ffn_w_out"][0]),
                        fvec=np.ascontiguousarray(inp["ffn_norm_g"][0][None, :]), convw=convw, convst=convst)
        if pools is None:
            maps[-1].update(cache_cmp=inp["cache_cmp_kv"][0].reshape(-1, 128, 256), cache_slc=inp["cache_slc_kv"][0].reshape(-1, 128, 256),
                            ptab=np.ascontiguousarray(inp["page_table"][4 * c:4 * c + 4]).astype(np.int32))
        else:
            maps[-1].update(pools(c))
        maps[-1].update(_sample_consts())
    return maps


def assemble(res, cores, out):
    for i, c in enumerate(cores):
        r = res[i]
        b, half = c // 2, c % 2
        c0 = half * 2048
        for j, name in enumerate(("cmp_kv_prompt", "slc_kv_prompt")):
            out[name][0, b, c0:c0 + 2048] = r["o_kv"][j].reshape(2048, 2, 2, 64)
        if half == 1:
            out["win_kv_prompt"][0, b] = r["o_kv"][2][2048 - 512:].reshape(512, 2, 2, 64)
            out["hgrn_prompt"][0, b] = r["o_hg_p"]
        out["cmp_kv_sample"][0, 4 * c:4 * c + 4] = r["o_kvs"][0].reshape(4, 8, 2, 2, 64)
        out["slc_kv_sample"][0, 4 * c:4 * c + 4] = r["o_kvs"][1].reshape(4, 8, 2, 2, 64)
        out["win_kv_sample"][0, 4 * c:4 * c + 4] = r["o_win_s"].reshape(4, 512, 2, 2, 64)
        out["hgrn_sample"][0, 4 * c:4 * c + 4] = r["o_hg_s"]
        out["y_prompt"][b, c0:c0 + 2048] = r["o_y"]
        out["y_sample"][4 * c:4 * c + 4] = r["o_ys"].reshape(4, 8, D)
        if half == 1:
            out["ffn_conv_prompt"][0, b] = r["o_cp"].reshape(128, 22, 2).transpose(2, 1, 0).reshape(2, 2816)
        out["ffn_conv_sample"][0, 4 * c:4 * c + 4] = r["o_cs"].reshape(128, 22, 4, 2).transpose(2, 3, 1, 0).reshape(4, 2, 2816)


OUT_SHAPES = dict(
    y_prompt=(4, 4096, 1024), y_sample=(32, 8, 1024),
    cmp_kv_prompt=(1, 4, 4096, 2, 2, 64), cmp_kv_sample=(1, 32, 8, 2, 2, 64),
    slc_kv_prompt=(1, 4, 4096, 2, 2, 64), slc_kv_sample=(1, 32, 8, 2, 2, 64),
    win_kv_prompt=(1, 4, 512, 2, 2, 64), win_kv_sample=(1, 32, 512, 2, 2, 64),
    hgrn_prompt=(1, 4, 4, 128, 128), hgrn_sample=(1, 32, 4, 128, 128),
    ffn_conv_prompt=(1, 4, 2, 2816), ffn_conv_sample=(1, 32, 2, 2816))
OUT_ORDER = ["y_prompt", "y_sample", "cmp_kv_prompt", "cmp_kv_sample", "slc_kv_prompt", "slc_kv_sample",
             "win_kv_prompt", "win_kv_sample", "hgrn_prompt", "hgrn_sample", "ffn_conv_prompt", "ffn_conv_sample"]


def kernel(**inp):
    inp = {n: np.asarray(v) for n, v in inp.items()}
    cores = list(range(8))
    prog = _get_program()
    maps = make_in_maps(inp, cores)
    res = run_bass_kernel_spmd(prog.nc, maps, core_ids=cores)
    out = {n: np.zeros(s, np.float32) for n, s in OUT_SHAPES.items()}
    assemble(res.results, cores, out)
    return tuple(out[n] for n in OUT_ORDER)
```

```python
import numpy as np
import concourse.bass as bass
import concourse.mybir as mybir
from concourse.bass_utils import run_bass_kernel_spmd
from contextlib import ExitStack

F32 = mybir.dt.float32
BF16 = mybir.dt.bfloat16
I32 = mybir.dt.int32
AF = mybir.ActivationFunctionType
ALU = mybir.AluOpType
AX = mybir.AxisListType

NDS = 40
PE_NOSELF = False
PIPE_B = True
NOSELF_CHAIN = True
EPS = 1e-6
D = 1024
NCOL1 = 3352
NPRE = 16
NOWN = 16
NT = NPRE + NOWN


class Buf:
    def __init__(self, name, t):
        self.name = name
        self.t = t
        self.w = {}
        self.r = {}
        self.ps = False

    def __getitem__(self, idx):
        return self.t[idx]


class KB:
    def __init__(self):
        self.nc = bass.Bass("TRN2", target_bir_lowering=False)
        nc = self.nc
        self.es = ExitStack()
        self.engs = {"pe": nc.tensor, "act": nc.scalar, "dve": nc.vector, "pool": nc.gpsimd, "sp": nc.sync}
        self.esem = {}
        self.ecnt = {}
        for e in ("pe", "act", "dve", "pool"):
            self.esem[e] = self.es.enter_context(nc.semaphore("sem_" + e))
            self.ecnt[e] = 0
        self.dsems = [self.es.enter_context(nc.semaphore("sem_d%d" % i)) for i in range(NDS)]
        self.dval = [0] * NDS
        self.dnext = 0
        self.waited = {e: {} for e in self.engs}
        self.nbuf = 0
        self.n_ins = 0
        self.n_wait = 0
        self.stk = [self.es]

    def push(self):
        self.stk.append(ExitStack())

    def barrier(self):
        for e in self.engs:
            for f in ("pe", "act", "dve", "pool"):
                if f != e:
                    self._wait(e, ("e", f), self.ecnt[f])
            for i in range(NDS):
                self._wait(e, ("d", i), self.dval[i])

    def pop(self):
        self.barrier()
        self.stk.pop().close()

    def sb(self, shape, dt, name=None):
        self.nbuf += 1
        name = (name or "sb") + "_%d" % self.nbuf
        return Buf(name, self.stk[-1].enter_context(self.nc.sbuf_tensor(name, list(shape), dt)))

    def ps(self, shape, dt, name=None):
        self.nbuf += 1
        name = name or "ps%d" % self.nbuf
        name = name + "_%d" % self.nbuf
        b = Buf(name, self.stk[-1].enter_context(self.nc.psum_tensor(name, list(shape), dt)))
        b.ps = True
        return b

    def dram(self, name, shape, dt, kind=None):
        if kind is None:
            t = self.nc.dram_tensor(name, list(shape), dt)
        else:
            t = self.nc.dram_tensor(name, list(shape), dt, kind=kind)
        return Buf(name, t.ap())

    def _sem(self, key):
        return self.esem[key[1]] if key[0] == "e" else self.dsems[key[1]]

    def _wait(self, eng, key, val):
        if val <= 0:
            return
        wd = self.waited[eng]
        if wd.get(key, 0) >= val:
            return
        self.engs[eng].wait_ge(self._sem(key), val)
        wd[key] = val
        self.n_wait += 1

    def _sync(self, eng, reads, writes, noself=False):
        need = {}
        for b in reads:
            for k, v in b.w.items():
                if need.get(k, 0) < v:
                    need[k] = v
        for b in writes:
            for k, v in b.w.items():
                if need.get(k, 0) < v:
                    need[k] = v
            for k, v in b.r.items():
                if need.get(k, 0) < v:
                    need[k] = v
        for k, v in need.items():
            if noself and eng == "pe" and k == ("e", "pe"):
                continue
            self._wait(eng, k, v)

    def op(self, eng, fn, reads=(), writes=(), inc=True, noself=False):
        self._sync(eng, reads, writes, noself)
        ins = fn(self.engs[eng])
        self.n_ins += 1
        if inc:
            self.ecnt[eng] += 1
            n = self.ecnt[eng]
            ins.then_inc(self.esem[eng], 1)
        else:
            n = self.ecnt[eng] + 1
        key = ("e", eng)
        for b in reads:
            d = b.w if b.ps else b.r
            if d.get(key, 0) < n:
                d[key] = n
        for b in writes:
            if b.w.get(key, 0) < n:
                b.w[key] = n
        return ins

    def dma(self, q, out_ap, in_ap, reads=(), writes=(), **kw):
        i = self.dnext
        self.dnext = (i + 1) % NDS
        self._wait(q, ("d", i), self.dval[i])
        self._sync(q, reads, writes)
        ins = self.engs[q].dma_start(out=out_ap, in_=in_ap, **kw)
        self.n_ins += 1
        self.dval[i] += 16
        ins.then_inc(self.dsems[i], 16)
        key = ("d", i)
        for b in reads:
            b.r[key] = self.dval[i]
        for b in writes:
            b.w[key] = self.dval[i]
        return ins

    def finish(self):
        for i in range(NDS):
            self._wait("sp", ("d", i), self.dval[i])
        for e in ("pe", "act", "dve", "pool"):
            self._wait("sp", ("e", e), self.ecnt[e])

    def mm(self, out_ap, pairs, reads, writes):
        n = len(pairs)
        for i, (l, r) in enumerate(pairs):
            self.op("pe", lambda e, l=l, r=r, i=i: e.matmul(out_ap, lhsT=l, rhs=r, start=(i == 0), stop=(i == n - 1)),
                    reads=reads, writes=writes, inc=True, noself=(NOSELF_CHAIN and i > 0))

    def act(self, out_ap, in_ap, func, reads, writes, **kw):
        return self.op("act", lambda e: e.activation(out=out_ap, in_=in_ap, func=func, **kw), reads=reads, writes=writes)

    def tt(self, eng, out_ap, a, b, op, reads, writes):
        return self.op(eng, lambda e: e.tensor_tensor(out=out_ap, in0=a, in1=b, op=op), reads=reads, writes=writes)

    def ts(self, eng, out_ap, a, s1, s2, op0, op1, reads, writes):
        if op1 is None:
            return self.op(eng, lambda e: e.tensor_scalar(out=out_ap, in0=a, scalar1=s1, scalar2=None, op0=op0),
                           reads=reads, writes=writes)
        return self.op(eng, lambda e: e.tensor_scalar(out=out_ap, in0=a, scalar1=s1, scalar2=s2, op0=op0, op1=op1),
                       reads=reads, writes=writes)


class _Stop(Exception):
    pass


def build_program(ntp=NT, with_sample=True, stop=0, npool=5120):
    k = KB()
    try:
        _build(k, ntp, with_sample, stop, npool)
    except _Stop:
        pass
    k.finish()
    return k


def _build(k, ntp, with_sample, stop, npool):
    def ck(n):
        if stop == n:
            raise _Stop()
    nc = k.nc
    P = 128
    xloc = k.dram("xloc", [NT * 128, D], F32, "ExternalInput")
    xs = k.dram("xs", [32, D], F32, "ExternalInput")
    cs_tab = k.dram("cs_tab", [NT * 128 + 32, 64], F32, "ExternalInput")
    w_in = k.dram("w_in", [D, 5400], F32, "ExternalInput")
    vecs = k.dram("vecs", [1, 1024 + 64 + 192 + 1024 + 128], F32, "ExternalInput")
    cmask = k.dram("cmask", [128, 6 * 128], F32, "ExternalInput")
    cind = k.dram("cind", [128, 8], F32, "ExternalInput")
    st_win = k.dram("st_win", [4, 512, 256], F32, "ExternalInput")
    st_hgrn = k.dram("st_hgrn", [4, 4, 128, 128], F32, "ExternalInput")
    ccolmask = k.dram("ccolmask", [128, 6 * 128], F32, "ExternalInput")
    scr_qk = k.dram("scr_qk", [NT * 128 + 32, 1280], BF16)
    scr_gate = k.dram("scr_gate", [(NOWN + 1) * 128 + 32, 24], F32)
    scr_hT = k.dram("scr_hT", [NOWN + 2, 128, 1024], BF16)
    scr_oab = k.dram("scr_oab", [(NOWN + 1) * 128 + 32, 1024], BF16)

    o_kv = k.dram("o_kv", [3, NOWN * 128, 256], F32, "ExternalOutput")
    o_kvs = k.dram("o_kvs", [2, 32, 256], F32, "ExternalOutput")
    o_win_s = k.dram("o_win_s", [4, 512, 256], F32, "ExternalOutput")
    o_hg_p = k.dram("o_hg_p", [4, 128, 128], F32, "ExternalOutput")
    o_hg_s = k.dram("o_hg_s", [4, 4, 128, 128], F32, "ExternalOutput")

    k.push()
    wsb = k.sb([P, 8, NCOL1], BF16, "wsb")
    for kc in range(8):
        k.dma("pool", wsb[:, kc, :], w_in[kc * 128:(kc + 1) * 128, 0:NCOL1], writes=[wsb])
    g_bc = k.sb([P, D], F32, "g_bc")
    k.dma("sp", g_bc[:], vecs[0:1, 0:1024].partition_broadcast(P), writes=[g_bc])
    gain = k.sb([P, 14, 64], F32, "gain")
    for h in range(8):
        k.dma("sp", gain[:, h, :], vecs[0:1, 1024:1088].partition_broadcast(P), writes=[gain])
    for i in range(3):
        for g in range(2):
            k.dma("sp", gain[:, 8 + 2 * i + g, :], vecs[0:1, 1088 + 64 * i:1088 + 64 * i + 64].partition_broadcast(P), writes=[gain])
    k.ts("dve", gain[:, 0:8, :], gain[:, 0:8, :], 0.125, None, ALU.mult, None, [gain], [gain])
    lgt = k.sb([P, 2, 512], F32, "lgt")
    k.dma("sp", lgt[:, 0, :], vecs[0:1, 1280:1792].partition_broadcast(P), writes=[lgt])
    k.dma("sp", lgt[:, 1, :], vecs[0:1, 1792:2304].partition_broadcast(P), writes=[lgt])
    lb = k.sb([P, 512], F32, "lb")
    oml = k.sb([P, 512], F32, "oml")
    k.tt("dve", lb[:], lgt[:, 0, :], lgt[:, 1, :], ALU.subtract, [lgt], [lb])
    k.act(lb[:], lb[:], AF.Sigmoid, [lb], [lb])
    k.ts("dve", oml[:], lb[:], -1.0, 1.0, ALU.mult, ALU.add, [lb], [oml])
    gn_bc = k.sb([P, 4, 128], F32, "gn_bc")
    for h in range(4):
        k.dma("sp", gn_bc[:, h, :], vecs[0:1, 2304:2432].partition_broadcast(P), writes=[gn_bc])
    cm = k.sb([P, 6, 128], F32, "cm")
    k.dma("sp", cm[:], cmask[:].rearrange("p (a b) -> p a b", b=128), writes=[cm])
    ci = k.sb([P, 8], F32, "ci")
    k.dma("sp", ci[:], cind[:], writes=[ci])
    ident_bf = k.sb([P, 128], BF16, "ident_bf")
    k.op("dve", lambda e: e.tensor_copy(out=ident_bf[:], in_=cm[:, 0, :]), [cm], [ident_bf])
    ones_col = k.sb([P, 1], F32, "ones_col")
    k.op("dve", lambda e: e.memset(ones_col[:], 1.0), [], [ones_col])

    ck(1)
    xt = [k.sb([P, D], F32, "xt%d" % i) for i in range(2)]
    junk = k.sb([P, D], BF16, "junk")
    xn = k.sb([P, D], BF16, "xn")
    hT = k.sb([P, 8, 128], BF16, "hT")
    st4 = k.sb([P, 8], F32, "st4")
    R = k.sb([P, 14, 64], F32, "R")
    R2 = k.sb([P, 14, 64], F32, "R2")
    T1 = k.sb([P, 14, 32], F32, "T1")
    T2 = k.sb([P, 14, 32], F32, "T2")
    st14 = k.sb([P, 16], F32, "st14")
    cs = k.sb([P, 64], F32, "cs")
    kvo = k.sb([P, 3, 256], F32, "kvo")
    qkv_bf = k.sb([P, 1280], BF16, "qkv_bf")
    gate_sb = k.sb([P, 24], F32, "gate_sb")
    qb = k.sb([P, 512], F32, "qb")
    u_sb = k.sb([P, 512], F32, "u_sb")
    logf = k.sb([P, 512], F32, "logf")
    kb = k.sb([P, 512], F32, "kb")
    vb = k.sb([P, 512], BF16, "vb")
    ggb = k.sb([P, 512], BF16, "ggb")
    ex = k.sb([P, 512], F32, "ex")
    qe = k.sb([P, 512], BF16, "qe")
    ke = k.sb([P, 512], BF16, "ke")
    kd = k.sb([P, 512], BF16, "kd")
    kdm = k.sb([P, 512], BF16, "kdm")
    dec = k.sb([P, 16], F32, "dec")
    S32 = [k.sb([P, 4, 128], F32, "S32_%d" % i) for i in range(5)]
    Sbf = [k.sb([P, 4, 128], BF16, "Sbf_%d" % i) for i in range(5)]

    psT = k.ps([P, 8, 128], BF16, "psT")
    psZ = [k.ps([P, 512], F32, "psZ%d" % i) for i in range(2)]
    psH = [k.ps([P, 512], F32, "psH%d" % i) for i in range(2)]
    psS = k.ps([P, 4, 128], F32, "psS")
    psA = k.ps([P, 4, 128], F32, "psA")
    psO = k.ps([P, 4, 128], F32, "psO")
    qkT = k.sb([P, 8, 128], BF16, "qkT")
    qeTm = [k.sb([P, 4, 128], BF16, "qeTm%d" % i) for i in range(4)]
    AmT = k.sb([P, 4, 128], BF16, "AmT")
    osq = k.sb([P, 512], F32, "osq")
    st8 = k.sb([P, 8], F32, "st8")
    oab = k.sb([P, 1024], BF16, "oab")
    ccol = k.sb([P, 6, 128], F32, "ccol")
    k.dma("sp", ccol[:], ccolmask[:].rearrange("p (a b) -> p a b", b=128), writes=[ccol])
    k.op("pool", lambda e: e.memset(oab[:], 0.0), [], [oab])

    k.op("dve", lambda e: e.memset(S32[0][:], 0.0), [], [S32[0]])
    k.op("pool", lambda e: e.memset(Sbf[0][:], 0.0), [], [Sbf[0]])
    if with_sample:
        for b in range(4):
            k.dma("sp", S32[1 + b][:], st_hgrn[b].rearrange("h k v -> k h v"), writes=[S32[1 + b]])
            k.op("act", lambda e, b=b: e.copy(out=Sbf[1 + b][:], in_=S32[1 + b][:]), [S32[1 + b]], [Sbf[1 + b]])
        for b in range(4):
            k.dma("sp", o_win_s[b, 0:504, :], st_win[b, 8:512, :], writes=[o_win_s])

    ck(2)
    zi = [0]

    def next_z():
        zi[0] ^= 1
        return psZ[zi[0]]

    def proj(c0, c1):
        z = next_z()
        k.mm(z[:, 0:c1 - c0], [(hT[:, kc, :], wsb[:, kc, c0:c1]) for kc in range(8)], [hT, wsb], [z])
        return z

    def do_tile(lt, rows, xsrc, csrc, full, kv_dst, sample):
        x = xt[lt % 2]
        rs = slice(0, rows)
        k.dma("sp", x[rs, :], xsrc, writes=[x])
        k.dma("sp", cs[rs, :], csrc, writes=[cs])
        ck(31)
        k.act(junk[rs, :], x[rs, :], AF.Square, [x], [junk, st4], accum_out=st4[rs, 0:1])
        k.ts("dve", st4[rs, 1:2], st4[rs, 0:1], 1.0 / D, EPS, ALU.mult, ALU.add, [st4], [st4])
        k.act(st4[rs, 2:3], st4[rs, 1:2], AF.Sqrt, [st4], [st4])
        k.op("dve", lambda e: e.reciprocal(out=st4[rs, 3:4], in_=st4[rs, 2:3]), [st4], [st4])
        k.op("dve", lambda e: e.scalar_tensor_tensor(out=xn[rs, :], in0=x[rs, :], scalar=st4[rs, 3:4], in1=g_bc[rs, :],
                                                     op0=ALU.mult, op1=ALU.mult), [x, st4, g_bc], [xn])
        ck(32)
        for kc in range(8):
            k.op("pe", lambda e, kc=kc: e.transpose(psT[:, kc, rs], xn[rs, kc * 128:(kc + 1) * 128], ident_bf[rs, rs]),
                 [xn, ident_bf], [psT], inc=True)
        ck(33)
        k.op("act", lambda e: e.copy(out=hT[:, :, rs], in_=psT[:, :, rs]), [psT], [hT])
        if full:
            ti = (NOWN + 1) if sample else (lt - (NPRE - 1))
            k.dma("pool", scr_hT[ti, :, :].rearrange("p (a b) -> p a b", b=128)[:, :, rs], hT[:, :, rs], reads=[hT], writes=[scr_hT])
        ck(3)

        def projr(c0, c1):
            z = next_z()
            k.mm(z[rs, 0:c1 - c0], [(hT[:, kc, rs], wsb[:, kc, c0:c1]) for kc in range(8)], [hT, wsb], [z])
            return z

        if full:
            z = projr(0, 512)
            ck(41)
            k.op("act", lambda e: e.copy(out=R[rs, 0:8, :], in_=z[rs, 0:512].rearrange("p (h d) -> p h d", d=64)), [z], [R])
        ck(42)
        z = projr(512, 1024)
        zv = z[rs, 0:512].rearrange("p (a j c) -> p a j c", a=2, j=2)
        k.op("act", lambda e: e.copy(out=R[rs, 8:12, :].rearrange("p (a g) d -> p a (g d)", a=2), in_=zv[:, :, 0, :]), [z], [R])
        k.op("dve", lambda e: e.tensor_copy(out=kvo[rs, 0:2, 128:256], in_=zv[:, :, 1, :]), [z], [kvo])
        ck(43)
        z = projr(1024, 1304)
        k.op("act", lambda e: e.copy(out=R[rs, 12:14, :].rearrange("p g d -> p (g d)"), in_=z[rs, 0:128]), [z], [R])
        k.op("dve", lambda e: e.tensor_copy(out=kvo[rs, 2, 128:256], in_=z[rs, 128:256]), [z], [kvo])
        if full:
            k.act(gate_sb[rs, :], z[rs, 256:280], AF.Sigmoid, [z], [gate_sb])
        ck(4)
        h0 = 0 if full else 8
        nh = 14 - h0
        k.tt("dve", R2[rs, h0:14, :], R[rs, h0:14, :], R[rs, h0:14, :], ALU.mult, [R], [R2])
        k.op("dve", lambda e: e.tensor_reduce(out=st14[rs, h0:14], in_=R2[rs, h0:14, :], axis=AX.X, op=ALU.add), [R2], [st14])
        k.ts("dve", st14[rs, h0:14], st14[rs, h0:14], 1.0 / 64, EPS, ALU.mult, ALU.add, [st14], [st14])
        k.act(st14[rs, h0:14], st14[rs, h0:14], AF.Sqrt, [st14], [st14])
        k.op("dve", lambda e: e.reciprocal(out=st14[rs, h0:14], in_=st14[rs, h0:14]), [st14], [st14])
        k.tt("pool", R2[rs, h0:14, :], R[rs, h0:14, :], gain[rs, h0:14, :], ALU.mult, [R, gain], [R2])
        k.tt("dve", R2[rs, h0:14, :], R2[rs, h0:14, :], st14[rs, h0:14].unsqueeze(2).to_broadcast([rows, nh, 64]), ALU.mult,
             [R2, st14], [R2])
        cosb = cs[rs, 0:32].unsqueeze(1).to_broadcast([rows, nh, 32])
        sinb = cs[rs, 32:64].unsqueeze(1).to_broadcast([rows, nh, 32])
        x1 = R2[rs, h0:14, 0:32]
        x2 = R2[rs, h0:14, 32:64]
        k.tt("dve", T1[rs, h0:14, :], x1, cosb, ALU.mult, [R2, cs], [T1])
        k.tt("pool", T2[rs, h0:14, :], x2, sinb, ALU.mult, [R2, cs], [T2])
        k.tt("dve", R[rs, h0:14, 0:32], T1[rs, h0:14, :], T2[rs, h0:14, :], ALU.subtract, [T1, T2], [R])
        k.tt("dve", T1[rs, h0:14, :], x2, cosb, ALU.mult, [R2, cs], [T1])
        k.tt("pool", T2[rs, h0:14, :], x1, sinb, ALU.mult, [R2, cs], [T2])
        k.tt("dve", R[rs, h0:14, 32:64], T1[rs, h0:14, :], T2[rs, h0:14, :], ALU.add, [T1, T2], [R])
        k.op("act", lambda e: e.copy(out=kvo[rs, :, 0:128], in_=R[rs, 8:14, :].rearrange("p (a g) d -> p a (g d)", a=3)), [R], [kvo])
        if kv_dst is not None:
            for i, dst in enumerate(kv_dst):
                if dst is not None:
                    k.dma("pool", dst[0], kvo[rs, i, :], reads=[kvo], writes=[dst[1]])
        if True:
            k.op("act", lambda e: e.copy(out=qkv_bf[rs, 0:512].rearrange("p (hh g d) -> p g hh d", hh=4, g=2),
                                         in_=R[rs, 0:8, :].rearrange("p (g hh) d -> p g hh d", g=2)), [R], [qkv_bf])
            k.op("act", lambda e: e.copy(out=qkv_bf[rs, 512:896].rearrange("p (h d) -> p h d", d=64), in_=R[rs, 8:14, :]), [R], [qkv_bf])
            k.op("dve", lambda e: e.tensor_copy(out=qkv_bf[rs, 896:1280].rearrange("p (a c) -> p a c", a=3), in_=kvo[rs, :, 128:256]), [kvo], [qkv_bf])
            k.dma("pool", scr_qk[lt * 128:lt * 128 + rows, :], qkv_bf[rs, :], reads=[qkv_bf], writes=[scr_qk])
            if full:
                gi_ = (NOWN + 1) if sample else (lt - (NPRE - 1))
                k.dma("pool", scr_gate[gi_ * 128:gi_ * 128 + rows, :], gate_sb[rs, :], reads=[gate_sb], writes=[scr_gate])
        ck(5)
        if full:
            z = projr(1304, 1816)
            k.act(qb[rs, :], z[rs, :], AF.Silu, [z], [qb])
        z = projr(1816, 2328)
        k.act(u_sb[rs, :], z[rs, :], AF.Sigmoid, [z], [u_sb])
        k.tt("dve", u_sb[rs, :], u_sb[rs, :], oml[rs, :], ALU.mult, [u_sb, oml], [u_sb])
        k.tt("dve", u_sb[rs, :], u_sb[rs, :], lb[rs, :], ALU.add, [u_sb, lb], [u_sb])
        k.act(logf[rs, :], u_sb[rs, :], AF.Ln, [u_sb], [logf])
        k.ts("pool", kb[rs, :], u_sb[rs, :], -1.0, 1.0, ALU.mult, ALU.add, [u_sb], [kb])
        z = projr(2328, 2840)
        k.op("act", lambda e: e.copy(out=vb[rs, :], in_=z[rs, :]), [z], [vb])
        if full:
            z = projr(2840, 3352)
            k.act(ex[rs, :], z[rs, :], AF.Silu, [z], [ex])
            k.tt("dve", ggb[rs, :], ex[rs, :], gn_bc[rs, :, :].rearrange("p h v -> p (h v)"), ALU.mult, [ex, gn_bc], [ggb])
        ck(6)
        mi = 3 if sample else 1
        nch = 4 if sample else 2
        i0 = 2 if sample else 0
        zD = psH[0]
        k.mm(zD[rs, :], [(cm[rs, mi + 1, rs], logf[rs, :])], [cm, logf], [zD])
        k.act(ex[rs, :], zD[rs, :], AF.Exp, [zD], [ex])
        k.tt("dve", kd[rs, :], kb[rs, :], ex[rs, :], ALU.mult, [kb, ex], [kd])
        if full:
            zB = psH[1]
            k.mm(zB[rs, :], [(cm[rs, mi, rs], logf[rs, :])], [cm, logf], [zB])
            k.act(ex[rs, :], zB[rs, :], AF.Exp, [zB], [ex])
            k.tt("dve", qe[rs, :], qb[rs, :], ex[rs, :], ALU.mult, [qb, ex], [qe])
            k.act(ex[rs, :], zB[rs, :], AF.Exp, [zB], [ex], scale=-1.0)
            k.tt("dve", ke[rs, :], kb[rs, :], ex[rs, :], ALU.mult, [kb, ex], [ke])
            for h in range(4):
                k.op("pe", lambda e, h=h: e.transpose(psT[:, h, rs], qe[rs, h * 128:(h + 1) * 128], ident_bf[rs, rs]), [qe, ident_bf], [psT])
                k.op("pe", lambda e, h=h: e.transpose(psT[:, 4 + h, rs], ke[rs, h * 128:(h + 1) * 128], ident_bf[rs, rs]), [ke, ident_bf], [psT])
            k.op("act", lambda e: e.copy(out=qkT[:, :, rs], in_=psT[:, :, rs]), [psT], [qkT])
            ccb = 2 if sample else 0
            for c in range(nch):
                k.tt("dve", qeTm[c][:, :, rs], qkT[:, 0:4, rs], ccol[:, ccb + c, rs].unsqueeze(1).to_broadcast([P, 4, rows]), ALU.mult,
                     [qkT, ccol], [qeTm[c]])
            for h in range(4):
                k.mm(psA[rs, h, rs], [(qkT[:, 4 + h, rs], qkT[:, h, rs])], [qkT], [psA])
            k.tt("dve", AmT[rs, :, rs], psA[rs, :, rs], cm[rs, mi, rs].unsqueeze(1).to_broadcast([rows, 4, rows]), ALU.mult,
                 [psA, cm], [AmT])
        first_o = [True]
        for c in range(nch):
            st = S32[0] if not sample else S32[1 + c]
            sbf = Sbf[0] if not sample else Sbf[1 + c]
            ind = ci[rs, i0 + c:i0 + c + 1]
            if full:
                for h in range(4):
                    fo = first_o[0]
                    first_o[0] = False
                    k.op("pe", lambda e, h=h, fo=fo, c=c, sbf=sbf: e.matmul(psO[rs, h, :], lhsT=qeTm[c][:, h, rs], rhs=sbf[:, h, :], start=fo, stop=False,
                                                                       skip_group_check=True), [qeTm[c], sbf], [psO])
            for h in range(4):
                k.mm(psS[:, h, 0:1], [(logf[rs, h * 128:(h + 1) * 128], ind)], [logf, ci], [psS])
            k.act(dec[:, 4 * c:4 * c + 4], psS[:, :, 0], AF.Exp, [psS], [dec])
            k.ts("pool", kdm[rs, :], kd[rs, :], ind, None, ALU.mult, None, [kd, ci], [kdm])
            for h in range(4):
                k.mm(psS[:, h, :], [(kdm[rs, h * 128:(h + 1) * 128], vb[rs, h * 128:(h + 1) * 128])], [kdm, vb], [psS])
            for h in range(4):
                k.op("dve", lambda e, h=h: e.scalar_tensor_tensor(out=st[:, h, :], in0=st[:, h, :], scalar=dec[:, 4 * c + h:4 * c + h + 1],
                                                                  in1=psS[:, h, :], op0=ALU.mult, op1=ALU.add),
                     [st, dec, psS], [st])
            k.op("act", lambda e: e.copy(out=sbf[:], in_=st[:]), [st], [sbf])
        if full:
            for h in range(4):
                k.op("pe", lambda e, h=h: e.matmul(psO[rs, h, :], lhsT=AmT[rs, h, rs], rhs=vb[rs, h * 128:(h + 1) * 128], start=False, stop=True,
                                                   skip_group_check=True), [AmT, vb], [psO])
            k.act(osq[rs, :], psO[rs, :, :].rearrange("p h v -> p (h v)"), AF.Square, [psO], [osq])
            k.op("dve", lambda e: e.tensor_reduce(out=st8[rs, 0:4], in_=osq[rs, :].rearrange("p (h v) -> p h v", h=4), axis=AX.X, op=ALU.add), [osq], [st8])
            k.ts("dve", st8[rs, 0:4], st8[rs, 0:4], 1.0 / 128, EPS, ALU.mult, ALU.add, [st8], [st8])
            k.act(st8[rs, 0:4], st8[rs, 0:4], AF.Sqrt, [st8], [st8])
            k.op("dve", lambda e: e.reciprocal(out=st8[rs, 4:8], in_=st8[rs, 0:4]), [st8], [st8])
            for h in range(4):
                k.op("dve", lambda e, h=h: e.scalar_tensor_tensor(out=oab[rs, 512 + h * 128:512 + (h + 1) * 128], in0=psO[rs, h, :], scalar=st8[rs, 4 + h:5 + h],
                                                                  in1=ggb[rs, h * 128:(h + 1) * 128], op0=ALU.mult, op1=ALU.mult),
                     [psO, st8, ggb], [oab])
            ck(7)
            ti = (NOWN + 1) if sample else (lt - (NPRE - 1))
            k.dma("pool", scr_oab[ti * 128:ti * 128 + rows, 512:1024], oab[rs, 512:1024], reads=[oab], writes=[scr_oab])
            if sample and npool == 0:
                k.dma("pool", scr_oab[ti * 128:ti * 128 + rows, 0:512], oab[rs, 0:512], reads=[oab], writes=[scr_oab])

    for lt in range(NT - ntp, NT):
        own = lt >= NPRE
        full = lt >= NPRE - 1
        kv_dst = None
        if own:
            r0 = (lt - NPRE) * 128
            kv_dst = [(o_kv[i, r0:r0 + 128, :], o_kv) for i in range(3)]
        do_tile(lt, 128, xloc[lt * 128:(lt + 1) * 128, :], cs_tab[lt * 128:(lt + 1) * 128, :], full, kv_dst, False)
    k.dma("sp", o_hg_p[:].rearrange("h k v -> k h v"), S32[0][:], reads=[S32[0]], writes=[o_hg_p])
    if with_sample:
        kv_dst = [(o_kvs[0, :, :], o_kvs), (o_kvs[1, :, :], o_kvs), None]
        do_tile(NT, 32, xs[:, :], cs_tab[NT * 128:NT * 128 + 32, :], True, kv_dst, True)
        for b in range(4):
            k.dma("sp", o_win_s[b, 504:512, :], kvo[8 * b:8 * b + 8, 2, :], reads=[kvo], writes=[o_win_s])
            k.dma("sp", o_hg_s[b].rearrange("h k v -> k h v"), S32[1 + b][:], reads=[S32[1 + b]], writes=[o_hg_s])
    k.pop()
    ck(10)
    pass1b(k, ck, ntp, locals())
    ck(20)
    env = dict(locals())
    env["identf_d"] = Buf("identf", k.nc_identf)
    env.update(k.shared)
    if with_sample and npool > 0:
        pass1c(k, ck, env, npool)
    ck(25)
    env["scr_x1"] = k.dram("scr_x1", [(NOWN + 1) * 128 + 32, D], F32)
    pass2(k, ck, env)
    ck(30)
    pass3(k, ck, env)


def pass1b(k, ck, ntp, env):
    P = 128
    scr_qk, scr_gate, scr_oab = env["scr_qk"], env["scr_gate"], env["scr_oab"]
    w1bd_d = k.dram("w1bd", [128, 64 * 128], F32, "ExternalInput")
    w2bd_d = k.dram("w2bd", [128, 2 * 128], F32, "ExternalInput")
    posvec_d = k.dram("posvec", [128, 64], F32, "ExternalInput")
    cover_d = k.dram("cover", [128, 2 * 62], F32, "ExternalInput")
    cmpB_d = k.dram("cmpB", [NOWN + 1, 128, 2 * 128], F32, "ExternalInput")
    selc_d = k.dram("selc", [NOWN + 1, 128, 2 * 64], F32, "ExternalInput")
    efull_d = k.dram("efull", [64, 4096], F32, "ExternalInput")
    triB_d = k.dram("triB", [128, 2 * 128], F32, "ExternalInput")
    pfx_d = k.dram("pfx", [1, 128], F32, "ExternalInput")
    identf_d = k.dram("identf", [128, 128], F32, "ExternalInput")
    k.nc_identf = identf_d.t
    k.shared = dict(w1bd_d=w1bd_d, w2bd_d=w2bd_d, posvec_d=posvec_d)

    k.push()
    W1 = k.sb([P, 64, 128], BF16, "W1")
    for a in range(4):
        k.dma("pool", W1[:, a * 16:(a + 1) * 16, :], w1bd_d[:, a * 2048:(a + 1) * 2048].rearrange("p (a b) -> p a b", b=128), writes=[W1])
    W2 = k.sb([P, 2, 128], BF16, "W2")
    k.dma("pool", W2[:], w2bd_d[:].rearrange("p (a b) -> p a b", b=128), writes=[W2])
    posv = k.sb([P, 64], BF16, "posv")
    k.dma("pool", posv[:], posvec_d[:], writes=[posv])
    efull = k.sb([64, 4096], BF16, "efull")
    k.dma("pool", efull[:], efull_d[:], writes=[efull])
    triB = k.sb([P, 2, 4, 128], BF16, "triB")
    for hh in range(4):
        k.dma("pool", triB[:, :, hh, :], triB_d[:].rearrange("p (a b) -> p a b", b=128), writes=[triB])
    pfx = k.sb([1, 128], BF16, "pfx")
    k.dma("pool", pfx[:], pfx_d[:], writes=[pfx])
    ones_row = k.sb([1, 512], BF16, "ones_row")
    k.op("dve", lambda e: e.memset(ones_row[:], 1.0), [], [ones_row])
    identf = k.sb([P, 128], F32, "identf")
    k.dma("sp", identf[:], identf_d[:], writes=[identf])
    ident_bf = k.sb([P, 128], BF16, "ident_bf2")
    k.op("dve", lambda e: e.tensor_copy(out=ident_bf[:], in_=identf[:]), [identf], [ident_bf])
    KS = k.sb([P, 2, NT * 128], BF16, "KS")
    KC2 = k.sb([P, 2, 256], BF16, "KC2")
    VX = k.sb([P, 2, NT, 2, 65], BF16, "VX")
    kvccT = k.sb([P, 2, 256], BF16, "kvccT")
    VcX = k.sb([P, 2, 2, 127], BF16, "VcX")
    k.op("pool", lambda e: e.memset(KC2[:], 0.0), [], [KC2])
    k.op("pool", lambda e: e.memset(kvccT[:], 0.0), [], [kvccT])
    k.op("dve", lambda e: e.memset(VX[:, :, :, :, 64:65], 1.0), [], [VX])
    k.op("dve", lambda e: e.memset(VcX[:, :, :, 0:64], 0.0), [], [VcX])
    k.op("dve", lambda e: e.memset(VcX[:, :, :, 64:65], 1.0), [], [VcX])
    for g in range(2):
        k.dma("pool", VcX[:, :, g, 65:127], cover_d[:].rearrange("p (a b) -> p a b", b=62), writes=[VcX])
    qkv = [k.sb([P, 1280], BF16, "qkv%d" % i) for i in range(2)]
    qT = k.sb([P, 4, 128], BF16, "qT")
    spre = k.sb([P, 2, 8], BF16, "spre")
    posW = k.sb([P, 2], F32, "posW")
    cmpB = k.sb([P, 2, 4, 128], BF16, "cmpB_t")
    selc = k.sb([P, 2, 64], F32, "selc_t")
    gate = k.sb([P, 24], F32, "gate_t")
    PT = [k.sb([P, 512], BF16, "PT%d" % i) for i in range(2)]
    ocs = k.sb([P, 2, 4, 127], F32, "ocs")
    osw = k.sb([P, 2, 4, 65], F32, "osw")
    rc = k.sb([P, 8], F32, "rc")
    coef = k.sb([P, 8], F32, "coef")
    imp = k.sb([P, 2, 64], F32, "imp")
    sc2 = k.sb([P, 64], F32, "sc2")
    top = k.sb([P, 16], F32, "top")
    nsel = k.sb([P, 2, 64], F32, "nsel")
    selT = k.sb([64, 2, 4, 128], BF16, "selT")
    oacc = k.sb([P, 8, 64], F32, "oacc")
    oa_bf = k.sb([P, 512], BF16, "oa_bf")
    psT = k.ps([P, 8, 128], BF16, "psTb")
    psZ = [k.ps([P, 512], F32, "psZb%d" % i) for i in range(2)]
    psV = [k.ps([P, 512], F32, "psVb%d" % i) for i in range(2)]
    psC = k.ps([P, 512], F32, "psCb")

    for j in range(2):
        for idx in range(32):
            k.op("pe", lambda e, j=j, idx=idx: e.matmul(psC[:, j:j + 1], lhsT=W1[:, j * 32 + idx, :], rhs=posv[:, j * 32 + idx:j * 32 + idx + 1],
                                                        start=(idx == 0), stop=(idx == 31)), [W1, posv], [psC])
    k.op("act", lambda e: e.copy(out=posW[:], in_=psC[:, 0:2]), [psC], [posW])
    ck(11)
    zi = [0]
    pi = [0]

    pend = [None]

    def flush():
        if pend[0] is not None:
            f = pend[0]
            pend[0] = None
            f()

    def attn_block(g, k_lhsT, biases, v_rhs, acc_ap, ncol, first):
        zi[0] ^= 1
        S = psZ[zi[0]]
        pairs = [(k_lhsT, qT[g * 64:(g + 1) * 64, :, :].rearrange("p a b -> p (a b)"))] + biases
        n = len(pairs)
        for i, (l, r) in enumerate(pairs):
            k.op("pe", lambda e, l=l, r=r, i=i: e.matmul(S[:, :], lhsT=l, rhs=r, start=(i == 0), stop=(i == n - 1)),
                 [KS, kvccT, qT, efull, selT, ident_bf, triB, cmpB, pfx, ones_row], [S])
        flush()
        pi[0] ^= 1
        pt = PT[pi[0]]
        k.act(pt[:], S[:], AF.Exp, [S], [pt])
        ab = acc_buf[0]

        def pv():
            for hh in range(4):
                k.op("pe", lambda e, hh=hh: e.matmul(acc_ap[:, hh, 0:ncol], lhsT=pt[:, hh * 128:(hh + 1) * 128], rhs=v_rhs,
                                                     start=(first and hh == 0), stop=False, skip_group_check=True), [pt, VX, VcX], [ab])
        if PIPE_B:
            pend[0] = pv
        else:
            pv()

    acc_buf = [None]

    for lt in range(NT - ntp, NT):
        full = lt >= NPRE - 1
        i_own = lt - (NPRE - 1)
        qk = qkv[lt % 2]
        k.dma("sp", qk[:], scr_qk[lt * 128:(lt + 1) * 128, :], reads=[scr_qk], writes=[qk])
        v14 = qk[:, 0:896].rearrange("p (h d) -> p h d", d=64)
        if full:
            for hh in range(4):
                k.op("pe", lambda e, hh=hh: e.transpose(psT[:, hh, :], qk[:, hh * 128:(hh + 1) * 128], ident_bf[:]), [qk, ident_bf], [psT])
        k.op("pe", lambda e: e.transpose(psT[:, 4, :], qk[:, 512:640], ident_bf[:]), [qk, ident_bf], [psT])
        k.op("pe", lambda e: e.transpose(psT[:, 5, :], qk[:, 640:768], ident_bf[:]), [qk, ident_bf], [psT])
        k.op("pe", lambda e: e.transpose(psT[:, 6, :], qk[:, 768:896], ident_bf[:]), [qk, ident_bf], [psT])
        k.op("pe", lambda e: e.transpose(psT[:, 7, :], qk[:, 896:1024], ident_bf[:]), [qk, ident_bf], [psT])
        if full:
            k.op("act", lambda e: e.copy(out=qT[:], in_=psT[:, 0:4, :]), [psT], [qT])
        k.op("act", lambda e: e.copy(out=KS[:, :, lt * 128:(lt + 1) * 128], in_=psT[:, 5:7, :]), [psT], [KS])
        k.op("dve", lambda e: e.tensor_copy(out=KC2[:, :, 0:128], in_=KC2[:, :, 128:256]), [KC2], [KC2])
        k.op("dve", lambda e: e.tensor_copy(out=KC2[:, :, 128:256], in_=psT[:, 4:8:3, :]), [psT], [KC2])
        k.op("pool", lambda e: e.tensor_copy(out=VX[:, :, lt, :, 0:64], in_=qk[:, 1024:1280].rearrange("p (a g d) -> p a g d", a=2, g=2)), [qk], [VX])
        c0 = max(8 * lt - 1, 0)
        nb = 8 * lt + 7 - c0
        for j in range(2):
            n = 0
            for r in range(2):
                for i16 in range(16):
                    st_col = 16 * (c0 + r) + i16 - 128 * lt + 128
                    k.op("pe", lambda e, j=j, r=r, i16=i16, st_col=st_col, n=n: e.matmul(
                        psC[:, 8 * j:8 * j + nb], lhsT=W1[:, j * 32 + r * 16 + i16, :], rhs=KC2[:, j, st_col:st_col + 16 * (nb - 1) + 1:16],
                        start=(n == 0), stop=(n == 31)), [W1, KC2], [psC])
                    n += 1
            k.act(spre[:, j, 0:nb], psC[:, 8 * j:8 * j + nb], AF.Silu, [psC, posW], [spre], bias=posW[:, j:j + 1])
        for j in range(2):
            k.mm(psC[:, 16 + 8 * j:16 + 8 * j + nb], [(W2[:, j, :], spre[:, j, 0:nb])], [W2, spre], [psC])
        k.op("act", lambda e: e.copy(out=kvccT[:, :, c0:c0 + nb], in_=psC[:, 16:32].rearrange("p (j c) -> p j c", j=2)[:, :, 0:nb]), [psC], [kvccT])
        if not full:
            continue
        ck(12)
        k.dma("sp", selc[:], selc_d[i_own].rearrange("p (a b) -> p a b", b=64), writes=[selc])
        k.dma("sp", gate[:], scr_gate[i_own * 128:(i_own + 1) * 128, :], reads=[scr_gate], writes=[gate])
        for hh in range(4):
            k.dma("pool", cmpB[:, :, hh, :], cmpB_d[i_own].rearrange("p (a b) -> p a b", b=128), writes=[cmpB])
        nchv = 1 if lt <= 15 else 2
        for ch in range(nchv):
            k.op("pe", lambda e, ch=ch: e.transpose(psT[:, ch, :], kvccT[:, 1, ch * 128:(ch + 1) * 128], ident_bf[:]), [kvccT, ident_bf], [psT])
        k.op("act", lambda e: e.copy(out=VcX[:, 0:nchv, :, 0:64], in_=psT[:, 0:nchv, :].rearrange("p c (g d) -> p c g d", g=2)), [psT], [VcX])
        for g in range(2):
            acc_buf[0] = psV[g]
            acc = psV[g][:, 0:508].rearrange("p (h c) -> p h c", c=127)
            for ch in range(nchv):
                attn_block(g, kvccT[g * 64:(g + 1) * 64, 0, ch * 128:(ch + 1) * 128],
                           [(ident_bf[:], cmpB[:, ch, :, :].rearrange("p a b -> p (a b)"))],
                           VcX[:, ch, g, :], acc, 127, ch == 0)
            flush()
            k.op("act", lambda e, g=g, acc=acc: e.copy(out=ocs[:, g, :, :], in_=acc), [psV[g]], [ocs])
        ck(13)
        k.ts("dve", rc[:], ocs[:, :, :, 64].rearrange("p g h -> p (g h)"), 1e-30, None, ALU.max, None, [ocs], [rc])
        k.op("dve", lambda e: e.reciprocal(out=rc[:], in_=rc[:]), [rc], [rc])
        gv = gate[:].rearrange("p (h b) -> p h b", b=3)
        k.tt("dve", coef[:], rc[:], gv[:, :, 0], ALU.mult, [rc, gate], [coef])
        for h in range(8):
            k.ts("dve", oacc[:, h, :], ocs[:, h // 4, h % 4, 0:64], coef[:, h:h + 1], None, ALU.mult, None, [ocs, coef], [oacc])
        for g in range(2):
            k.ts("dve", imp[:, g, 0:62], ocs[:, g, 0, 65:127], rc[:, 4 * g:4 * g + 1], None, ALU.mult, None, [ocs, rc], [imp])
            for hh in range(1, 4):
                k.op("dve", lambda e, g=g, hh=hh: e.scalar_tensor_tensor(out=imp[:, g, 0:62], in0=ocs[:, g, hh, 65:127], scalar=rc[:, 4 * g + hh:4 * g + hh + 1],
                                                                         in1=imp[:, g, 0:62], op0=ALU.mult, op1=ALU.add), [ocs, rc, imp], [imp])
        k.op("dve", lambda e: e.memset(imp[:, :, 62:64], 0.0), [], [imp])
        for g in range(2):
            k.tt("dve", imp[:, g, :], imp[:, g, :], selc[:, 0, :], ALU.mult, [imp, selc], [imp])
            k.tt("dve", imp[:, g, :], imp[:, g, :], selc[:, 1, :], ALU.add, [imp, selc], [imp])
            k.op("dve", lambda e, g=g: e.max(out=top[:, 0:8], in_=imp[:, g, :]), [imp], [top])
            k.op("dve", lambda e, g=g: e.match_replace(out=sc2[:], in_to_replace=top[:, 0:8], in_values=imp[:, g, :], imm_value=-1e30), [imp, top], [sc2])
            k.op("dve", lambda e: e.max(out=top[:, 8:16], in_=sc2[:]), [sc2], [top])
            k.ts("dve", top[:, 15:16], top[:, 15:16], -0.5, None, ALU.max, None, [top], [top])
            k.ts("dve", nsel[:, g, :], imp[:, g, :], top[:, 15:16], None, ALU.is_ge, None, [imp, top], [nsel])
            k.ts("dve", nsel[:, g, :], nsel[:, g, :], 30000.0, -30000.0, ALU.mult, ALU.add, [nsel], [nsel])
            k.mm(psC[0:64, 64 + 128 * g:64 + 128 * (g + 1)], [(nsel[:, g, :], identf[:])], [nsel, identf], [psC])
        for hh in range(4):
            k.op("act", lambda e, hh=hh: e.copy(out=selT[:, :, hh, :], in_=psC[0:64, 64:320].rearrange("p (g t) -> p g t", g=2)), [psC], [selT])
        ck(14)
        for g in range(2):
            acc_buf[0] = psV[g]
            acc = psV[g][:, 0:260].rearrange("p (h c) -> p h c", c=65)
            for kb in range(0, lt + 1):
                biases = [(efull[:, kb * 128:(kb + 1) * 128], selT[:, g, :, :].rearrange("p a b -> p (a b)"))]
                if kb == lt:
                    biases.append((ident_bf[:], triB[:, 0, :, :].rearrange("p a b -> p (a b)")))
                attn_block(g, KS[g * 64:(g + 1) * 64, 0, kb * 128:(kb + 1) * 128], biases, VX[:, 0, kb, g, :], acc, 65, kb == 0)
            flush()
            k.op("act", lambda e, g=g, acc=acc: e.copy(out=osw[:, g, :, :], in_=acc), [psV[g]], [osw])
        for br in (1, 2):
            if br == 2:
                for g in range(2):
                    acc_buf[0] = psV[g]
                    acc = psV[g][:, 0:260].rearrange("p (h c) -> p h c", c=65)
                    kb0 = max(lt - 4, 0)
                    for kb in range(kb0, lt + 1):
                        biases = []
                        if kb == lt:
                            biases.append((ident_bf[:], triB[:, 0, :, :].rearrange("p a b -> p (a b)")))
                        if kb == lt - 4:
                            biases.append((ident_bf[:], triB[:, 1, :, :].rearrange("p a b -> p (a b)")))
                        if kb < NPRE:
                            biases.append((pfx[:], ones_row[:]))
                        attn_block(g, KS[g * 64:(g + 1) * 64, 1, kb * 128:(kb + 1) * 128], biases, VX[:, 1, kb, g, :], acc, 65, kb == kb0)
                    flush()
                    k.op("act", lambda e, g=g, acc=acc: e.copy(out=osw[:, g, :, :], in_=acc), [psV[g]], [osw])
            k.ts("dve", rc[:], osw[:, :, :, 64].rearrange("p g h -> p (g h)"), 1e-30, None, ALU.max, None, [osw], [rc])
            k.op("dve", lambda e: e.reciprocal(out=rc[:], in_=rc[:]), [rc], [rc])
            k.tt("dve", coef[:], rc[:], gv[:, :, br], ALU.mult, [rc, gate], [coef])
            for h in range(8):
                k.op("dve", lambda e, h=h: e.scalar_tensor_tensor(out=oacc[:, h, :], in0=osw[:, h // 4, h % 4, 0:64], scalar=coef[:, h:h + 1],
                                                                  in1=oacc[:, h, :], op0=ALU.mult, op1=ALU.add), [osw, coef, oacc], [oacc])
        k.op("act", lambda e: e.copy(out=oa_bf[:], in_=oacc[:].rearrange("p h d -> p (h d)")), [oacc], [oa_bf])
        k.dma("pool", scr_oab[i_own * 128:(i_own + 1) * 128, 0:512], oa_bf[:], reads=[oa_bf], writes=[scr_oab])
    k.pop()


def pass2(k, ck, env):
    P = 128
    xloc, xs, w_in, scr_hT, scr_oab = env["xloc"], env["xs"], env["w_in"], env["scr_hT"], env["scr_oab"]
    w_br_d = k.dram("w_branch", [D, D], F32, "ExternalInput")
    w_out_d = k.dram("w_out", [D, D], F32, "ExternalInput")
    identf_d = env["identf_d"]
    scr_x1 = env["scr_x1"]
    k.push()
    wmg = k.sb([P, 8, 2048], BF16, "wmg")
    wbr = k.sb([P, 8, 1024], BF16, "wbr")
    wo = k.sb([P, 8, 1024], BF16, "wo")
    for kc in range(8):
        k.dma("pool", wmg[:, kc, :], w_in[kc * 128:(kc + 1) * 128, NCOL1:5400], writes=[wmg])
        k.dma("pool", wbr[:, kc, :], w_br_d[kc * 128:(kc + 1) * 128, :], writes=[wbr])
        k.dma("pool", wo[:, kc, :], w_out_d[kc * 128:(kc + 1) * 128, :], writes=[wo])
    ident_bf = k.sb([P, 128], BF16, "ident_bf3")
    k.dma("pool", ident_bf[:], identf_d[:], writes=[ident_bf])
    xt = [k.sb([P, D], F32, "x2_%d" % i) for i in range(2)]
    hT = [k.sb([P, 8, 128], BF16, "hT2_%d" % i) for i in range(2)]
    oab = [k.sb([P, 1024], BF16, "oab2_%d" % i) for i in range(2)]
    mab = k.sb([P, 2048], BF16, "mab")
    oT = k.sb([P, 8, 128], BF16, "oT")
    m1 = k.sb([P, 1024], F32, "m1")
    m2 = k.sb([P, 1024], F32, "m2")
    mbf = k.sb([P, 1024], BF16, "mbf")
    mT = k.sb([P, 8, 128], BF16, "mT")
    x1 = k.sb([P, 1024], F32, "x1")
    psT = k.ps([P, 8, 128], BF16, "psT2")
    psZ = [k.ps([P, 512], F32, "psZ2_%d" % i) for i in range(3)]
    zi = [0]

    def nz():
        zi[0] = (zi[0] + 1) % 3
        return psZ[zi[0]]

    for ti in range(NOWN + 2):
        sample = ti == NOWN + 1
        rows = 32 if sample else 128
        rs = slice(0, rows)
        x = xt[ti % 2]
        h = hT[ti % 2]
        ob = oab[ti % 2]
        xsrc = xs[:, :] if sample else xloc[(NPRE - 1 + ti) * 128:(NPRE + ti) * 128, :]
        k.dma("sp", x[rs, :], xsrc, writes=[x])
        k.dma("sp", h[:, :, rs], scr_hT[ti, :, :].rearrange("p (a b) -> p a b", b=128)[:, :, rs], reads=[scr_hT], writes=[h])
        k.dma("sp", ob[rs, :], scr_oab[ti * 128:ti * 128 + rows, :], reads=[scr_oab], writes=[ob])
        for c in range(4):
            z = nz()
            k.mm(z[rs, :], [(h[:, kc, rs], wmg[:, kc, c * 512:(c + 1) * 512]) for kc in range(8)], [h, wmg], [z])
            k.act(mab[rs, c * 512:(c + 1) * 512], z[rs, :], AF.Sigmoid, [z], [mab])
        for kc in range(8):
            k.op("pe", lambda e, kc=kc: e.transpose(psT[:, kc, rs], ob[rs, kc * 128:(kc + 1) * 128], ident_bf[rs, rs]), [ob, ident_bf], [psT])
        k.op("act", lambda e: e.copy(out=oT[:, :, rs], in_=psT[:, :, rs]), [psT], [oT])
        for c in range(2):
            z = nz()
            k.mm(z[rs, :], [(oT[:, kc, rs], wbr[:, kc, c * 512:(c + 1) * 512]) for kc in range(4)], [oT, wbr], [z])
            k.tt("dve", m1[rs, c * 512:(c + 1) * 512], z[rs, :], mab[rs, c * 512:(c + 1) * 512], ALU.mult, [z, mab], [m1])
            z = nz()
            k.mm(z[rs, :], [(oT[:, kc, rs], wbr[:, kc, c * 512:(c + 1) * 512]) for kc in range(4, 8)], [oT, wbr], [z])
            k.tt("dve", m2[rs, c * 512:(c + 1) * 512], z[rs, :], mab[rs, 1024 + c * 512:1024 + (c + 1) * 512], ALU.mult, [z, mab], [m2])
        k.tt("pool", mbf[rs, :], m1[rs, :], m2[rs, :], ALU.add, [m1, m2], [mbf])
        for kc in range(8):
            k.op("pe", lambda e, kc=kc: e.transpose(psT[:, kc, rs], mbf[rs, kc * 128:(kc + 1) * 128], ident_bf[rs, rs]), [mbf, ident_bf], [psT])
        k.op("act", lambda e: e.copy(out=mT[:, :, rs], in_=psT[:, :, rs]), [psT], [mT])
        for c in range(2):
            z = nz()
            k.mm(z[rs, :], [(mT[:, kc, rs], wo[:, kc, c * 512:(c + 1) * 512]) for kc in range(8)], [mT, wo], [z])
            k.tt("dve", x1[rs, c * 512:(c + 1) * 512], z[rs, :], x[rs, c * 512:(c + 1) * 512], ALU.add, [z, x], [x1])
        k.dma("pool", scr_x1[ti * 128:ti * 128 + rows, :], x1[rs, :], reads=[x1], writes=[scr_x1])
    k.pop()


def pass3(k, ck, env):
    P = 128
    scr_x1, identf_d = env["scr_x1"], env["identf_d"]
    f1_d = k.dram("ffn_w_in", [D, 5632], F32, "ExternalInput")
    f2_d = k.dram("ffn_w_out", [2816, D], F32, "ExternalInput")
    fvec_d = k.dram("fvec", [1, 1024], F32, "ExternalInput")
    cw_d = k.dram("convw", [128, 22 * 4], F32, "ExternalInput")
    cst_d = k.dram("convst", [128, 22 * 8], F32, "ExternalInput")
    o_y = k.dram("o_y", [NOWN * 128, D], F32, "ExternalOutput")
    o_ys = k.dram("o_ys", [32, D], F32, "ExternalOutput")
    o_cp = k.dram("o_cp", [128, 22 * 2], F32, "ExternalOutput")
    o_cs = k.dram("o_cs", [128, 22 * 8], F32, "ExternalOutput")
    k.push()
    wf1 = k.sb([P, 8, 5632], BF16, "wf1")
    wf2 = k.sb([P, 22, 1024], BF16, "wf2")
    for kc in range(8):
        k.dma("pool", wf1[:, kc, :], f1_d[kc * 128:(kc + 1) * 128, :], writes=[wf1])
    for fc in range(22):
        k.dma("pool", wf2[:, fc, :], f2_d[fc * 128:(fc + 1) * 128, :], writes=[wf2])
    ident_bf = k.sb([P, 128], BF16, "ident_bf4")
    k.dma("pool", ident_bf[:], identf_d[:], writes=[ident_bf])
    g2 = k.sb([P, D], F32, "g2")
    k.dma("sp", g2[:], fvec_d[0:1, :].partition_broadcast(P), writes=[g2])
    cw = k.sb([P, 22, 4], F32, "cw")
    k.dma("sp", cw[:], cw_d[:].rearrange("p (a b) -> p a b", b=4), writes=[cw])
    x1 = [k.sb([P, D], F32, "x3_%d" % i) for i in range(2)]
    junk = k.sb([P, D], BF16, "junk3")
    h2 = k.sb([P, D], BF16, "h2")
    h2T = k.sb([P, 8, 128], BF16, "h2T")
    st4 = k.sb([P, 8], F32, "st4_3")
    aT = k.sb([P, 22, 130], F32, "aT")
    t1 = k.sb([P, 11, 128], F32, "t1")
    t2 = k.sb([P, 11, 128], F32, "t2")
    gT = k.sb([P, 22, 128], BF16, "gT")
    y = k.sb([P, D], F32, "y")
    psT = k.ps([P, 8, 128], BF16, "psT3")
    psZ = [k.ps([P, 512], F32, "psZ3_%d" % i) for i in range(3)]
    zi = [0]

    def nz():
        zi[0] = (zi[0] + 1) % 3
        return psZ[zi[0]]

    k.op("dve", lambda e: e.memset(aT[:], 0.0), [], [aT])
    for ti in range(NOWN + 2):
        sample = ti == NOWN + 1
        rows = 32 if sample else 128
        rs = slice(0, rows)
        nbat, T = (4, 8) if sample else (1, 128)
        x = x1[ti % 2]
        k.dma("sp", x[rs, :], scr_x1[ti * 128:ti * 128 + rows, :], reads=[scr_x1], writes=[x])
        k.act(junk[rs, :], x[rs, :], AF.Square, [x], [junk, st4], accum_out=st4[rs, 0:1])
        k.ts("dve", st4[rs, 1:2], st4[rs, 0:1], 1.0 / D, EPS, ALU.mult, ALU.add, [st4], [st4])
        k.act(st4[rs, 2:3], st4[rs, 1:2], AF.Sqrt, [st4], [st4])
        k.op("dve", lambda e: e.reciprocal(out=st4[rs, 3:4], in_=st4[rs, 2:3]), [st4], [st4])
        k.op("dve", lambda e: e.scalar_tensor_tensor(out=h2[rs, :], in0=x[rs, :], scalar=st4[rs, 3:4], in1=g2[rs, :],
                                                     op0=ALU.mult, op1=ALU.mult), [x, st4, g2], [h2])
        for kc in range(8):
            k.op("pe", lambda e, kc=kc: e.transpose(psT[:, kc, rs], h2[rs, kc * 128:(kc + 1) * 128], ident_bf[rs, rs]), [h2, ident_bf], [psT])
        k.op("act", lambda e: e.copy(out=h2T[:, :, rs], in_=psT[:, :, rs]), [psT], [h2T])
        av = aT[:, :, 0:nbat * (T + 2)].rearrange("p f (b t) -> p f b t", b=nbat)
        if sample:
            for b in range(4):
                k.dma("sp", av[:, :, b, 0:2], cst_d[:].rearrange("p (f b j) -> p f b j", f=22, b=4)[:, :, b, :], writes=[aT])
        for f0 in range(0, 22, 4):
            n = min(4, 22 - f0)
            z = nz()
            for j in range(n):
                fc = f0 + j
                k.mm(z[:, j * 128:j * 128 + rows], [(wf1[:, kc, fc * 128:(fc + 1) * 128], h2T[:, kc, rs]) for kc in range(8)], [wf1, h2T], [z])
            k.op("act", lambda e, f0=f0, n=n, z=z: e.copy(out=av[:, f0:f0 + n, :, 2:2 + T],
                                                       in_=z[:, 0:n * 128].rearrange("p (f t) -> p f t", t=128)[:, :, 0:rows].rearrange("p f (b t) -> p f b t", b=nbat)),
                 [z], [aT])
        for hf in range(2):
            fs = slice(hf * 11, hf * 11 + 11)
            tv1 = t1[:, :, 0:rows].rearrange("p f (b t) -> p f b t", b=nbat)
            tv2 = t2[:, :, 0:rows].rearrange("p f (b t) -> p f b t", b=nbat)

            def wb(j):
                return cw[:, fs, j:j + 1].unsqueeze(3).to_broadcast([P, 11, nbat, T])
            k.tt("dve", tv1, av[:, fs, :, 2:2 + T], wb(2), ALU.mult, [aT, cw], [t1])
            k.tt("pool", tv2, av[:, fs, :, 1:1 + T], wb(1), ALU.mult, [aT, cw], [t2])
            k.tt("dve", tv1, tv1, tv2, ALU.add, [t1, t2], [t1])
            k.tt("pool", tv2, av[:, fs, :, 0:T], wb(0), ALU.mult, [aT, cw], [t2])
            k.tt("dve", tv1, tv1, tv2, ALU.add, [t1, t2], [t1])
            k.tt("dve", tv1, tv1, wb(3), ALU.add, [t1, cw], [t1])
            k.act(t1[:, :, 0:rows], t1[:, :, 0:rows], AF.Silu, [t1], [t1])
            for f0 in range(hf * 11, hf * 11 + 11, 4):
                n = min(4, hf * 11 + 11 - f0)
                z = nz()
                for j in range(n):
                    fc = f0 + j
                    k.mm(z[:, j * 128:j * 128 + rows], [(wf1[:, kc, 2816 + fc * 128:2816 + (fc + 1) * 128], h2T[:, kc, rs]) for kc in range(8)], [wf1, h2T], [z])
                k.tt("dve", gT[:, f0:f0 + n, 0:rows], t1[:, f0 - hf * 11:f0 - hf * 11 + n, 0:rows],
                     z[:, 0:n * 128].rearrange("p (f t) -> p f t", t=128)[:, :, 0:rows], ALU.mult, [t1, z], [gT])
        for c in range(2):
            z = nz()
            k.mm(z[rs, :], [(gT[:, fc, rs], wf2[:, fc, c * 512:(c + 1) * 512]) for fc in range(22)], [gT, wf2], [z])
            k.tt("dve", y[rs, c * 512:(c + 1) * 512], z[rs, :], x[rs, c * 512:(c + 1) * 512], ALU.add, [z, x], [y])
        if sample:
            k.dma("pool", o_ys[:, :], y[rs, :], reads=[y], writes=[o_ys])
            for b in range(4):
                k.dma("pool", o_cs[:].rearrange("p (f b j) -> p f b j", f=22, b=4)[:, :, b, :], av[:, :, b, T:T + 2], reads=[aT], writes=[o_cs])
        else:
            if ti >= 1:
                k.dma("pool", o_y[(ti - 1) * 128:ti * 128, :], y[rs, :], reads=[y], writes=[o_y])
            if ti == NOWN:
                k.dma("pool", o_cp[:].rearrange("p (f j) -> p f j", j=2), aT[:, :, 128:130], reads=[aT], writes=[o_cp])
            k.op("pool", lambda e: e.tensor_copy(out=aT[:, :, 0:2], in_=aT[:, :, 128:130]), [aT], [aT])
    k.pop()


def pass1c(k, ck, env, npool):
    P = 128
    nc = k.nc
    scr_qk, scr_gate, scr_oab = env["scr_qk"], env["scr_gate"], env["scr_oab"]
    st_win = env["st_win"]
    identf_d = env["identf_d"]
    cpool = k.dram("cache_cmp", [npool, 128, 256], F32, "ExternalInput")
    spool = k.dram("cache_slc", [npool, 128, 256], F32, "ExternalInput")
    ptab_d = k.dram("ptab", [4, 128], I32, "ExternalInput")
    selcS_d = k.dram("selcS", [8, 2 * 384], F32, "ExternalInput")
    selm_d = k.dram("selm", [32, 8 + 16 * 32], F32, "ExternalInput")
    rep_d = k.dram("rep", [8, 32], F32, "ExternalInput")
    ef_d = k.dram("efull128", [128, 8192], F32, "ExternalInput")
    triS_d = k.dram("triBs", [128, 2 * 32], F32, "ExternalInput")
    w1bd_d, w2bd_d, posvec_d = env["w1bd_d"], env["w2bd_d"], env["posvec_d"]
    k.push()
    W1 = k.sb([P, 64, 128], BF16, "W1c")
    for a in range(4):
        k.dma("pool", W1[:, a * 16:(a + 1) * 16, :], w1bd_d[:, a * 2048:(a + 1) * 2048].rearrange("p (a b) -> p a b", b=128), writes=[W1])
    W2 = k.sb([P, 2, 128], BF16, "W2c")
    k.dma("pool", W2[:], w2bd_d[:].rearrange("p (a b) -> p a b", b=128), writes=[W2])
    posv = k.sb([P, 64], BF16, "posvc")
    k.dma("pool", posv[:], posvec_d[:], writes=[posv])
    ef = k.sb([P, 8192], BF16, "ef128")
    k.dma("pool", ef[:], ef_d[:], writes=[ef])
    triS = k.sb([P, 2, 32], BF16, "triS")
    k.dma("pool", triS[:], triS_d[:].rearrange("p (a b) -> p a b", b=32), writes=[triS])
    identf = k.sb([P, 128], F32, "identfc")
    k.dma("sp", identf[:], identf_d[:], writes=[identf])
    ident_bf = k.sb([P, 128], BF16, "identbc")
    k.op("dve", lambda e: e.tensor_copy(out=ident_bf[:], in_=identf[:]), [identf], [ident_bf])
    selcS = k.sb([8, 2, 384], F32, "selcS")
    k.dma("sp", selcS[:], selcS_d[:].rearrange("p (a b) -> p a b", b=384), writes=[selcS])
    selm = k.sb([32, 8 + 512], F32, "selm")
    k.dma("sp", selm[:], selm_d[:], writes=[selm])
    rep = k.sb([8, 32], F32, "rep")
    k.dma("sp", rep[:], rep_d[:], writes=[rep])
    iot_d = k.dram("iot", [128, 1], F32, "ExternalInput")
    ptb = k.sb([P, 512], I32, "ptb")
    for b in range(4):
        k.dma("sp", ptb[:, b * 128:(b + 1) * 128], ptab_d[b:b + 1, :].partition_broadcast(P), writes=[ptb])
    io = k.sb([P, 1], F32, "iotc")
    k.dma("sp", io[:], iot_d[:], writes=[io])
    idxf = k.sb([P, 512], F32, "idxf")
    pidx = k.sb([P, 512], I32, "pidx")
    k.op("dve", lambda e: e.tensor_copy(out=idxf[:], in_=ptb[:]), [ptb], [idxf])
    k.op("dve", lambda e: e.tensor_scalar(out=idxf[:], in0=idxf[:], scalar1=128.0, scalar2=io[:, 0:1], op0=ALU.mult, op1=ALU.add), [idxf, io], [idxf])
    k.op("dve", lambda e: e.tensor_copy(out=pidx[:], in_=idxf[:]), [idxf], [pidx])
    KCb = k.sb([P, 2, 16384], BF16, "KCb")
    KSb = k.sb([P, 16384 + 128], BF16, "KSb")
    VXb = k.sb([P, 129, 2, 65], BF16, "VXb")
    KWb = k.sb([P, 640], BF16, "KWb")
    VWb = k.sb([P, 5, 2, 65], BF16, "VWb")
    kvc = k.sb([P, 2, 1024], BF16, "kvc")
    vcb = k.sb([P, 8, 128], BF16, "vcb")
    k.op("pool", lambda e: e.memset(KSb[:, 16384:16512], 0.0), [], [KSb])
    k.op("pool", lambda e: e.memset(KWb[:, 512:640], 0.0), [], [KWb])
    k.op("pool", lambda e: e.memset(VXb[:, :, :, 64:65], 1.0), [], [VXb])
    k.op("pool", lambda e: e.memset(VXb[:, 128, :, 0:64], 0.0), [], [VXb])
    k.op("pool", lambda e: e.memset(VWb[:, :, :, 64:65], 1.0), [], [VWb])
    k.op("pool", lambda e: e.memset(VWb[:, 4, :, 0:64], 0.0), [], [VWb])
    k.op("pool", lambda e: e.memset(kvc[:], 0.0), [], [kvc])
    pg = [k.sb([P, 2, 256], F32, "pg%d" % i) for i in range(3)]
    qs = k.sb([8, 1280], BF16, "qs")
    qTs = k.sb([P, 4, 8], BF16, "qTs")
    spre = k.sb([P, 2, 128], BF16, "sprec")
    posW = k.sb([P, 2], F32, "posWc")
    pS = k.sb([32, 1024], F32, "pS")
    pbf = k.sb([32, 1024], BF16, "pbf")
    pTs = k.sb([P, 8, 32], BF16, "pTs")
    st = k.sb([32, 8], F32, "stc")
    impr = k.sb([32, 256], F32, "impr")
    sc = k.sb([8, 2, 384], F32, "scc")
    sc2 = k.sb([8, 384], F32, "sc2c")
    top = k.sb([8, 16], F32, "topc")
    selTs = k.sb([P, 3, 2, 32], BF16, "selTs")
    PTs = [k.sb([P, 32], BF16, "PTs%d" % i) for i in range(2)]
    obr = k.sb([32, 3, 2, 65], F32, "obr")
    gate_r = k.sb([32, 2, 3], F32, "gate_r")
    rcs = k.sb([32, 8], F32, "rcs")
    ofin = k.sb([32, 2, 64], F32, "ofin")
    oas = k.sb([32, 512], BF16, "oas")
    psT = k.ps([P, 8, 128], BF16, "psTc")
    psF = k.ps([P, 4, 128], F32, "psFc")
    psS = k.ps([P, 1024], F32, "psSc")
    psC = k.ps([P, 512], F32, "psCc")
    psA = k.ps([P, 512], F32, "psAc")
    psO = k.ps([P, 512], F32, "psOc")
    psX = k.ps([P, 512], F32, "psXc")
    pendc = [None]
    for j in range(2):
        for idx in range(32):
            k.op("pe", lambda e, j=j, idx=idx: e.matmul(psC[:, j:j + 1], lhsT=W1[:, j * 32 + idx, :], rhs=posv[:, j * 32 + idx:j * 32 + idx + 1],
                                                        start=(idx == 0), stop=(idx == 31)), [W1, posv], [psC])
    k.op("act", lambda e: e.copy(out=posW[:], in_=psC[:, 0:2]), [psC], [posW])
    k.op("dve", lambda e: e.memset(pS[:], 0.0), [], [pS])
    first_o = [True]
    rn = [0]
    for b in range(4):
        r0 = NT * 128 + 8 * b
        k.dma("sp", qs[:], scr_qk[r0:r0 + 8, :], reads=[scr_qk], writes=[qs])
        for hh in range(4):
            k.dma("sp", gate_r[hh * 8:hh * 8 + 8, :, :],
                  scr_gate[NOWN * 128 + 128 + 8 * b:NOWN * 128 + 128 + 8 * b + 8, :].rearrange("p (g x) -> p g x", g=2)[:, :, hh * 3:hh * 3 + 3],
                  reads=[scr_gate], writes=[gate_r])
        for hh in range(4):
            k.op("pe", lambda e, hh=hh: e.transpose(psT[:, hh, 0:8], qs[:, hh * 128:(hh + 1) * 128], ident_bf[0:8, 0:8]), [qs, ident_bf], [psT])
        k.op("pe", lambda e: e.transpose(psT[:, 4, 0:8], qs[:, 640:768], ident_bf[0:8, 0:8]), [qs, ident_bf], [psT])
        k.op("pe", lambda e: e.transpose(psT[:, 5, 0:8], qs[:, 768:896], ident_bf[0:8, 0:8]), [qs, ident_bf], [psT])
        k.op("act", lambda e: e.copy(out=qTs[:], in_=psT[:, 0:4, 0:8]), [psT], [qTs])
        k.op("act", lambda e: e.copy(out=KSb[:, 16384:16392], in_=psT[:, 4, 0:8]), [psT], [KSb])
        k.op("act", lambda e: e.copy(out=KWb[:, 512:520], in_=psT[:, 5, 0:8]), [psT], [KWb])
        k.dma("sp", VXb[0:8, 128, :, 0:64], scr_qk[r0:r0 + 8, 1024:1152].rearrange("p (g d) -> p g d", g=2), reads=[scr_qk], writes=[VXb])
        k.dma("sp", VWb[0:8, 4, :, 0:64], scr_qk[r0:r0 + 8, 1152:1280].rearrange("p (g d) -> p g d", g=2), reads=[scr_qk], writes=[VWb])
        qflat = qTs[:].rearrange("p a b -> p (a b)")
        for p in range(128):
            t_pg = pg[p % 3]
            for which, pool_d in ((0, cpool), (1, spool)):
                k._sync("pool", [pidx], [t_pg])
                di = k.dnext
                k.dnext = (k.dnext + 1) % NDS
                k._wait("pool", ("d", di), k.dval[di])
                ins = nc.gpsimd.indirect_dma_start(out=t_pg[:, which, :], out_offset=None, in_=pool_d[:].rearrange("n r f -> (n r) f"),
                                                   in_offset=bass.IndirectOffsetOnAxis(ap=pidx[:, b * 128 + p:b * 128 + p + 1], axis=0))
                k.n_ins += 1
                k.dval[di] += 16
                ins.then_inc(k.dsems[di], 16)
                t_pg.w[("d", di)] = k.dval[di]
            k.op("pe", lambda e, t_pg=t_pg: e.transpose(psF[:, 0, :], t_pg[:, 0, 0:128], identf[:]), [t_pg, identf], [psF])
            k.op("pe", lambda e, t_pg=t_pg: e.transpose(psF[:, 1, :], t_pg[:, 0, 128:256], identf[:]), [t_pg, identf], [psF])
            k.op("pe", lambda e, t_pg=t_pg: e.transpose(psF[:, 2, :], t_pg[:, 1, 0:128], identf[:]), [t_pg, identf], [psF])
            k.op("act", lambda e, p=p: e.copy(out=KCb[:, :, p * 128:(p + 1) * 128], in_=psF[:, 0:2, :]), [psF], [KCb])
            k.op("dve", lambda e, p=p: e.tensor_copy(out=KSb[:, p * 128:(p + 1) * 128], in_=psF[:, 2, :]), [psF], [KSb])
            k.op("dve", lambda e, p=p, t_pg=t_pg: e.tensor_copy(out=VXb[:, p, :, 0:64], in_=t_pg[:, 1, 128:256].rearrange("p (g d) -> p g d", g=2)), [t_pg], [VXb])
        for i in range(4):
            t_pg = pg[i % 3]
            k.dma("sp", t_pg[:, 0, :], st_win[b, i * 128:(i + 1) * 128, :], writes=[t_pg])
            k.op("pe", lambda e, t_pg=t_pg: e.transpose(psF[:, 3, :], t_pg[:, 0, 0:128], identf[:]), [t_pg, identf], [psF])
            k.op("act", lambda e, i=i: e.copy(out=KWb[:, i * 128:(i + 1) * 128], in_=psF[:, 3, :]), [psF], [KWb])
            k.op("dve", lambda e, i=i, t_pg=t_pg: e.tensor_copy(out=VWb[:, i, :, 0:64], in_=t_pg[:, 0, 128:256].rearrange("p (g d) -> p g d", g=2)), [t_pg], [VWb])
        for gi in range(8):
            c0 = max(128 * gi - 1, 0)
            nb = 128 * gi + 127 - c0
            for j in range(2):
                n = 0
                for r in range(2):
                    for i16 in range(16):
                        s0 = 16 * (c0 + r) + i16
                        k.op("pe", lambda e, j=j, r=r, i16=i16, s0=s0, n=n, nb=nb: e.matmul(
                            psC[:, 128 * j:128 * j + nb], lhsT=W1[:, j * 32 + r * 16 + i16, :], rhs=KCb[:, j, s0:s0 + 16 * (nb - 1) + 1:16],
                            start=(n == 0), stop=(n == 31)), [W1, KCb], [psC])
                        n += 1
                k.act(spre[:, j, 0:nb], psC[:, 128 * j:128 * j + nb], AF.Silu, [psC, posW], [spre], bias=posW[:, j:j + 1])
            for j in range(2):
                k.mm(psC[:, 256 + 128 * j:256 + 128 * j + nb], [(W2[:, j, :], spre[:, j, 0:nb])], [W2, spre], [psC])
            k.op("act", lambda e, c0=c0, nb=nb: e.copy(out=kvc[:, :, c0:c0 + nb], in_=psC[:, 256:512].rearrange("p (j c) -> p j c", j=2)[:, :, 0:nb]), [psC], [kvc])
        for ch in range(8):
            k.op("pe", lambda e, ch=ch: e.transpose(psT[:, ch, :], kvc[:, 1, ch * 128:(ch + 1) * 128], ident_bf[:]), [kvc, ident_bf], [psT])
        k.op("act", lambda e: e.copy(out=vcb[:], in_=psT[:]), [psT], [vcb])
        for g in range(2):
            gs = slice(g * 64, (g + 1) * 64)
            for hf in range(2):
                k.mm(psS[0:32, hf * 512:(hf + 1) * 512], [(qflat[gs, :], kvc[gs, 0, hf * 512:(hf + 1) * 512])], [qTs, kvc], [psS])
            k.op("dve", lambda e: e.tensor_reduce(out=st[:, 0:1], in_=psS[0:32, 0:1023], axis=AX.X, op=ALU.max), [psS], [st])
            k.ts("dve", st[:, 1:2], st[:, 0:1], -1.0, None, ALU.mult, None, [st], [st])
            k.act(pS[:, 0:1023], psS[0:32, 0:1023], AF.Exp, [psS, st], [pS, st], bias=st[:, 1:2], accum_out=st[:, 2:3])
            k.op("dve", lambda e: e.reciprocal(out=st[:, 3:4], in_=st[:, 2:3]), [st], [st])
            k.ts("dve", pS[:, 0:1023], pS[:, 0:1023], st[:, 3:4], None, ALU.mult, None, [pS, st], [pS])
            k.op("act", lambda e: e.copy(out=pbf[:], in_=pS[:]), [pS], [pbf])
            k.op("dve", lambda e: e.tensor_reduce(out=impr[:], in_=pS[:].rearrange("p (s f) -> p s f", f=4), axis=AX.X, op=ALU.add), [pS], [impr])
            k.tt("dve", impr[:, 1:256], impr[:, 1:256], pS[:, 3:1020:4], ALU.add, [impr, pS], [impr])
            k.mm(psA[0:8, 0:256], [(selm[:, 0:8], impr[:])], [selm, impr], [psA])
            k.op("dve", lambda e, g=g: e.memset(sc[:, g, :], 0.0), [], [sc])
            k.op("act", lambda e, g=g: e.copy(out=sc[:, g, 0:256], in_=psA[0:8, 0:256]), [psA], [sc])
            for ch in range(8):
                k.op("pe", lambda e, ch=ch: e.transpose(psT[:, ch, 0:32], pbf[:, ch * 128:(ch + 1) * 128], ident_bf[0:32, 0:32]), [pbf, ident_bf], [psT])
            k.op("act", lambda e: e.copy(out=pTs[:], in_=psT[:, :, 0:32]), [psT], [pTs])
            k.mm(psA[0:32, 256:320], [(pTs[:, ch, :], vcb[:, ch, g * 64:(g + 1) * 64]) for ch in range(8)], [pTs, vcb], [psA])
            k.op("act", lambda e, g=g: e.copy(out=obr[:, 0, g, 0:64], in_=psA[0:32, 256:320]), [psA], [obr])
            k.op("dve", lambda e, g=g: e.memset(obr[:, 0, g, 64:65], 1.0), [], [obr])
            k.tt("dve", sc[:, g, :], sc[:, g, :], selcS[:, 0, :], ALU.mult, [sc, selcS], [sc])
            k.tt("dve", sc[:, g, :], sc[:, g, :], selcS[:, 1, :], ALU.add, [sc, selcS], [sc])
            k.op("dve", lambda e, g=g: e.max(out=top[:, 0:8], in_=sc[:, g, :]), [sc], [top])
            k.op("dve", lambda e, g=g: e.match_replace(out=sc2[:], in_to_replace=top[:, 0:8], in_values=sc[:, g, :], imm_value=-1e30), [sc, top], [sc2])
            k.op("dve", lambda e: e.max(out=top[:, 8:16], in_=sc2[:]), [sc2], [top])
            k.ts("dve", top[:, 15:16], top[:, 15:16], -0.5, None, ALU.max, None, [top], [top])
            k.ts("dve", sc[:, g, :], sc[:, g, :], top[:, 15:16], None, ALU.is_ge, None, [sc, top], [sc])
            k.ts("dve", sc[:, g, :], sc[:, g, :], 30000.0, -30000.0, ALU.mult, ALU.add, [sc], [sc])
            for c3 in range(3):
                k.mm(psA[:, 320 + 32 * c3:352 + 32 * c3], [(sc[:, g, c3 * 128:(c3 + 1) * 128], rep[:])], [sc, rep], [psA])
            k.op("act", lambda e, g=g: e.copy(out=selTs[:, :, g, :], in_=psA[:, 320:416].rearrange("p (c t) -> p c t", c=3)), [psA], [selTs])
        ck(141 + b)
        pi = 0
        for br in (1, 2):
            for g in range(2):
                gs = slice(g * 64, (g + 1) * 64)
                nkb = 129 if br == 1 else 5
                for kb in range(nkb):
                    if br == 1:
                        pairs = [(KSb[gs, kb * 128:(kb + 1) * 128], qflat[gs, :]),
                                 (ef[:, (kb % 64) * 128:(kb % 64) * 128 + 128], selTs[:, (2 * kb) // 128, g, :])]
                        if kb == 128:
                            pairs.append((ident_bf[:], triS[:, 0, :]))
                        vr = VXb[:, kb, g, :]
                    else:
                        pairs = [(KWb[gs, kb * 128:(kb + 1) * 128], qflat[gs, :])]
                        if kb == 0:
                            pairs.append((ident_bf[:], triS[:, 1, :]))
                        if kb == 4:
                            pairs.append((ident_bf[:], triS[:, 0, :]))
                        vr = VWb[:, kb, g, :]
                    Sb = (psC, psX)[kb % 2]
                    S = Sb[:, 0:32]
                    n = len(pairs)
                    for i, (l, r) in enumerate(pairs):
                        k.op("pe", lambda e, l=l, r=r, i=i, S=S, n=n: e.matmul(S, lhsT=l, rhs=r, start=(i == 0), stop=(i == n - 1)),
                             [KSb, KWb, qTs, ef, selTs, ident_bf, triS], [Sb])
                    if pendc[0] is not None:
                        f = pendc[0]
                        pendc[0] = None
                        f()
                    pi ^= 1
                    pt = PTs[pi]
                    k.act(pt[:], S, AF.Exp, [Sb], [pt])

                    def pv(pt=pt, vr=vr, kb=kb, nkb=nkb):
                        k.op("pe", lambda e: e.matmul(psA[0:32, 416 + 0:416 + 65], lhsT=pt[:], rhs=vr, start=(kb == 0), stop=(kb == nkb - 1)),
                             [pt, VXb, VWb], [psA])
                    pendc[0] = pv
                if pendc[0] is not None:
                    f = pendc[0]
                    pendc[0] = None
                    f()
                k.op("act", lambda e, br=br, g=g: e.copy(out=obr[:, br, g, :], in_=psA[0:32, 416:481]), [psA], [obr])
        k.ts("dve", rcs[:, 0:6], obr[:, :, :, 64].rearrange("p a g -> p (a g)"), 1e-30, None, ALU.max, None, [obr], [rcs])
        k.op("dve", lambda e: e.reciprocal(out=rcs[:, 0:6], in_=rcs[:, 0:6]), [rcs], [rcs])
        for g in range(2):
            for br in range(3):
                k.tt("dve", st[:, 4:5], rcs[:, br * 2 + g:br * 2 + g + 1], gate_r[:, g, br:br + 1], ALU.mult, [rcs, gate_r], [st])
                if br == 0:
                    k.ts("dve", ofin[:, g, :], obr[:, 0, g, 0:64], st[:, 4:5], None, ALU.mult, None, [obr, st], [ofin])
                else:
                    k.op("dve", lambda e, g=g, br=br: e.scalar_tensor_tensor(out=ofin[:, g, :], in0=obr[:, br, g, 0:64], scalar=st[:, 4:5], in1=ofin[:, g, :],
                                                                             op0=ALU.mult, op1=ALU.add), [obr, st, ofin], [ofin])
            for hh in range(4):
                fo = first_o[0]
                first_o[0] = False
                k.op("pe", lambda e, g=g, hh=hh, fo=fo, b=b: e.matmul(psO[0:32, (g * 4 + hh) * 64:(g * 4 + hh + 1) * 64],
                                                                   lhsT=selm[:, 8 + (hh * 4 + b) * 32:8 + (hh * 4 + b + 1) * 32], rhs=ofin[:, g, :],
                                                                   start=fo, stop=False, skip_group_check=True), [selm, ofin], [psO])
    k.op("act", lambda e: e.copy(out=oas[:], in_=psO[0:32, :]), [psO], [oas])
    k.dma("pool", scr_oab[(NOWN + 1) * 128:(NOWN + 1) * 128 + 32, 0:512], oas[:], reads=[oas], writes=[scr_oab])
    k.pop()


def _consts():
    cm = np.zeros((128, 6, 128), np.float32)
    cm[:, 0, :] = np.eye(128, dtype=np.float32)
    s = np.arange(128)[:, None]
    t = np.arange(128)[None, :]
    same_p = (s // 64) == (t // 64)
    cm[:, 1, :] = ((s <= t) & same_p)
    cm[:, 2, :] = ((s > t) & same_p)
    same_s = ((s // 8) == (t // 8)) & (s < 32) & (t < 32)
    cm[:, 3, :] = ((s <= t) & same_s)
    cm[:, 4, :] = ((s > t) & same_s)
    cc = np.zeros((128, 6, 128), np.float32)
    cc[:, 0, 0:64] = 1
    cc[:, 1, 64:128] = 1
    for b in range(4):
        cc[:, 2 + b, 8 * b:8 * b + 8] = 1
    ci = np.zeros((128, 8), np.float32)
    ci[0:64, 0] = 1
    ci[64:128, 1] = 1
    for b in range(4):
        ci[8 * b:8 * b + 8, 2 + b] = 1
    return cm.reshape(128, 768), ci, cc.reshape(128, 768)


def _rope_tab(pos):
    inv = (10000.0 ** (-np.arange(32, dtype=np.float32) / np.float32(32))).astype(np.float32)
    ang = pos.astype(np.float32)[:, None] * inv[None, :]
    return np.concatenate([np.cos(ang), np.sin(ang)], axis=1).astype(np.float32)


def _nsa_consts(inp, half):
    w1 = inp["cmp_w1"][0]
    w2 = inp["cmp_w2"][0]
    pe = inp["cmp_pos_emb"][0]
    w1bd = np.zeros((2, 64, 64, 2, 64), np.float32)
    w1r = w1.reshape(2, 32, 64, 64)
    for g in range(2):
        w1bd[g, :, :, g, :] = w1r.transpose(2, 0, 1, 3).reshape(64, 64, 64)
    w1bd = w1bd.reshape(128, 64 * 128)
    w2bd = np.zeros((2, 64, 2, 2, 64), np.float32)
    for g in range(2):
        w2bd[g, :, :, g, :] = w2.transpose(1, 0, 2)
    w2bd = w2bd.reshape(128, 256)
    posvec = np.tile(pe.transpose(2, 0, 1).reshape(64, 64), (2, 1)).astype(np.float32)
    c = np.arange(256)[:, None]
    sidx = np.arange(62)[None, :]
    cov = ((c >= 4 * sidx - 1) & (c <= 4 * sidx + 3)).astype(np.float32)
    cover = cov.reshape(2, 128, 62).transpose(1, 0, 2).reshape(128, 124)
    NEG = -30000.0
    n_t = NOWN + 1
    cmpB = np.zeros((n_t, 128, 2, 128), np.float32)
    selc = np.zeros((n_t, 128, 2, 64), np.float32)
    cl = np.arange(128)[:, None]
    t = np.arange(128)[None, :]
    soff = 32 if half == 0 else 0
    coff = 128 if half == 0 else 0
    for i in range(n_t):
        lt = NPRE - 1 + i
        for ch in range(2):
            cc = ch * 128 + cl
            vis = (16 * cc + 31 <= 128 * lt + t) & (cc - coff >= 0) & (cc <= 254)
            cmpB[i, :, ch, :] = np.where(vis, 0.0, NEG)
        l = 128 * lt + np.arange(128)[:, None]
        s = np.arange(64)[None, :]
        sg = s - soff
        cur_g = l // 64 - soff
        valid = (64 * s <= l) & (sg >= 0)
        forced = (sg >= 0) & ((sg == 0) | (sg == cur_g) | (sg == cur_g - 1)) & (cur_g >= 0)
        selc[i, :, 0, :] = (valid & ~forced)
        selc[i, :, 1, :] = np.where(forced, 1e4, np.where(valid, 0.0, -1.0))
    key = np.arange(4096)[None, :]
    efull = (key // 64 == np.arange(64)[:, None]).astype(np.float32)
    kk = np.arange(128)[:, None]
    triB = np.zeros((128, 2, 128), np.float32)
    triB[:, 0, :] = np.where(kk > t, NEG, 0.0)
    triB[:, 1, :] = np.where(kk <= t, NEG, 0.0)
    pfx = np.full((1, 128), NEG if half == 0 else 0.0, np.float32)
    return dict(w1bd=w1bd, w2bd=w2bd, posvec=posvec, cover=cover, cmpB=cmpB.reshape(n_t, 128, 256), selc=selc.reshape(n_t, 128, 128),
                efull=efull, triB=triB.reshape(128, 256), pfx=pfx, identf=np.eye(128, dtype=np.float32))


def _sample_consts():
    NEG = -30000.0
    selcS = np.zeros((8, 2, 384), np.float32)
    s = np.arange(384)
    valid = s <= 256
    forced = (s == 0) | (s == 255) | (s == 256)
    selcS[:, 0, :] = (valid & ~forced)[None, :]
    selcS[:, 1, :] = np.where(forced, 1e4, np.where(valid, 0.0, -1.0))[None, :]
    selm = np.zeros((32, 8 + 512), np.float32)
    rep = np.zeros((8, 32), np.float32)
    for hh in range(4):
        for t in range(8):
            selm[hh * 8 + t, t] = 1
            rep[t, hh * 8 + t] = 1
            for b in range(4):
                selm[hh * 8 + t, 8 + (hh * 4 + b) * 32 + b * 8 + t] = 1
    key = np.arange(8192)[None, :]
    ef = (key // 64 == np.arange(128)[:, None]).astype(np.float32)
    r = np.arange(128)[:, None]
    tt = (np.arange(32) % 8)[None, :]
    tri = np.zeros((128, 2, 32), np.float32)
    tri[:, 0, :] = np.where(r > tt, NEG, 0.0)
    tri[:, 1, :] = np.where(r <= tt, NEG, 0.0)
    return dict(selcS=selcS.reshape(8, 768), selm=selm, rep=rep, efull128=ef, triBs=tri.reshape(128, 64),
                iot=np.arange(128, dtype=np.float32)[:, None])


_CACHE = {}


def _get_program(key=(NT, True, 0, 5120)):
    if key not in _CACHE:
        _CACHE[key] = build_program(*key)
    return _CACHE[key]


def make_in_maps(inp, cores, pools=None):
    cmask, cind, ccolmask = _consts()
    vecs = np.concatenate([inp["attn_norm_g"][0], inp["q_norm_g"][0], inp["k_norm_g"][0].reshape(-1),
                           inp["hgrn_lb_logits"].reshape(-1), inp["hgrn_norm_g"][0]]).astype(np.float32)[None, :]
    w_in = np.ascontiguousarray(inp["w_in"][0])
    maps = []
    for c in cores:
        b, half = c // 2, c % 2
        c0 = half * 2048
        xloc = np.zeros((NT * 128, D), np.float32)
        if half == 0:
            xloc[2048:] = inp["x_prompt"][b, 0:2048]
        else:
            xloc[:] = inp["x_prompt"][b, 0:4096]
        pos = np.arange(NT * 128) + c0 - 2048
        cs_tab = np.concatenate([_rope_tab(pos), _rope_tab(16384 + (np.arange(32) % 8))], axis=0)
        maps.append(dict(
            xloc=xloc, xs=np.ascontiguousarray(inp["x_sample"][4 * c:4 * c + 4].reshape(32, D)), cs_tab=cs_tab,
            w_in=w_in, vecs=vecs, cmask=cmask, cind=cind, ccolmask=ccolmask,
            st_win=np.ascontiguousarray(inp["state_win_kv"][0, 4 * c:4 * c + 4].reshape(4, 512, 256)),
            st_hgrn=np.ascontiguousarray(inp["state_hgrn"][0, 4 * c:4 * c + 4]),
        ))
        maps[-1].update(_nsa_consts(inp, half))
        cwv = np.concatenate([inp["ffn_conv_w"][0], inp["ffn_conv_b"]], axis=0)
        convw = np.ascontiguousarray(cwv.reshape(4, 22, 128).transpose(2, 1, 0)).reshape(128, 88)
        cst = inp["state_ffn_conv"][0, 4 * c:4 * c + 4]
        convst = np.ascontiguousarray(cst.reshape(4, 2, 22, 128).transpose(3, 2, 0, 1)).reshape(128, 176)
        maps[-1].update(w_branch=np.ascontiguousarray(inp["w_branch"][0]), w_out=np.ascontiguousarray(inp["w_out"][0]),
                        ffn_w_in=np.ascontiguousarray(inp["ffn_w_in"][0]), ffn_w_out=np.ascontiguousarray(inp["ffn_w_out"][0]),
                        fvec=np.ascontiguousarray(inp["ffn_norm_g"][0][None, :]), convw=convw, convst=convst)
        if pools is None:
            maps[-1].update(cache_cmp=inp["cache_cmp_kv"][0].reshape(-1, 128, 256), cache_slc=inp["cache_slc_kv"][0].reshape(-1, 128, 256),
                            ptab=np.ascontiguousarray(inp["page_table"][4 * c:4 * c + 4]).astype(np.int32))
        else:
            maps[-1].update(pools(c))
        maps[-1].update(_sample_consts())
    return maps


def assemble(res, cores, out):
    for i, c in enumerate(cores):
        r = res[i]
        b, half = c // 2, c % 2
        c0 = half * 2048
        for j, name in enumerate(("cmp_kv_prompt", "slc_kv_prompt")):
            out[name][0, b, c0:c0 + 2048] = r["o_kv"][j].reshape(2048, 2, 2, 64)
        if half == 1:
            out["win_kv_prompt"][0, b] = r["o_kv"][2][2048 - 512:].reshape(512, 2, 2, 64)
            out["hgrn_prompt"][0, b] = r["o_hg_p"]
        out["cmp_kv_sample"][0, 4 * c:4 * c + 4] = r["o_kvs"][0].reshape(4, 8, 2, 2, 64)
        out["slc_kv_sample"][0, 4 * c:4 * c + 4] = r["o_kvs"][1].reshape(4, 8, 2, 2, 64)
        out["win_kv_sample"][0, 4 * c:4 * c + 4] = r["o_win_s"].reshape(4, 512, 2, 2, 64)
        out["hgrn_sample"][0, 4 * c:4 * c + 4] = r["o_hg_s"]
        out["y_prompt"][b, c0:c0 + 2048] = r["o_y"]
        out["y_sample"][4 * c:4 * c + 4] = r["o_ys"].reshape(4, 8, D)
        if half == 1:
            out["ffn_conv_prompt"][0, b] = r["o_cp"].reshape(128, 22, 2).transpose(2, 1, 0).reshape(2, 2816)
        out["ffn_conv_sample"][0, 4 * c:4 * c + 4] = r["o_cs"].reshape(128, 22, 4, 2).transpose(2, 3, 1, 0).reshape(4, 2, 2816)


OUT_SHAPES = dict(
    y_prompt=(4, 4096, 1024), y_sample=(32, 8, 1024),
    cmp_kv_prompt=(1, 4, 4096, 2, 2, 64), cmp_kv_sample=(1, 32, 8, 2, 2, 64),
    slc_kv_prompt=(1, 4, 4096, 2, 2, 64), slc_kv_sample=(1, 32, 8, 2, 2, 64),
    win_kv_prompt=(1, 4, 512, 2, 2, 64), win_kv_sample=(1, 32, 512, 2, 2, 64),
    hgrn_prompt=(1, 4, 4, 128, 128), hgrn_sample=(1, 32, 4, 128, 128),
    ffn_conv_prompt=(1, 4, 2, 2816), ffn_conv_sample=(1, 32, 2, 2816))
OUT_ORDER = ["y_prompt", "y_sample", "cmp_kv_prompt", "cmp_kv_sample", "slc_kv_prompt", "slc_kv_sample",
             "win_kv_prompt", "win_kv_sample", "hgrn_prompt", "hgrn_sample", "ffn_conv_prompt", "ffn_conv_sample"]


def kernel(**inp):
    inp = {n: np.asarray(v) for n, v in inp.items()}
    cores = list(range(8))
    prog = _get_program()
    maps = make_in_maps(inp, cores)
    res = run_bass_kernel_spmd(prog.nc, maps, core_ids=cores)
    out = {n: np.zeros(s, np.float32) for n, s in OUT_SHAPES.items()}
    assemble(res.results, cores, out)
    return tuple(out[n] for n in OUT_ORDER)
```

```python
import numpy as np
import concourse.bass as bass
import concourse.mybir as mybir
from concourse.bass_utils import run_bass_kernel_spmd
from contextlib import ExitStack

F32 = mybir.dt.float32
BF16 = mybir.dt.bfloat16
I32 = mybir.dt.int32
AF = mybir.ActivationFunctionType
ALU = mybir.AluOpType
AX = mybir.AxisListType

NDS = 40
PE_NOSELF = False
PIPE_B = True
NOSELF_CHAIN = True
EPS = 1e-6
D = 1024
NCOL1 = 3352
NPRE = 16
NOWN = 16
NT = NPRE + NOWN


class Buf:
    def __init__(self, name, t):
        self.name = name
        self.t = t
        self.w = {}
        self.r = {}
        self.ps = False

    def __getitem__(self, idx):
        return self.t[idx]


class KB:
    def __init__(self):
        self.nc = bass.Bass("TRN2", target_bir_lowering=False)
        nc = self.nc
        self.es = ExitStack()
        self.engs = {"pe": nc.tensor, "act": nc.scalar, "dve": nc.vector, "pool": nc.gpsimd, "sp": nc.sync}
        self.esem = {}
        self.ecnt = {}
        for e in ("pe", "act", "dve", "pool"):
            self.esem[e] = self.es.enter_context(nc.semaphore("sem_" + e))
            self.ecnt[e] = 0
        self.dsems = [self.es.enter_context(nc.semaphore("sem_d%d" % i)) for i in range(NDS)]
        self.dval = [0] * NDS
        self.dnext = 0
        self.waited = {e: {} for e in self.engs}
        self.nbuf = 0
        self.n_ins = 0
        self.n_wait = 0
        self.stk = [self.es]

    def push(self):
        self.stk.append(ExitStack())

    def barrier(self):
        for e in self.engs:
            for f in ("pe", "act", "dve", "pool"):
                if f != e:
                    self._wait(e, ("e", f), self.ecnt[f])
            for i in range(NDS):
                self._wait(e, ("d", i), self.dval[i])

    def pop(self):
        self.barrier()
        self.stk.pop().close()

    def sb(self, shape, dt, name=None):
        self.nbuf += 1
        name = (name or "sb") + "_%d" % self.nbuf
        return Buf(name, self.stk[-1].enter_context(self.nc.sbuf_tensor(name, list(shape), dt)))

    def ps(self, shape, dt, name=None):
        self.nbuf += 1
        name = name or "ps%d" % self.nbuf
        name = name + "_%d" % self.nbuf
        b = Buf(name, self.stk[-1].enter_context(self.nc.psum_tensor(name, list(shape), dt)))
        b.ps = True
        return b

    def dram(self, name, shape, dt, kind=None):
        if kind is None:
            t = self.nc.dram_tensor(name, list(shape), dt)
        else:
            t = self.nc.dram_tensor(name, list(shape), dt, kind=kind)
        return Buf(name, t.ap())

    def _sem(self, key):
        return self.esem[key[1]] if key[0] == "e" else self.dsems[key[1]]

    def _wait(self, eng, key, val):
        if val <= 0:
            return
        wd = self.waited[eng]
        if wd.get(key, 0) >= val:
            return
        self.engs[eng].wait_ge(self._sem(key), val)
        wd[key] = val
        self.n_wait += 1

    def _sync(self, eng, reads, writes, noself=False):
        need = {}
        for b in reads:
            for k, v in b.w.items():
                if need.get(k, 0) < v:
                    need[k] = v
        for b in writes:
            for k, v in b.w.items():
                if need.get(k, 0) < v:
                    need[k] = v
            for k, v in b.r.items():
                if need.get(k, 0) < v:
                    need[k] = v
        for k, v in need.items():
            if noself and eng == "pe" and k == ("e", "pe"):
                continue
            self._wait(eng, k, v)

    def op(self, eng, fn, reads=(), writes=(), inc=True, noself=False, tag=None):
        if eng == "pe":
            if tag is not None and tag == getattr(self, "last_pe_tag", None):
                noself = True
            self.last_pe_tag = tag
        self._sync(eng, reads, writes, noself)
        ins = fn(self.engs[eng])
        self.n_ins += 1
        if inc:
            self.ecnt[eng] += 1
            n = self.ecnt[eng]
            ins.then_inc(self.esem[eng], 1)
        else:
            n = self.ecnt[eng] + 1
        key = ("e", eng)
        for b in reads:
            d = b.w if b.ps else b.r
            if d.get(key, 0) < n:
                d[key] = n
        for b in writes:
            if b.w.get(key, 0) < n:
                b.w[key] = n
        return ins

    def dma(self, q, out_ap, in_ap, reads=(), writes=(), **kw):
        i = self.dnext
        self.dnext = (i + 1) % NDS
        self._wait(q, ("d", i), self.dval[i])
        self._sync(q, reads, writes)
        ins = self.engs[q].dma_start(out=out_ap, in_=in_ap, **kw)
        self.n_ins += 1
        self.dval[i] += 16
        ins.then_inc(self.dsems[i], 16)
        key = ("d", i)
        for b in reads:
            b.r[key] = self.dval[i]
        for b in writes:
            b.w[key] = self.dval[i]
        return ins

    def finish(self):
        for i in range(NDS):
            self._wait("sp", ("d", i), self.dval[i])
        for e in ("pe", "act", "dve", "pool"):
            self._wait("sp", ("e", e), self.ecnt[e])

    def mm(self, out_ap, pairs, reads, writes):
        n = len(pairs)
        for i, (l, r) in enumerate(pairs):
            self.op("pe", lambda e, l=l, r=r, i=i: e.matmul(out_ap, lhsT=l, rhs=r, start=(i == 0), stop=(i == n - 1)),
                    reads=reads, writes=writes, inc=True, noself=(NOSELF_CHAIN and i > 0))

    def act(self, out_ap, in_ap, func, reads, writes, **kw):
        return self.op("act", lambda e: e.activation(out=out_ap, in_=in_ap, func=func, **kw), reads=reads, writes=writes)

    def tt(self, eng, out_ap, a, b, op, reads, writes):
        return self.op(eng, lambda e: e.tensor_tensor(out=out_ap, in0=a, in1=b, op=op), reads=reads, writes=writes)

    def ts(self, eng, out_ap, a, s1, s2, op0, op1, reads, writes):
        if op1 is None:
            return self.op(eng, lambda e: e.tensor_scalar(out=out_ap, in0=a, scalar1=s1, scalar2=None, op0=op0),
                           reads=reads, writes=writes)
        return self.op(eng, lambda e: e.tensor_scalar(out=out_ap, in0=a, scalar1=s1, scalar2=s2, op0=op0, op1=op1),
                       reads=reads, writes=writes)


class _Stop(Exception):
    pass


def build_program(ntp=NT, with_sample=True, stop=0, npool=5120):
    k = KB()
    try:
        _build(k, ntp, with_sample, stop, npool)
    except _Stop:
        pass
    k.finish()
    return k


def _build(k, ntp, with_sample, stop, npool):
    def ck(n):
        if stop == n:
            raise _Stop()
    nc = k.nc
    P = 128
    xloc = k.dram("xloc", [NT * 128, D], F32, "ExternalInput")
    xs = k.dram("xs", [32, D], F32, "ExternalInput")
    cs_tab = k.dram("cs_tab", [NT * 128 + 32, 64], F32, "ExternalInput")
    w_in = k.dram("w_in", [D, 5400], F32, "ExternalInput")
    vecs = k.dram("vecs", [1, 1024 + 64 + 192 + 1024 + 128], F32, "ExternalInput")
    cmask = k.dram("cmask", [128, 6 * 128], F32, "ExternalInput")
    cind = k.dram("cind", [128, 8], F32, "ExternalInput")
    st_win = k.dram("st_win", [4, 512, 256], F32, "ExternalInput")
    st_hgrn = k.dram("st_hgrn", [4, 4, 128, 128], F32, "ExternalInput")
    ccolmask = k.dram("ccolmask", [128, 6 * 128], F32, "ExternalInput")
    scr_qk = k.dram("scr_qk", [NT * 128 + 32, 1280], BF16)
    scr_gate = k.dram("scr_gate", [(NOWN + 1) * 128 + 32, 24], F32)
    scr_hT = k.dram("scr_hT", [NOWN + 2, 128, 1024], BF16)
    scr_oab = k.dram("scr_oab", [(NOWN + 1) * 128 + 32, 1024], BF16)

    o_kv = k.dram("o_kv", [3, NOWN * 128, 256], F32, "ExternalOutput")
    o_kvs = k.dram("o_kvs", [2, 32, 256], F32, "ExternalOutput")
    o_win_s = k.dram("o_win_s", [4, 512, 256], F32, "ExternalOutput")
    o_hg_p = k.dram("o_hg_p", [4, 128, 128], F32, "ExternalOutput")
    o_hg_s = k.dram("o_hg_s", [4, 4, 128, 128], F32, "ExternalOutput")

    k.push()
    wsb = k.sb([P, 8, NCOL1], BF16, "wsb")
    for kc in range(8):
        k.dma("pool", wsb[:, kc, :], w_in[kc * 128:(kc + 1) * 128, 0:NCOL1], writes=[wsb])
    g_bc = k.sb([P, D], F32, "g_bc")
    k.dma("sp", g_bc[:], vecs[0:1, 0:1024].partition_broadcast(P), writes=[g_bc])
    gain = k.sb([P, 14, 64], F32, "gain")
    for h in range(8):
        k.dma("sp", gain[:, h, :], vecs[0:1, 1024:1088].partition_broadcast(P), writes=[gain])
    for i in range(3):
        for g in range(2):
            k.dma("sp", gain[:, 8 + 2 * i + g, :], vecs[0:1, 1088 + 64 * i:1088 + 64 * i + 64].partition_broadcast(P), writes=[gain])
    k.ts("dve", gain[:, 0:8, :], gain[:, 0:8, :], 0.125, None, ALU.mult, None, [gain], [gain])
    lgt = k.sb([P, 2, 512], F32, "lgt")
    k.dma("sp", lgt[:, 0, :], vecs[0:1, 1280:1792].partition_broadcast(P), writes=[lgt])
    k.dma("sp", lgt[:, 1, :], vecs[0:1, 1792:2304].partition_broadcast(P), writes=[lgt])
    lb = k.sb([P, 512], F32, "lb")
    oml = k.sb([P, 512], F32, "oml")
    k.tt("dve", lb[:], lgt[:, 0, :], lgt[:, 1, :], ALU.subtract, [lgt], [lb])
    k.act(lb[:], lb[:], AF.Sigmoid, [lb], [lb])
    k.ts("dve", oml[:], lb[:], -1.0, 1.0, ALU.mult, ALU.add, [lb], [oml])
    gn_bc = k.sb([P, 4, 128], F32, "gn_bc")
    for h in range(4):
        k.dma("sp", gn_bc[:, h, :], vecs[0:1, 2304:2432].partition_broadcast(P), writes=[gn_bc])
    cm = k.sb([P, 6, 128], F32, "cm")
    k.dma("sp", cm[:], cmask[:].rearrange("p (a b) -> p a b", b=128), writes=[cm])
    ci = k.sb([P, 8], F32, "ci")
    k.dma("sp", ci[:], cind[:], writes=[ci])
    ident_bf = k.sb([P, 128], BF16, "ident_bf")
    k.op("dve", lambda e: e.tensor_copy(out=ident_bf[:], in_=cm[:, 0, :]), [cm], [ident_bf])
    ones_col = k.sb([P, 1], F32, "ones_col")
    k.op("dve", lambda e: e.memset(ones_col[:], 1.0), [], [ones_col])

    ck(1)
    xt = [k.sb([P, D], F32, "xt%d" % i) for i in range(2)]
    junk = k.sb([P, D], BF16, "junk")
    xn = k.sb([P, D], BF16, "xn")
    hT = k.sb([P, 8, 128], BF16, "hT")
    st4 = k.sb([P, 8], F32, "st4")
    R = k.sb([P, 14, 64], F32, "R")
    R2 = k.sb([P, 14, 64], F32, "R2")
    T1 = k.sb([P, 14, 32], F32, "T1")
    T2 = k.sb([P, 14, 32], F32, "T2")
    st14 = k.sb([P, 16], F32, "st14")
    cs = k.sb([P, 64], F32, "cs")
    kvo = k.sb([P, 3, 256], F32, "kvo")
    qkv_bf = k.sb([P, 1280], BF16, "qkv_bf")
    gate_sb = k.sb([P, 24], F32, "gate_sb")
    qb = k.sb([P, 512], F32, "qb")
    u_sb = k.sb([P, 512], F32, "u_sb")
    logf = k.sb([P, 512], F32, "logf")
    kb = k.sb([P, 512], F32, "kb")
    vb = k.sb([P, 512], BF16, "vb")
    ggb = k.sb([P, 512], BF16, "ggb")
    ex = k.sb([P, 512], F32, "ex")
    qe = k.sb([P, 512], BF16, "qe")
    ke = k.sb([P, 512], BF16, "ke")
    kd = k.sb([P, 512], BF16, "kd")
    kdm = k.sb([P, 512], BF16, "kdm")
    dec = k.sb([P, 16], F32, "dec")
    S32 = [k.sb([P, 4, 128], F32, "S32_%d" % i) for i in range(5)]
    Sbf = [k.sb([P, 4, 128], BF16, "Sbf_%d" % i) for i in range(5)]

    psT = k.ps([P, 8, 128], BF16, "psT")
    psZ = [k.ps([P, 512], F32, "psZ%d" % i) for i in range(2)]
    psH = [k.ps([P, 512], F32, "psH%d" % i) for i in range(2)]
    psS = k.ps([P, 4, 128], F32, "psS")
    psA = k.ps([P, 4, 128], F32, "psA")
    psO = k.ps([P, 4, 128], F32, "psO")
    qkT = k.sb([P, 8, 128], BF16, "qkT")
    qeTm = [k.sb([P, 4, 128], BF16, "qeTm%d" % i) for i in range(4)]
    AmT = k.sb([P, 4, 128], BF16, "AmT")
    osq = k.sb([P, 512], F32, "osq")
    st8 = k.sb([P, 8], F32, "st8")
    oab = k.sb([P, 1024], BF16, "oab")
    ccol = k.sb([P, 6, 128], F32, "ccol")
    k.dma("sp", ccol[:], ccolmask[:].rearrange("p (a b) -> p a b", b=128), writes=[ccol])
    k.op("pool", lambda e: e.memset(oab[:], 0.0), [], [oab])

    k.op("dve", lambda e: e.memset(S32[0][:], 0.0), [], [S32[0]])
    k.op("pool", lambda e: e.memset(Sbf[0][:], 0.0), [], [Sbf[0]])
    if with_sample:
        for b in range(4):
            k.dma("sp", S32[1 + b][:], st_hgrn[b].rearrange("h k v -> k h v"), writes=[S32[1 + b]])
            k.op("act", lambda e, b=b: e.copy(out=Sbf[1 + b][:], in_=S32[1 + b][:]), [S32[1 + b]], [Sbf[1 + b]])
        for b in range(4):
            k.dma("sp", o_win_s[b, 0:504, :], st_win[b, 8:512, :], writes=[o_win_s])

    ck(2)
    zi = [0]

    def next_z():
        zi[0] ^= 1
        return psZ[zi[0]]

    def proj(c0, c1):
        z = next_z()
        k.mm(z[:, 0:c1 - c0], [(hT[:, kc, :], wsb[:, kc, c0:c1]) for kc in range(8)], [hT, wsb], [z])
        return z

    def do_tile(lt, rows, xsrc, csrc, full, kv_dst, sample):
        x = xt[lt % 2]
        rs = slice(0, rows)
        k.dma("sp", x[rs, :], xsrc, writes=[x])
        k.dma("sp", cs[rs, :], csrc, writes=[cs])
        ck(31)
        k.act(junk[rs, :], x[rs, :], AF.Square, [x], [junk, st4], accum_out=st4[rs, 0:1])
        k.ts("dve", st4[rs, 1:2], st4[rs, 0:1], 1.0 / D, EPS, ALU.mult, ALU.add, [st4], [st4])
        k.act(st4[rs, 2:3], st4[rs, 1:2], AF.Sqrt, [st4], [st4])
        k.op("dve", lambda e: e.reciprocal(out=st4[rs, 3:4], in_=st4[rs, 2:3]), [st4], [st4])
        k.op("dve", lambda e: e.scalar_tensor_tensor(out=xn[rs, :], in0=x[rs, :], scalar=st4[rs, 3:4], in1=g_bc[rs, :],
                                                     op0=ALU.mult, op1=ALU.mult), [x, st4, g_bc], [xn])
        ck(32)
        for kc in range(8):
            k.op("pe", lambda e, kc=kc: e.transpose(psT[:, kc, rs], xn[rs, kc * 128:(kc + 1) * 128], ident_bf[rs, rs]),
                 [xn, ident_bf], [psT], inc=True)
        ck(33)
        k.op("act", lambda e: e.copy(out=hT[:, :, rs], in_=psT[:, :, rs]), [psT], [hT])
        if full:
            ti = (NOWN + 1) if sample else (lt - (NPRE - 1))
            k.dma("pool", scr_hT[ti, :, :].rearrange("p (a b) -> p a b", b=128)[:, :, rs], hT[:, :, rs], reads=[hT], writes=[scr_hT])
        ck(3)

        def projr(c0, c1):
            z = next_z()
            k.mm(z[rs, 0:c1 - c0], [(hT[:, kc, rs], wsb[:, kc, c0:c1]) for kc in range(8)], [hT, wsb], [z])
            return z

        if full:
            z = projr(0, 512)
            ck(41)
            k.op("act", lambda e: e.copy(out=R[rs, 0:8, :], in_=z[rs, 0:512].rearrange("p (h d) -> p h d", d=64)), [z], [R])
        ck(42)
        z = projr(512, 1024)
        zv = z[rs, 0:512].rearrange("p (a j c) -> p a j c", a=2, j=2)
        k.op("act", lambda e: e.copy(out=R[rs, 8:12, :].rearrange("p (a g) d -> p a (g d)", a=2), in_=zv[:, :, 0, :]), [z], [R])
        k.op("dve", lambda e: e.tensor_copy(out=kvo[rs, 0:2, 128:256], in_=zv[:, :, 1, :]), [z], [kvo])
        ck(43)
        z = projr(1024, 1304)
        k.op("act", lambda e: e.copy(out=R[rs, 12:14, :].rearrange("p g d -> p (g d)"), in_=z[rs, 0:128]), [z], [R])
        k.op("dve", lambda e: e.tensor_copy(out=kvo[rs, 2, 128:256], in_=z[rs, 128:256]), [z], [kvo])
        if full:
            k.act(gate_sb[rs, :], z[rs, 256:280], AF.Sigmoid, [z], [gate_sb])
        ck(4)
        h0 = 0 if full else 8
        nh = 14 - h0
        k.tt("dve", R2[rs, h0:14, :], R[rs, h0:14, :], R[rs, h0:14, :], ALU.mult, [R], [R2])
        k.op("dve", lambda e: e.tensor_reduce(out=st14[rs, h0:14], in_=R2[rs, h0:14, :], axis=AX.X, op=ALU.add), [R2], [st14])
        k.ts("dve", st14[rs, h0:14], st14[rs, h0:14], 1.0 / 64, EPS, ALU.mult, ALU.add, [st14], [st14])
        k.act(st14[rs, h0:14], st14[rs, h0:14], AF.Sqrt, [st14], [st14])
        k.op("dve", lambda e: e.reciprocal(out=st14[rs, h0:14], in_=st14[rs, h0:14]), [st14], [st14])
        k.tt("pool", R2[rs, h0:14, :], R[rs, h0:14, :], gain[rs, h0:14, :], ALU.mult, [R, gain], [R2])
        k.tt("dve", R2[rs, h0:14, :], R2[rs, h0:14, :], st14[rs, h0:14].unsqueeze(2).to_broadcast([rows, nh, 64]), ALU.mult,
             [R2, st14], [R2])
        cosb = cs[rs, 0:32].unsqueeze(1).to_broadcast([rows, nh, 32])
        sinb = cs[rs, 32:64].unsqueeze(1).to_broadcast([rows, nh, 32])
        x1 = R2[rs, h0:14, 0:32]
        x2 = R2[rs, h0:14, 32:64]
        k.tt("dve", T1[rs, h0:14, :], x1, cosb, ALU.mult, [R2, cs], [T1])
        k.tt("pool", T2[rs, h0:14, :], x2, sinb, ALU.mult, [R2, cs], [T2])
        k.tt("dve", R[rs, h0:14, 0:32], T1[rs, h0:14, :], T2[rs, h0:14, :], ALU.subtract, [T1, T2], [R])
        k.tt("dve", T1[rs, h0:14, :], x2, cosb, ALU.mult, [R2, cs], [T1])
        k.tt("pool", T2[rs, h0:14, :], x1, sinb, ALU.mult, [R2, cs], [T2])
        k.tt("dve", R[rs, h0:14, 32:64], T1[rs, h0:14, :], T2[rs, h0:14, :], ALU.add, [T1, T2], [R])
        k.op("act", lambda e: e.copy(out=kvo[rs, :, 0:128], in_=R[rs, 8:14, :].rearrange("p (a g) d -> p a (g d)", a=3)), [R], [kvo])
        if kv_dst is not None:
            for i, dst in enumerate(kv_dst):
                if dst is not None:
                    k.dma("pool", dst[0], kvo[rs, i, :], reads=[kvo], writes=[dst[1]])
        if True:
            k.op("act", lambda e: e.copy(out=qkv_bf[rs, 0:512].rearrange("p (hh g d) -> p g hh d", hh=4, g=2),
                                         in_=R[rs, 0:8, :].rearrange("p (g hh) d -> p g hh d", g=2)), [R], [qkv_bf])
            k.op("act", lambda e: e.copy(out=qkv_bf[rs, 512:896].rearrange("p (h d) -> p h d", d=64), in_=R[rs, 8:14, :]), [R], [qkv_bf])
            k.op("dve", lambda e: e.tensor_copy(out=qkv_bf[rs, 896:1280].rearrange("p (a c) -> p a c", a=3), in_=kvo[rs, :, 128:256]), [kvo], [qkv_bf])
            k.dma("pool", scr_qk[lt * 128:lt * 128 + rows, :], qkv_bf[rs, :], reads=[qkv_bf], writes=[scr_qk])
            if full:
                gi_ = (NOWN + 1) if sample else (lt - (NPRE - 1))
                k.dma("pool", scr_gate[gi_ * 128:gi_ * 128 + rows, :], gate_sb[rs, :], reads=[gate_sb], writes=[scr_gate])
        ck(5)
        if full:
            z = projr(1304, 1816)
            k.act(qb[rs, :], z[rs, :], AF.Silu, [z], [qb])
        z = projr(1816, 2328)
        k.act(u_sb[rs, :], z[rs, :], AF.Sigmoid, [z], [u_sb])
        k.tt("dve", u_sb[rs, :], u_sb[rs, :], oml[rs, :], ALU.mult, [u_sb, oml], [u_sb])
        k.tt("dve", u_sb[rs, :], u_sb[rs, :], lb[rs, :], ALU.add, [u_sb, lb], [u_sb])
        k.act(logf[rs, :], u_sb[rs, :], AF.Ln, [u_sb], [logf])
        k.ts("pool", kb[rs, :], u_sb[rs, :], -1.0, 1.0, ALU.mult, ALU.add, [u_sb], [kb])
        z = projr(2328, 2840)
        k.op("act", lambda e: e.copy(out=vb[rs, :], in_=z[rs, :]), [z], [vb])
        if full:
            z = projr(2840, 3352)
            k.act(ex[rs, :], z[rs, :], AF.Silu, [z], [ex])
            k.tt("dve", ggb[rs, :], ex[rs, :], gn_bc[rs, :, :].rearrange("p h v -> p (h v)"), ALU.mult, [ex, gn_bc], [ggb])
        ck(6)
        mi = 3 if sample else 1
        nch = 4 if sample else 2
        i0 = 2 if sample else 0
        zD = psH[0]
        k.mm(zD[rs, :], [(cm[rs, mi + 1, rs], logf[rs, :])], [cm, logf], [zD])
        k.act(ex[rs, :], zD[rs, :], AF.Exp, [zD], [ex])
        k.tt("dve", kd[rs, :], kb[rs, :], ex[rs, :], ALU.mult, [kb, ex], [kd])
        if full:
            zB = psH[1]
            k.mm(zB[rs, :], [(cm[rs, mi, rs], logf[rs, :])], [cm, logf], [zB])
            k.act(ex[rs, :], zB[rs, :], AF.Exp, [zB], [ex])
            k.tt("dve", qe[rs, :], qb[rs, :], ex[rs, :], ALU.mult, [qb, ex], [qe])
            k.act(ex[rs, :], zB[rs, :], AF.Exp, [zB], [ex], scale=-1.0)
            k.tt("dve", ke[rs, :], kb[rs, :], ex[rs, :], ALU.mult, [kb, ex], [ke])
            for h in range(4):
                k.op("pe", lambda e, h=h: e.transpose(psT[:, h, rs], qe[rs, h * 128:(h + 1) * 128], ident_bf[rs, rs]), [qe, ident_bf], [psT])
                k.op("pe", lambda e, h=h: e.transpose(psT[:, 4 + h, rs], ke[rs, h * 128:(h + 1) * 128], ident_bf[rs, rs]), [ke, ident_bf], [psT])
            k.op("act", lambda e: e.copy(out=qkT[:, :, rs], in_=psT[:, :, rs]), [psT], [qkT])
            ccb = 2 if sample else 0
            for c in range(nch):
                k.tt("dve", qeTm[c][:, :, rs], qkT[:, 0:4, rs], ccol[:, ccb + c, rs].unsqueeze(1).to_broadcast([P, 4, rows]), ALU.mult,
                     [qkT, ccol], [qeTm[c]])
            for h in range(4):
                k.mm(psA[rs, h, rs], [(qkT[:, 4 + h, rs], qkT[:, h, rs])], [qkT], [psA])
            k.tt("dve", AmT[rs, :, rs], psA[rs, :, rs], cm[rs, mi, rs].unsqueeze(1).to_broadcast([rows, 4, rows]), ALU.mult,
                 [psA, cm], [AmT])
        first_o = [True]
        for c in range(nch):
            st = S32[0] if not sample else S32[1 + c]
            sbf = Sbf[0] if not sample else Sbf[1 + c]
            ind = ci[rs, i0 + c:i0 + c + 1]
            if full:
                for h in range(4):
                    fo = first_o[0]
                    first_o[0] = False
                    k.op("pe", lambda e, h=h, fo=fo, c=c, sbf=sbf: e.matmul(psO[rs, h, :], lhsT=qeTm[c][:, h, rs], rhs=sbf[:, h, :], start=fo, stop=False,
                                                                       skip_group_check=True), [qeTm[c], sbf], [psO])
            for h in range(4):
                k.mm(psS[:, h, 0:1], [(logf[rs, h * 128:(h + 1) * 128], ind)], [logf, ci], [psS])
            k.act(dec[:, 4 * c:4 * c + 4], psS[:, :, 0], AF.Exp, [psS], [dec])
            k.ts("pool", kdm[rs, :], kd[rs, :], ind, None, ALU.mult, None, [kd, ci], [kdm])
            for h in range(4):
                k.mm(psS[:, h, :], [(kdm[rs, h * 128:(h + 1) * 128], vb[rs, h * 128:(h + 1) * 128])], [kdm, vb], [psS])
            for h in range(4):
                k.op("dve", lambda e, h=h: e.scalar_tensor_tensor(out=st[:, h, :], in0=st[:, h, :], scalar=dec[:, 4 * c + h:4 * c + h + 1],
                                                                  in1=psS[:, h, :], op0=ALU.mult, op1=ALU.add),
                     [st, dec, psS], [st])
            k.op("act", lambda e: e.copy(out=sbf[:], in_=st[:]), [st], [sbf])
        if full:
            for h in range(4):
                k.op("pe", lambda e, h=h: e.matmul(psO[rs, h, :], lhsT=AmT[rs, h, rs], rhs=vb[rs, h * 128:(h + 1) * 128], start=False, stop=True,
                                                   skip_group_check=True), [AmT, vb], [psO])
            k.act(osq[rs, :], psO[rs, :, :].rearrange("p h v -> p (h v)"), AF.Square, [psO], [osq])
            k.op("dve", lambda e: e.tensor_reduce(out=st8[rs, 0:4], in_=osq[rs, :].rearrange("p (h v) -> p h v", h=4), axis=AX.X, op=ALU.add), [osq], [st8])
            k.ts("dve", st8[rs, 0:4], st8[rs, 0:4], 1.0 / 128, EPS, ALU.mult, ALU.add, [st8], [st8])
            k.act(st8[rs, 0:4], st8[rs, 0:4], AF.Sqrt, [st8], [st8])
            k.op("dve", lambda e: e.reciprocal(out=st8[rs, 4:8], in_=st8[rs, 0:4]), [st8], [st8])
            for h in range(4):
                k.op("dve", lambda e, h=h: e.scalar_tensor_tensor(out=oab[rs, 512 + h * 128:512 + (h + 1) * 128], in0=psO[rs, h, :], scalar=st8[rs, 4 + h:5 + h],
                                                                  in1=ggb[rs, h * 128:(h + 1) * 128], op0=ALU.mult, op1=ALU.mult),
                     [psO, st8, ggb], [oab])
            ck(7)
            ti = (NOWN + 1) if sample else (lt - (NPRE - 1))
            k.dma("pool", scr_oab[ti * 128:ti * 128 + rows, 512:1024], oab[rs, 512:1024], reads=[oab], writes=[scr_oab])
            if sample and npool == 0:
                k.dma("pool", scr_oab[ti * 128:ti * 128 + rows, 0:512], oab[rs, 0:512], reads=[oab], writes=[scr_oab])

    for lt in range(NT - ntp, NT):
        own = lt >= NPRE
        full = lt >= NPRE - 1
        kv_dst = None
        if own:
            r0 = (lt - NPRE) * 128
            kv_dst = [(o_kv[i, r0:r0 + 128, :], o_kv) for i in range(3)]
        do_tile(lt, 128, xloc[lt * 128:(lt + 1) * 128, :], cs_tab[lt * 128:(lt + 1) * 128, :], full, kv_dst, False)
    k.dma("sp", o_hg_p[:].rearrange("h k v -> k h v"), S32[0][:], reads=[S32[0]], writes=[o_hg_p])
    if with_sample:
        kv_dst = [(o_kvs[0, :, :], o_kvs), (o_kvs[1, :, :], o_kvs), None]
        do_tile(NT, 32, xs[:, :], cs_tab[NT * 128:NT * 128 + 32, :], True, kv_dst, True)
        for b in range(4):
            k.dma("sp", o_win_s[b, 504:512, :], kvo[8 * b:8 * b + 8, 2, :], reads=[kvo], writes=[o_win_s])
            k.dma("sp", o_hg_s[b].rearrange("h k v -> k h v"), S32[1 + b][:], reads=[S32[1 + b]], writes=[o_hg_s])
    k.pop()
    ck(10)
    pass1b(k, ck, ntp, locals())
    ck(20)
    env = dict(locals())
    env["identf_d"] = Buf("identf", k.nc_identf)
    env.update(k.shared)
    if with_sample and npool > 0:
        pass1c(k, ck, env, npool)
    ck(25)
    env["scr_x1"] = k.dram("scr_x1", [(NOWN + 1) * 128 + 32, D], F32)
    pass2(k, ck, env)
    ck(30)
    pass3(k, ck, env)


def pass1b(k, ck, ntp, env):
    P = 128
    scr_qk, scr_gate, scr_oab = env["scr_qk"], env["scr_gate"], env["scr_oab"]
    w1bd_d = k.dram("w1bd", [128, 64 * 128], F32, "ExternalInput")
    w2bd_d = k.dram("w2bd", [128, 2 * 128], F32, "ExternalInput")
    posvec_d = k.dram("posvec", [128, 64], F32, "ExternalInput")
    cover_d = k.dram("cover", [128, 2 * 62], F32, "ExternalInput")
    cmpB_d = k.dram("cmpB", [NOWN + 1, 128, 2 * 128], F32, "ExternalInput")
    selc_d = k.dram("selc", [NOWN + 1, 128, 2 * 64], F32, "ExternalInput")
    efull_d = k.dram("efull", [64, 4096], F32, "ExternalInput")
    triB_d = k.dram("triB", [128, 2 * 128], F32, "ExternalInput")
    pfx_d = k.dram("pfx", [1, 128], F32, "ExternalInput")
    identf_d = k.dram("identf", [128, 128], F32, "ExternalInput")
    k.nc_identf = identf_d.t
    k.shared = dict(w1bd_d=w1bd_d, w2bd_d=w2bd_d, posvec_d=posvec_d)

    k.push()
    W1 = k.sb([P, 64, 128], BF16, "W1")
    for a in range(4):
        k.dma("pool", W1[:, a * 16:(a + 1) * 16, :], w1bd_d[:, a * 2048:(a + 1) * 2048].rearrange("p (a b) -> p a b", b=128), writes=[W1])
    W2 = k.sb([P, 2, 128], BF16, "W2")
    k.dma("pool", W2[:], w2bd_d[:].rearrange("p (a b) -> p a b", b=128), writes=[W2])
    posv = k.sb([P, 64], BF16, "posv")
    k.dma("pool", posv[:], posvec_d[:], writes=[posv])
    efull = k.sb([P, 4096], BF16, "efull")
    k.dma("pool", efull[0:64, :], efull_d[:], writes=[efull])
    k.dma("pool", efull[64:128, :], efull_d[:], writes=[efull])
    triB = k.sb([P, 2, 4, 128], BF16, "triB")
    for hh in range(4):
        k.dma("pool", triB[:, :, hh, :], triB_d[:].rearrange("p (a b) -> p a b", b=128), writes=[triB])
    pfx = k.sb([1, 128], BF16, "pfx")
    k.dma("pool", pfx[:], pfx_d[:], writes=[pfx])
    ones_row = k.sb([1, 512], BF16, "ones_row")
    k.op("dve", lambda e: e.memset(ones_row[:], 1.0), [], [ones_row])
    identf = k.sb([P, 128], F32, "identf")
    k.dma("sp", identf[:], identf_d[:], writes=[identf])
    ident_bf = k.sb([P, 128], BF16, "ident_bf2")
    k.op("dve", lambda e: e.tensor_copy(out=ident_bf[:], in_=identf[:]), [identf], [ident_bf])
    KS = k.sb([P, 2, NT * 128], BF16, "KS")
    KC2 = k.sb([P, 2, 256], BF16, "KC2")
    VX = k.sb([P, 2, NT, 2, 65], BF16, "VX")
    kvccT = k.sb([P, 2, 256], BF16, "kvccT")
    VcX = k.sb([P, 2, 2, 127], BF16, "VcX")
    k.op("pool", lambda e: e.memset(KC2[:], 0.0), [], [KC2])
    k.op("pool", lambda e: e.memset(kvccT[:], 0.0), [], [kvccT])
    k.op("dve", lambda e: e.memset(VX[:, :, :, :, 64:65], 1.0), [], [VX])
    k.op("dve", lambda e: e.memset(VcX[:, :, :, 0:64], 0.0), [], [VcX])
    k.op("dve", lambda e: e.memset(VcX[:, :, :, 64:65], 1.0), [], [VcX])
    for g in range(2):
        k.dma("pool", VcX[:, :, g, 65:127], cover_d[:].rearrange("p (a b) -> p a b", b=62), writes=[VcX])
    qkv = [k.sb([P, 1280], BF16, "qkv%d" % i) for i in range(2)]
    qT = k.sb([P, 4, 128], BF16, "qT")
    spre = k.sb([P, 2, 8], BF16, "spre")
    posW = k.sb([P, 2], F32, "posW")
    cmpB = k.sb([P, 2, 4, 128], BF16, "cmpB_t")
    selc = k.sb([P, 2, 64], F32, "selc_t")
    gate = k.sb([P, 24], F32, "gate_t")
    PT = [k.sb([P, 1024], BF16, "PT%d" % i) for i in range(2)]
    ocs = k.sb([P, 2, 4, 127], F32, "ocs")
    osw = k.sb([P, 2, 4, 65], F32, "osw")
    rc = k.sb([P, 8], F32, "rc")
    coef = k.sb([P, 8], F32, "coef")
    imp = k.sb([P, 2, 64], F32, "imp")
    sc2 = k.sb([P, 64], F32, "sc2")
    top = k.sb([P, 16], F32, "top")
    nsel = k.sb([P, 2, 128], F32, "nsel")
    selT = k.sb([P, 2, 4, 128], BF16, "selT")
    oacc = k.sb([P, 8, 64], F32, "oacc")
    oa_bf = k.sb([P, 512], BF16, "oa_bf")
    psT = k.ps([P, 8, 128], BF16, "psTb")
    psZ = [k.ps([P, 1024], F32, "psZb%d" % i) for i in range(2)]
    psV = [k.ps([P, 512], F32, "psVb%d" % i) for i in range(2)]
    psC = k.ps([P, 512], F32, "psCb")

    for j in range(2):
        for idx in range(32):
            k.op("pe", lambda e, j=j, idx=idx: e.matmul(psC[:, j:j + 1], lhsT=W1[:, j * 32 + idx, :], rhs=posv[:, j * 32 + idx:j * 32 + idx + 1],
                                                        start=(idx == 0), stop=(idx == 31)), [W1, posv], [psC])
    k.op("act", lambda e: e.copy(out=posW[:], in_=psC[:, 0:2]), [psC], [posW])
    ck(11)
    zi = [0]
    pi = [0]

    pend = [None]

    cur = []

    def emit():
        blocks = list(cur)
        del cur[:]
        if not blocks:
            return
        zi[0] ^= 1
        S = psZ[zi[0]]
        for j, (g, k_lhsT, biases, v_rhs, acc_ap, ncol, first, ab) in enumerate(blocks):
            pairs = [(k_lhsT, qT[g * 64:(g + 1) * 64, :, :].rearrange("p a b -> p (a b)"))] + biases
            n = len(pairs)
            for i, (l, r) in enumerate(pairs):
                k.op("pe", lambda e, l=l, r=r, i=i, j=j, n=n: e.matmul(S[:, j * 512:(j + 1) * 512], lhsT=l, rhs=r, start=(i == 0), stop=(i == n - 1)),
                     [KS, kvccT, qT, efull, selT, ident_bf, triB, cmpB, pfx, ones_row], [S], tag=("S", l.partition_size(), l.base_partition()))
        if pend[0] is not None:
            f = pend[0]
            pend[0] = None
            f()
        pi[0] ^= 1
        pt = PT[pi[0]]
        nb_ = len(blocks)
        k.act(pt[:, 0:nb_ * 512], S[:, 0:nb_ * 512], AF.Exp, [S], [pt])

        def pv():
            for j, (g, k_lhsT, biases, v_rhs, acc_ap, ncol, first, ab) in enumerate(blocks):
                for hh in range(4):
                    k.op("pe", lambda e, hh=hh, j=j, acc_ap=acc_ap, ncol=ncol, v_rhs=v_rhs, first=first: e.matmul(
                        acc_ap[:, hh, 0:ncol], lhsT=pt[:, j * 512 + hh * 128:j * 512 + (hh + 1) * 128], rhs=v_rhs,
                        start=(first and hh == 0), stop=False, skip_group_check=True), [pt, VX, VcX], [ab], tag=("PV", ncol))
        pend[0] = pv

    def flush():
        emit()
        if pend[0] is not None:
            f = pend[0]
            pend[0] = None
            f()

    def attn_block(g, k_lhsT, biases, v_rhs, acc_ap, ncol, first):
        cur.append((g, k_lhsT, biases, v_rhs, acc_ap, ncol, first, acc_buf[0]))
        if len(cur) == 2:
            emit()

    acc_buf = [None]

    for lt in range(NT - ntp, NT):
        full = lt >= NPRE - 1
        i_own = lt - (NPRE - 1)
        qk = qkv[lt % 2]
        k.dma("sp", qk[:], scr_qk[lt * 128:(lt + 1) * 128, :], reads=[scr_qk], writes=[qk])
        v14 = qk[:, 0:896].rearrange("p (h d) -> p h d", d=64)
        if full:
            for hh in range(4):
                k.op("pe", lambda e, hh=hh: e.transpose(psT[:, hh, :], qk[:, hh * 128:(hh + 1) * 128], ident_bf[:]), [qk, ident_bf], [psT])
        k.op("pe", lambda e: e.transpose(psT[:, 4, :], qk[:, 512:640], ident_bf[:]), [qk, ident_bf], [psT])
        k.op("pe", lambda e: e.transpose(psT[:, 5, :], qk[:, 640:768], ident_bf[:]), [qk, ident_bf], [psT])
        k.op("pe", lambda e: e.transpose(psT[:, 6, :], qk[:, 768:896], ident_bf[:]), [qk, ident_bf], [psT])
        k.op("pe", lambda e: e.transpose(psT[:, 7, :], qk[:, 896:1024], ident_bf[:]), [qk, ident_bf], [psT])
        if full:
            k.op("act", lambda e: e.copy(out=qT[:], in_=psT[:, 0:4, :]), [psT], [qT])
        k.op("act", lambda e: e.copy(out=KS[:, :, lt * 128:(lt + 1) * 128], in_=psT[:, 5:7, :]), [psT], [KS])
        k.op("dve", lambda e: e.tensor_copy(out=KC2[:, :, 0:128], in_=KC2[:, :, 128:256]), [KC2], [KC2])
        k.op("dve", lambda e: e.tensor_copy(out=KC2[:, :, 128:256], in_=psT[:, 4:8:3, :]), [psT], [KC2])
        k.op("pool", lambda e: e.tensor_copy(out=VX[:, :, lt, :, 0:64], in_=qk[:, 1024:1280].rearrange("p (a g d) -> p a g d", a=2, g=2)), [qk], [VX])
        c0 = max(8 * lt - 1, 0)
        nb = 8 * lt + 7 - c0
        for j in range(2):
            n = 0
            for r in range(2):
                for i16 in range(16):
                    st_col = 16 * (c0 + r) + i16 - 128 * lt + 128
                    k.op("pe", lambda e, j=j, r=r, i16=i16, st_col=st_col, n=n: e.matmul(
                        psC[:, 8 * j:8 * j + nb], lhsT=W1[:, j * 32 + r * 16 + i16, :], rhs=KC2[:, j, st_col:st_col + 16 * (nb - 1) + 1:16],
                        start=(n == 0), stop=(n == 31)), [W1, KC2], [psC], tag=("CMP", j))
                    n += 1
            k.act(spre[:, j, 0:nb], psC[:, 8 * j:8 * j + nb], AF.Silu, [psC, posW], [spre], bias=posW[:, j:j + 1])
        for j in range(2):
            k.mm(psC[:, 16 + 8 * j:16 + 8 * j + nb], [(W2[:, j, :], spre[:, j, 0:nb])], [W2, spre], [psC])
        k.op("act", lambda e: e.copy(out=kvccT[:, :, c0:c0 + nb], in_=psC[:, 16:32].rearrange("p (j c) -> p j c", j=2)[:, :, 0:nb]), [psC], [kvccT])
        if not full:
            continue
        ck(12)
        k.dma("sp", selc[:], selc_d[i_own].rearrange("p (a b) -> p a b", b=64), writes=[selc])
        k.dma("sp", gate[:], scr_gate[i_own * 128:(i_own + 1) * 128, :], reads=[scr_gate], writes=[gate])
        for hh in range(4):
            k.dma("pool", cmpB[:, :, hh, :], cmpB_d[i_own].rearrange("p (a b) -> p a b", b=128), writes=[cmpB])
        nchv = 1 if lt <= 15 else 2
        for ch in range(nchv):
            k.op("pe", lambda e, ch=ch: e.transpose(psT[:, ch, :], kvccT[:, 1, ch * 128:(ch + 1) * 128], ident_bf[:]), [kvccT, ident_bf], [psT])
        k.op("act", lambda e: e.copy(out=VcX[:, 0:nchv, :, 0:64], in_=psT[:, 0:nchv, :].rearrange("p c (g d) -> p c g d", g=2)), [psT], [VcX])
        for g in range(2):
            acc_buf[0] = psV[g]
            acc = psV[g][:, 0:508].rearrange("p (h c) -> p h c", c=127)
            for ch in range(nchv):
                attn_block(g, kvccT[g * 64:(g + 1) * 64, 0, ch * 128:(ch + 1) * 128],
                           [(ident_bf[:], cmpB[:, ch, :, :].rearrange("p a b -> p (a b)"))],
                           VcX[:, ch, g, :], acc, 127, ch == 0)
            flush()
            k.op("act", lambda e, g=g, acc=acc: e.copy(out=ocs[:, g, :, :], in_=acc), [psV[g]], [ocs])
        ck(13)
        k.ts("dve", rc[:], ocs[:, :, :, 64].rearrange("p g h -> p (g h)"), 1e-30, None, ALU.max, None, [ocs], [rc])
        k.op("dve", lambda e: e.reciprocal(out=rc[:], in_=rc[:]), [rc], [rc])
        gv = gate[:].rearrange("p (h b) -> p h b", b=3)
        k.tt("dve", coef[:], rc[:], gv[:, :, 0], ALU.mult, [rc, gate], [coef])
        for h in range(8):
            k.ts("dve", oacc[:, h, :], ocs[:, h // 4, h % 4, 0:64], coef[:, h:h + 1], None, ALU.mult, None, [ocs, coef], [oacc])
        for g in range(2):
            k.ts("dve", imp[:, g, 0:62], ocs[:, g, 0, 65:127], rc[:, 4 * g:4 * g + 1], None, ALU.mult, None, [ocs, rc], [imp])
            for hh in range(1, 4):
                k.op("dve", lambda e, g=g, hh=hh: e.scalar_tensor_tensor(out=imp[:, g, 0:62], in0=ocs[:, g, hh, 65:127], scalar=rc[:, 4 * g + hh:4 * g + hh + 1],
                                                                         in1=imp[:, g, 0:62], op0=ALU.mult, op1=ALU.add), [ocs, rc, imp], [imp])
        k.op("dve", lambda e: e.memset(imp[:, :, 62:64], 0.0), [], [imp])
        for g in range(2):
            k.tt("dve", imp[:, g, :], imp[:, g, :], selc[:, 0, :], ALU.mult, [imp, selc], [imp])
            k.tt("dve", imp[:, g, :], imp[:, g, :], selc[:, 1, :], ALU.add, [imp, selc], [imp])
            k.op("dve", lambda e, g=g: e.max(out=top[:, 0:8], in_=imp[:, g, :]), [imp], [top])
            k.op("dve", lambda e, g=g: e.match_replace(out=sc2[:], in_to_replace=top[:, 0:8], in_values=imp[:, g, :], imm_value=-1e30), [imp, top], [sc2])
            k.op("dve", lambda e: e.max(out=top[:, 8:16], in_=sc2[:]), [sc2], [top])
            k.ts("dve", top[:, 15:16], top[:, 15:16], -0.5, None, ALU.max, None, [top], [top])
            k.ts("dve", nsel[:, g, 0:64], imp[:, g, :], top[:, 15:16], None, ALU.is_ge, None, [imp, top], [nsel])
            k.ts("dve", nsel[:, g, 0:64], nsel[:, g, 0:64], 30000.0, -30000.0, ALU.mult, ALU.add, [nsel], [nsel])
            k.op("dve", lambda e, g=g: e.tensor_copy(out=nsel[:, g, 64:128], in_=nsel[:, g, 0:64]), [nsel], [nsel])
            k.mm(psC[:, 64 + 128 * g:64 + 128 * (g + 1)], [(nsel[:, g, :], identf[:])], [nsel, identf], [psC])
        for hh in range(4):
            k.op("act", lambda e, hh=hh: e.copy(out=selT[:, :, hh, :], in_=psC[:, 64:320].rearrange("p (g t) -> p g t", g=2)), [psC], [selT])
        ck(14)
        for g in range(2):
            acc_buf[0] = psV[g]
            acc = psV[g][:, 0:260].rearrange("p (h c) -> p h c", c=65)
            for kb in range(0, lt + 1):
                biases = [(efull[g * 64:(g + 1) * 64, kb * 128:(kb + 1) * 128], selT[g * 64:(g + 1) * 64, g, :, :].rearrange("p a b -> p (a b)"))]
                if kb == lt:
                    biases.append((ident_bf[:], triB[:, 0, :, :].rearrange("p a b -> p (a b)")))
                attn_block(g, KS[g * 64:(g + 1) * 64, 0, kb * 128:(kb + 1) * 128], biases, VX[:, 0, kb, g, :], acc, 65, kb == 0)
            flush()
            k.op("act", lambda e, g=g, acc=acc: e.copy(out=osw[:, g, :, :], in_=acc), [psV[g]], [osw])
        for br in (1, 2):
            if br == 2:
                for g in range(2):
                    acc_buf[0] = psV[g]
                    acc = psV[g][:, 0:260].rearrange("p (h c) -> p h c", c=65)
                    kb0 = max(lt - 4, 0)
                    for kb in range(kb0, lt + 1):
                        biases = []
                        if kb == lt:
                            biases.append((ident_bf[:], triB[:, 0, :, :].rearrange("p a b -> p (a b)")))
                        if kb == lt - 4:
                            biases.append((ident_bf[:], triB[:, 1, :, :].rearrange("p a b -> p (a b)")))
                        if kb < NPRE:
                            biases.append((pfx[:], ones_row[:]))
                        attn_block(g, KS[g * 64:(g + 1) * 64, 1, kb * 128:(kb + 1) * 128], biases, VX[:, 1, kb, g, :], acc, 65, kb == kb0)
                    flush()
                    k.op("act", lambda e, g=g, acc=acc: e.copy(out=osw[:, g, :, :], in_=acc), [psV[g]], [osw])
            k.ts("dve", rc[:], osw[:, :, :, 64].rearrange("p g h -> p (g h)"), 1e-30, None, ALU.max, None, [osw], [rc])
            k.op("dve", lambda e: e.reciprocal(out=rc[:], in_=rc[:]), [rc], [rc])
            k.tt("dve", coef[:], rc[:], gv[:, :, br], ALU.mult, [rc, gate], [coef])
            for h in range(8):
                k.op("dve", lambda e, h=h: e.scalar_tensor_tensor(out=oacc[:, h, :], in0=osw[:, h // 4, h % 4, 0:64], scalar=coef[:, h:h + 1],
                                                                  in1=oacc[:, h, :], op0=ALU.mult, op1=ALU.add), [osw, coef, oacc], [oacc])
        k.op("act", lambda e: e.copy(out=oa_bf[:], in_=oacc[:].rearrange("p h d -> p (h d)")), [oacc], [oa_bf])
        k.dma("pool", scr_oab[i_own * 128:(i_own + 1) * 128, 0:512], oa_bf[:], reads=[oa_bf], writes=[scr_oab])
    k.pop()


def pass2(k, ck, env):
    P = 128
    xloc, xs, w_in, scr_hT, scr_oab = env["xloc"], env["xs"], env["w_in"], env["scr_hT"], env["scr_oab"]
    w_br_d = k.dram("w_branch", [D, D], F32, "ExternalInput")
    w_out_d = k.dram("w_out", [D, D], F32, "ExternalInput")
    identf_d = env["identf_d"]
    scr_x1 = env["scr_x1"]
    k.push()
    wmg = k.sb([P, 8, 2048], BF16, "wmg")
    wbr = k.sb([P, 8, 1024], BF16, "wbr")
    wo = k.sb([P, 8, 1024], BF16, "wo")
    for kc in range(8):
        k.dma("pool", wmg[:, kc, :], w_in[kc * 128:(kc + 1) * 128, NCOL1:5400], writes=[wmg])
        k.dma("pool", wbr[:, kc, :], w_br_d[kc * 128:(kc + 1) * 128, :], writes=[wbr])
        k.dma("pool", wo[:, kc, :], w_out_d[kc * 128:(kc + 1) * 128, :], writes=[wo])
    ident_bf = k.sb([P, 128], BF16, "ident_bf3")
    k.dma("pool", ident_bf[:], identf_d[:], writes=[ident_bf])
    xt = [k.sb([P, D], F32, "x2_%d" % i) for i in range(2)]
    hT = [k.sb([P, 8, 128], BF16, "hT2_%d" % i) for i in range(2)]
    oab = [k.sb([P, 1024], BF16, "oab2_%d" % i) for i in range(2)]
    mab = k.sb([P, 2048], BF16, "mab")
    oT = k.sb([P, 8, 128], BF16, "oT")
    m1 = k.sb([P, 1024], F32, "m1")
    m2 = k.sb([P, 1024], F32, "m2")
    mbf = k.sb([P, 1024], BF16, "mbf")
    mT = k.sb([P, 8, 128], BF16, "mT")
    x1 = k.sb([P, 1024], F32, "x1")
    psT = k.ps([P, 8, 128], BF16, "psT2")
    psZ = [k.ps([P, 512], F32, "psZ2_%d" % i) for i in range(3)]
    zi = [0]

    def nz():
        zi[0] = (zi[0] + 1) % 3
        return psZ[zi[0]]

    for ti in range(NOWN + 2):
        sample = ti == NOWN + 1
        rows = 32 if sample else 128
        rs = slice(0, rows)
        x = xt[ti % 2]
        h = hT[ti % 2]
        ob = oab[ti % 2]
        xsrc = xs[:, :] if sample else xloc[(NPRE - 1 + ti) * 128:(NPRE + ti) * 128, :]
        k.dma("sp", x[rs, :], xsrc, writes=[x])
        k.dma("sp", h[:, :, rs], scr_hT[ti, :, :].rearrange("p (a b) -> p a b", b=128)[:, :, rs], reads=[scr_hT], writes=[h])
        k.dma("sp", ob[rs, :], scr_oab[ti * 128:ti * 128 + rows, :], reads=[scr_oab], writes=[ob])
        for c in range(4):
            z = nz()
            k.mm(z[rs, :], [(h[:, kc, rs], wmg[:, kc, c * 512:(c + 1) * 512]) for kc in range(8)], [h, wmg], [z])
            k.act(mab[rs, c * 512:(c + 1) * 512], z[rs, :], AF.Sigmoid, [z], [mab])
        for kc in range(8):
            k.op("pe", lambda e, kc=kc: e.transpose(psT[:, kc, rs], ob[rs, kc * 128:(kc + 1) * 128], ident_bf[rs, rs]), [ob, ident_bf], [psT])
        k.op("act", lambda e: e.copy(out=oT[:, :, rs], in_=psT[:, :, rs]), [psT], [oT])
        for c in range(2):
            z = nz()
            k.mm(z[rs, :], [(oT[:, kc, rs], wbr[:, kc, c * 512:(c + 1) * 512]) for kc in range(4)], [oT, wbr], [z])
            k.tt("dve", m1[rs, c * 512:(c + 1) * 512], z[rs, :], mab[rs, c * 512:(c + 1) * 512], ALU.mult, [z, mab], [m1])
            z = nz()
            k.mm(z[rs, :], [(oT[:, kc, rs], wbr[:, kc, c * 512:(c + 1) * 512]) for kc in range(4, 8)], [oT, wbr], [z])
            k.tt("dve", m2[rs, c * 512:(c + 1) * 512], z[rs, :], mab[rs, 1024 + c * 512:1024 + (c + 1) * 512], ALU.mult, [z, mab], [m2])
        k.tt("pool", mbf[rs, :], m1[rs, :], m2[rs, :], ALU.add, [m1, m2], [mbf])
        for kc in range(8):
            k.op("pe", lambda e, kc=kc: e.transpose(psT[:, kc, rs], mbf[rs, kc * 128:(kc + 1) * 128], ident_bf[rs, rs]), [mbf, ident_bf], [psT])
        k.op("act", lambda e: e.copy(out=mT[:, :, rs], in_=psT[:, :, rs]), [psT], [mT])
        for c in range(2):
            z = nz()
            k.mm(z[rs, :], [(mT[:, kc, rs], wo[:, kc, c * 512:(c + 1) * 512]) for kc in range(8)], [mT, wo], [z])
            k.tt("dve", x1[rs, c * 512:(c + 1) * 512], z[rs, :], x[rs, c * 512:(c + 1) * 512], ALU.add, [z, x], [x1])
        k.dma("pool", scr_x1[ti * 128:ti * 128 + rows, :], x1[rs, :], reads=[x1], writes=[scr_x1])
    k.pop()


def pass3(k, ck, env):
    P = 128
    scr_x1, identf_d = env["scr_x1"], env["identf_d"]
    f1_d = k.dram("ffn_w_in", [D, 5632], F32, "ExternalInput")
    f2_d = k.dram("ffn_w_out", [2816, D], F32, "ExternalInput")
    fvec_d = k.dram("fvec", [1, 1024], F32, "ExternalInput")
    cw_d = k.dram("convw", [128, 22 * 4], F32, "ExternalInput")
    cst_d = k.dram("convst", [128, 22 * 8], F32, "ExternalInput")
    o_y = k.dram("o_y", [NOWN * 128, D], F32, "ExternalOutput")
    o_ys = k.dram("o_ys", [32, D], F32, "ExternalOutput")
    o_cp = k.dram("o_cp", [128, 22 * 2], F32, "ExternalOutput")
    o_cs = k.dram("o_cs", [128, 22 * 8], F32, "ExternalOutput")
    k.push()
    wf1 = k.sb([P, 8, 5632], BF16, "wf1")
    wf2 = k.sb([P, 22, 1024], BF16, "wf2")
    for kc in range(8):
        k.dma("pool", wf1[:, kc, :], f1_d[kc * 128:(kc + 1) * 128, :], writes=[wf1])
    for fc in range(22):
        k.dma("pool", wf2[:, fc, :], f2_d[fc * 128:(fc + 1) * 128, :], writes=[wf2])
    ident_bf = k.sb([P, 128], BF16, "ident_bf4")
    k.dma("pool", ident_bf[:], identf_d[:], writes=[ident_bf])
    g2 = k.sb([P, D], F32, "g2")
    k.dma("sp", g2[:], fvec_d[0:1, :].partition_broadcast(P), writes=[g2])
    cw = k.sb([P, 22, 4], F32, "cw")
    k.dma("sp", cw[:], cw_d[:].rearrange("p (a b) -> p a b", b=4), writes=[cw])
    x1 = [k.sb([P, D], F32, "x3_%d" % i) for i in range(2)]
    junk = k.sb([P, D], BF16, "junk3")
    h2 = k.sb([P, D], BF16, "h2")
    h2T = k.sb([P, 8, 128], BF16, "h2T")
    st4 = k.sb([P, 8], F32, "st4_3")
    aT = k.sb([P, 22, 130], F32, "aT")
    t1 = k.sb([P, 11, 128], F32, "t1")
    t2 = k.sb([P, 11, 128], F32, "t2")
    gT = k.sb([P, 22, 128], BF16, "gT")
    y = k.sb([P, D], F32, "y")
    psT = k.ps([P, 8, 128], BF16, "psT3")
    psZ = [k.ps([P, 512], F32, "psZ3_%d" % i) for i in range(3)]
    zi = [0]

    def nz():
        zi[0] = (zi[0] + 1) % 3
        return psZ[zi[0]]

    k.op("dve", lambda e: e.memset(aT[:], 0.0), [], [aT])
    for ti in range(NOWN + 2):
        sample = ti == NOWN + 1
        rows = 32 if sample else 128
        rs = slice(0, rows)
        nbat, T = (4, 8) if sample else (1, 128)
        x = x1[ti % 2]
        k.dma("sp", x[rs, :], scr_x1[ti * 128:ti * 128 + rows, :], reads=[scr_x1], writes=[x])
        k.act(junk[rs, :], x[rs, :], AF.Square, [x], [junk, st4], accum_out=st4[rs, 0:1])
        k.ts("dve", st4[rs, 1:2], st4[rs, 0:1], 1.0 / D, EPS, ALU.mult, ALU.add, [st4], [st4])
        k.act(st4[rs, 2:3], st4[rs, 1:2], AF.Sqrt, [st4], [st4])
        k.op("dve", lambda e: e.reciprocal(out=st4[rs, 3:4], in_=st4[rs, 2:3]), [st4], [st4])
        k.op("dve", lambda e: e.scalar_tensor_tensor(out=h2[rs, :], in0=x[rs, :], scalar=st4[rs, 3:4], in1=g2[rs, :],
                                                     op0=ALU.mult, op1=ALU.mult), [x, st4, g2], [h2])
        for kc in range(8):
            k.op("pe", lambda e, kc=kc: e.transpose(psT[:, kc, rs], h2[rs, kc * 128:(kc + 1) * 128], ident_bf[rs, rs]), [h2, ident_bf], [psT])
        k.op("act", lambda e: e.copy(out=h2T[:, :, rs], in_=psT[:, :, rs]), [psT], [h2T])
        av = aT[:, :, 0:nbat * (T + 2)].rearrange("p f (b t) -> p f b t", b=nbat)
        if sample:
            for b in range(4):
                k.dma("sp", av[:, :, b, 0:2], cst_d[:].rearrange("p (f b j) -> p f b j", f=22, b=4)[:, :, b, :], writes=[aT])
        for f0 in range(0, 22, 4):
            n = min(4, 22 - f0)
            z = nz()
            for j in range(n):
                fc = f0 + j
                k.mm(z[:, j * 128:j * 128 + rows], [(wf1[:, kc, fc * 128:(fc + 1) * 128], h2T[:, kc, rs]) for kc in range(8)], [wf1, h2T], [z])
            k.op("act", lambda e, f0=f0, n=n, z=z: e.copy(out=av[:, f0:f0 + n, :, 2:2 + T],
                                                       in_=z[:, 0:n * 128].rearrange("p (f t) -> p f t", t=128)[:, :, 0:rows].rearrange("p f (b t) -> p f b t", b=nbat)),
                 [z], [aT])
        for hf in range(2):
            fs = slice(hf * 11, hf * 11 + 11)
            tv1 = t1[:, :, 0:rows].rearrange("p f (b t) -> p f b t", b=nbat)
            tv2 = t2[:, :, 0:rows].rearrange("p f (b t) -> p f b t", b=nbat)

            def wb(j):
                return cw[:, fs, j:j + 1].unsqueeze(3).to_broadcast([P, 11, nbat, T])
            k.tt("dve", tv1, av[:, fs, :, 2:2 + T], wb(2), ALU.mult, [aT, cw], [t1])
            k.tt("pool", tv2, av[:, fs, :, 1:1 + T], wb(1), ALU.mult, [aT, cw], [t2])
            k.tt("dve", tv1, tv1, tv2, ALU.add, [t1, t2], [t1])
            k.tt("pool", tv2, av[:, fs, :, 0:T], wb(0), ALU.mult, [aT, cw], [t2])
            k.tt("dve", tv1, tv1, tv2, ALU.add, [t1, t2], [t1])
            k.tt("dve", tv1, tv1, wb(3), ALU.add, [t1, cw], [t1])
            k.act(t1[:, :, 0:rows], t1[:, :, 0:rows], AF.Silu, [t1], [t1])
            for f0 in range(hf * 11, hf * 11 + 11, 4):
                n = min(4, hf * 11 + 11 - f0)
                z = nz()
                for j in range(n):
                    fc = f0 + j
                    k.mm(z[:, j * 128:j * 128 + rows], [(wf1[:, kc, 2816 + fc * 128:2816 + (fc + 1) * 128], h2T[:, kc, rs]) for kc in range(8)], [wf1, h2T], [z])
                k.tt("dve", gT[:, f0:f0 + n, 0:rows], t1[:, f0 - hf * 11:f0 - hf * 11 + n, 0:rows],
                     z[:, 0:n * 128].rearrange("p (f t) -> p f t", t=128)[:, :, 0:rows], ALU.mult, [t1, z], [gT])
        for c in range(2):
            z = nz()
            k.mm(z[rs, :], [(gT[:, fc, rs], wf2[:, fc, c * 512:(c + 1) * 512]) for fc in range(22)], [gT, wf2], [z])
            k.tt("dve", y[rs, c * 512:(c + 1) * 512], z[rs, :], x[rs, c * 512:(c + 1) * 512], ALU.add, [z, x], [y])
        if sample:
            k.dma("pool", o_ys[:, :], y[rs, :], reads=[y], writes=[o_ys])
            for b in range(4):
                k.dma("pool", o_cs[:].rearrange("p (f b j) -> p f b j", f=22, b=4)[:, :, b, :], av[:, :, b, T:T + 2], reads=[aT], writes=[o_cs])
        else:
            if ti >= 1:
                k.dma("pool", o_y[(ti - 1) * 128:ti * 128, :], y[rs, :], reads=[y], writes=[o_y])
            if ti == NOWN:
                k.dma("pool", o_cp[:].rearrange("p (f j) -> p f j", j=2), aT[:, :, 128:130], reads=[aT], writes=[o_cp])
            k.op("pool", lambda e: e.tensor_copy(out=aT[:, :, 0:2], in_=aT[:, :, 128:130]), [aT], [aT])
    k.pop()


def pass1c(k, ck, env, npool):
    P = 128
    nc = k.nc
    scr_qk, scr_gate, scr_oab = env["scr_qk"], env["scr_gate"], env["scr_oab"]
    st_win = env["st_win"]
    identf_d = env["identf_d"]
    cpool = k.dram("cache_cmp", [npool, 128, 256], F32, "ExternalInput")
    spool = k.dram("cache_slc", [npool, 128, 256], F32, "ExternalInput")
    ptab_d = k.dram("ptab", [4, 128], I32, "ExternalInput")
    selcS_d = k.dram("selcS", [8, 2 * 384], F32, "ExternalInput")
    selm_d = k.dram("selm", [32, 8 + 16 * 32], F32, "ExternalInput")
    rep_d = k.dram("rep", [8, 32], F32, "ExternalInput")
    ef_d = k.dram("efull128", [128, 8192], F32, "ExternalInput")
    triS_d = k.dram("triBs", [128, 2 * 32], F32, "ExternalInput")
    w1bd_d, w2bd_d, posvec_d = env["w1bd_d"], env["w2bd_d"], env["posvec_d"]
    k.push()
    W1 = k.sb([P, 64, 128], BF16, "W1c")
    for a in range(4):
        k.dma("pool", W1[:, a * 16:(a + 1) * 16, :], w1bd_d[:, a * 2048:(a + 1) * 2048].rearrange("p (a b) -> p a b", b=128), writes=[W1])
    W2 = k.sb([P, 2, 128], BF16, "W2c")
    k.dma("pool", W2[:], w2bd_d[:].rearrange("p (a b) -> p a b", b=128), writes=[W2])
    posv = k.sb([P, 64], BF16, "posvc")
    k.dma("pool", posv[:], posvec_d[:], writes=[posv])
    ef = k.sb([P, 8192], BF16, "ef128")
    k.dma("pool", ef[:], ef_d[:], writes=[ef])
    triS = k.sb([P, 2, 32], BF16, "triS")
    k.dma("pool", triS[:], triS_d[:].rearrange("p (a b) -> p a b", b=32), writes=[triS])
    identf = k.sb([P, 128], F32, "identfc")
    k.dma("sp", identf[:], identf_d[:], writes=[identf])
    ident_bf = k.sb([P, 128], BF16, "identbc")
    k.op("dve", lambda e: e.tensor_copy(out=ident_bf[:], in_=identf[:]), [identf], [ident_bf])
    selcS = k.sb([8, 2, 384], F32, "selcS")
    k.dma("sp", selcS[:], selcS_d[:].rearrange("p (a b) -> p a b", b=384), writes=[selcS])
    selm = k.sb([32, 8 + 512], F32, "selm")
    k.dma("sp", selm[:], selm_d[:], writes=[selm])
    rep = k.sb([8, 32], F32, "rep")
    k.dma("sp", rep[:], rep_d[:], writes=[rep])
    iot_d = k.dram("iot", [128, 1], F32, "ExternalInput")
    ptb = k.sb([P, 512], I32, "ptb")
    for b in range(4):
        k.dma("sp", ptb[:, b * 128:(b + 1) * 128], ptab_d[b:b + 1, :].partition_broadcast(P), writes=[ptb])
    io = k.sb([P, 1], F32, "iotc")
    k.dma("sp", io[:], iot_d[:], writes=[io])
    idxf = k.sb([P, 512], F32, "idxf")
    pidx = k.sb([P, 512], I32, "pidx")
    k.op("dve", lambda e: e.tensor_copy(out=idxf[:], in_=ptb[:]), [ptb], [idxf])
    k.op("dve", lambda e: e.tensor_scalar(out=idxf[:], in0=idxf[:], scalar1=128.0, scalar2=io[:, 0:1], op0=ALU.mult, op1=ALU.add), [idxf, io], [idxf])
    k.op("dve", lambda e: e.tensor_copy(out=pidx[:], in_=idxf[:]), [idxf], [pidx])
    KCb = k.sb([P, 2, 16384], BF16, "KCb")
    KSb = k.sb([P, 16384 + 128], BF16, "KSb")
    VXb = k.sb([P, 129, 2, 65], BF16, "VXb")
    KWb = k.sb([P, 640], BF16, "KWb")
    VWb = k.sb([P, 5, 2, 65], BF16, "VWb")
    kvc = k.sb([P, 2, 1024], BF16, "kvc")
    vcb = k.sb([P, 8, 128], BF16, "vcb")
    k.op("pool", lambda e: e.memset(KSb[:, 16384:16512], 0.0), [], [KSb])
    k.op("pool", lambda e: e.memset(KWb[:, 512:640], 0.0), [], [KWb])
    k.op("pool", lambda e: e.memset(VXb[:, :, :, 64:65], 1.0), [], [VXb])
    k.op("pool", lambda e: e.memset(VXb[:, 128, :, 0:64], 0.0), [], [VXb])
    k.op("pool", lambda e: e.memset(VWb[:, :, :, 64:65], 1.0), [], [VWb])
    k.op("pool", lambda e: e.memset(VWb[:, 4, :, 0:64], 0.0), [], [VWb])
    k.op("pool", lambda e: e.memset(kvc[:], 0.0), [], [kvc])
    pg = [k.sb([P, 2, 256], F32, "pg%d" % i) for i in range(2)]
    qs = k.sb([8, 1280], BF16, "qs")
    qTs = k.sb([P, 4, 8], BF16, "qTs")
    spre = k.sb([P, 2, 128], BF16, "sprec")
    posW = k.sb([P, 2], F32, "posWc")
    pS = k.sb([32, 1024], F32, "pS")
    pbf = k.sb([32, 1024], BF16, "pbf")
    pTs = k.sb([P, 8, 32], BF16, "pTs")
    st = k.sb([32, 8], F32, "stc")
    impr = k.sb([32, 256], F32, "impr")
    sc = k.sb([8, 2, 384], F32, "scc")
    sc2 = k.sb([8, 384], F32, "sc2c")
    top = k.sb([8, 16], F32, "topc")
    selTs = k.sb([P, 3, 2, 32], BF16, "selTs")
    PTs = [k.sb([P, 256], BF16, "PTs%d" % i) for i in range(2)]
    obr = k.sb([32, 3, 2, 65], F32, "obr")
    gate_r = k.sb([32, 2, 3], F32, "gate_r")
    rcs = k.sb([32, 8], F32, "rcs")
    ofin = k.sb([32, 2, 64], F32, "ofin")
    oas = k.sb([32, 512], BF16, "oas")
    psT = k.ps([P, 8, 128], BF16, "psTc")
    psF = k.ps([P, 4, 128], F32, "psFc")
    psS = k.ps([P, 1024], F32, "psSc")
    psC = k.ps([P, 512], F32, "psCc")
    psA = k.ps([P, 512], F32, "psAc")
    psO = k.ps([P, 512], F32, "psOc")
    psX = k.ps([P, 512], F32, "psXc")
    pendc = [None]
    for j in range(2):
        for idx in range(32):
            k.op("pe", lambda e, j=j, idx=idx: e.matmul(psC[:, j:j + 1], lhsT=W1[:, j * 32 + idx, :], rhs=posv[:, j * 32 + idx:j * 32 + idx + 1],
                                                        start=(idx == 0), stop=(idx == 31)), [W1, posv], [psC])
    k.op("act", lambda e: e.copy(out=posW[:], in_=psC[:, 0:2]), [psC], [posW])
    k.op("dve", lambda e: e.memset(pS[:], 0.0), [], [pS])
    first_o = [True]
    rn = [0]
    for b in range(4):
        r0 = NT * 128 + 8 * b
        k.dma("sp", qs[:], scr_qk[r0:r0 + 8, :], reads=[scr_qk], writes=[qs])
        for hh in range(4):
            k.dma("sp", gate_r[hh * 8:hh * 8 + 8, :, :],
                  scr_gate[NOWN * 128 + 128 + 8 * b:NOWN * 128 + 128 + 8 * b + 8, :].rearrange("p (g x) -> p g x", g=2)[:, :, hh * 3:hh * 3 + 3],
                  reads=[scr_gate], writes=[gate_r])
        for hh in range(4):
            k.op("pe", lambda e, hh=hh: e.transpose(psT[:, hh, 0:8], qs[:, hh * 128:(hh + 1) * 128], ident_bf[0:8, 0:8]), [qs, ident_bf], [psT])
        k.op("pe", lambda e: e.transpose(psT[:, 4, 0:8], qs[:, 640:768], ident_bf[0:8, 0:8]), [qs, ident_bf], [psT])
        k.op("pe", lambda e: e.transpose(psT[:, 5, 0:8], qs[:, 768:896], ident_bf[0:8, 0:8]), [qs, ident_bf], [psT])
        k.op("act", lambda e: e.copy(out=qTs[:], in_=psT[:, 0:4, 0:8]), [psT], [qTs])
        k.op("act", lambda e: e.copy(out=KSb[:, 16384:16392], in_=psT[:, 4, 0:8]), [psT], [KSb])
        k.op("act", lambda e: e.copy(out=KWb[:, 512:520], in_=psT[:, 5, 0:8]), [psT], [KWb])
        k.dma("sp", VXb[0:8, 128, :, 0:64], scr_qk[r0:r0 + 8, 1024:1152].rearrange("p (g d) -> p g d", g=2), reads=[scr_qk], writes=[VXb])
        k.dma("sp", VWb[0:8, 4, :, 0:64], scr_qk[r0:r0 + 8, 1152:1280].rearrange("p (g d) -> p g d", g=2), reads=[scr_qk], writes=[VWb])
        qflat = qTs[:].rearrange("p a b -> p (a b)")
        for p in range(128):
            t_pg = pg[p % 2]
            for which, pool_d in ((0, cpool), (1, spool)):
                k._sync("pool", [pidx], [t_pg])
                di = k.dnext
                k.dnext = (k.dnext + 1) % NDS
                k._wait("pool", ("d", di), k.dval[di])
                ins = nc.gpsimd.indirect_dma_start(out=t_pg[:, which, :], out_offset=None, in_=pool_d[:].rearrange("n r f -> (n r) f"),
                                                   in_offset=bass.IndirectOffsetOnAxis(ap=pidx[:, b * 128 + p:b * 128 + p + 1], axis=0))
                k.n_ins += 1
                k.dval[di] += 16
                ins.then_inc(k.dsems[di], 16)
                t_pg.w[("d", di)] = k.dval[di]
            k.op("pe", lambda e, t_pg=t_pg: e.transpose(psF[:, 0, :], t_pg[:, 0, 0:128], identf[:]), [t_pg, identf], [psF])
            k.op("pe", lambda e, t_pg=t_pg: e.transpose(psF[:, 1, :], t_pg[:, 0, 128:256], identf[:]), [t_pg, identf], [psF])
            k.op("pe", lambda e, t_pg=t_pg: e.transpose(psF[:, 2, :], t_pg[:, 1, 0:128], identf[:]), [t_pg, identf], [psF])
            k.op("act", lambda e, p=p: e.copy(out=KCb[:, :, p * 128:(p + 1) * 128], in_=psF[:, 0:2, :]), [psF], [KCb])
            k.op("dve", lambda e, p=p: e.tensor_copy(out=KSb[:, p * 128:(p + 1) * 128], in_=psF[:, 2, :]), [psF], [KSb])
            k.op("dve", lambda e, p=p, t_pg=t_pg: e.tensor_copy(out=VXb[:, p, :, 0:64], in_=t_pg[:, 1, 128:256].rearrange("p (g d) -> p g d", g=2)), [t_pg], [VXb])
        for i in range(4):
            t_pg = pg[i % 2]
            k.dma("sp", t_pg[:, 0, :], st_win[b, i * 128:(i + 1) * 128, :], writes=[t_pg])
            k.op("pe", lambda e, t_pg=t_pg: e.transpose(psF[:, 3, :], t_pg[:, 0, 0:128], identf[:]), [t_pg, identf], [psF])
            k.op("act", lambda e, i=i: e.copy(out=KWb[:, i * 128:(i + 1) * 128], in_=psF[:, 3, :]), [psF], [KWb])
            k.op("dve", lambda e, i=i, t_pg=t_pg: e.tensor_copy(out=VWb[:, i, :, 0:64], in_=t_pg[:, 0, 128:256].rearrange("p (g d) -> p g d", g=2)), [t_pg], [VWb])
        for gi in range(8):
            c0 = max(128 * gi - 1, 0)
            nb = 128 * gi + 127 - c0
            for j in range(2):
                n = 0
                for r in range(2):
                    for i16 in range(16):
                        s0 = 16 * (c0 + r) + i16
                        k.op("pe", lambda e, j=j, r=r, i16=i16, s0=s0, n=n, nb=nb: e.matmul(
                            psC[:, 128 * j:128 * j + nb], lhsT=W1[:, j * 32 + r * 16 + i16, :], rhs=KCb[:, j, s0:s0 + 16 * (nb - 1) + 1:16],
                            start=(n == 0), stop=(n == 31)), [W1, KCb], [psC], tag=("CMPc", j))
                        n += 1
                k.act(spre[:, j, 0:nb], psC[:, 128 * j:128 * j + nb], AF.Silu, [psC, posW], [spre], bias=posW[:, j:j + 1])
            for j in range(2):
                k.mm(psC[:, 256 + 128 * j:256 + 128 * j + nb], [(W2[:, j, :], spre[:, j, 0:nb])], [W2, spre], [psC])
            k.op("act", lambda e, c0=c0, nb=nb: e.copy(out=kvc[:, :, c0:c0 + nb], in_=psC[:, 256:512].rearrange("p (j c) -> p j c", j=2)[:, :, 0:nb]), [psC], [kvc])
        for ch in range(8):
            k.op("pe", lambda e, ch=ch: e.transpose(psT[:, ch, :], kvc[:, 1, ch * 128:(ch + 1) * 128], ident_bf[:]), [kvc, ident_bf], [psT])
        k.op("act", lambda e: e.copy(out=vcb[:], in_=psT[:]), [psT], [vcb])
        for g in range(2):
            gs = slice(g * 64, (g + 1) * 64)
            for hf in range(2):
                k.mm(psS[0:32, hf * 512:(hf + 1) * 512], [(qflat[gs, :], kvc[gs, 0, hf * 512:(hf + 1) * 512])], [qTs, kvc], [psS])
            k.op("dve", lambda e: e.tensor_reduce(out=st[:, 0:1], in_=psS[0:32, 0:1023], axis=AX.X, op=ALU.max), [psS], [st])
            k.ts("dve", st[:, 1:2], st[:, 0:1], -1.0, None, ALU.mult, None, [st], [st])
            k.act(pS[:, 0:1023], psS[0:32, 0:1023], AF.Exp, [psS, st], [pS, st], bias=st[:, 1:2], accum_out=st[:, 2:3])
            k.op("dve", lambda e: e.reciprocal(out=st[:, 3:4], in_=st[:, 2:3]), [st], [st])
            k.ts("dve", pS[:, 0:1023], pS[:, 0:1023], st[:, 3:4], None, ALU.mult, None, [pS, st], [pS])
            k.op("act", lambda e: e.copy(out=pbf[:], in_=pS[:]), [pS], [pbf])
            k.op("dve", lambda e: e.tensor_reduce(out=impr[:], in_=pS[:].rearrange("p (s f) -> p s f", f=4), axis=AX.X, op=ALU.add), [pS], [impr])
            k.tt("dve", impr[:, 1:256], impr[:, 1:256], pS[:, 3:1020:4], ALU.add, [impr, pS], [impr])
            k.mm(psA[0:8, 0:256], [(selm[:, 0:8], impr[:])], [selm, impr], [psA])
            k.op("dve", lambda e, g=g: e.memset(sc[:, g, :], 0.0), [], [sc])
            k.op("act", lambda e, g=g: e.copy(out=sc[:, g, 0:256], in_=psA[0:8, 0:256]), [psA], [sc])
            for ch in range(8):
                k.op("pe", lambda e, ch=ch: e.transpose(psT[:, ch, 0:32], pbf[:, ch * 128:(ch + 1) * 128], ident_bf[0:32, 0:32]), [pbf, ident_bf], [psT])
            k.op("act", lambda e: e.copy(out=pTs[:], in_=psT[:, :, 0:32]), [psT], [pTs])
            k.mm(psA[0:32, 256:320], [(pTs[:, ch, :], vcb[:, ch, g * 64:(g + 1) * 64]) for ch in range(8)], [pTs, vcb], [psA])
            k.op("act", lambda e, g=g: e.copy(out=obr[:, 0, g, 0:64], in_=psA[0:32, 256:320]), [psA], [obr])
            k.op("dve", lambda e, g=g: e.memset(obr[:, 0, g, 64:65], 1.0), [], [obr])
            k.tt("dve", sc[:, g, :], sc[:, g, :], selcS[:, 0, :], ALU.mult, [sc, selcS], [sc])
            k.tt("dve", sc[:, g, :], sc[:, g, :], selcS[:, 1, :], ALU.add, [sc, selcS], [sc])
            k.op("dve", lambda e, g=g: e.max(out=top[:, 0:8], in_=sc[:, g, :]), [sc], [top])
            k.op("dve", lambda e, g=g: e.match_replace(out=sc2[:], in_to_replace=top[:, 0:8], in_values=sc[:, g, :], imm_value=-1e30), [sc, top], [sc2])
            k.op("dve", lambda e: e.max(out=top[:, 8:16], in_=sc2[:]), [sc2], [top])
            k.ts("dve", top[:, 15:16], top[:, 15:16], -0.5, None, ALU.max, None, [top], [top])
            k.ts("dve", sc[:, g, :], sc[:, g, :], top[:, 15:16], None, ALU.is_ge, None, [sc, top], [sc])
            k.ts("dve", sc[:, g, :], sc[:, g, :], 30000.0, -30000.0, ALU.mult, ALU.add, [sc], [sc])
            for c3 in range(3):
                k.mm(psA[:, 320 + 32 * c3:352 + 32 * c3], [(sc[:, g, c3 * 128:(c3 + 1) * 128], rep[:])], [sc, rep], [psA])
            k.op("act", lambda e, g=g: e.copy(out=selTs[:, :, g, :], in_=psA[:, 320:416].rearrange("p (c t) -> p c t", c=3)), [psA], [selTs])
        ck(141 + b)
        pi = 0
        for br in (1, 2):
            for g in range(2):
                gs = slice(g * 64, (g + 1) * 64)
                nkb = 129 if br == 1 else 5
                blocks = []
                for kb in range(nkb):
                    if br == 1:
                        sp_ = (KSb[gs, kb * 128:(kb + 1) * 128], qflat[gs, :])
                        bs_ = [(ef[:, (kb % 64) * 128:(kb % 64) * 128 + 128], selTs[:, (2 * kb) // 128, g, :])]
                        if kb == 128:
                            bs_.append((ident_bf[:], triS[:, 0, :]))
                        vr = VXb[:, kb, g, :]
                    else:
                        sp_ = (KWb[gs, kb * 128:(kb + 1) * 128], qflat[gs, :])
                        bs_ = []
                        if kb == 0:
                            bs_.append((ident_bf[:], triS[:, 1, :]))
                        if kb == 4:
                            bs_.append((ident_bf[:], triS[:, 0, :]))
                        vr = VWb[:, kb, g, :]
                    blocks.append((sp_, bs_, vr, kb))
                gi_ = 0
                for c0_ in range(0, nkb, 8):
                    grp = blocks[c0_:c0_ + 8]
                    ng = len(grp)
                    Sb = (psC, psX)[gi_ % 2]
                    gi_ += 1
                    for j, (sp_, bs_, vr, kb) in enumerate(grp):
                        k.op("pe", lambda e, sp_=sp_, j=j, Sb=Sb: e.matmul(Sb[:, j * 32:(j + 1) * 32], lhsT=sp_[0], rhs=sp_[1], start=(j == 0), stop=False,
                                                                     skip_group_check=True), [KSb, KWb, qTs], [Sb], tag=("S1c", g))
                    for j, (sp_, bs_, vr, kb) in enumerate(grp):
                        for (l, r) in bs_:
                            k.op("pe", lambda e, l=l, r=r, j=j, Sb=Sb: e.matmul(Sb[:, j * 32:(j + 1) * 32], lhsT=l, rhs=r, start=False, stop=False,
                                                                             skip_group_check=True), [ef, selTs, ident_bf, triS], [Sb], tag=("E1c",))
                    if pendc[0] is not None:
                        f = pendc[0]
                        pendc[0] = None
                        f()
                    pi ^= 1
                    pt = PTs[pi]
                    k.act(pt[:, 0:ng * 32], Sb[:, 0:ng * 32], AF.Exp, [Sb], [pt])

                    def pv(pt=pt, grp=grp, nkb=nkb):
                        for j, (sp_, bs_, vr, kb) in enumerate(grp):
                            k.op("pe", lambda e, j=j, vr=vr, kb=kb: e.matmul(psA[0:32, 416 + 0:416 + 65], lhsT=pt[:, j * 32:(j + 1) * 32], rhs=vr,
                                                                          start=(kb == 0), stop=(kb == nkb - 1)), [pt, VXb, VWb], [psA], tag=("PV1c",))
                    pendc[0] = pv
                if pendc[0] is not None:
                    f = pendc[0]
                    pendc[0] = None
                    f()
                k.op("act", lambda e, br=br, g=g: e.copy(out=obr[:, br, g, :], in_=psA[0:32, 416:481]), [psA], [obr])
        k.ts("dve", rcs[:, 0:6], obr[:, :, :, 64].rearrange("p a g -> p (a g)"), 1e-30, None, ALU.max, None, [obr], [rcs])
        k.op("dve", lambda e: e.reciprocal(out=rcs[:, 0:6], in_=rcs[:, 0:6]), [rcs], [rcs])
        for g in range(2):
            for br in range(3):
                k.tt("dve", st[:, 4:5], rcs[:, br * 2 + g:br * 2 + g + 1], gate_r[:, g, br:br + 1], ALU.mult, [rcs, gate_r], [st])
                if br == 0:
                    k.ts("dve", ofin[:, g, :], obr[:, 0, g, 0:64], st[:, 4:5], None, ALU.mult, None, [obr, st], [ofin])
                else:
                    k.op("dve", lambda e, g=g, br=br: e.scalar_tensor_tensor(out=ofin[:, g, :], in0=obr[:, br, g, 0:64], scalar=st[:, 4:5], in1=ofin[:, g, :],
                                                                             op0=ALU.mult, op1=ALU.add), [obr, st, ofin], [ofin])
            for hh in range(4):
                fo = first_o[0]
                first_o[0] = False
                k.op("pe", lambda e, g=g, hh=hh, fo=fo, b=b: e.matmul(psO[0:32, (g * 4 + hh) * 64:(g * 4 + hh + 1) * 64],
                                                                   lhsT=selm[:, 8 + (hh * 4 + b) * 32:8 + (hh * 4 + b + 1) * 32], rhs=ofin[:, g, :],
                                                                   start=fo, stop=False, skip_group_check=True), [selm, ofin], [psO])
    k.op("act", lambda e: e.copy(out=oas[:], in_=psO[0:32, :]), [psO], [oas])
    k.dma("pool", scr_oab[(NOWN + 1) * 128:(NOWN + 1) * 128 + 32, 0:512], oas[:], reads=[oas], writes=[scr_oab])
    k.pop()


def _consts():
    cm = np.zeros((128, 6, 128), np.float32)
    cm[:, 0, :] = np.eye(128, dtype=np.float32)
    s = np.arange(128)[:, None]
    t = np.arange(128)[None, :]
    same_p = (s // 64) == (t // 64)
    cm[:, 1, :] = ((s <= t) & same_p)
    cm[:, 2, :] = ((s > t) & same_p)
    same_s = ((s // 8) == (t // 8)) & (s < 32) & (t < 32)
    cm[:, 3, :] = ((s <= t) & same_s)
    cm[:, 4, :] = ((s > t) & same_s)
    cc = np.zeros((128, 6, 128), np.float32)
    cc[:, 0, 0:64] = 1
    cc[:, 1, 64:128] = 1
    for b in range(4):
        cc[:, 2 + b, 8 * b:8 * b + 8] = 1
    ci = np.zeros((128, 8), np.float32)
    ci[0:64, 0] = 1
    ci[64:128, 1] = 1
    for b in range(4):
        ci[8 * b:8 * b + 8, 2 + b] = 1
    return cm.reshape(128, 768), ci, cc.reshape(128, 768)


def _rope_tab(pos):
    inv = (10000.0 ** (-np.arange(32, dtype=np.float32) / np.float32(32))).astype(np.float32)
    ang = pos.astype(np.float32)[:, None] * inv[None, :]
    return np.concatenate([np.cos(ang), np.sin(ang)], axis=1).astype(np.float32)


def _nsa_consts(inp, half):
    w1 = inp["cmp_w1"][0]
    w2 = inp["cmp_w2"][0]
    pe = inp["cmp_pos_emb"][0]
    w1bd = np.zeros((2, 64, 64, 2, 64), np.float32)
    w1r = w1.reshape(2, 32, 64, 64)
    for g in range(2):
        w1bd[g, :, :, g, :] = w1r.transpose(2, 0, 1, 3).reshape(64, 64, 64)
    w1bd = w1bd.reshape(128, 64 * 128)
    w2bd = np.zeros((2, 64, 2, 2, 64), np.float32)
    for g in range(2):
        w2bd[g, :, :, g, :] = w2.transpose(1, 0, 2)
    w2bd = w2bd.reshape(128, 256)
    posvec = np.tile(pe.transpose(2, 0, 1).reshape(64, 64), (2, 1)).astype(np.float32)
    c = np.arange(256)[:, None]
    sidx = np.arange(62)[None, :]
    cov = ((c >= 4 * sidx - 1) & (c <= 4 * sidx + 3)).astype(np.float32)
    cover = cov.reshape(2, 128, 62).transpose(1, 0, 2).reshape(128, 124)
    NEG = -30000.0
    n_t = NOWN + 1
    cmpB = np.zeros((n_t, 128, 2, 128), np.float32)
    selc = np.zeros((n_t, 128, 2, 64), np.float32)
    cl = np.arange(128)[:, None]
    t = np.arange(128)[None, :]
    soff = 32 if half == 0 else 0
    coff = 128 if half == 0 else 0
    for i in range(n_t):
        lt = NPRE - 1 + i
        for ch in range(2):
            cc = ch * 128 + cl
            vis = (16 * cc + 31 <= 128 * lt + t) & (cc - coff >= 0) & (cc <= 254)
            cmpB[i, :, ch, :] = np.where(vis, 0.0, NEG)
        l = 128 * lt + np.arange(128)[:, None]
        s = np.arange(64)[None, :]
        sg = s - soff
        cur_g = l // 64 - soff
        valid = (64 * s <= l) & (sg >= 0)
        forced = (sg >= 0) & ((sg == 0) | (sg == cur_g) | (sg == cur_g - 1)) & (cur_g >= 0)
        selc[i, :, 0, :] = (valid & ~forced)
        selc[i, :, 1, :] = np.where(forced, 1e4, np.where(valid, 0.0, -1.0))
    key = np.arange(4096)[None, :]
    efull = (key // 64 == np.arange(64)[:, None]).astype(np.float32)
    kk = np.arange(128)[:, None]
    triB = np.zeros((128, 2, 128), np.float32)
    triB[:, 0, :] = np.where(kk > t, NEG, 0.0)
    triB[:, 1, :] = np.where(kk <= t, NEG, 0.0)
    pfx = np.full((1, 128), NEG if half == 0 else 0.0, np.float32)
    return dict(w1bd=w1bd, w2bd=w2bd, posvec=posvec, cover=cover, cmpB=cmpB.reshape(n_t, 128, 256), selc=selc.reshape(n_t, 128, 128),
                efull=efull, triB=triB.reshape(128, 256), pfx=pfx, identf=np.eye(128, dtype=np.float32))


def _sample_consts():
    NEG = -30000.0
    selcS = np.zeros((8, 2, 384), np.float32)
    s = np.arange(384)
    valid = s <= 256
    forced = (s == 0) | (s == 255) | (s == 256)
    selcS[:, 0, :] = (valid & ~forced)[None, :]
    selcS[:, 1, :] = np.where(forced, 1e4, np.where(valid, 0.0, -1.0))[None, :]
    selm = np.zeros((32, 8 + 512), np.float32)
    rep = np.zeros((8, 32), np.float32)
    for hh in range(4):
        for t in range(8):
            selm[hh * 8 + t, t] = 1
            rep[t, hh * 8 + t] = 1
            for b in range(4):
                selm[hh * 8 + t, 8 + (hh * 4 + b) * 32 + b * 8 + t] = 1
    key = np.arange(8192)[None, :]
    ef = (key // 64 == np.arange(128)[:, None]).astype(np.float32)
    r = np.arange(128)[:, None]
    tt = (np.arange(32) % 8)[None, :]
    tri = np.zeros((128, 2, 32), np.float32)
    tri[:, 0, :] = np.where(r > tt, NEG, 0.0)
    tri[:, 1, :] = np.where(r <= tt, NEG, 0.0)
    return dict(selcS=selcS.reshape(8, 768), selm=selm, rep=rep, efull128=ef, triBs=tri.reshape(128, 64),
                iot=np.arange(128, dtype=np.float32)[:, None])


_CACHE = {}


def _get_program(key=(NT, True, 0, 5120)):
    if key not in _CACHE:
        _CACHE[key] = build_program(*key)
    return _CACHE[key]


def make_in_maps(inp, cores, pools=None):
    cmask, cind, ccolmask = _consts()
    vecs = np.concatenate([inp["attn_norm_g"][0], inp["q_norm_g"][0], inp["k_norm_g"][0].reshape(-1),
                           inp["hgrn_lb_logits"].reshape(-1), inp["hgrn_norm_g"][0]]).astype(np.float32)[None, :]
    w_in = np.ascontiguousarray(inp["w_in"][0])
    maps = []
    for c in cores:
        b, half = c // 2, c % 2
        c0 = half * 2048
        xloc = np.zeros((NT * 128, D), np.float32)
        if half == 0:
            xloc[2048:] = inp["x_prompt"][b, 0:2048]
        else:
            xloc[:] = inp["x_prompt"][b, 0:4096]
        pos = np.arange(NT * 128) + c0 - 2048
        cs_tab = np.concatenate([_rope_tab(pos), _rope_tab(16384 + (np.arange(32) % 8))], axis=0)
        maps.append(dict(
            xloc=xloc, xs=np.ascontiguousarray(inp["x_sample"][4 * c:4 * c + 4].reshape(32, D)), cs_tab=cs_tab,
            w_in=w_in, vecs=vecs, cmask=cmask, cind=cind, ccolmask=ccolmask,
            st_win=np.ascontiguousarray(inp["state_win_kv"][0, 4 * c:4 * c + 4].reshape(4, 512, 256)),
            st_hgrn=np.ascontiguousarray(inp["state_hgrn"][0, 4 * c:4 * c + 4]),
        ))
        maps[-1].update(_nsa_consts(inp, half))
        cwv = np.concatenate([inp["ffn_conv_w"][0], inp["ffn_conv_b"]], axis=0)
        convw = np.ascontiguousarray(cwv.reshape(4, 22, 128).transpose(2, 1, 0)).reshape(128, 88)
        cst = inp["state_ffn_conv"][0, 4 * c:4 * c + 4]
        convst = np.ascontiguousarray(cst.reshape(4, 2, 22, 128).transpose(3, 2, 0, 1)).reshape(128, 176)
        maps[-1].update(w_branch=np.ascontiguousarray(inp["w_branch"][0]), w_out=np.ascontiguousarray(inp["w_out"][0]),
                        ffn_w_in=np.ascontiguousarray(inp["ffn_w_in"][0]), ffn_w_out=np.ascontiguousarray(inp["ffn_w_out"][0]),
                        fvec=np.ascontiguousarray(inp["ffn_norm_g"][0][None, :]), convw=convw, convst=convst)
        if pools is None:
            maps[-1].update(cache_cmp=inp["cache_cmp_kv"][0].reshape(-1, 128, 256), cache_slc=inp["cache_slc_kv"][0].reshape(-1, 128, 256),
                            ptab=np.ascontiguousarray(inp["page_table"][4 * c:4 * c + 4]).astype(np.int32))
        else:
            maps[-1].update(pools(c))
        maps[-1].update(_sample_consts())
    return maps


def assemble(res, cores, out):
    for i, c in enumerate(cores):
        r = res[i]
        b, half = c // 2, c % 2
        c0 = half * 2048
        for j, name in enumerate(("cmp_kv_prompt", "slc_kv_prompt")):
            out[name][0, b, c0:c0 + 2048] = r["o_kv"][j].reshape(2048, 2, 2, 64)
        if half == 1:
            out["win_kv_prompt"][0, b] = r["o_kv"][2][2048 - 512:].reshape(512, 2, 2, 64)
            out["hgrn_prompt"][0, b] = r["o_hg_p"]
        out["cmp_kv_sample"][0, 4 * c:4 * c + 4] = r["o_kvs"][0].reshape(4, 8, 2, 2, 64)
        out["slc_kv_sample"][0, 4 * c:4 * c + 4] = r["o_kvs"][1].reshape(4, 8, 2, 2, 64)
        out["win_kv_sample"][0, 4 * c:4 * c + 4] = r["o_win_s"].reshape(4, 512, 2, 2, 64)
        out["hgrn_sample"][0, 4 * c:4 * c + 4] = r["o_hg_s"]
        out["y_prompt"][b, c0:c0 + 2048] = r["o_y"]
        out["y_sample"][4 * c:4 * c + 4] = r["o_ys"].reshape(4, 8, D)
        if half == 1:
            out["ffn_conv_prompt"][0, b] = r["o_cp"].reshape(128, 22, 2).transpose(2, 1, 0).reshape(2, 2816)
        out["ffn_conv_sample"][0, 4 * c:4 * c + 4] = r["o_cs"].reshape(128, 22, 4, 2).transpose(2, 3, 1, 0).reshape(4, 2, 2816)


OUT_SHAPES = dict(
    y_prompt=(4, 4096, 1024), y_sample=(32, 8, 1024),
    cmp_kv_prompt=(1, 4, 4096, 2, 2, 64), cmp_kv_sample=(1, 32, 8, 2, 2, 64),
    slc_kv_prompt=(1, 4, 4096, 2, 2, 64), slc_kv_sample=(1, 32, 8, 2, 2, 64),
    win_kv_prompt=(1, 4, 512, 2, 2, 64), win_kv_sample=(1, 32, 512, 2, 2, 64),
    hgrn_prompt=(1, 4, 4, 128, 128), hgrn_sample=(1, 32, 4, 128, 128),
    ffn_conv_prompt=(1, 4, 2, 2816), ffn_conv_sample=(1, 32, 2, 2816))
OUT_ORDER = ["y_prompt", "y_sample", "cmp_kv_prompt", "cmp_kv_sample", "slc_kv_prompt", "slc_kv_sample",
             "win_kv_prompt", "win_kv_sample", "hgrn_prompt", "hgrn_sample", "ffn_conv_prompt", "ffn_conv_sample"]


def kernel(**inp):
    inp = {n: np.asarray(v) for n, v in inp.items()}
    cores = list(range(8))
    prog = _get_program()
    maps = make_in_maps(inp, cores)
    res = run_bass_kernel_spmd(prog.nc, maps, core_ids=cores)
    out = {n: np.zeros(s, np.float32) for n, s in OUT_SHAPES.items()}
    assemble(res.results, cores, out)
    return tuple(out[n] for n in OUT_ORDER)
```

```python
import numpy as np
import concourse.bass as bass
import concourse.mybir as mybir
from concourse.bass_utils import run_bass_kernel_spmd
from contextlib import ExitStack

F32 = mybir.dt.float32
BF16 = mybir.dt.bfloat16
I32 = mybir.dt.int32
AF = mybir.ActivationFunctionType
ALU = mybir.AluOpType
AX = mybir.AxisListType

NDS = 40
PE_NOSELF = False
PIPE_B = True
NOSELF_CHAIN = True
EPS = 1e-6
D = 1024
NCOL1 = 3352
NPRE = 16
NOWN = 16
NT = NPRE + NOWN


class Buf:
    def __init__(self, name, t):
        self.name = name
        self.t = t
        self.w = {}
        self.r = {}
        self.ps = False

    def __getitem__(self, idx):
        return self.t[idx]


class KB:
    def __init__(self):
        self.nc = bass.Bass("TRN2", target_bir_lowering=False)
        nc = self.nc
        self.es = ExitStack()
        self.engs = {"pe": nc.tensor, "act": nc.scalar, "dve": nc.vector, "pool": nc.gpsimd, "sp": nc.sync}
        self.esem = {}
        self.ecnt = {}
        for e in ("pe", "act", "dve", "pool"):
            self.esem[e] = self.es.enter_context(nc.semaphore("sem_" + e))
            self.ecnt[e] = 0
        self.dsems = [self.es.enter_context(nc.semaphore("sem_d%d" % i)) for i in range(NDS)]
        self.dval = [0] * NDS
        self.dnext = 0
        self.waited = {e: {} for e in self.engs}
        self.nbuf = 0
        self.n_ins = 0
        self.n_wait = 0
        self.stk = [self.es]

    def push(self):
        self.stk.append(ExitStack())

    def barrier(self):
        for e in self.engs:
            for f in ("pe", "act", "dve", "pool"):
                if f != e:
                    self._wait(e, ("e", f), self.ecnt[f])
            for i in range(NDS):
                self._wait(e, ("d", i), self.dval[i])

    def pop(self):
        self.barrier()
        self.stk.pop().close()

    def sb(self, shape, dt, name=None):
        self.nbuf += 1
        name = (name or "sb") + "_%d" % self.nbuf
        return Buf(name, self.stk[-1].enter_context(self.nc.sbuf_tensor(name, list(shape), dt)))

    def ps(self, shape, dt, name=None):
        self.nbuf += 1
        name = name or "ps%d" % self.nbuf
        name = name + "_%d" % self.nbuf
        b = Buf(name, self.stk[-1].enter_context(self.nc.psum_tensor(name, list(shape), dt)))
        b.ps = True
        return b

    def dram(self, name, shape, dt, kind=None):
        if kind is None:
            t = self.nc.dram_tensor(name, list(shape), dt)
        else:
            t = self.nc.dram_tensor(name, list(shape), dt, kind=kind)
        return Buf(name, t.ap())

    def _sem(self, key):
        return self.esem[key[1]] if key[0] == "e" else self.dsems[key[1]]

    def _wait(self, eng, key, val):
        if val <= 0:
            return
        wd = self.waited[eng]
        if wd.get(key, 0) >= val:
            return
        self.engs[eng].wait_ge(self._sem(key), val)
        wd[key] = val
        self.n_wait += 1

    def _sync(self, eng, reads, writes, noself=False):
        need = {}
        for b in reads:
            for k, v in b.w.items():
                if need.get(k, 0) < v:
                    need[k] = v
        for b in writes:
            for k, v in b.w.items():
                if need.get(k, 0) < v:
                    need[k] = v
            for k, v in b.r.items():
                if need.get(k, 0) < v:
                    need[k] = v
        for k, v in need.items():
            if noself and eng == "pe" and k == ("e", "pe"):
                continue
            self._wait(eng, k, v)

    def op(self, eng, fn, reads=(), writes=(), inc=True, noself=False, tag=None):
        if eng == "pe":
            if tag is not None and tag == getattr(self, "last_pe_tag", None):
                noself = True
            self.last_pe_tag = tag
        self._sync(eng, reads, writes, noself)
        ins = fn(self.engs[eng])
        self.n_ins += 1
        if inc:
            self.ecnt[eng] += 1
            n = self.ecnt[eng]
            ins.then_inc(self.esem[eng], 1)
        else:
            n = self.ecnt[eng] + 1
        key = ("e", eng)
        for b in reads:
            d = b.w if b.ps else b.r
            if d.get(key, 0) < n:
                d[key] = n
        for b in writes:
            if b.w.get(key, 0) < n:
                b.w[key] = n
        return ins

    def dma(self, q, out_ap, in_ap, reads=(), writes=(), **kw):
        i = self.dnext
        self.dnext = (i + 1) % NDS
        self._wait(q, ("d", i), self.dval[i])
        self._sync(q, reads, writes)
        ins = self.engs[q].dma_start(out=out_ap, in_=in_ap, **kw)
        self.n_ins += 1
        self.dval[i] += 16
        ins.then_inc(self.dsems[i], 16)
        key = ("d", i)
        for b in reads:
            b.r[key] = self.dval[i]
        for b in writes:
            b.w[key] = self.dval[i]
        return ins

    def finish(self):
        for i in range(NDS):
            self._wait("sp", ("d", i), self.dval[i])
        for e in ("pe", "act", "dve", "pool"):
            self._wait("sp", ("e", e), self.ecnt[e])

    def mm(self, out_ap, pairs, reads, writes):
        n = len(pairs)
        for i, (l, r) in enumerate(pairs):
            self.op("pe", lambda e, l=l, r=r, i=i: e.matmul(out_ap, lhsT=l, rhs=r, start=(i == 0), stop=(i == n - 1)),
                    reads=reads, writes=writes, inc=True, noself=(NOSELF_CHAIN and i > 0))

    def act(self, out_ap, in_ap, func, reads, writes, **kw):
        return self.op("act", lambda e: e.activation(out=out_ap, in_=in_ap, func=func, **kw), reads=reads, writes=writes)

    def tt(self, eng, out_ap, a, b, op, reads, writes):
        return self.op(eng, lambda e: e.tensor_tensor(out=out_ap, in0=a, in1=b, op=op), reads=reads, writes=writes)

    def ts(self, eng, out_ap, a, s1, s2, op0, op1, reads, writes):
        if op1 is None:
            return self.op(eng, lambda e: e.tensor_scalar(out=out_ap, in0=a, scalar1=s1, scalar2=None, op0=op0),
                           reads=reads, writes=writes)
        return self.op(eng, lambda e: e.tensor_scalar(out=out_ap, in0=a, scalar1=s1, scalar2=s2, op0=op0, op1=op1),
                       reads=reads, writes=writes)


class _Stop(Exception):
    pass


def build_program(ntp=NT, with_sample=True, stop=0, npool=5120):
    k = KB()
    try:
        _build(k, ntp, with_sample, stop, npool)
    except _Stop:
        pass
    k.finish()
    return k


def _build(k, ntp, with_sample, stop, npool):
    def ck(n):
        if stop == n:
            raise _Stop()
    nc = k.nc
    P = 128
    xloc = k.dram("xloc", [NT * 128, D], F32, "ExternalInput")
    xs = k.dram("xs", [32, D], F32, "ExternalInput")
    cs_tab = k.dram("cs_tab", [NT * 128 + 32, 64], F32, "ExternalInput")
    w_in = k.dram("w_in", [D, 5400], F32, "ExternalInput")
    vecs = k.dram("vecs", [1, 1024 + 64 + 192 + 1024 + 128], F32, "ExternalInput")
    cmask = k.dram("cmask", [128, 6 * 128], F32, "ExternalInput")
    cind = k.dram("cind", [128, 8], F32, "ExternalInput")
    st_win = k.dram("st_win", [4, 512, 256], F32, "ExternalInput")
    st_hgrn = k.dram("st_hgrn", [4, 4, 128, 128], F32, "ExternalInput")
    ccolmask = k.dram("ccolmask", [128, 6 * 128], F32, "ExternalInput")
    scr_qk = k.dram("scr_qk", [NT * 128 + 32, 1280], BF16)
    scr_gate = k.dram("scr_gate", [(NOWN + 1) * 128 + 32, 24], F32)
    scr_hT = k.dram("scr_hT", [NOWN + 2, 128, 1024], BF16)
    scr_oab = k.dram("scr_oab", [(NOWN + 1) * 128 + 32, 1024], BF16)

    o_kv = k.dram("o_kv", [3, NOWN * 128, 256], F32, "ExternalOutput")
    o_kvs = k.dram("o_kvs", [2, 32, 256], F32, "ExternalOutput")
    o_win_s = k.dram("o_win_s", [4, 512, 256], F32, "ExternalOutput")
    o_hg_p = k.dram("o_hg_p", [4, 128, 128], F32, "ExternalOutput")
    o_hg_s = k.dram("o_hg_s", [4, 4, 128, 128], F32, "ExternalOutput")

    k.push()
    wsb = k.sb([P, 8, NCOL1], BF16, "wsb")
    for kc in range(8):
        k.dma("pool", wsb[:, kc, :], w_in[kc * 128:(kc + 1) * 128, 0:NCOL1], writes=[wsb])
    g_bc = k.sb([P, D], F32, "g_bc")
    k.dma("sp", g_bc[:], vecs[0:1, 0:1024].partition_broadcast(P), writes=[g_bc])
    gain = k.sb([P, 14, 64], F32, "gain")
    for h in range(8):
        k.dma("sp", gain[:, h, :], vecs[0:1, 1024:1088].partition_broadcast(P), writes=[gain])
    for i in range(3):
        for g in range(2):
            k.dma("sp", gain[:, 8 + 2 * i + g, :], vecs[0:1, 1088 + 64 * i:1088 + 64 * i + 64].partition_broadcast(P), writes=[gain])
    k.ts("dve", gain[:, 0:8, :], gain[:, 0:8, :], 0.125, None, ALU.mult, None, [gain], [gain])
    lgt = k.sb([P, 2, 512], F32, "lgt")
    k.dma("sp", lgt[:, 0, :], vecs[0:1, 1280:1792].partition_broadcast(P), writes=[lgt])
    k.dma("sp", lgt[:, 1, :], vecs[0:1, 1792:2304].partition_broadcast(P), writes=[lgt])
    lb = k.sb([P, 512], F32, "lb")
    oml = k.sb([P, 512], F32, "oml")
    k.tt("dve", lb[:], lgt[:, 0, :], lgt[:, 1, :], ALU.subtract, [lgt], [lb])
    k.act(lb[:], lb[:], AF.Sigmoid, [lb], [lb])
    k.ts("dve", oml[:], lb[:], -1.0, 1.0, ALU.mult, ALU.add, [lb], [oml])
    gn_bc = k.sb([P, 4, 128], F32, "gn_bc")
    for h in range(4):
        k.dma("sp", gn_bc[:, h, :], vecs[0:1, 2304:2432].partition_broadcast(P), writes=[gn_bc])
    cm = k.sb([P, 6, 128], F32, "cm")
    k.dma("sp", cm[:], cmask[:].rearrange("p (a b) -> p a b", b=128), writes=[cm])
    ci = k.sb([P, 8], F32, "ci")
    k.dma("sp", ci[:], cind[:], writes=[ci])
    ident_bf = k.sb([P, 128], BF16, "ident_bf")
    k.op("dve", lambda e: e.tensor_copy(out=ident_bf[:], in_=cm[:, 0, :]), [cm], [ident_bf])
    ones_col = k.sb([P, 1], F32, "ones_col")
    k.op("dve", lambda e: e.memset(ones_col[:], 1.0), [], [ones_col])

    ck(1)
    xt = [k.sb([P, D], F32, "xt%d" % i) for i in range(2)]
    junk = k.sb([P, D], BF16, "junk")
    xn = k.sb([P, D], BF16, "xn")
    hT = k.sb([P, 8, 128], BF16, "hT")
    st4 = k.sb([P, 8], F32, "st4")
    R = k.sb([P, 14, 64], F32, "R")
    R2 = k.sb([P, 14, 64], F32, "R2")
    T1 = k.sb([P, 14, 32], F32, "T1")
    T2 = k.sb([P, 14, 32], F32, "T2")
    st14 = k.sb([P, 16], F32, "st14")
    cs = k.sb([P, 64], F32, "cs")
    kvo = k.sb([P, 3, 256], F32, "kvo")
    qkv_bf = k.sb([P, 1280], BF16, "qkv_bf")
    gate_sb = k.sb([P, 24], F32, "gate_sb")
    qb = k.sb([P, 512], F32, "qb")
    u_sb = k.sb([P, 512], F32, "u_sb")
    logf = k.sb([P, 512], F32, "logf")
    kb = k.sb([P, 512], F32, "kb")
    vb = k.sb([P, 512], BF16, "vb")
    ggb = k.sb([P, 512], BF16, "ggb")
    ex = k.sb([P, 512], F32, "ex")
    qe = k.sb([P, 512], BF16, "qe")
    ke = k.sb([P, 512], BF16, "ke")
    kd = k.sb([P, 512], BF16, "kd")
    kdm = k.sb([P, 512], BF16, "kdm")
    dec = k.sb([P, 16], F32, "dec")
    S32 = [k.sb([P, 4, 128], F32, "S32_%d" % i) for i in range(5)]
    Sbf = [k.sb([P, 4, 128], BF16, "Sbf_%d" % i) for i in range(5)]

    psT = k.ps([P, 8, 128], BF16, "psT")
    psZ = [k.ps([P, 512], F32, "psZ%d" % i) for i in range(2)]
    psH = [k.ps([P, 512], F32, "psH%d" % i) for i in range(2)]
    psS = k.ps([P, 4, 128], F32, "psS")
    psA = k.ps([P, 4, 128], F32, "psA")
    psO = k.ps([P, 4, 128], F32, "psO")
    qkT = k.sb([P, 8, 128], BF16, "qkT")
    qeTm = [k.sb([P, 4, 128], BF16, "qeTm%d" % i) for i in range(4)]
    AmT = k.sb([P, 4, 128], BF16, "AmT")
    osq = k.sb([P, 512], F32, "osq")
    st8 = k.sb([P, 8], F32, "st8")
    oab = k.sb([P, 1024], BF16, "oab")
    ccol = k.sb([P, 6, 128], F32, "ccol")
    k.dma("sp", ccol[:], ccolmask[:].rearrange("p (a b) -> p a b", b=128), writes=[ccol])
    k.op("pool", lambda e: e.memset(oab[:], 0.0), [], [oab])

    k.op("dve", lambda e: e.memset(S32[0][:], 0.0), [], [S32[0]])
    k.op("pool", lambda e: e.memset(Sbf[0][:], 0.0), [], [Sbf[0]])
    if with_sample:
        for b in range(4):
            k.dma("sp", S32[1 + b][:], st_hgrn[b].rearrange("h k v -> k h v"), writes=[S32[1 + b]])
            k.op("act", lambda e, b=b: e.copy(out=Sbf[1 + b][:], in_=S32[1 + b][:]), [S32[1 + b]], [Sbf[1 + b]])
        for b in range(4):
            k.dma("sp", o_win_s[b, 0:504, :], st_win[b, 8:512, :], writes=[o_win_s])

    ck(2)
    zi = [0]

    def next_z():
        zi[0] ^= 1
        return psZ[zi[0]]

    def proj(c0, c1):
        z = next_z()
        k.mm(z[:, 0:c1 - c0], [(hT[:, kc, :], wsb[:, kc, c0:c1]) for kc in range(8)], [hT, wsb], [z])
        return z

    def do_tile(lt, rows, xsrc, csrc, full, kv_dst, sample):
        x = xt[lt % 2]
        rs = slice(0, rows)
        k.dma("sp", x[rs, :], xsrc, writes=[x])
        k.dma("sp", cs[rs, :], csrc, writes=[cs])
        ck(31)
        k.act(junk[rs, :], x[rs, :], AF.Square, [x], [junk, st4], accum_out=st4[rs, 0:1])
        k.ts("dve", st4[rs, 1:2], st4[rs, 0:1], 1.0 / D, EPS, ALU.mult, ALU.add, [st4], [st4])
        k.act(st4[rs, 2:3], st4[rs, 1:2], AF.Sqrt, [st4], [st4])
        k.op("dve", lambda e: e.reciprocal(out=st4[rs, 3:4], in_=st4[rs, 2:3]), [st4], [st4])
        k.op("dve", lambda e: e.scalar_tensor_tensor(out=xn[rs, :], in0=x[rs, :], scalar=st4[rs, 3:4], in1=g_bc[rs, :],
                                                     op0=ALU.mult, op1=ALU.mult), [x, st4, g_bc], [xn])
        ck(32)
        for kc in range(8):
            k.op("pe", lambda e, kc=kc: e.transpose(psT[:, kc, rs], xn[rs, kc * 128:(kc + 1) * 128], ident_bf[rs, rs]),
                 [xn, ident_bf], [psT], inc=True, tag=("T",))
        ck(33)
        k.op("act", lambda e: e.copy(out=hT[:, :, rs], in_=psT[:, :, rs]), [psT], [hT])
        if full:
            ti = (NOWN + 1) if sample else (lt - (NPRE - 1))
            k.dma("pool", scr_hT[ti, :, :].rearrange("p (a b) -> p a b", b=128)[:, :, rs], hT[:, :, rs], reads=[hT], writes=[scr_hT])
        ck(3)

        def projr(c0, c1):
            z = next_z()
            k.mm(z[rs, 0:c1 - c0], [(hT[:, kc, rs], wsb[:, kc, c0:c1]) for kc in range(8)], [hT, wsb], [z])
            return z

        if full:
            z = projr(0, 512)
            ck(41)
            k.op("act", lambda e: e.copy(out=R[rs, 0:8, :], in_=z[rs, 0:512].rearrange("p (h d) -> p h d", d=64)), [z], [R])
        ck(42)
        z = projr(512, 1024)
        zv = z[rs, 0:512].rearrange("p (a j c) -> p a j c", a=2, j=2)
        k.op("act", lambda e: e.copy(out=R[rs, 8:12, :].rearrange("p (a g) d -> p a (g d)", a=2), in_=zv[:, :, 0, :]), [z], [R])
        k.op("dve", lambda e: e.tensor_copy(out=kvo[rs, 0:2, 128:256], in_=zv[:, :, 1, :]), [z], [kvo])
        ck(43)
        z = projr(1024, 1304)
        k.op("act", lambda e: e.copy(out=R[rs, 12:14, :].rearrange("p g d -> p (g d)"), in_=z[rs, 0:128]), [z], [R])
        k.op("dve", lambda e: e.tensor_copy(out=kvo[rs, 2, 128:256], in_=z[rs, 128:256]), [z], [kvo])
        if full:
            k.act(gate_sb[rs, :], z[rs, 256:280], AF.Sigmoid, [z], [gate_sb])
        ck(4)
        h0 = 0 if full else 8
        nh = 14 - h0
        k.tt("dve", R2[rs, h0:14, :], R[rs, h0:14, :], R[rs, h0:14, :], ALU.mult, [R], [R2])
        k.op("dve", lambda e: e.tensor_reduce(out=st14[rs, h0:14], in_=R2[rs, h0:14, :], axis=AX.X, op=ALU.add), [R2], [st14])
        k.ts("dve", st14[rs, h0:14], st14[rs, h0:14], 1.0 / 64, EPS, ALU.mult, ALU.add, [st14], [st14])
        k.act(st14[rs, h0:14], st14[rs, h0:14], AF.Sqrt, [st14], [st14])
        k.op("dve", lambda e: e.reciprocal(out=st14[rs, h0:14], in_=st14[rs, h0:14]), [st14], [st14])
        k.tt("pool", R2[rs, h0:14, :], R[rs, h0:14, :], gain[rs, h0:14, :], ALU.mult, [R, gain], [R2])
        k.tt("dve", R2[rs, h0:14, :], R2[rs, h0:14, :], st14[rs, h0:14].unsqueeze(2).to_broadcast([rows, nh, 64]), ALU.mult,
             [R2, st14], [R2])
        cosb = cs[rs, 0:32].unsqueeze(1).to_broadcast([rows, nh, 32])
        sinb = cs[rs, 32:64].unsqueeze(1).to_broadcast([rows, nh, 32])
        x1 = R2[rs, h0:14, 0:32]
        x2 = R2[rs, h0:14, 32:64]
        k.tt("dve", T1[rs, h0:14, :], x1, cosb, ALU.mult, [R2, cs], [T1])
        k.tt("pool", T2[rs, h0:14, :], x2, sinb, ALU.mult, [R2, cs], [T2])
        k.tt("dve", R[rs, h0:14, 0:32], T1[rs, h0:14, :], T2[rs, h0:14, :], ALU.subtract, [T1, T2], [R])
        k.tt("dve", T1[rs, h0:14, :], x2, cosb, ALU.mult, [R2, cs], [T1])
        k.tt("pool", T2[rs, h0:14, :], x1, sinb, ALU.mult, [R2, cs], [T2])
        k.tt("dve", R[rs, h0:14, 32:64], T1[rs, h0:14, :], T2[rs, h0:14, :], ALU.add, [T1, T2], [R])
        k.op("act", lambda e: e.copy(out=kvo[rs, :, 0:128], in_=R[rs, 8:14, :].rearrange("p (a g) d -> p a (g d)", a=3)), [R], [kvo])
        if kv_dst is not None:
            for i, dst in enumerate(kv_dst):
                if dst is not None:
                    k.dma("pool", dst[0], kvo[rs, i, :], reads=[kvo], writes=[dst[1]])
        if True:
            k.op("act", lambda e: e.copy(out=qkv_bf[rs, 0:512].rearrange("p (hh g d) -> p g hh d", hh=4, g=2),
                                         in_=R[rs, 0:8, :].rearrange("p (g hh) d -> p g hh d", g=2)), [R], [qkv_bf])
            k.op("act", lambda e: e.copy(out=qkv_bf[rs, 512:896].rearrange("p (h d) -> p h d", d=64), in_=R[rs, 8:14, :]), [R], [qkv_bf])
            k.op("dve", lambda e: e.tensor_copy(out=qkv_bf[rs, 896:1280].rearrange("p (a c) -> p a c", a=3), in_=kvo[rs, :, 128:256]), [kvo], [qkv_bf])
            k.dma("pool", scr_qk[lt * 128:lt * 128 + rows, :], qkv_bf[rs, :], reads=[qkv_bf], writes=[scr_qk])
            if full:
                gi_ = (NOWN + 1) if sample else (lt - (NPRE - 1))
                k.dma("pool", scr_gate[gi_ * 128:gi_ * 128 + rows, :], gate_sb[rs, :], reads=[gate_sb], writes=[scr_gate])
        ck(5)
        if full:
            z = projr(1304, 1816)
            k.act(qb[rs, :], z[rs, :], AF.Silu, [z], [qb])
        z = projr(1816, 2328)
        k.act(u_sb[rs, :], z[rs, :], AF.Sigmoid, [z], [u_sb])
        k.tt("dve", u_sb[rs, :], u_sb[rs, :], oml[rs, :], ALU.mult, [u_sb, oml], [u_sb])
        k.tt("dve", u_sb[rs, :], u_sb[rs, :], lb[rs, :], ALU.add, [u_sb, lb], [u_sb])
        k.act(logf[rs, :], u_sb[rs, :], AF.Ln, [u_sb], [logf])
        k.ts("pool", kb[rs, :], u_sb[rs, :], -1.0, 1.0, ALU.mult, ALU.add, [u_sb], [kb])
        z = projr(2328, 2840)
        k.op("act", lambda e: e.copy(out=vb[rs, :], in_=z[rs, :]), [z], [vb])
        if full:
            z = projr(2840, 3352)
            k.act(ex[rs, :], z[rs, :], AF.Silu, [z], [ex])
            k.tt("dve", ggb[rs, :], ex[rs, :], gn_bc[rs, :, :].rearrange("p h v -> p (h v)"), ALU.mult, [ex, gn_bc], [ggb])
        ck(6)
        mi = 3 if sample else 1
        nch = 4 if sample else 2
        i0 = 2 if sample else 0
        zD = psH[0]
        k.mm(zD[rs, :], [(cm[rs, mi + 1, rs], logf[rs, :])], [cm, logf], [zD])
        k.act(ex[rs, :], zD[rs, :], AF.Exp, [zD], [ex])
        k.tt("dve", kd[rs, :], kb[rs, :], ex[rs, :], ALU.mult, [kb, ex], [kd])
        if full:
            zB = psH[1]
            k.mm(zB[rs, :], [(cm[rs, mi, rs], logf[rs, :])], [cm, logf], [zB])
            k.act(ex[rs, :], zB[rs, :], AF.Exp, [zB], [ex])
            k.tt("dve", qe[rs, :], qb[rs, :], ex[rs, :], ALU.mult, [qb, ex], [qe])
            k.act(ex[rs, :], zB[rs, :], AF.Exp, [zB], [ex], scale=-1.0)
            k.tt("dve", ke[rs, :], kb[rs, :], ex[rs, :], ALU.mult, [kb, ex], [ke])
            for h in range(4):
                k.op("pe", lambda e, h=h: e.transpose(psT[:, h, rs], qe[rs, h * 128:(h + 1) * 128], ident_bf[rs, rs]), [qe, ident_bf], [psT], tag=("T",))
                k.op("pe", lambda e, h=h: e.transpose(psT[:, 4 + h, rs], ke[rs, h * 128:(h + 1) * 128], ident_bf[rs, rs]), [ke, ident_bf], [psT], tag=("T",))
            k.op("act", lambda e: e.copy(out=qkT[:, :, rs], in_=psT[:, :, rs]), [psT], [qkT])
            ccb = 2 if sample else 0
            for c in range(nch):
                k.tt("dve", qeTm[c][:, :, rs], qkT[:, 0:4, rs], ccol[:, ccb + c, rs].unsqueeze(1).to_broadcast([P, 4, rows]), ALU.mult,
                     [qkT, ccol], [qeTm[c]])
            for h in range(4):
                k.mm(psA[rs, h, rs], [(qkT[:, 4 + h, rs], qkT[:, h, rs])], [qkT], [psA])
            k.tt("dve", AmT[rs, :, rs], psA[rs, :, rs], cm[rs, mi, rs].unsqueeze(1).to_broadcast([rows, 4, rows]), ALU.mult,
                 [psA, cm], [AmT])
        first_o = [True]
        for c in range(nch):
            st = S32[0] if not sample else S32[1 + c]
            sbf = Sbf[0] if not sample else Sbf[1 + c]
            ind = ci[rs, i0 + c:i0 + c + 1]
            if full:
                for h in range(4):
                    fo = first_o[0]
                    first_o[0] = False
                    k.op("pe", lambda e, h=h, fo=fo, c=c, sbf=sbf: e.matmul(psO[rs, h, :], lhsT=qeTm[c][:, h, rs], rhs=sbf[:, h, :], start=fo, stop=False,
                                                                       skip_group_check=True), [qeTm[c], sbf], [psO])
            for h in range(4):
                k.mm(psS[:, h, 0:1], [(logf[rs, h * 128:(h + 1) * 128], ind)], [logf, ci], [psS])
            k.act(dec[:, 4 * c:4 * c + 4], psS[:, :, 0], AF.Exp, [psS], [dec])
            k.ts("pool", kdm[rs, :], kd[rs, :], ind, None, ALU.mult, None, [kd, ci], [kdm])
            for h in range(4):
                k.mm(psS[:, h, :], [(kdm[rs, h * 128:(h + 1) * 128], vb[rs, h * 128:(h + 1) * 128])], [kdm, vb], [psS])
            for h in range(4):
                k.op("dve", lambda e, h=h: e.scalar_tensor_tensor(out=st[:, h, :], in0=st[:, h, :], scalar=dec[:, 4 * c + h:4 * c + h + 1],
                                                                  in1=psS[:, h, :], op0=ALU.mult, op1=ALU.add),
                     [st, dec, psS], [st])
            k.op("act", lambda e: e.copy(out=sbf[:], in_=st[:]), [st], [sbf])
        if full:
            for h in range(4):
                k.op("pe", lambda e, h=h: e.matmul(psO[rs, h, :], lhsT=AmT[rs, h, rs], rhs=vb[rs, h * 128:(h + 1) * 128], start=False, stop=True,
                                                   skip_group_check=True), [AmT, vb], [psO])
            k.act(osq[rs, :], psO[rs, :, :].rearrange("p h v -> p (h v)"), AF.Square, [psO], [osq])
            k.op("dve", lambda e: e.tensor_reduce(out=st8[rs, 0:4], in_=osq[rs, :].rearrange("p (h v) -> p h v", h=4), axis=AX.X, op=ALU.add), [osq], [st8])
            k.ts("dve", st8[rs, 0:4], st8[rs, 0:4], 1.0 / 128, EPS, ALU.mult, ALU.add, [st8], [st8])
            k.act(st8[rs, 0:4], st8[rs, 0:4], AF.Sqrt, [st8], [st8])
            k.op("dve", lambda e: e.reciprocal(out=st8[rs, 4:8], in_=st8[rs, 0:4]), [st8], [st8])
            for h in range(4):
                k.op("dve", lambda e, h=h: e.scalar_tensor_tensor(out=oab[rs, 512 + h * 128:512 + (h + 1) * 128], in0=psO[rs, h, :], scalar=st8[rs, 4 + h:5 + h],
                                                                  in1=ggb[rs, h * 128:(h + 1) * 128], op0=ALU.mult, op1=ALU.mult),
                     [psO, st8, ggb], [oab])
            ck(7)
            ti = (NOWN + 1) if sample else (lt - (NPRE - 1))
            k.dma("pool", scr_oab[ti * 128:ti * 128 + rows, 512:1024], oab[rs, 512:1024], reads=[oab], writes=[scr_oab])
            if sample and npool == 0:
                k.dma("pool", scr_oab[ti * 128:ti * 128 + rows, 0:512], oab[rs, 0:512], reads=[oab], writes=[scr_oab])

    for lt in range(NT - ntp, NT):
        own = lt >= NPRE
        full = lt >= NPRE - 1
        kv_dst = None
        if own:
            r0 = (lt - NPRE) * 128
            kv_dst = [(o_kv[i, r0:r0 + 128, :], o_kv) for i in range(3)]
        do_tile(lt, 128, xloc[lt * 128:(lt + 1) * 128, :], cs_tab[lt * 128:(lt + 1) * 128, :], full, kv_dst, False)
    k.dma("sp", o_hg_p[:].rearrange("h k v -> k h v"), S32[0][:], reads=[S32[0]], writes=[o_hg_p])
    if with_sample:
        kv_dst = [(o_kvs[0, :, :], o_kvs), (o_kvs[1, :, :], o_kvs), None]
        do_tile(NT, 32, xs[:, :], cs_tab[NT * 128:NT * 128 + 32, :], True, kv_dst, True)
        for b in range(4):
            k.dma("sp", o_win_s[b, 504:512, :], kvo[8 * b:8 * b + 8, 2, :], reads=[kvo], writes=[o_win_s])
            k.dma("sp", o_hg_s[b].rearrange("h k v -> k h v"), S32[1 + b][:], reads=[S32[1 + b]], writes=[o_hg_s])
    k.pop()
    ck(10)
    pass1b(k, ck, ntp, locals())
    ck(20)
    env = dict(locals())
    env["identf_d"] = Buf("identf", k.nc_identf)
    env.update(k.shared)
    if with_sample and npool > 0:
        pass1c(k, ck, env, npool)
    ck(25)
    env["scr_x1"] = k.dram("scr_x1", [(NOWN + 1) * 128 + 32, D], F32)
    pass2(k, ck, env)
    ck(30)
    pass3(k, ck, env)


def pass1b(k, ck, ntp, env):
    P = 128
    scr_qk, scr_gate, scr_oab = env["scr_qk"], env["scr_gate"], env["scr_oab"]
    w1bd_d = k.dram("w1bd", [128, 64 * 128], F32, "ExternalInput")
    w2bd_d = k.dram("w2bd", [128, 2 * 128], F32, "ExternalInput")
    posvec_d = k.dram("posvec", [128, 64], F32, "ExternalInput")
    cover_d = k.dram("cover", [128, 2 * 62], F32, "ExternalInput")
    cmpB_d = k.dram("cmpB", [NOWN + 1, 128, 2 * 128], F32, "ExternalInput")
    selc_d = k.dram("selc", [NOWN + 1, 128, 2 * 64], F32, "ExternalInput")
    efull_d = k.dram("efull", [64, 4096], F32, "ExternalInput")
    triB_d = k.dram("triB", [128, 2 * 128], F32, "ExternalInput")
    pfx_d = k.dram("pfx", [1, 128], F32, "ExternalInput")
    identf_d = k.dram("identf", [128, 128], F32, "ExternalInput")
    k.nc_identf = identf_d.t
    k.shared = dict(w1bd_d=w1bd_d, w2bd_d=w2bd_d, posvec_d=posvec_d)

    k.push()
    W1 = k.sb([P, 64, 128], BF16, "W1")
    for a in range(4):
        k.dma("pool", W1[:, a * 16:(a + 1) * 16, :], w1bd_d[:, a * 2048:(a + 1) * 2048].rearrange("p (a b) -> p a b", b=128), writes=[W1])
    W2 = k.sb([P, 2, 128], BF16, "W2")
    k.dma("pool", W2[:], w2bd_d[:].rearrange("p (a b) -> p a b", b=128), writes=[W2])
    posv = k.sb([P, 64], BF16, "posv")
    k.dma("pool", posv[:], posvec_d[:], writes=[posv])
    efull = k.sb([P, 4096], BF16, "efull")
    k.dma("pool", efull[0:64, :], efull_d[:], writes=[efull])
    k.dma("pool", efull[64:128, :], efull_d[:], writes=[efull])
    triB = k.sb([P, 2, 4, 128], BF16, "triB")
    for hh in range(4):
        k.dma("pool", triB[:, :, hh, :], triB_d[:].rearrange("p (a b) -> p a b", b=128), writes=[triB])
    pfx = k.sb([1, 128], BF16, "pfx")
    k.dma("pool", pfx[:], pfx_d[:], writes=[pfx])
    ones_row = k.sb([1, 512], BF16, "ones_row")
    k.op("dve", lambda e: e.memset(ones_row[:], 1.0), [], [ones_row])
    identf = k.sb([P, 128], F32, "identf")
    k.dma("sp", identf[:], identf_d[:], writes=[identf])
    ident_bf = k.sb([P, 128], BF16, "ident_bf2")
    k.op("dve", lambda e: e.tensor_copy(out=ident_bf[:], in_=identf[:]), [identf], [ident_bf])
    KS = k.sb([P, 2, NT * 128], BF16, "KS")
    KC2 = k.sb([P, 2, 256], BF16, "KC2")
    VX = k.sb([P, 2, NT, 2, 65], BF16, "VX")
    kvccT = k.sb([P, 2, 256], BF16, "kvccT")
    VcX = k.sb([P, 2, 2, 127], BF16, "VcX")
    k.op("pool", lambda e: e.memset(KC2[:], 0.0), [], [KC2])
    k.op("pool", lambda e: e.memset(kvccT[:], 0.0), [], [kvccT])
    k.op("dve", lambda e: e.memset(VX[:, :, :, :, 64:65], 1.0), [], [VX])
    k.op("dve", lambda e: e.memset(VcX[:, :, :, 0:64], 0.0), [], [VcX])
    k.op("dve", lambda e: e.memset(VcX[:, :, :, 64:65], 1.0), [], [VcX])
    for g in range(2):
        k.dma("pool", VcX[:, :, g, 65:127], cover_d[:].rearrange("p (a b) -> p a b", b=62), writes=[VcX])
    qkv = [k.sb([P, 1280], BF16, "qkv%d" % i) for i in range(2)]
    qT = k.sb([P, 4, 128], BF16, "qT")
    spre = k.sb([P, 2, 8], BF16, "spre")
    posW = k.sb([P, 2], F32, "posW")
    cmpB = k.sb([P, 2, 4, 128], BF16, "cmpB_t")
    selc = k.sb([P, 2, 64], F32, "selc_t")
    gate = k.sb([P, 24], F32, "gate_t")
    PT = [k.sb([P, 1024], BF16, "PT%d" % i) for i in range(2)]
    ocs = k.sb([P, 2, 4, 127], F32, "ocs")
    osw = k.sb([P, 2, 4, 65], F32, "osw")
    rc = k.sb([P, 8], F32, "rc")
    coef = k.sb([P, 8], F32, "coef")
    imp = k.sb([P, 2, 64], F32, "imp")
    sc2 = k.sb([P, 64], F32, "sc2")
    top = k.sb([P, 16], F32, "top")
    nsel = k.sb([P, 2, 128], F32, "nsel")
    selT = k.sb([P, 2, 4, 128], BF16, "selT")
    oacc = k.sb([P, 8, 64], F32, "oacc")
    oa_bf = k.sb([P, 512], BF16, "oa_bf")
    psT = k.ps([P, 8, 128], BF16, "psTb")
    psZ = [k.ps([P, 1024], F32, "psZb%d" % i) for i in range(2)]
    psV = [k.ps([P, 512], F32, "psVb%d" % i) for i in range(2)]
    psC = k.ps([P, 512], F32, "psCb")

    for j in range(2):
        for idx in range(32):
            k.op("pe", lambda e, j=j, idx=idx: e.matmul(psC[:, j:j + 1], lhsT=W1[:, j * 32 + idx, :], rhs=posv[:, j * 32 + idx:j * 32 + idx + 1],
                                                        start=(idx == 0), stop=(idx == 31)), [W1, posv], [psC])
    k.op("act", lambda e: e.copy(out=posW[:], in_=psC[:, 0:2]), [psC], [posW])
    ck(11)
    zi = [0]
    pi = [0]

    pend = [None]

    cur = []

    def emit():
        blocks = list(cur)
        del cur[:]
        if not blocks:
            return
        zi[0] ^= 1
        S = psZ[zi[0]]
        for j, (g, k_lhsT, biases, v_rhs, acc_ap, ncol, first, ab) in enumerate(blocks):
            pairs = [(k_lhsT, qT[g * 64:(g + 1) * 64, :, :].rearrange("p a b -> p (a b)"))] + biases
            n = len(pairs)
            for i, (l, r) in enumerate(pairs):
                k.op("pe", lambda e, l=l, r=r, i=i, j=j, n=n: e.matmul(S[:, j * 512:(j + 1) * 512], lhsT=l, rhs=r, start=(i == 0), stop=(i == n - 1)),
                     [KS, kvccT, qT, efull, selT, ident_bf, triB, cmpB, pfx, ones_row], [S], tag=("S", l.partition_size(), l.base_partition()))
        if pend[0] is not None:
            f = pend[0]
            pend[0] = None
            f()
        pi[0] ^= 1
        pt = PT[pi[0]]
        nb_ = len(blocks)
        k.act(pt[:, 0:nb_ * 512], S[:, 0:nb_ * 512], AF.Exp, [S], [pt])

        def pv():
            for j, (g, k_lhsT, biases, v_rhs, acc_ap, ncol, first, ab) in enumerate(blocks):
                for hh in range(4):
                    k.op("pe", lambda e, hh=hh, j=j, acc_ap=acc_ap, ncol=ncol, v_rhs=v_rhs, first=first: e.matmul(
                        acc_ap[:, hh, 0:ncol], lhsT=pt[:, j * 512 + hh * 128:j * 512 + (hh + 1) * 128], rhs=v_rhs,
                        start=(first and hh == 0), stop=False, skip_group_check=True), [pt, VX, VcX], [ab], tag=("PV", ncol))
        pend[0] = pv

    def flush():
        emit()
        if pend[0] is not None:
            f = pend[0]
            pend[0] = None
            f()

    def attn_block(g, k_lhsT, biases, v_rhs, acc_ap, ncol, first):
        cur.append((g, k_lhsT, biases, v_rhs, acc_ap, ncol, first, acc_buf[0]))
        if len(cur) == 2:
            emit()

    acc_buf = [None]

    for lt in range(NT - ntp, NT):
        full = lt >= NPRE - 1
        i_own = lt - (NPRE - 1)
        qk = qkv[lt % 2]
        k.dma("sp", qk[:], scr_qk[lt * 128:(lt + 1) * 128, :], reads=[scr_qk], writes=[qk])
        v14 = qk[:, 0:896].rearrange("p (h d) -> p h d", d=64)
        if full:
            for hh in range(4):
                k.op("pe", lambda e, hh=hh: e.transpose(psT[:, hh, :], qk[:, hh * 128:(hh + 1) * 128], ident_bf[:]), [qk, ident_bf], [psT], tag=("T",))
        k.op("pe", lambda e: e.transpose(psT[:, 4, :], qk[:, 512:640], ident_bf[:]), [qk, ident_bf], [psT], tag=("T",))
        k.op("pe", lambda e: e.transpose(psT[:, 5, :], qk[:, 640:768], ident_bf[:]), [qk, ident_bf], [psT], tag=("T",))
        k.op("pe", lambda e: e.transpose(psT[:, 6, :], qk[:, 768:896], ident_bf[:]), [qk, ident_bf], [psT], tag=("T",))
        k.op("pe", lambda e: e.transpose(psT[:, 7, :], qk[:, 896:1024], ident_bf[:]), [qk, ident_bf], [psT], tag=("T",))
        if full:
            k.op("act", lambda e: e.copy(out=qT[:], in_=psT[:, 0:4, :]), [psT], [qT])
        k.op("act", lambda e: e.copy(out=KS[:, :, lt * 128:(lt + 1) * 128], in_=psT[:, 5:7, :]), [psT], [KS])
        k.op("dve", lambda e: e.tensor_copy(out=KC2[:, :, 0:128], in_=KC2[:, :, 128:256]), [KC2], [KC2])
        k.op("dve", lambda e: e.tensor_copy(out=KC2[:, :, 128:256], in_=psT[:, 4:8:3, :]), [psT], [KC2])
        k.op("pool", lambda e: e.tensor_copy(out=VX[:, :, lt, :, 0:64], in_=qk[:, 1024:1280].rearrange("p (a g d) -> p a g d", a=2, g=2)), [qk], [VX])
        c0 = max(8 * lt - 1, 0)
        nb = 8 * lt + 7 - c0
        for j in range(2):
            n = 0
            for r in range(2):
                for i16 in range(16):
                    st_col = 16 * (c0 + r) + i16 - 128 * lt + 128
                    k.op("pe", lambda e, j=j, r=r, i16=i16, st_col=st_col, n=n: e.matmul(
                        psC[:, 8 * j:8 * j + nb], lhsT=W1[:, j * 32 + r * 16 + i16, :], rhs=KC2[:, j, st_col:st_col + 16 * (nb - 1) + 1:16],
                        start=(n == 0), stop=(n == 31)), [W1, KC2], [psC], tag=("CMP", j))
                    n += 1
            k.act(spre[:, j, 0:nb], psC[:, 8 * j:8 * j + nb], AF.Silu, [psC, posW], [spre], bias=posW[:, j:j + 1])
        for j in range(2):
            k.mm(psC[:, 16 + 8 * j:16 + 8 * j + nb], [(W2[:, j, :], spre[:, j, 0:nb])], [W2, spre], [psC])
        k.op("act", lambda e: e.copy(out=kvccT[:, :, c0:c0 + nb], in_=psC[:, 16:32].rearrange("p (j c) -> p j c", j=2)[:, :, 0:nb]), [psC], [kvccT])
        if not full:
            continue
        ck(12)
        k.dma("sp", selc[:], selc_d[i_own].rearrange("p (a b) -> p a b", b=64), writes=[selc])
        k.dma("sp", gate[:], scr_gate[i_own * 128:(i_own + 1) * 128, :], reads=[scr_gate], writes=[gate])
        for hh in range(4):
            k.dma("pool", cmpB[:, :, hh, :], cmpB_d[i_own].rearrange("p (a b) -> p a b", b=128), writes=[cmpB])
        nchv = 1 if lt <= 15 else 2
        for ch in range(nchv):
            k.op("pe", lambda e, ch=ch: e.transpose(psT[:, ch, :], kvccT[:, 1, ch * 128:(ch + 1) * 128], ident_bf[:]), [kvccT, ident_bf], [psT], tag=("T",))
        k.op("act", lambda e: e.copy(out=VcX[:, 0:nchv, :, 0:64], in_=psT[:, 0:nchv, :].rearrange("p c (g d) -> p c g d", g=2)), [psT], [VcX])
        for g in range(2):
            acc_buf[0] = psV[g]
            acc = psV[g][:, 0:508].rearrange("p (h c) -> p h c", c=127)
            for ch in range(nchv):
                attn_block(g, kvccT[g * 64:(g + 1) * 64, 0, ch * 128:(ch + 1) * 128],
                           [(ident_bf[:], cmpB[:, ch, :, :].rearrange("p a b -> p (a b)"))],
                           VcX[:, ch, g, :], acc, 127, ch == 0)
            flush()
            k.op("act", lambda e, g=g, acc=acc: e.copy(out=ocs[:, g, :, :], in_=acc), [psV[g]], [ocs])
        ck(13)
        k.ts("dve", rc[:], ocs[:, :, :, 64].rearrange("p g h -> p (g h)"), 1e-30, None, ALU.max, None, [ocs], [rc])
        k.op("dve", lambda e: e.reciprocal(out=rc[:], in_=rc[:]), [rc], [rc])
        gv = gate[:].rearrange("p (h b) -> p h b", b=3)
        k.tt("dve", coef[:], rc[:], gv[:, :, 0], ALU.mult, [rc, gate], [coef])
        for h in range(8):
            k.ts("dve", oacc[:, h, :], ocs[:, h // 4, h % 4, 0:64], coef[:, h:h + 1], None, ALU.mult, None, [ocs, coef], [oacc])
        for g in range(2):
            k.ts("dve", imp[:, g, 0:62], ocs[:, g, 0, 65:127], rc[:, 4 * g:4 * g + 1], None, ALU.mult, None, [ocs, rc], [imp])
            for hh in range(1, 4):
                k.op("dve", lambda e, g=g, hh=hh: e.scalar_tensor_tensor(out=imp[:, g, 0:62], in0=ocs[:, g, hh, 65:127], scalar=rc[:, 4 * g + hh:4 * g + hh + 1],
                                                                         in1=imp[:, g, 0:62], op0=ALU.mult, op1=ALU.add), [ocs, rc, imp], [imp])
        k.op("dve", lambda e: e.memset(imp[:, :, 62:64], 0.0), [], [imp])
        for g in range(2):
            k.tt("dve", imp[:, g, :], imp[:, g, :], selc[:, 0, :], ALU.mult, [imp, selc], [imp])
            k.tt("dve", imp[:, g, :], imp[:, g, :], selc[:, 1, :], ALU.add, [imp, selc], [imp])
            k.op("dve", lambda e, g=g: e.max(out=top[:, 0:8], in_=imp[:, g, :]), [imp], [top])
            k.op("dve", lambda e, g=g: e.match_replace(out=sc2[:], in_to_replace=top[:, 0:8], in_values=imp[:, g, :], imm_value=-1e30), [imp, top], [sc2])
            k.op("dve", lambda e: e.max(out=top[:, 8:16], in_=sc2[:]), [sc2], [top])
            k.ts("dve", top[:, 15:16], top[:, 15:16], -0.5, None, ALU.max, None, [top], [top])
            k.ts("dve", nsel[:, g, 0:64], imp[:, g, :], top[:, 15:16], None, ALU.is_ge, None, [imp, top], [nsel])
            k.ts("dve", nsel[:, g, 0:64], nsel[:, g, 0:64], 30000.0, -30000.0, ALU.mult, ALU.add, [nsel], [nsel])
            k.op("dve", lambda e, g=g: e.tensor_copy(out=nsel[:, g, 64:128], in_=nsel[:, g, 0:64]), [nsel], [nsel])
            k.mm(psC[:, 64 + 128 * g:64 + 128 * (g + 1)], [(nsel[:, g, :], identf[:])], [nsel, identf], [psC])
        for hh in range(4):
            k.op("act", lambda e, hh=hh: e.copy(out=selT[:, :, hh, :], in_=psC[:, 64:320].rearrange("p (g t) -> p g t", g=2)), [psC], [selT])
        ck(14)
        for g in range(2):
            acc_buf[0] = psV[g]
            acc = psV[g][:, 0:260].rearrange("p (h c) -> p h c", c=65)
            for kb in range(0, lt + 1):
                biases = [(efull[g * 64:(g + 1) * 64, kb * 128:(kb + 1) * 128], selT[g * 64:(g + 1) * 64, g, :, :].rearrange("p a b -> p (a b)"))]
                if kb == lt:
                    biases.append((ident_bf[:], triB[:, 0, :, :].rearrange("p a b -> p (a b)")))
                attn_block(g, KS[g * 64:(g + 1) * 64, 0, kb * 128:(kb + 1) * 128], biases, VX[:, 0, kb, g, :], acc, 65, kb == 0)
            flush()
            k.op("act", lambda e, g=g, acc=acc: e.copy(out=osw[:, g, :, :], in_=acc), [psV[g]], [osw])
        for br in (1, 2):
            if br == 2:
                for g in range(2):
                    acc_buf[0] = psV[g]
                    acc = psV[g][:, 0:260].rearrange("p (h c) -> p h c", c=65)
                    kb0 = max(lt - 4, 0)
                    for kb in range(kb0, lt + 1):
                        biases = []
                        if kb == lt:
                            biases.append((ident_bf[:], triB[:, 0, :, :].rearrange("p a b -> p (a b)")))
                        if kb == lt - 4:
                            biases.append((ident_bf[:], triB[:, 1, :, :].rearrange("p a b -> p (a b)")))
                        if kb < NPRE:
                            biases.append((pfx[:], ones_row[:]))
                        attn_block(g, KS[g * 64:(g + 1) * 64, 1, kb * 128:(kb + 1) * 128], biases, VX[:, 1, kb, g, :], acc, 65, kb == kb0)
                    flush()
                    k.op("act", lambda e, g=g, acc=acc: e.copy(out=osw[:, g, :, :], in_=acc), [psV[g]], [osw])
            k.ts("dve", rc[:], osw[:, :, :, 64].rearrange("p g h -> p (g h)"), 1e-30, None, ALU.max, None, [osw], [rc])
            k.op("dve", lambda e: e.reciprocal(out=rc[:], in_=rc[:]), [rc], [rc])
            k.tt("dve", coef[:], rc[:], gv[:, :, br], ALU.mult, [rc, gate], [coef])
            for h in range(8):
                k.op("dve", lambda e, h=h: e.scalar_tensor_tensor(out=oacc[:, h, :], in0=osw[:, h // 4, h % 4, 0:64], scalar=coef[:, h:h + 1],
                                                                  in1=oacc[:, h, :], op0=ALU.mult, op1=ALU.add), [osw, coef, oacc], [oacc])
        k.op("act", lambda e: e.copy(out=oa_bf[:], in_=oacc[:].rearrange("p h d -> p (h d)")), [oacc], [oa_bf])
        k.dma("pool", scr_oab[i_own * 128:(i_own + 1) * 128, 0:512], oa_bf[:], reads=[oa_bf], writes=[scr_oab])
    k.pop()


def pass2(k, ck, env):
    P = 128
    xloc, xs, w_in, scr_hT, scr_oab = env["xloc"], env["xs"], env["w_in"], env["scr_hT"], env["scr_oab"]
    w_br_d = k.dram("w_branch", [D, D], F32, "ExternalInput")
    w_out_d = k.dram("w_out", [D, D], F32, "ExternalInput")
    identf_d = env["identf_d"]
    scr_x1 = env["scr_x1"]
    k.push()
    wmg = k.sb([P, 8, 2048], BF16, "wmg")
    wbr = k.sb([P, 8, 1024], BF16, "wbr")
    wo = k.sb([P, 8, 1024], BF16, "wo")
    for kc in range(8):
        k.dma("pool", wmg[:, kc, :], w_in[kc * 128:(kc + 1) * 128, NCOL1:5400], writes=[wmg])
        k.dma("pool", wbr[:, kc, :], w_br_d[kc * 128:(kc + 1) * 128, :], writes=[wbr])
        k.dma("pool", wo[:, kc, :], w_out_d[kc * 128:(kc + 1) * 128, :], writes=[wo])
    ident_bf = k.sb([P, 128], BF16, "ident_bf3")
    k.dma("pool", ident_bf[:], identf_d[:], writes=[ident_bf])
    xt = [k.sb([P, D], F32, "x2_%d" % i) for i in range(2)]
    hT = [k.sb([P, 8, 128], BF16, "hT2_%d" % i) for i in range(2)]
    oab = [k.sb([P, 1024], BF16, "oab2_%d" % i) for i in range(2)]
    mab = k.sb([P, 2048], BF16, "mab")
    oT = k.sb([P, 8, 128], BF16, "oT")
    m1 = k.sb([P, 1024], F32, "m1")
    m2 = k.sb([P, 1024], F32, "m2")
    mbf = k.sb([P, 1024], BF16, "mbf")
    mT = k.sb([P, 8, 128], BF16, "mT")
    x1 = k.sb([P, 1024], F32, "x1")
    psT = k.ps([P, 8, 128], BF16, "psT2")
    psZ = [k.ps([P, 512], F32, "psZ2_%d" % i) for i in range(3)]
    zi = [0]

    def nz():
        zi[0] = (zi[0] + 1) % 3
        return psZ[zi[0]]

    for ti in range(NOWN + 2):
        sample = ti == NOWN + 1
        rows = 32 if sample else 128
        rs = slice(0, rows)
        x = xt[ti % 2]
        h = hT[ti % 2]
        ob = oab[ti % 2]
        xsrc = xs[:, :] if sample else xloc[(NPRE - 1 + ti) * 128:(NPRE + ti) * 128, :]
        k.dma("sp", x[rs, :], xsrc, writes=[x])
        k.dma("sp", h[:, :, rs], scr_hT[ti, :, :].rearrange("p (a b) -> p a b", b=128)[:, :, rs], reads=[scr_hT], writes=[h])
        k.dma("sp", ob[rs, :], scr_oab[ti * 128:ti * 128 + rows, :], reads=[scr_oab], writes=[ob])
        for c in range(4):
            z = nz()
            k.mm(z[rs, :], [(h[:, kc, rs], wmg[:, kc, c * 512:(c + 1) * 512]) for kc in range(8)], [h, wmg], [z])
            k.act(mab[rs, c * 512:(c + 1) * 512], z[rs, :], AF.Sigmoid, [z], [mab])
        for kc in range(8):
            k.op("pe", lambda e, kc=kc: e.transpose(psT[:, kc, rs], ob[rs, kc * 128:(kc + 1) * 128], ident_bf[rs, rs]), [ob, ident_bf], [psT], tag=("T",))
        k.op("act", lambda e: e.copy(out=oT[:, :, rs], in_=psT[:, :, rs]), [psT], [oT])
        for c in range(2):
            z = nz()
            k.mm(z[rs, :], [(oT[:, kc, rs], wbr[:, kc, c * 512:(c + 1) * 512]) for kc in range(4)], [oT, wbr], [z])
            k.tt("dve", m1[rs, c * 512:(c + 1) * 512], z[rs, :], mab[rs, c * 512:(c + 1) * 512], ALU.mult, [z, mab], [m1])
            z = nz()
            k.mm(z[rs, :], [(oT[:, kc, rs], wbr[:, kc, c * 512:(c + 1) * 512]) for kc in range(4, 8)], [oT, wbr], [z])
            k.tt("dve", m2[rs, c * 512:(c + 1) * 512], z[rs, :], mab[rs, 1024 + c * 512:1024 + (c + 1) * 512], ALU.mult, [z, mab], [m2])
        k.tt("pool", mbf[rs, :], m1[rs, :], m2[rs, :], ALU.add, [m1, m2], [mbf])
        for kc in range(8):
            k.op("pe", lambda e, kc=kc: e.transpose(psT[:, kc, rs], mbf[rs, kc * 128:(kc + 1) * 128], ident_bf[rs, rs]), [mbf, ident_bf], [psT], tag=("T",))
        k.op("act", lambda e: e.copy(out=mT[:, :, rs], in_=psT[:, :, rs]), [psT], [mT])
        for c in range(2):
            z = nz()
            k.mm(z[rs, :], [(mT[:, kc, rs], wo[:, kc, c * 512:(c + 1) * 512]) for kc in range(8)], [mT, wo], [z])
            k.tt("dve", x1[rs, c * 512:(c + 1) * 512], z[rs, :], x[rs, c * 512:(c + 1) * 512], ALU.add, [z, x], [x1])
        k.dma("pool", scr_x1[ti * 128:ti * 128 + rows, :], x1[rs, :], reads=[x1], writes=[scr_x1])
    k.pop()


def pass3(k, ck, env):
    P = 128
    scr_x1, identf_d = env["scr_x1"], env["identf_d"]
    f1_d = k.dram("ffn_w_in", [D, 5632], F32, "ExternalInput")
    f2_d = k.dram("ffn_w_out", [2816, D], F32, "ExternalInput")
    fvec_d = k.dram("fvec", [1, 1024], F32, "ExternalInput")
    cw_d = k.dram("convw", [128, 22 * 4], F32, "ExternalInput")
    cst_d = k.dram("convst", [128, 22 * 8], F32, "ExternalInput")
    o_y = k.dram("o_y", [NOWN * 128, D], F32, "ExternalOutput")
    o_ys = k.dram("o_ys", [32, D], F32, "ExternalOutput")
    o_cp = k.dram("o_cp", [128, 22 * 2], F32, "ExternalOutput")
    o_cs = k.dram("o_cs", [128, 22 * 8], F32, "ExternalOutput")
    k.push()
    wf1 = k.sb([P, 8, 5632], BF16, "wf1")
    wf2 = k.sb([P, 22, 1024], BF16, "wf2")
    for kc in range(8):
        k.dma("pool", wf1[:, kc, :], f1_d[kc * 128:(kc + 1) * 128, :], writes=[wf1])
    for fc in range(22):
        k.dma("pool", wf2[:, fc, :], f2_d[fc * 128:(fc + 1) * 128, :], writes=[wf2])
    ident_bf = k.sb([P, 128], BF16, "ident_bf4")
    k.dma("pool", ident_bf[:], identf_d[:], writes=[ident_bf])
    g2 = k.sb([P, D], F32, "g2")
    k.dma("sp", g2[:], fvec_d[0:1, :].partition_broadcast(P), writes=[g2])
    cw = k.sb([P, 22, 4], F32, "cw")
    k.dma("sp", cw[:], cw_d[:].rearrange("p (a b) -> p a b", b=4), writes=[cw])
    x1 = [k.sb([P, D], F32, "x3_%d" % i) for i in range(2)]
    junk = k.sb([P, D], BF16, "junk3")
    h2 = k.sb([P, D], BF16, "h2")
    h2T = k.sb([P, 8, 128], BF16, "h2T")
    st4 = k.sb([P, 8], F32, "st4_3")
    aT = k.sb([P, 22, 130], F32, "aT")
    t1 = k.sb([P, 11, 128], F32, "t1")
    t2 = k.sb([P, 11, 128], F32, "t2")
    gT = k.sb([P, 22, 128], BF16, "gT")
    y = k.sb([P, D], F32, "y")
    psT = k.ps([P, 8, 128], BF16, "psT3")
    psZ = [k.ps([P, 512], F32, "psZ3_%d" % i) for i in range(3)]
    zi = [0]

    def nz():
        zi[0] = (zi[0] + 1) % 3
        return psZ[zi[0]]

    k.op("dve", lambda e: e.memset(aT[:], 0.0), [], [aT])
    for ti in range(NOWN + 2):
        sample = ti == NOWN + 1
        rows = 32 if sample else 128
        rs = slice(0, rows)
        nbat, T = (4, 8) if sample else (1, 128)
        x = x1[ti % 2]
        k.dma("sp", x[rs, :], scr_x1[ti * 128:ti * 128 + rows, :], reads=[scr_x1], writes=[x])
        k.act(junk[rs, :], x[rs, :], AF.Square, [x], [junk, st4], accum_out=st4[rs, 0:1])
        k.ts("dve", st4[rs, 1:2], st4[rs, 0:1], 1.0 / D, EPS, ALU.mult, ALU.add, [st4], [st4])
        k.act(st4[rs, 2:3], st4[rs, 1:2], AF.Sqrt, [st4], [st4])
        k.op("dve", lambda e: e.reciprocal(out=st4[rs, 3:4], in_=st4[rs, 2:3]), [st4], [st4])
        k.op("dve", lambda e: e.scalar_tensor_tensor(out=h2[rs, :], in0=x[rs, :], scalar=st4[rs, 3:4], in1=g2[rs, :],
                                                     op0=ALU.mult, op1=ALU.mult), [x, st4, g2], [h2])
        for kc in range(8):
            k.op("pe", lambda e, kc=kc: e.transpose(psT[:, kc, rs], h2[rs, kc * 128:(kc + 1) * 128], ident_bf[rs, rs]), [h2, ident_bf], [psT], tag=("T",))
        k.op("act", lambda e: e.copy(out=h2T[:, :, rs], in_=psT[:, :, rs]), [psT], [h2T])
        av = aT[:, :, 0:nbat * (T + 2)].rearrange("p f (b t) -> p f b t", b=nbat)
        if sample:
            for b in range(4):
                k.dma("sp", av[:, :, b, 0:2], cst_d[:].rearrange("p (f b j) -> p f b j", f=22, b=4)[:, :, b, :], writes=[aT])
        for f0 in range(0, 22, 4):
            n = min(4, 22 - f0)
            z = nz()
            for j in range(n):
                fc = f0 + j
                k.mm(z[:, j * 128:j * 128 + rows], [(wf1[:, kc, fc * 128:(fc + 1) * 128], h2T[:, kc, rs]) for kc in range(8)], [wf1, h2T], [z])
            k.op("act", lambda e, f0=f0, n=n, z=z: e.copy(out=av[:, f0:f0 + n, :, 2:2 + T],
                                                       in_=z[:, 0:n * 128].rearrange("p (f t) -> p f t", t=128)[:, :, 0:rows].rearrange("p f (b t) -> p f b t", b=nbat)),
                 [z], [aT])
        for hf in range(2):
            fs = slice(hf * 11, hf * 11 + 11)
            tv1 = t1[:, :, 0:rows].rearrange("p f (b t) -> p f b t", b=nbat)
            tv2 = t2[:, :, 0:rows].rearrange("p f (b t) -> p f b t", b=nbat)

            def wb(j):
                return cw[:, fs, j:j + 1].unsqueeze(3).to_broadcast([P, 11, nbat, T])
            k.tt("dve", tv1, av[:, fs, :, 2:2 + T], wb(2), ALU.mult, [aT, cw], [t1])
            k.tt("pool", tv2, av[:, fs, :, 1:1 + T], wb(1), ALU.mult, [aT, cw], [t2])
            k.tt("dve", tv1, tv1, tv2, ALU.add, [t1, t2], [t1])
            k.tt("pool", tv2, av[:, fs, :, 0:T], wb(0), ALU.mult, [aT, cw], [t2])
            k.tt("dve", tv1, tv1, tv2, ALU.add, [t1, t2], [t1])
            k.tt("dve", tv1, tv1, wb(3), ALU.add, [t1, cw], [t1])
            k.act(t1[:, :, 0:rows], t1[:, :, 0:rows], AF.Silu, [t1], [t1])
            for f0 in range(hf * 11, hf * 11 + 11, 4):
                n = min(4, hf * 11 + 11 - f0)
                z = nz()
                for j in range(n):
                    fc = f0 + j
                    k.mm(z[:, j * 128:j * 128 + rows], [(wf1[:, kc, 2816 + fc * 128:2816 + (fc + 1) * 128], h2T[:, kc, rs]) for kc in range(8)], [wf1, h2T], [z])
                k.tt("dve", gT[:, f0:f0 + n, 0:rows], t1[:, f0 - hf * 11:f0 - hf * 11 + n, 0:rows],
                     z[:, 0:n * 128].rearrange("p (f t) -> p f t", t=128)[:, :, 0:rows], ALU.mult, [t1, z], [gT])
        for c in range(2):
            z = nz()
            k.mm(z[rs, :], [(gT[:, fc, rs], wf2[:, fc, c * 512:(c + 1) * 512]) for fc in range(22)], [gT, wf2], [z])
            k.tt("dve", y[rs, c * 512:(c + 1) * 512], z[rs, :], x[rs, c * 512:(c + 1) * 512], ALU.add, [z, x], [y])
        if sample:
            k.dma("pool", o_ys[:, :], y[rs, :], reads=[y], writes=[o_ys])
            for b in range(4):
                k.dma("pool", o_cs[:].rearrange("p (f b j) -> p f b j", f=22, b=4)[:, :, b, :], av[:, :, b, T:T + 2], reads=[aT], writes=[o_cs])
        else:
            if ti >= 1:
                k.dma("pool", o_y[(ti - 1) * 128:ti * 128, :], y[rs, :], reads=[y], writes=[o_y])
            if ti == NOWN:
                k.dma("pool", o_cp[:].rearrange("p (f j) -> p f j", j=2), aT[:, :, 128:130], reads=[aT], writes=[o_cp])
            k.op("pool", lambda e: e.tensor_copy(out=aT[:, :, 0:2], in_=aT[:, :, 128:130]), [aT], [aT])
    k.pop()


def pass1c(k, ck, env, npool):
    P = 128
    nc = k.nc
    scr_qk, scr_gate, scr_oab = env["scr_qk"], env["scr_gate"], env["scr_oab"]
    st_win = env["st_win"]
    identf_d = env["identf_d"]
    cpool = k.dram("cache_cmp", [npool, 128, 256], F32, "ExternalInput")
    spool = k.dram("cache_slc", [npool, 128, 256], F32, "ExternalInput")
    ptab_d = k.dram("ptab", [4, 128], I32, "ExternalInput")
    selcS_d = k.dram("selcS", [8, 2 * 384], F32, "ExternalInput")
    selm_d = k.dram("selm", [32, 8 + 16 * 32], F32, "ExternalInput")
    rep_d = k.dram("rep", [8, 32], F32, "ExternalInput")
    ef_d = k.dram("efull128", [128, 8192], F32, "ExternalInput")
    triS_d = k.dram("triBs", [128, 2 * 32], F32, "ExternalInput")
    w1bd_d, w2bd_d, posvec_d = env["w1bd_d"], env["w2bd_d"], env["posvec_d"]
    k.push()
    W1 = k.sb([P, 64, 128], BF16, "W1c")
    for a in range(4):
        k.dma("pool", W1[:, a * 16:(a + 1) * 16, :], w1bd_d[:, a * 2048:(a + 1) * 2048].rearrange("p (a b) -> p a b", b=128), writes=[W1])
    W2 = k.sb([P, 2, 128], BF16, "W2c")
    k.dma("pool", W2[:], w2bd_d[:].rearrange("p (a b) -> p a b", b=128), writes=[W2])
    posv = k.sb([P, 64], BF16, "posvc")
    k.dma("pool", posv[:], posvec_d[:], writes=[posv])
    ef = k.sb([P, 8192], BF16, "ef128")
    k.dma("pool", ef[:], ef_d[:], writes=[ef])
    triS = k.sb([P, 2, 32], BF16, "triS")
    k.dma("pool", triS[:], triS_d[:].rearrange("p (a b) -> p a b", b=32), writes=[triS])
    identf = k.sb([P, 128], F32, "identfc")
    k.dma("sp", identf[:], identf_d[:], writes=[identf])
    ident_bf = k.sb([P, 128], BF16, "identbc")
    k.op("dve", lambda e: e.tensor_copy(out=ident_bf[:], in_=identf[:]), [identf], [ident_bf])
    selcS = k.sb([8, 2, 384], F32, "selcS")
    k.dma("sp", selcS[:], selcS_d[:].rearrange("p (a b) -> p a b", b=384), writes=[selcS])
    selm = k.sb([32, 8 + 512], F32, "selm")
    k.dma("sp", selm[:], selm_d[:], writes=[selm])
    rep = k.sb([8, 32], F32, "rep")
    k.dma("sp", rep[:], rep_d[:], writes=[rep])
    iot_d = k.dram("iot", [128, 1], F32, "ExternalInput")
    ptb = k.sb([P, 512], I32, "ptb")
    for b in range(4):
        k.dma("sp", ptb[:, b * 128:(b + 1) * 128], ptab_d[b:b + 1, :].partition_broadcast(P), writes=[ptb])
    io = k.sb([P, 1], F32, "iotc")
    k.dma("sp", io[:], iot_d[:], writes=[io])
    idxf = k.sb([P, 512], F32, "idxf")
    pidx = k.sb([P, 512], I32, "pidx")
    k.op("dve", lambda e: e.tensor_copy(out=idxf[:], in_=ptb[:]), [ptb], [idxf])
    k.op("dve", lambda e: e.tensor_scalar(out=idxf[:], in0=idxf[:], scalar1=128.0, scalar2=io[:, 0:1], op0=ALU.mult, op1=ALU.add), [idxf, io], [idxf])
    k.op("dve", lambda e: e.tensor_copy(out=pidx[:], in_=idxf[:]), [idxf], [pidx])
    KCb = k.sb([P, 2, 2048 + 16], BF16, "KCb")
    KSb = k.sb([P, 16384 + 128], BF16, "KSb")
    VXb = k.sb([P, 129, 2, 65], BF16, "VXb")
    KWb = k.sb([P, 640], BF16, "KWb")
    VWb = k.sb([P, 5, 2, 65], BF16, "VWb")
    kvc = k.sb([P, 2, 1024], BF16, "kvc")
    vcb = k.sb([P, 8, 128], BF16, "vcb")
    k.op("pool", lambda e: e.memset(KSb[:, 16384:16512], 0.0), [], [KSb])
    k.op("pool", lambda e: e.memset(KWb[:, 512:640], 0.0), [], [KWb])
    k.op("pool", lambda e: e.memset(VXb[:, :, :, 64:65], 1.0), [], [VXb])
    k.op("pool", lambda e: e.memset(VXb[:, 128, :, 0:64], 0.0), [], [VXb])
    k.op("pool", lambda e: e.memset(VWb[:, :, :, 64:65], 1.0), [], [VWb])
    k.op("pool", lambda e: e.memset(VWb[:, 4, :, 0:64], 0.0), [], [VWb])
    k.op("pool", lambda e: e.memset(kvc[:], 0.0), [], [kvc])
    pg = [k.sb([P, 2, 256], F32, "pg%d" % i) for i in range(8)]
    qs = k.sb([8, 1280], BF16, "qs")
    qTs = k.sb([P, 4, 8], BF16, "qTs")
    spre = k.sb([P, 2, 128], BF16, "sprec")
    posW = k.sb([P, 2], F32, "posWc")
    pS = k.sb([32, 1024], F32, "pS")
    pbf = k.sb([32, 1024], BF16, "pbf")
    pTs = k.sb([P, 8, 32], BF16, "pTs")
    st = k.sb([32, 8], F32, "stc")
    impr = k.sb([32, 256], F32, "impr")
    sc = k.sb([8, 2, 384], F32, "scc")
    sc2 = k.sb([8, 384], F32, "sc2c")
    top = k.sb([8, 16], F32, "topc")
    selTs = k.sb([P, 3, 2, 32], BF16, "selTs")
    PTs = [k.sb([P, 256], BF16, "PTs%d" % i) for i in range(2)]
    obr = k.sb([32, 3, 2, 65], F32, "obr")
    gate_r = k.sb([32, 2, 3], F32, "gate_r")
    rcs = k.sb([32, 8], F32, "rcs")
    ofin = k.sb([32, 2, 64], F32, "ofin")
    oas = k.sb([32, 512], BF16, "oas")
    psT = k.ps([P, 8, 128], BF16, "psTc")
    psF = k.ps([P, 4, 128], F32, "psFc")
    psS = k.ps([P, 1024], F32, "psSc")
    psC = k.ps([P, 512], F32, "psCc")
    psA = k.ps([P, 512], F32, "psAc")
    psO = k.ps([P, 512], F32, "psOc")
    psX = k.ps([P, 512], F32, "psXc")
    pendc = [None]
    for j in range(2):
        for idx in range(32):
            k.op("pe", lambda e, j=j, idx=idx: e.matmul(psC[:, j:j + 1], lhsT=W1[:, j * 32 + idx, :], rhs=posv[:, j * 32 + idx:j * 32 + idx + 1],
                                                        start=(idx == 0), stop=(idx == 31)), [W1, posv], [psC])
    k.op("act", lambda e: e.copy(out=posW[:], in_=psC[:, 0:2]), [psC], [posW])
    k.op("dve", lambda e: e.memset(pS[:], 0.0), [], [pS])

    def compress_group(gi):
        c0 = max(128 * gi - 1, 0)
        nb = 128 * gi + 127 - c0
        for j in range(2):
            n = 0
            for r in range(2):
                for i16 in range(16):
                    s0 = 16 * (c0 + r) + i16 - 2048 * gi + 16
                    k.op("pe", lambda e, j=j, r=r, i16=i16, s0=s0, n=n, nb=nb: e.matmul(
                        psC[:, 128 * j:128 * j + nb], lhsT=W1[:, j * 32 + r * 16 + i16, :], rhs=KCb[:, j, s0:s0 + 16 * (nb - 1) + 1:16],
                        start=(n == 0), stop=(n == 31)), [W1, KCb], [psC], tag=("CMPc", j))
                    n += 1
            k.act(spre[:, j, 0:nb], psC[:, 128 * j:128 * j + nb], AF.Silu, [psC, posW], [spre], bias=posW[:, j:j + 1])
        for j in range(2):
            k.mm(psC[:, 256 + 128 * j:256 + 128 * j + nb], [(W2[:, j, :], spre[:, j, 0:nb])], [W2, spre], [psC])
        k.op("act", lambda e, c0=c0, nb=nb: e.copy(out=kvc[:, :, c0:c0 + nb], in_=psC[:, 256:512].rearrange("p (j c) -> p j c", j=2)[:, :, 0:nb]), [psC], [kvc])
        k.op("dve", lambda e: e.tensor_copy(out=KCb[:, :, 0:16], in_=KCb[:, :, 2048:2064]), [KCb], [KCb])

    first_o = [True]
    rn = [0]
    for b in range(4):
        r0 = NT * 128 + 8 * b
        k.dma("sp", qs[:], scr_qk[r0:r0 + 8, :], reads=[scr_qk], writes=[qs])
        for hh in range(4):
            k.dma("sp", gate_r[hh * 8:hh * 8 + 8, :, :],
                  scr_gate[NOWN * 128 + 128 + 8 * b:NOWN * 128 + 128 + 8 * b + 8, :].rearrange("p (g x) -> p g x", g=2)[:, :, hh * 3:hh * 3 + 3],
                  reads=[scr_gate], writes=[gate_r])
        for hh in range(4):
            k.op("pe", lambda e, hh=hh: e.transpose(psT[:, hh, 0:8], qs[:, hh * 128:(hh + 1) * 128], ident_bf[0:8, 0:8]), [qs, ident_bf], [psT], tag=("T",))
        k.op("pe", lambda e: e.transpose(psT[:, 4, 0:8], qs[:, 640:768], ident_bf[0:8, 0:8]), [qs, ident_bf], [psT], tag=("T",))
        k.op("pe", lambda e: e.transpose(psT[:, 5, 0:8], qs[:, 768:896], ident_bf[0:8, 0:8]), [qs, ident_bf], [psT], tag=("T",))
        k.op("act", lambda e: e.copy(out=qTs[:], in_=psT[:, 0:4, 0:8]), [psT], [qTs])
        k.op("act", lambda e: e.copy(out=KSb[:, 16384:16392], in_=psT[:, 4, 0:8]), [psT], [KSb])
        k.op("act", lambda e: e.copy(out=KWb[:, 512:520], in_=psT[:, 5, 0:8]), [psT], [KWb])
        k.dma("sp", VXb[0:8, 128, :, 0:64], scr_qk[r0:r0 + 8, 1024:1152].rearrange("p (g d) -> p g d", g=2), reads=[scr_qk], writes=[VXb])
        k.dma("sp", VWb[0:8, 4, :, 0:64], scr_qk[r0:r0 + 8, 1152:1280].rearrange("p (g d) -> p g d", g=2), reads=[scr_qk], writes=[VWb])
        qflat = qTs[:].rearrange("p a b -> p (a b)")
        for p in range(128):
            t_pg = pg[p % 8]
            for which, pool_d in ((0, cpool), (1, spool)):
                k._sync("pool", [pidx], [t_pg])
                di = k.dnext
                k.dnext = (k.dnext + 1) % NDS
                k._wait("pool", ("d", di), k.dval[di])
                ins = nc.gpsimd.indirect_dma_start(out=t_pg[:, which, :], out_offset=None, in_=pool_d[:].rearrange("n r f -> (n r) f"),
                                                   in_offset=bass.IndirectOffsetOnAxis(ap=pidx[:, b * 128 + p:b * 128 + p + 1], axis=0))
                k.n_ins += 1
                k.dval[di] += 16
                ins.then_inc(k.dsems[di], 16)
                t_pg.w[("d", di)] = k.dval[di]
            k.op("pe", lambda e, t_pg=t_pg: e.transpose(psF[:, 0, :], t_pg[:, 0, 0:128], identf[:]), [t_pg, identf], [psF], tag=("T",))
            k.op("pe", lambda e, t_pg=t_pg: e.transpose(psF[:, 1, :], t_pg[:, 0, 128:256], identf[:]), [t_pg, identf], [psF], tag=("T",))
            k.op("pe", lambda e, t_pg=t_pg: e.transpose(psF[:, 2, :], t_pg[:, 1, 0:128], identf[:]), [t_pg, identf], [psF], tag=("T",))
            k.op("act", lambda e, p=p: e.copy(out=KCb[:, :, 16 + (p % 16) * 128:16 + (p % 16 + 1) * 128], in_=psF[:, 0:2, :]), [psF], [KCb])
            k.op("dve", lambda e, p=p: e.tensor_copy(out=KSb[:, p * 128:(p + 1) * 128], in_=psF[:, 2, :]), [psF], [KSb])
            k.op("dve", lambda e, p=p, t_pg=t_pg: e.tensor_copy(out=VXb[:, p, :, 0:64], in_=t_pg[:, 1, 128:256].rearrange("p (g d) -> p g d", g=2)), [t_pg], [VXb])
            if p % 16 == 15:
                compress_group(p // 16)
        for i in range(4):
            t_pg = pg[i % 8]
            k.dma("sp", t_pg[:, 0, :], st_win[b, i * 128:(i + 1) * 128, :], writes=[t_pg])
            k.op("pe", lambda e, t_pg=t_pg: e.transpose(psF[:, 3, :], t_pg[:, 0, 0:128], identf[:]), [t_pg, identf], [psF], tag=("T",))
            k.op("act", lambda e, i=i: e.copy(out=KWb[:, i * 128:(i + 1) * 128], in_=psF[:, 3, :]), [psF], [KWb])
            k.op("dve", lambda e, i=i, t_pg=t_pg: e.tensor_copy(out=VWb[:, i, :, 0:64], in_=t_pg[:, 0, 128:256].rearrange("p (g d) -> p g d", g=2)), [t_pg], [VWb])
        for ch in range(8):
            k.op("pe", lambda e, ch=ch: e.transpose(psT[:, ch, :], kvc[:, 1, ch * 128:(ch + 1) * 128], ident_bf[:]), [kvc, ident_bf], [psT], tag=("T",))
        k.op("act", lambda e: e.copy(out=vcb[:], in_=psT[:]), [psT], [vcb])
        for g in range(2):
            gs = slice(g * 64, (g + 1) * 64)
            for hf in range(2):
                k.mm(psS[0:32, hf * 512:(hf + 1) * 512], [(qflat[gs, :], kvc[gs, 0, hf * 512:(hf + 1) * 512])], [qTs, kvc], [psS])
            k.op("dve", lambda e: e.tensor_reduce(out=st[:, 0:1], in_=psS[0:32, 0:1023], axis=AX.X, op=ALU.max), [psS], [st])
            k.ts("dve", st[:, 1:2], st[:, 0:1], -1.0, None, ALU.mult, None, [st], [st])
            k.act(pS[:, 0:1023], psS[0:32, 0:1023], AF.Exp, [psS, st], [pS, st], bias=st[:, 1:2], accum_out=st[:, 2:3])
            k.op("dve", lambda e: e.reciprocal(out=st[:, 3:4], in_=st[:, 2:3]), [st], [st])
            k.ts("dve", pS[:, 0:1023], pS[:, 0:1023], st[:, 3:4], None, ALU.mult, None, [pS, st], [pS])
            k.op("act", lambda e: e.copy(out=pbf[:], in_=pS[:]), [pS], [pbf])
            k.op("dve", lambda e: e.tensor_reduce(out=impr[:], in_=pS[:].rearrange("p (s f) -> p s f", f=4), axis=AX.X, op=ALU.add), [pS], [impr])
            k.tt("dve", impr[:, 1:256], impr[:, 1:256], pS[:, 3:1020:4], ALU.add, [impr, pS], [impr])
            k.mm(psA[0:8, 0:256], [(selm[:, 0:8], impr[:])], [selm, impr], [psA])
            k.op("dve", lambda e, g=g: e.memset(sc[:, g, :], 0.0), [], [sc])
            k.op("act", lambda e, g=g: e.copy(out=sc[:, g, 0:256], in_=psA[0:8, 0:256]), [psA], [sc])
            for ch in range(8):
                k.op("pe", lambda e, ch=ch: e.transpose(psT[:, ch, 0:32], pbf[:, ch * 128:(ch + 1) * 128], ident_bf[0:32, 0:32]), [pbf, ident_bf], [psT], tag=("T",))
            k.op("act", lambda e: e.copy(out=pTs[:], in_=psT[:, :, 0:32]), [psT], [pTs])
            k.mm(psA[0:32, 256:320], [(pTs[:, ch, :], vcb[:, ch, g * 64:(g + 1) * 64]) for ch in range(8)], [pTs, vcb], [psA])
            k.op("act", lambda e, g=g: e.copy(out=obr[:, 0, g, 0:64], in_=psA[0:32, 256:320]), [psA], [obr])
            k.op("dve", lambda e, g=g: e.memset(obr[:, 0, g, 64:65], 1.0), [], [obr])
            k.tt("dve", sc[:, g, :], sc[:, g, :], selcS[:, 0, :], ALU.mult, [sc, selcS], [sc])
            k.tt("dve", sc[:, g, :], sc[:, g, :], selcS[:, 1, :], ALU.add, [sc, selcS], [sc])
            k.op("dve", lambda e, g=g: e.max(out=top[:, 0:8], in_=sc[:, g, :]), [sc], [top])
            k.op("dve", lambda e, g=g: e.match_replace(out=sc2[:], in_to_replace=top[:, 0:8], in_values=sc[:, g, :], imm_value=-1e30), [sc, top], [sc2])
            k.op("dve", lambda e: e.max(out=top[:, 8:16], in_=sc2[:]), [sc2], [top])
            k.ts("dve", top[:, 15:16], top[:, 15:16], -0.5, None, ALU.max, None, [top], [top])
            k.ts("dve", sc[:, g, :], sc[:, g, :], top[:, 15:16], None, ALU.is_ge, None, [sc, top], [sc])
            k.ts("dve", sc[:, g, :], sc[:, g, :], 30000.0, -30000.0, ALU.mult, ALU.add, [sc], [sc])
            for c3 in range(3):
                k.mm(psA[:, 320 + 32 * c3:352 + 32 * c3], [(sc[:, g, c3 * 128:(c3 + 1) * 128], rep[:])], [sc, rep], [psA])
            k.op("act", lambda e, g=g: e.copy(out=selTs[:, :, g, :], in_=psA[:, 320:416].rearrange("p (c t) -> p c t", c=3)), [psA], [selTs])
        ck(141 + b)
        pi = 0
        for br in (1, 2):
            for g in range(2):
                gs = slice(g * 64, (g + 1) * 64)
                nkb = 129 if br == 1 else 5
                blocks = []
                for kb in range(nkb):
                    if br == 1:
                        sp_ = (KSb[gs, kb * 128:(kb + 1) * 128], qflat[gs, :])
                        bs_ = [(ef[:, (kb % 64) * 128:(kb % 64) * 128 + 128], selTs[:, (2 * kb) // 128, g, :])]
                        if kb == 128:
                            bs_.append((ident_bf[:], triS[:, 0, :]))
                        vr = VXb[:, kb, g, :]
                    else:
                        sp_ = (KWb[gs, kb * 128:(kb + 1) * 128], qflat[gs, :])
                        bs_ = []
                        if kb == 0:
                            bs_.append((ident_bf[:], triS[:, 1, :]))
                        if kb == 4:
                            bs_.append((ident_bf[:], triS[:, 0, :]))
                        vr = VWb[:, kb, g, :]
                    blocks.append((sp_, bs_, vr, kb))
                gi_ = 0
                for c0_ in range(0, nkb, 8):
                    grp = blocks[c0_:c0_ + 8]
                    ng = len(grp)
                    Sb = (psC, psX)[gi_ % 2]
                    gi_ += 1
                    for j, (sp_, bs_, vr, kb) in enumerate(grp):
                        k.op("pe", lambda e, sp_=sp_, j=j, Sb=Sb: e.matmul(Sb[:, j * 32:(j + 1) * 32], lhsT=sp_[0], rhs=sp_[1], start=(j == 0), stop=False,
                                                                     skip_group_check=True), [KSb, KWb, qTs], [Sb], tag=("S1c", g))
                    for j, (sp_, bs_, vr, kb) in enumerate(grp):
                        for (l, r) in bs_:
                            k.op("pe", lambda e, l=l, r=r, j=j, Sb=Sb: e.matmul(Sb[:, j * 32:(j + 1) * 32], lhsT=l, rhs=r, start=False, stop=False,
                                                                             skip_group_check=True), [ef, selTs, ident_bf, triS], [Sb], tag=("E1c",))
                    if pendc[0] is not None:
                        f = pendc[0]
                        pendc[0] = None
                        f()
                    pi ^= 1
                    pt = PTs[pi]
                    k.act(pt[:, 0:ng * 32], Sb[:, 0:ng * 32], AF.Exp, [Sb], [pt])

                    def pv(pt=pt, grp=grp, nkb=nkb):
                        for j, (sp_, bs_, vr, kb) in enumerate(grp):
                            k.op("pe", lambda e, j=j, vr=vr, kb=kb: e.matmul(psA[0:32, 416 + 0:416 + 65], lhsT=pt[:, j * 32:(j + 1) * 32], rhs=vr,
                                                                          start=(kb == 0), stop=(kb == nkb - 1)), [pt, VXb, VWb], [psA], tag=("PV1c",))
                    pendc[0] = pv
                if pendc[0] is not None:
                    f = pendc[0]
                    pendc[0] = None
                    f()
                k.op("act", lambda e, br=br, g=g: e.copy(out=obr[:, br, g, :], in_=psA[0:32, 416:481]), [psA], [obr])
        k.ts("dve", rcs[:, 0:6], obr[:, :, :, 64].rearrange("p a g -> p (a g)"), 1e-30, None, ALU.max, None, [obr], [rcs])
        k.op("dve", lambda e: e.reciprocal(out=rcs[:, 0:6], in_=rcs[:, 0:6]), [rcs], [rcs])
        for g in range(2):
            for br in range(3):
                k.tt("dve", st[:, 4:5], rcs[:, br * 2 + g:br * 2 + g + 1], gate_r[:, g, br:br + 1], ALU.mult, [rcs, gate_r], [st])
                if br == 0:
                    k.ts("dve", ofin[:, g, :], obr[:, 0, g, 0:64], st[:, 4:5], None, ALU.mult, None, [obr, st], [ofin])
                else:
                    k.op("dve", lambda e, g=g, br=br: e.scalar_tensor_tensor(out=ofin[:, g, :], in0=obr[:, br, g, 0:64], scalar=st[:, 4:5], in1=ofin[:, g, :],
                                                                             op0=ALU.mult, op1=ALU.add), [obr, st, ofin], [ofin])
            for hh in range(4):
                fo = first_o[0]
                first_o[0] = False
                k.op("pe", lambda e, g=g, hh=hh, fo=fo, b=b: e.matmul(psO[0:32, (g * 4 + hh) * 64:(g * 4 + hh + 1) * 64],
                                                                   lhsT=selm[:, 8 + (hh * 4 + b) * 32:8 + (hh * 4 + b + 1) * 32], rhs=ofin[:, g, :],
                                                                   start=fo, stop=False, skip_group_check=True), [selm, ofin], [psO])
    k.op("act", lambda e: e.copy(out=oas[:], in_=psO[0:32, :]), [psO], [oas])
    k.dma("pool", scr_oab[(NOWN + 1) * 128:(NOWN + 1) * 128 + 32, 0:512], oas[:], reads=[oas], writes=[scr_oab])
    k.pop()


def _consts():
    cm = np.zeros((128, 6, 128), np.float32)
    cm[:, 0, :] = np.eye(128, dtype=np.float32)
    s = np.arange(128)[:, None]
    t = np.arange(128)[None, :]
    same_p = (s // 64) == (t // 64)
    cm[:, 1, :] = ((s <= t) & same_p)
    cm[:, 2, :] = ((s > t) & same_p)
    same_s = ((s // 8) == (t // 8)) & (s < 32) & (t < 32)
    cm[:, 3, :] = ((s <= t) & same_s)
    cm[:, 4, :] = ((s > t) & same_s)
    cc = np.zeros((128, 6, 128), np.float32)
    cc[:, 0, 0:64] = 1
    cc[:, 1, 64:128] = 1
    for b in range(4):
        cc[:, 2 + b, 8 * b:8 * b + 8] = 1
    ci = np.zeros((128, 8), np.float32)
    ci[0:64, 0] = 1
    ci[64:128, 1] = 1
    for b in range(4):
        ci[8 * b:8 * b + 8, 2 + b] = 1
    return cm.reshape(128, 768), ci, cc.reshape(128, 768)


def _rope_tab(pos):
    inv = (10000.0 ** (-np.arange(32, dtype=np.float32) / np.float32(32))).astype(np.float32)
    ang = pos.astype(np.float32)[:, None] * inv[None, :]
    return np.concatenate([np.cos(ang), np.sin(ang)], axis=1).astype(np.float32)


def _nsa_consts(inp, half):
    w1 = inp["cmp_w1"][0]
    w2 = inp["cmp_w2"][0]
    pe = inp["cmp_pos_emb"][0]
    w1bd = np.zeros((2, 64, 64, 2, 64), np.float32)
    w1r = w1.reshape(2, 32, 64, 64)
    for g in range(2):
        w1bd[g, :, :, g, :] = w1r.transpose(2, 0, 1, 3).reshape(64, 64, 64)
    w1bd = w1bd.reshape(128, 64 * 128)
    w2bd = np.zeros((2, 64, 2, 2, 64), np.float32)
    for g in range(2):
        w2bd[g, :, :, g, :] = w2.transpose(1, 0, 2)
    w2bd = w2bd.reshape(128, 256)
    posvec = np.tile(pe.transpose(2, 0, 1).reshape(64, 64), (2, 1)).astype(np.float32)
    c = np.arange(256)[:, None]
    sidx = np.arange(62)[None, :]
    cov = ((c >= 4 * sidx - 1) & (c <= 4 * sidx + 3)).astype(np.float32)
    cover = cov.reshape(2, 128, 62).transpose(1, 0, 2).reshape(128, 124)
    NEG = -30000.0
    n_t = NOWN + 1
    cmpB = np.zeros((n_t, 128, 2, 128), np.float32)
    selc = np.zeros((n_t, 128, 2, 64), np.float32)
    cl = np.arange(128)[:, None]
    t = np.arange(128)[None, :]
    soff = 32 if half == 0 else 0
    coff = 128 if half == 0 else 0
    for i in range(n_t):
        lt = NPRE - 1 + i
        for ch in range(2):
            cc = ch * 128 + cl
            vis = (16 * cc + 31 <= 128 * lt + t) & (cc - coff >= 0) & (cc <= 254)
            cmpB[i, :, ch, :] = np.where(vis, 0.0, NEG)
        l = 128 * lt + np.arange(128)[:, None]
        s = np.arange(64)[None, :]
        sg = s - soff
        cur_g = l // 64 - soff
        valid = (64 * s <= l) & (sg >= 0)
        forced = (sg >= 0) & ((sg == 0) | (sg == cur_g) | (sg == cur_g - 1)) & (cur_g >= 0)
        selc[i, :, 0, :] = (valid & ~forced)
        selc[i, :, 1, :] = np.where(forced, 1e4, np.where(valid, 0.0, -1.0))
    key = np.arange(4096)[None, :]
    efull = (key // 64 == np.arange(64)[:, None]).astype(np.float32)
    kk = np.arange(128)[:, None]
    triB = np.zeros((128, 2, 128), np.float32)
    triB[:, 0, :] = np.where(kk > t, NEG, 0.0)
    triB[:, 1, :] = np.where(kk <= t, NEG, 0.0)
    pfx = np.full((1, 128), NEG if half == 0 else 0.0, np.float32)
    return dict(w1bd=w1bd, w2bd=w2bd, posvec=posvec, cover=cover, cmpB=cmpB.reshape(n_t, 128, 256), selc=selc.reshape(n_t, 128, 128),
                efull=efull, triB=triB.reshape(128, 256), pfx=pfx, identf=np.eye(128, dtype=np.float32))


def _sample_consts():
    NEG = -30000.0
    selcS = np.zeros((8, 2, 384), np.float32)
    s = np.arange(384)
    valid = s <= 256
    forced = (s == 0) | (s == 255) | (s == 256)
    selcS[:, 0, :] = (valid & ~forced)[None, :]
    selcS[:, 1, :] = np.where(forced, 1e4, np.where(valid, 0.0, -1.0))[None, :]
    selm = np.zeros((32, 8 + 512), np.float32)
    rep = np.zeros((8, 32), np.float32)
    for hh in range(4):
        for t in range(8):
            selm[hh * 8 + t, t] = 1
            rep[t, hh * 8 + t] = 1
            for b in range(4):
                selm[hh * 8 + t, 8 + (hh * 4 + b) * 32 + b * 8 + t] = 1
    key = np.arange(8192)[None, :]
    ef = (key // 64 == np.arange(128)[:, None]).astype(np.float32)
    r = np.arange(128)[:, None]
    tt = (np.arange(32) % 8)[None, :]
    tri = np.zeros((128, 2, 32), np.float32)
    tri[:, 0, :] = np.where(r > tt, NEG, 0.0)
    tri[:, 1, :] = np.where(r <= tt, NEG, 0.0)
    return dict(selcS=selcS.reshape(8, 768), selm=selm, rep=rep, efull128=ef, triBs=tri.reshape(128, 64),
                iot=np.arange(128, dtype=np.float32)[:, None])


_CACHE = {}


def _get_program(key=(NT, True, 0, 5120)):
    if key not in _CACHE:
        _CACHE[key] = build_program(*key)
    return _CACHE[key]


def make_in_maps(inp, cores, pools=None):
    cmask, cind, ccolmask = _consts()
    vecs = np.concatenate([inp["attn_norm_g"][0], inp["q_norm_g"][0], inp["k_norm_g"][0].reshape(-1),
                           inp["hgrn_lb_logits"].reshape(-1), inp["hgrn_norm_g"][0]]).astype(np.float32)[None, :]
    w_in = np.ascontiguousarray(inp["w_in"][0])
    maps = []
    for c in cores:
        b, half = c // 2, c % 2
        c0 = half * 2048
        xloc = np.zeros((NT * 128, D), np.float32)
        if half == 0:
            xloc[2048:] = inp["x_prompt"][b, 0:2048]
        else:
            xloc[:] = inp["x_prompt"][b, 0:4096]
        pos = np.arange(NT * 128) + c0 - 2048
        cs_tab = np.concatenate([_rope_tab(pos), _rope_tab(16384 + (np.arange(32) % 8))], axis=0)
        maps.append(dict(
            xloc=xloc, xs=np.ascontiguousarray(inp["x_sample"][4 * c:4 * c + 4].reshape(32, D)), cs_tab=cs_tab,
            w_in=w_in, vecs=vecs, cmask=cmask, cind=cind, ccolmask=ccolmask,
            st_win=np.ascontiguousarray(inp["state_win_kv"][0, 4 * c:4 * c + 4].reshape(4, 512, 256)),
            st_hgrn=np.ascontiguousarray(inp["state_hgrn"][0, 4 * c:4 * c + 4]),
        ))
        maps[-1].update(_nsa_consts(inp, half))
        cwv = np.concatenate([inp["ffn_conv_w"][0], inp["ffn_conv_b"]], axis=0)
        convw = np.ascontiguousarray(cwv.reshape(4, 22, 128).transpose(2, 1, 0)).reshape(128, 88)
        cst = inp["state_ffn_conv"][0, 4 * c:4 * c + 4]
        convst = np.ascontiguousarray(cst.reshape(4, 2, 22, 128).transpose(3, 2, 0, 1)).reshape(128, 176)
        maps[-1].update(w_branch=np.ascontiguousarray(inp["w_branch"][0]), w_out=np.ascontiguousarray(inp["w_out"][0]),
                        ffn_w_in=np.ascontiguousarray(inp["ffn_w_in"][0]), ffn_w_out=np.ascontiguousarray(inp["ffn_w_out"][0]),
                        fvec=np.ascontiguousarray(inp["ffn_norm_g"][0][None, :]), convw=convw, convst=convst)
        if pools is None:
            maps[-1].update(cache_cmp=inp["cache_cmp_kv"][0].reshape(-1, 128, 256), cache_slc=inp["cache_slc_kv"][0].reshape(-1, 128, 256),
                            ptab=np.ascontiguousarray(inp["page_table"][4 * c:4 * c + 4]).astype(np.int32))
        else:
            maps[-1].update(pools(c))
        maps[-1].update(_sample_consts())
    return maps


def assemble(res, cores, out):
    for i, c in enumerate(cores):
        r = res[i]
        b, half = c // 2, c % 2
        c0 = half * 2048
        for j, name in enumerate(("cmp_kv_prompt", "slc_kv_prompt")):
            out[name][0, b, c0:c0 + 2048] = r["o_kv"][j].reshape(2048, 2, 2, 64)
        if half == 1:
            out["win_kv_prompt"][0, b] = r["o_kv"][2][2048 - 512:].reshape(512, 2, 2, 64)
            out["hgrn_prompt"][0, b] = r["o_hg_p"]
        out["cmp_kv_sample"][0, 4 * c:4 * c + 4] = r["o_kvs"][0].reshape(4, 8, 2, 2, 64)
        out["slc_kv_sample"][0, 4 * c:4 * c + 4] = r["o_kvs"][1].reshape(4, 8, 2, 2, 64)
        out["win_kv_sample"][0, 4 * c:4 * c + 4] = r["o_win_s"].reshape(4, 512, 2, 2, 64)
        out["hgrn_sample"][0, 4 * c:4 * c + 4] = r["o_hg_s"]
        out["y_prompt"][b, c0:c0 + 2048] = r["o_y"]
        out["y_sample"][4 * c:4 * c + 4] = r["o_ys"].reshape(4, 8, D)
        if half == 1:
            out["ffn_conv_prompt"][0, b] = r["o_cp"].reshape(128, 22, 2).transpose(2, 1, 0).reshape(2, 2816)
        out["ffn_conv_sample"][0, 4 * c:4 * c + 4] = r["o_cs"].reshape(128, 22, 4, 2).transpose(2, 3, 1, 0).reshape(4, 2, 2816)


OUT_SHAPES = dict(
    y_prompt=(4, 4096, 1024), y_sample=(32, 8, 1024),
    cmp_kv_prompt=(1, 4, 4096, 2, 2, 64), cmp_kv_sample=(1, 32, 8, 2, 2, 64),
    slc_kv_prompt=(1, 4, 4096, 2, 2, 64), slc_kv_sample=(1, 32, 8, 2, 2, 64),
    win_kv_prompt=(1, 4, 512, 2, 2, 64), win_kv_sample=(1, 32, 512, 2, 2, 64),
    hgrn_prompt=(1, 4, 4, 128, 128), hgrn_sample=(1, 32, 4, 128, 128),
    ffn_conv_prompt=(1, 4, 2, 2816), ffn_conv_sample=(1, 32, 2, 2816))
OUT_ORDER = ["y_prompt", "y_sample", "cmp_kv_prompt", "cmp_kv_sample", "slc_kv_prompt", "slc_kv_sample",
             "win_kv_prompt", "win_kv_sample", "hgrn_prompt", "hgrn_sample", "ffn_conv_prompt", "ffn_conv_sample"]


def kernel(**inp):
    inp = {n: np.asarray(v) for n, v in inp.items()}
    cores = list(range(8))
    prog = _get_program()
    maps = make_in_maps(inp, cores)
    res = run_bass_kernel_spmd(prog.nc, maps, core_ids=cores)
    out = {n: np.zeros(s, np.float32) for n, s in OUT_SHAPES.items()}
    assemble(res.results, cores, out)
    return tuple(out[n] for n in OUT_ORDER)
```

```python
import numpy as np
import concourse.bass as bass
import concourse.mybir as mybir
from concourse.bass_utils import run_bass_kernel_spmd
from contextlib import ExitStack

F32 = mybir.dt.float32
BF16 = mybir.dt.bfloat16
I32 = mybir.dt.int32
AF = mybir.ActivationFunctionType
ALU = mybir.AluOpType
AX = mybir.AxisListType

NDS = 40
PE_NOSELF = False
PIPE_B = True
NOSELF_CHAIN = True
EPS = 1e-6
D = 1024
NCOL1 = 3352
NPRE = 16
NOWN = 16
NT = NPRE + NOWN


class Buf:
    def __init__(self, name, t):
        self.name = name
        self.t = t
        self.w = {}
        self.r = {}
        self.ps = False

    def __getitem__(self, idx):
        return self.t[idx]


class KB:
    def __init__(self):
        self.nc = bass.Bass("TRN2", target_bir_lowering=False)
        nc = self.nc
        self.es = ExitStack()
        self.engs = {"pe": nc.tensor, "act": nc.scalar, "dve": nc.vector, "pool": nc.gpsimd, "sp": nc.sync}
        self.esem = {}
        self.ecnt = {}
        for e in ("pe", "act", "dve", "pool"):
            self.esem[e] = self.es.enter_context(nc.semaphore("sem_" + e))
            self.ecnt[e] = 0
        self.dsems = [self.es.enter_context(nc.semaphore("sem_d%d" % i)) for i in range(NDS)]
        self.dval = [0] * NDS
        self.dnext = 0
        self.waited = {e: {} for e in self.engs}
        self.nbuf = 0
        self.n_ins = 0
        self.n_wait = 0
        self.stk = [self.es]

    def push(self):
        self.stk.append(ExitStack())

    def barrier(self):
        for e in self.engs:
            for f in ("pe", "act", "dve", "pool"):
                if f != e:
                    self._wait(e, ("e", f), self.ecnt[f])
            for i in range(NDS):
                self._wait(e, ("d", i), self.dval[i])

    def pop(self):
        self.barrier()
        self.stk.pop().close()

    def sb(self, shape, dt, name=None):
        self.nbuf += 1
        name = (name or "sb") + "_%d" % self.nbuf
        return Buf(name, self.stk[-1].enter_context(self.nc.sbuf_tensor(name, list(shape), dt)))

    def ps(self, shape, dt, name=None):
        self.nbuf += 1
        name = name or "ps%d" % self.nbuf
        name = name + "_%d" % self.nbuf
        b = Buf(name, self.stk[-1].enter_context(self.nc.psum_tensor(name, list(shape), dt)))
        b.ps = True
        return b

    def dram(self, name, shape, dt, kind=None):
        if kind is None:
            t = self.nc.dram_tensor(name, list(shape), dt)
        else:
            t = self.nc.dram_tensor(name, list(shape), dt, kind=kind)
        return Buf(name, t.ap())

    def _sem(self, key):
        return self.esem[key[1]] if key[0] == "e" else self.dsems[key[1]]

    def _wait(self, eng, key, val):
        if val <= 0:
            return
        wd = self.waited[eng]
        if wd.get(key, 0) >= val:
            return
        self.engs[eng].wait_ge(self._sem(key), val)
        wd[key] = val
        self.n_wait += 1

    def _sync(self, eng, reads, writes, noself=False):
        need = {}
        for b in reads:
            for k, v in b.w.items():
                if need.get(k, 0) < v:
                    need[k] = v
        for b in writes:
            for k, v in b.w.items():
                if need.get(k, 0) < v:
                    need[k] = v
            for k, v in b.r.items():
                if need.get(k, 0) < v:
                    need[k] = v
        for k, v in need.items():
            if noself and eng == "pe" and k == ("e", "pe"):
                continue
            self._wait(eng, k, v)

    def op(self, eng, fn, reads=(), writes=(), inc=True, noself=False, tag=None):
        if eng == "pe":
            if tag is not None and tag == getattr(self, "last_pe_tag", None):
                noself = True
            self.last_pe_tag = tag
        self._sync(eng, reads, writes, noself)
        ins = fn(self.engs[eng])
        self.n_ins += 1
        if inc:
            self.ecnt[eng] += 1
            n = self.ecnt[eng]
            ins.then_inc(self.esem[eng], 1)
        else:
            n = self.ecnt[eng] + 1
        key = ("e", eng)
        for b in reads:
            d = b.w if b.ps else b.r
            if d.get(key, 0) < n:
                d[key] = n
        for b in writes:
            if b.w.get(key, 0) < n:
                b.w[key] = n
        return ins

    def dma(self, q, out_ap, in_ap, reads=(), writes=(), **kw):
        i = self.dnext
        self.dnext = (i + 1) % NDS
        self._wait(q, ("d", i), self.dval[i])
        self._sync(q, reads, writes)
        ins = self.engs[q].dma_start(out=out_ap, in_=in_ap, **kw)
        self.n_ins += 1
        self.dval[i] += 16
        ins.then_inc(self.dsems[i], 16)
        key = ("d", i)
        for b in reads:
            b.r[key] = self.dval[i]
        for b in writes:
            b.w[key] = self.dval[i]
        return ins

    def finish(self):
        for i in range(NDS):
            self._wait("sp", ("d", i), self.dval[i])
        for e in ("pe", "act", "dve", "pool"):
            self._wait("sp", ("e", e), self.ecnt[e])

    def mm(self, out_ap, pairs, reads, writes):
        n = len(pairs)
        for i, (l, r) in enumerate(pairs):
            self.op("pe", lambda e, l=l, r=r, i=i: e.matmul(out_ap, lhsT=l, rhs=r, start=(i == 0), stop=(i == n - 1)),
                    reads=reads, writes=writes, inc=True, noself=(NOSELF_CHAIN and i > 0))

    def act(self, out_ap, in_ap, func, reads, writes, **kw):
        return self.op("act", lambda e: e.activation(out=out_ap, in_=in_ap, func=func, **kw), reads=reads, writes=writes)

    def tt(self, eng, out_ap, a, b, op, reads, writes):
        return self.op(eng, lambda e: e.tensor_tensor(out=out_ap, in0=a, in1=b, op=op), reads=reads, writes=writes)

    def ts(self, eng, out_ap, a, s1, s2, op0, op1, reads, writes):
        if op1 is None:
            return self.op(eng, lambda e: e.tensor_scalar(out=out_ap, in0=a, scalar1=s1, scalar2=None, op0=op0),
                           reads=reads, writes=writes)
        return self.op(eng, lambda e: e.tensor_scalar(out=out_ap, in0=a, scalar1=s1, scalar2=s2, op0=op0, op1=op1),
                       reads=reads, writes=writes)


class _Stop(Exception):
    pass


def build_program(ntp=NT, with_sample=True, stop=0, npool=5120):
    k = KB()
    try:
        _build(k, ntp, with_sample, stop, npool)
    except _Stop:
        pass
    k.finish()
    return k


def _build(k, ntp, with_sample, stop, npool):
    def ck(n):
        if stop == n:
            raise _Stop()
    nc = k.nc
    P = 128
    xloc = k.dram("xloc", [NT * 128, D], F32, "ExternalInput")
    xs = k.dram("xs", [32, D], F32, "ExternalInput")
    cs_tab = k.dram("cs_tab", [NT * 128 + 32, 64], F32, "ExternalInput")
    w_in = k.dram("w_in", [D, 5400], F32, "ExternalInput")
    vecs = k.dram("vecs", [1, 1024 + 64 + 192 + 1024 + 128], F32, "ExternalInput")
    cmask = k.dram("cmask", [128, 6 * 128], F32, "ExternalInput")
    cind = k.dram("cind", [128, 8], F32, "ExternalInput")
    st_win = k.dram("st_win", [4, 512, 256], F32, "ExternalInput")
    st_hgrn = k.dram("st_hgrn", [4, 4, 128, 128], F32, "ExternalInput")
    ccolmask = k.dram("ccolmask", [128, 6 * 128], F32, "ExternalInput")
    scr_qk = k.dram("scr_qk", [NT * 128 + 32, 1280], BF16)
    scr_gate = k.dram("scr_gate", [(NOWN + 1) * 128 + 32, 24], F32)
    scr_hT = k.dram("scr_hT", [NOWN + 2, 128, 1024], BF16)
    scr_oab = k.dram("scr_oab", [(NOWN + 1) * 128 + 32, 1024], BF16)

    o_kv = k.dram("o_kv", [3, NOWN * 128, 256], F32, "ExternalOutput")
    o_kvs = k.dram("o_kvs", [2, 32, 256], F32, "ExternalOutput")
    o_win_s = k.dram("o_win_s", [4, 512, 256], F32, "ExternalOutput")
    o_hg_p = k.dram("o_hg_p", [4, 128, 128], F32, "ExternalOutput")
    o_hg_s = k.dram("o_hg_s", [4, 4, 128, 128], F32, "ExternalOutput")

    k.push()
    wsb = k.sb([P, 8, NCOL1], BF16, "wsb")
    for kc in range(8):
        k.dma("pool", wsb[:, kc, :], w_in[kc * 128:(kc + 1) * 128, 0:NCOL1], writes=[wsb])
    g_bc = k.sb([P, D], F32, "g_bc")
    k.dma("sp", g_bc[:], vecs[0:1, 0:1024].partition_broadcast(P), writes=[g_bc])
    gain = k.sb([P, 14, 64], F32, "gain")
    for h in range(8):
        k.dma("sp", gain[:, h, :], vecs[0:1, 1024:1088].partition_broadcast(P), writes=[gain])
    for i in range(3):
        for g in range(2):
            k.dma("sp", gain[:, 8 + 2 * i + g, :], vecs[0:1, 1088 + 64 * i:1088 + 64 * i + 64].partition_broadcast(P), writes=[gain])
    k.ts("dve", gain[:, 0:8, :], gain[:, 0:8, :], 0.125, None, ALU.mult, None, [gain], [gain])
    lgt = k.sb([P, 2, 512], F32, "lgt")
    k.dma("sp", lgt[:, 0, :], vecs[0:1, 1280:1792].partition_broadcast(P), writes=[lgt])
    k.dma("sp", lgt[:, 1, :], vecs[0:1, 1792:2304].partition_broadcast(P), writes=[lgt])
    lb = k.sb([P, 512], F32, "lb")
    oml = k.sb([P, 512], F32, "oml")
    k.tt("dve", lb[:], lgt[:, 0, :], lgt[:, 1, :], ALU.subtract, [lgt], [lb])
    k.act(lb[:], lb[:], AF.Sigmoid, [lb], [lb])
    k.ts("dve", oml[:], lb[:], -1.0, 1.0, ALU.mult, ALU.add, [lb], [oml])
    gn_bc = k.sb([P, 4, 128], F32, "gn_bc")
    for h in range(4):
        k.dma("sp", gn_bc[:, h, :], vecs[0:1, 2304:2432].partition_broadcast(P), writes=[gn_bc])
    cm = k.sb([P, 6, 128], F32, "cm")
    k.dma("sp", cm[:], cmask[:].rearrange("p (a b) -> p a b", b=128), writes=[cm])
    ci = k.sb([P, 8], F32, "ci")
    k.dma("sp", ci[:], cind[:], writes=[ci])
    ident_bf = k.sb([P, 128], BF16, "ident_bf")
    k.op("dve", lambda e: e.tensor_copy(out=ident_bf[:], in_=cm[:, 0, :]), [cm], [ident_bf])
    ones_col = k.sb([P, 1], F32, "ones_col")
    k.op("dve", lambda e: e.memset(ones_col[:], 1.0), [], [ones_col])

    ck(1)
    xt = [k.sb([P, D], F32, "xt%d" % i) for i in range(2)]
    junk_2 = [k.sb([P, D], BF16, "junk%d" % i) for i in range(2)]
    xn_2 = [k.sb([P, D], BF16, "xn%d" % i) for i in range(2)]
    hT_2 = [k.sb([P, 8, 128], BF16, "hT%d" % i) for i in range(2)]
    st4_2 = [k.sb([P, 8], F32, "st4%d" % i) for i in range(2)]
    R_2 = [k.sb([P, 14, 64], F32, "R%d" % i) for i in range(2)]
    R2_2 = [k.sb([P, 14, 64], F32, "R2%d" % i) for i in range(2)]
    T1_2 = [k.sb([P, 14, 32], F32, "T1%d" % i) for i in range(2)]
    T2_2 = [k.sb([P, 14, 32], F32, "T2%d" % i) for i in range(2)]
    st14_2 = [k.sb([P, 16], F32, "st14%d" % i) for i in range(2)]
    cs_2 = [k.sb([P, 64], F32, "cs%d" % i) for i in range(2)]
    kvo_2 = [k.sb([P, 3, 256], F32, "kvo%d" % i) for i in range(2)]
    qkv_bf_2 = [k.sb([P, 1280], BF16, "qkv_bf%d" % i) for i in range(2)]
    gate_sb_2 = [k.sb([P, 24], F32, "gate_sb%d" % i) for i in range(2)]
    qb_2 = [k.sb([P, 512], F32, "qb%d" % i) for i in range(2)]
    u_sb_2 = [k.sb([P, 512], F32, "u_sb%d" % i) for i in range(2)]
    logf_2 = [k.sb([P, 512], F32, "logf%d" % i) for i in range(2)]
    kb_2 = [k.sb([P, 512], F32, "kb%d" % i) for i in range(2)]
    vb_2 = [k.sb([P, 512], BF16, "vb%d" % i) for i in range(2)]
    ggb_2 = [k.sb([P, 512], BF16, "ggb%d" % i) for i in range(2)]
    ex_2 = [k.sb([P, 512], F32, "ex%d" % i) for i in range(2)]
    qe_2 = [k.sb([P, 512], BF16, "qe%d" % i) for i in range(2)]
    ke_2 = [k.sb([P, 512], BF16, "ke%d" % i) for i in range(2)]
    kd_2 = [k.sb([P, 512], BF16, "kd%d" % i) for i in range(2)]
    kdm_2 = [k.sb([P, 512], BF16, "kdm%d" % i) for i in range(2)]
    dec_2 = [k.sb([P, 16], F32, "dec%d" % i) for i in range(2)]
    S32 = [k.sb([P, 4, 128], F32, "S32_%d" % i) for i in range(5)]
    Sbf = [k.sb([P, 4, 128], BF16, "Sbf_%d" % i) for i in range(5)]

    psT = k.ps([P, 8, 128], BF16, "psT")
    psZ = [k.ps([P, 512], F32, "psZ%d" % i) for i in range(2)]
    psH = [k.ps([P, 512], F32, "psH%d" % i) for i in range(2)]
    psS = k.ps([P, 4, 128], F32, "psS")
    psA = k.ps([P, 4, 128], F32, "psA")
    psO = k.ps([P, 4, 128], F32, "psO")
    qkT_2 = [k.sb([P, 8, 128], BF16, "qkT%d" % i) for i in range(2)]
    qeTm = [k.sb([P, 4, 128], BF16, "qeTm%d" % i) for i in range(4)]
    AmT_2 = [k.sb([P, 4, 128], BF16, "AmT%d" % i) for i in range(2)]
    osq_2 = [k.sb([P, 512], F32, "osq%d" % i) for i in range(2)]
    st8_2 = [k.sb([P, 8], F32, "st8%d" % i) for i in range(2)]
    oab = k.sb([P, 1024], BF16, "oab")
    ccol = k.sb([P, 6, 128], F32, "ccol")
    k.dma("sp", ccol[:], ccolmask[:].rearrange("p (a b) -> p a b", b=128), writes=[ccol])
    k.op("pool", lambda e: e.memset(oab[:], 0.0), [], [oab])

    k.op("dve", lambda e: e.memset(S32[0][:], 0.0), [], [S32[0]])
    k.op("pool", lambda e: e.memset(Sbf[0][:], 0.0), [], [Sbf[0]])
    if with_sample:
        for b in range(4):
            k.dma("sp", S32[1 + b][:], st_hgrn[b].rearrange("h k v -> k h v"), writes=[S32[1 + b]])
            k.op("act", lambda e, b=b: e.copy(out=Sbf[1 + b][:], in_=S32[1 + b][:]), [S32[1 + b]], [Sbf[1 + b]])
        for b in range(4):
            k.dma("sp", o_win_s[b, 0:504, :], st_win[b, 8:512, :], writes=[o_win_s])

    ck(2)
    zi = [0]

    def next_z():
        zi[0] ^= 1
        return psZ[zi[0]]

    def proj(c0, c1):
        z = next_z()
        k.mm(z[:, 0:c1 - c0], [(hT[:, kc, :], wsb[:, kc, c0:c1]) for kc in range(8)], [hT, wsb], [z])
        return z

    def do_tile(lt, rows, xsrc, csrc, full, kv_dst, sample):
        x = xt[lt % 2]
        rs = slice(0, rows)
        junk = junk_2[lt % 2]
        xn = xn_2[lt % 2]
        hT = hT_2[lt % 2]
        st4 = st4_2[lt % 2]
        R = R_2[lt % 2]
        R2 = R2_2[lt % 2]
        T1 = T1_2[lt % 2]
        T2 = T2_2[lt % 2]
        st14 = st14_2[lt % 2]
        cs = cs_2[lt % 2]
        kvo = kvo_2[lt % 2]
        qkv_bf = qkv_bf_2[lt % 2]
        gate_sb = gate_sb_2[lt % 2]
        qb = qb_2[lt % 2]
        u_sb = u_sb_2[lt % 2]
        logf = logf_2[lt % 2]
        kb = kb_2[lt % 2]
        vb = vb_2[lt % 2]
        ggb = ggb_2[lt % 2]
        ex = ex_2[lt % 2]
        qe = qe_2[lt % 2]
        ke = ke_2[lt % 2]
        kd = kd_2[lt % 2]
        kdm = kdm_2[lt % 2]
        dec = dec_2[lt % 2]
        qkT = qkT_2[lt % 2]
        AmT = AmT_2[lt % 2]
        osq = osq_2[lt % 2]
        st8 = st8_2[lt % 2]
        k.dma("sp", x[rs, :], xsrc, writes=[x])
        k.dma("sp", cs[rs, :], csrc, writes=[cs])
        ck(31)
        k.act(junk[rs, :], x[rs, :], AF.Square, [x], [junk, st4], accum_out=st4[rs, 0:1])
        k.ts("dve", st4[rs, 1:2], st4[rs, 0:1], 1.0 / D, EPS, ALU.mult, ALU.add, [st4], [st4])
        k.act(st4[rs, 2:3], st4[rs, 1:2], AF.Sqrt, [st4], [st4])
        k.op("dve", lambda e: e.reciprocal(out=st4[rs, 3:4], in_=st4[rs, 2:3]), [st4], [st4])
        k.op("dve", lambda e: e.scalar_tensor_tensor(out=xn[rs, :], in0=x[rs, :], scalar=st4[rs, 3:4], in1=g_bc[rs, :],
                                                     op0=ALU.mult, op1=ALU.mult), [x, st4, g_bc], [xn])
        ck(32)
        for kc in range(8):
            k.op("pe", lambda e, kc=kc: e.transpose(psT[:, kc, rs], xn[rs, kc * 128:(kc + 1) * 128], ident_bf[rs, rs]),
                 [xn, ident_bf], [psT], inc=True, tag=("T",))
        ck(33)
        k.op("act", lambda e: e.copy(out=hT[:, :, rs], in_=psT[:, :, rs]), [psT], [hT])
        if full:
            ti = (NOWN + 1) if sample else (lt - (NPRE - 1))
            k.dma("pool", scr_hT[ti, :, :].rearrange("p (a b) -> p a b", b=128)[:, :, rs], hT[:, :, rs], reads=[hT], writes=[scr_hT])
        ck(3)

        def projr(c0, c1):
            z = next_z()
            k.mm(z[rs, 0:c1 - c0], [(hT[:, kc, rs], wsb[:, kc, c0:c1]) for kc in range(8)], [hT, wsb], [z])
            return z

        if full:
            z = projr(0, 512)
            ck(41)
            k.op("act", lambda e: e.copy(out=R[rs, 0:8, :], in_=z[rs, 0:512].rearrange("p (h d) -> p h d", d=64)), [z], [R])
        ck(42)
        z = projr(512, 1024)
        zv = z[rs, 0:512].rearrange("p (a j c) -> p a j c", a=2, j=2)
        k.op("act", lambda e: e.copy(out=R[rs, 8:12, :].rearrange("p (a g) d -> p a (g d)", a=2), in_=zv[:, :, 0, :]), [z], [R])
        k.op("dve", lambda e: e.tensor_copy(out=kvo[rs, 0:2, 128:256], in_=zv[:, :, 1, :]), [z], [kvo])
        ck(43)
        z = projr(1024, 1304)
        k.op("act", lambda e: e.copy(out=R[rs, 12:14, :].rearrange("p g d -> p (g d)"), in_=z[rs, 0:128]), [z], [R])
        k.op("dve", lambda e: e.tensor_copy(out=kvo[rs, 2, 128:256], in_=z[rs, 128:256]), [z], [kvo])
        if full:
            k.act(gate_sb[rs, :], z[rs, 256:280], AF.Sigmoid, [z], [gate_sb])
        ck(4)
        h0 = 0 if full else 8
        nh = 14 - h0
        k.tt("dve", R2[rs, h0:14, :], R[rs, h0:14, :], R[rs, h0:14, :], ALU.mult, [R], [R2])
        k.op("dve", lambda e: e.tensor_reduce(out=st14[rs, h0:14], in_=R2[rs, h0:14, :], axis=AX.X, op=ALU.add), [R2], [st14])
        k.ts("dve", st14[rs, h0:14], st14[rs, h0:14], 1.0 / 64, EPS, ALU.mult, ALU.add, [st14], [st14])
        k.act(st14[rs, h0:14], st14[rs, h0:14], AF.Sqrt, [st14], [st14])
        k.op("dve", lambda e: e.reciprocal(out=st14[rs, h0:14], in_=st14[rs, h0:14]), [st14], [st14])
        k.tt("pool", R2[rs, h0:14, :], R[rs, h0:14, :], gain[rs, h0:14, :], ALU.mult, [R, gain], [R2])
        k.tt("dve", R2[rs, h0:14, :], R2[rs, h0:14, :], st14[rs, h0:14].unsqueeze(2).to_broadcast([rows, nh, 64]), ALU.mult,
             [R2, st14], [R2])
        cosb = cs[rs, 0:32].unsqueeze(1).to_broadcast([rows, nh, 32])
        sinb = cs[rs, 32:64].unsqueeze(1).to_broadcast([rows, nh, 32])
        x1 = R2[rs, h0:14, 0:32]
        x2 = R2[rs, h0:14, 32:64]
        k.tt("dve", T1[rs, h0:14, :], x1, cosb, ALU.mult, [R2, cs], [T1])
        k.tt("pool", T2[rs, h0:14, :], x2, sinb, ALU.mult, [R2, cs], [T2])
        k.tt("dve", R[rs, h0:14, 0:32], T1[rs, h0:14, :], T2[rs, h0:14, :], ALU.subtract, [T1, T2], [R])
        k.tt("dve", T1[rs, h0:14, :], x2, cosb, ALU.mult, [R2, cs], [T1])
        k.tt("pool", T2[rs, h0:14, :], x1, sinb, ALU.mult, [R2, cs], [T2])
        k.tt("dve", R[rs, h0:14, 32:64], T1[rs, h0:14, :], T2[rs, h0:14, :], ALU.add, [T1, T2], [R])
        k.op("act", lambda e: e.copy(out=kvo[rs, :, 0:128], in_=R[rs, 8:14, :].rearrange("p (a g) d -> p a (g d)", a=3)), [R], [kvo])
        if kv_dst is not None:
            for i, dst in enumerate(kv_dst):
                if dst is not None:
                    k.dma("pool", dst[0], kvo[rs, i, :], reads=[kvo], writes=[dst[1]])
        if True:
            k.op("act", lambda e: e.copy(out=qkv_bf[rs, 0:512].rearrange("p (hh g d) -> p g hh d", hh=4, g=2),
                                         in_=R[rs, 0:8, :].rearrange("p (g hh) d -> p g hh d", g=2)), [R], [qkv_bf])
            k.op("act", lambda e: e.copy(out=qkv_bf[rs, 512:896].rearrange("p (h d) -> p h d", d=64), in_=R[rs, 8:14, :]), [R], [qkv_bf])
            k.op("dve", lambda e: e.tensor_copy(out=qkv_bf[rs, 896:1280].rearrange("p (a c) -> p a c", a=3), in_=kvo[rs, :, 128:256]), [kvo], [qkv_bf])
            k.dma("pool", scr_qk[lt * 128:lt * 128 + rows, :], qkv_bf[rs, :], reads=[qkv_bf], writes=[scr_qk])
            if full:
                gi_ = (NOWN + 1) if sample else (lt - (NPRE - 1))
                k.dma("pool", scr_gate[gi_ * 128:gi_ * 128 + rows, :], gate_sb[rs, :], reads=[gate_sb], writes=[scr_gate])
        ck(5)
        if full:
            z = projr(1304, 1816)
            k.act(qb[rs, :], z[rs, :], AF.Silu, [z], [qb])
        z = projr(1816, 2328)
        k.act(u_sb[rs, :], z[rs, :], AF.Sigmoid, [z], [u_sb])
        k.tt("dve", u_sb[rs, :], u_sb[rs, :], oml[rs, :], ALU.mult, [u_sb, oml], [u_sb])
        k.tt("dve", u_sb[rs, :], u_sb[rs, :], lb[rs, :], ALU.add, [u_sb, lb], [u_sb])
        k.act(logf[rs, :], u_sb[rs, :], AF.Ln, [u_sb], [logf])
        k.ts("pool", kb[rs, :], u_sb[rs, :], -1.0, 1.0, ALU.mult, ALU.add, [u_sb], [kb])
        z = projr(2328, 2840)
        k.op("act", lambda e: e.copy(out=vb[rs, :], in_=z[rs, :]), [z], [vb])
        if full:
            z = projr(2840, 3352)
            k.act(ex[rs, :], z[rs, :], AF.Silu, [z], [ex])
            k.tt("dve", ggb[rs, :], ex[rs, :], gn_bc[rs, :, :].rearrange("p h v -> p (h v)"), ALU.mult, [ex, gn_bc], [ggb])
        ck(6)
        mi = 3 if sample else 1
        nch = 4 if sample else 2
        i0 = 2 if sample else 0
        zD = psH[0]
        k.mm(zD[rs, :], [(cm[rs, mi + 1, rs], logf[rs, :])], [cm, logf], [zD])
        k.act(ex[rs, :], zD[rs, :], AF.Exp, [zD], [ex])
        k.tt("dve", kd[rs, :], kb[rs, :], ex[rs, :], ALU.mult, [kb, ex], [kd])
        if full:
            zB = psH[1]
            k.mm(zB[rs, :], [(cm[rs, mi, rs], logf[rs, :])], [cm, logf], [zB])
            k.act(ex[rs, :], zB[rs, :], AF.Exp, [zB], [ex])
            k.tt("dve", qe[rs, :], qb[rs, :], ex[rs, :], ALU.mult, [qb, ex], [qe])
            k.act(ex[rs, :], zB[rs, :], AF.Exp, [zB], [ex], scale=-1.0)
            k.tt("dve", ke[rs, :], kb[rs, :], ex[rs, :], ALU.mult, [kb, ex], [ke])
            for h in range(4):
                k.op("pe", lambda e, h=h: e.transpose(psT[:, h, rs], qe[rs, h * 128:(h + 1) * 128], ident_bf[rs, rs]), [qe, ident_bf], [psT], tag=("T",))
                k.op("pe", lambda e, h=h: e.transpose(psT[:, 4 + h, rs], ke[rs, h * 128:(h + 1) * 128], ident_bf[rs, rs]), [ke, ident_bf], [psT], tag=("T",))
            k.op("act", lambda e: e.copy(out=qkT[:, :, rs], in_=psT[:, :, rs]), [psT], [qkT])
            ccb = 2 if sample else 0
            for c in range(nch):
                k.tt("dve", qeTm[c][:, :, rs], qkT[:, 0:4, rs], ccol[:, ccb + c, rs].unsqueeze(1).to_broadcast([P, 4, rows]), ALU.mult,
                     [qkT, ccol], [qeTm[c]])
            for h in range(4):
                k.mm(psA[rs, h, rs], [(qkT[:, 4 + h, rs], qkT[:, h, rs])], [qkT], [psA])
            k.tt("dve", AmT[rs, :, rs], psA[rs, :, rs], cm[rs, mi, rs].unsqueeze(1).to_broadcast([rows, 4, rows]), ALU.mult,
                 [psA, cm], [AmT])
        first_o = [True]
        for c in range(nch):
            st = S32[0] if not sample else S32[1 + c]
            sbf = Sbf[0] if not sample else Sbf[1 + c]
            ind = ci[rs, i0 + c:i0 + c + 1]
            if full:
                for h in range(4):
                    fo = first_o[0]
                    first_o[0] = False
                    k.op("pe", lambda e, h=h, fo=fo, c=c, sbf=sbf: e.matmul(psO[rs, h, :], lhsT=qeTm[c][:, h, rs], rhs=sbf[:, h, :], start=fo, stop=False,
                                                                       skip_group_check=True), [qeTm[c], sbf], [psO])
            for h in range(4):
                k.mm(psS[:, h, 0:1], [(logf[rs, h * 128:(h + 1) * 128], ind)], [logf, ci], [psS])
            k.act(dec[:, 4 * c:4 * c + 4], psS[:, :, 0], AF.Exp, [psS], [dec])
            k.ts("pool", kdm[rs, :], kd[rs, :], ind, None, ALU.mult, None, [kd, ci], [kdm])
            for h in range(4):
                k.mm(psS[:, h, :], [(kdm[rs, h * 128:(h + 1) * 128], vb[rs, h * 128:(h + 1) * 128])], [kdm, vb], [psS])
            for h in range(4):
                k.op("dve", lambda e, h=h: e.scalar_tensor_tensor(out=st[:, h, :], in0=st[:, h, :], scalar=dec[:, 4 * c + h:4 * c + h + 1],
                                                                  in1=psS[:, h, :], op0=ALU.mult, op1=ALU.add),
                     [st, dec, psS], [st])
            k.op("act", lambda e: e.copy(out=sbf[:], in_=st[:]), [st], [sbf])
        if full:
            for h in range(4):
                k.op("pe", lambda e, h=h: e.matmul(psO[rs, h, :], lhsT=AmT[rs, h, rs], rhs=vb[rs, h * 128:(h + 1) * 128], start=False, stop=True,
                                                   skip_group_check=True), [AmT, vb], [psO])
            k.act(osq[rs, :], psO[rs, :, :].rearrange("p h v -> p (h v)"), AF.Square, [psO], [osq])
            k.op("dve", lambda e: e.tensor_reduce(out=st8[rs, 0:4], in_=osq[rs, :].rearrange("p (h v) -> p h v", h=4), axis=AX.X, op=ALU.add), [osq], [st8])
            k.ts("dve", st8[rs, 0:4], st8[rs, 0:4], 1.0 / 128, EPS, ALU.mult, ALU.add, [st8], [st8])
            k.act(st8[rs, 0:4], st8[rs, 0:4], AF.Sqrt, [st8], [st8])
            k.op("dve", lambda e: e.reciprocal(out=st8[rs, 4:8], in_=st8[rs, 0:4]), [st8], [st8])
            for h in range(4):
                k.op("dve", lambda e, h=h: e.scalar_tensor_tensor(out=oab[rs, 512 + h * 128:512 + (h + 1) * 128], in0=psO[rs, h, :], scalar=st8[rs, 4 + h:5 + h],
                                                                  in1=ggb[rs, h * 128:(h + 1) * 128], op0=ALU.mult, op1=ALU.mult),
                     [psO, st8, ggb], [oab])
            ck(7)
            ti = (NOWN + 1) if sample else (lt - (NPRE - 1))
            k.dma("pool", scr_oab[ti * 128:ti * 128 + rows, 512:1024], oab[rs, 512:1024], reads=[oab], writes=[scr_oab])
            if sample and npool == 0:
                k.dma("pool", scr_oab[ti * 128:ti * 128 + rows, 0:512], oab[rs, 0:512], reads=[oab], writes=[scr_oab])

    for lt in range(NT - ntp, NT):
        own = lt >= NPRE
        full = lt >= NPRE - 1
        kv_dst = None
        if own:
            r0 = (lt - NPRE) * 128
            kv_dst = [(o_kv[i, r0:r0 + 128, :], o_kv) for i in range(3)]
        do_tile(lt, 128, xloc[lt * 128:(lt + 1) * 128, :], cs_tab[lt * 128:(lt + 1) * 128, :], full, kv_dst, False)
    k.dma("sp", o_hg_p[:].rearrange("h k v -> k h v"), S32[0][:], reads=[S32[0]], writes=[o_hg_p])
    if with_sample:
        kv_dst = [(o_kvs[0, :, :], o_kvs), (o_kvs[1, :, :], o_kvs), None]
        do_tile(NT, 32, xs[:, :], cs_tab[NT * 128:NT * 128 + 32, :], True, kv_dst, True)
        for b in range(4):
            k.dma("sp", o_win_s[b, 504:512, :], kvo_2[NT % 2][8 * b:8 * b + 8, 2, :], reads=[kvo_2[NT % 2]], writes=[o_win_s])
            k.dma("sp", o_hg_s[b].rearrange("h k v -> k h v"), S32[1 + b][:], reads=[S32[1 + b]], writes=[o_hg_s])
    k.pop()
    ck(10)
    pass1b(k, ck, ntp, locals())
    ck(20)
    env = dict(locals())
    env["identf_d"] = Buf("identf", k.nc_identf)
    env.update(k.shared)
    if with_sample and npool > 0:
        pass1c(k, ck, env, npool)
    ck(25)
    env["scr_x1"] = k.dram("scr_x1", [(NOWN + 1) * 128 + 32, D], F32)
    pass2(k, ck, env)
    ck(30)
    pass3(k, ck, env)


def pass1b(k, ck, ntp, env):
    P = 128
    scr_qk, scr_gate, scr_oab = env["scr_qk"], env["scr_gate"], env["scr_oab"]
    w1bd_d = k.dram("w1bd", [128, 64 * 128], F32, "ExternalInput")
    w2bd_d = k.dram("w2bd", [128, 2 * 128], F32, "ExternalInput")
    posvec_d = k.dram("posvec", [128, 64], F32, "ExternalInput")
    cover_d = k.dram("cover", [128, 2 * 62], F32, "ExternalInput")
    cmpB_d = k.dram("cmpB", [NOWN + 1, 128, 2 * 128], F32, "ExternalInput")
    selc_d = k.dram("selc", [NOWN + 1, 128, 2 * 64], F32, "ExternalInput")
    efull_d = k.dram("efull", [64, 4096], F32, "ExternalInput")
    triB_d = k.dram("triB", [128, 2 * 128], F32, "ExternalInput")
    pfx_d = k.dram("pfx", [1, 128], F32, "ExternalInput")
    identf_d = k.dram("identf", [128, 128], F32, "ExternalInput")
    k.nc_identf = identf_d.t
    k.shared = dict(w1bd_d=w1bd_d, w2bd_d=w2bd_d, posvec_d=posvec_d)

    k.push()
    W1 = k.sb([P, 64, 128], BF16, "W1")
    for a in range(4):
        k.dma("pool", W1[:, a * 16:(a + 1) * 16, :], w1bd_d[:, a * 2048:(a + 1) * 2048].rearrange("p (a b) -> p a b", b=128), writes=[W1])
    W2 = k.sb([P, 2, 128], BF16, "W2")
    k.dma("pool", W2[:], w2bd_d[:].rearrange("p (a b) -> p a b", b=128), writes=[W2])
    posv = k.sb([P, 64], BF16, "posv")
    k.dma("pool", posv[:], posvec_d[:], writes=[posv])
    efull = k.sb([P, 4096], BF16, "efull")
    k.dma("pool", efull[0:64, :], efull_d[:], writes=[efull])
    k.dma("pool", efull[64:128, :], efull_d[:], writes=[efull])
    triB = k.sb([P, 2, 4, 128], BF16, "triB")
    for hh in range(4):
        k.dma("pool", triB[:, :, hh, :], triB_d[:].rearrange("p (a b) -> p a b", b=128), writes=[triB])
    pfx = k.sb([1, 128], BF16, "pfx")
    k.dma("pool", pfx[:], pfx_d[:], writes=[pfx])
    ones_row = k.sb([1, 512], BF16, "ones_row")
    k.op("dve", lambda e: e.memset(ones_row[:], 1.0), [], [ones_row])
    identf = k.sb([P, 128], F32, "identf")
    k.dma("sp", identf[:], identf_d[:], writes=[identf])
    ident_bf = k.sb([P, 128], BF16, "ident_bf2")
    k.op("dve", lambda e: e.tensor_copy(out=ident_bf[:], in_=identf[:]), [identf], [ident_bf])
    KS = k.sb([P, 2, NT * 128], BF16, "KS")
    KC2 = k.sb([P, 2, 256], BF16, "KC2")
    VX = k.sb([P, 2, NT, 2, 65], BF16, "VX")
    kvccT = k.sb([P, 2, 256], BF16, "kvccT")
    VcX = k.sb([P, 2, 2, 127], BF16, "VcX")
    k.op("pool", lambda e: e.memset(KC2[:], 0.0), [], [KC2])
    k.op("pool", lambda e: e.memset(kvccT[:], 0.0), [], [kvccT])
    k.op("dve", lambda e: e.memset(VX[:, :, :, :, 64:65], 1.0), [], [VX])
    k.op("dve", lambda e: e.memset(VcX[:, :, :, 0:64], 0.0), [], [VcX])
    k.op("dve", lambda e: e.memset(VcX[:, :, :, 64:65], 1.0), [], [VcX])
    for g in range(2):
        k.dma("pool", VcX[:, :, g, 65:127], cover_d[:].rearrange("p (a b) -> p a b", b=62), writes=[VcX])
    qkv = [k.sb([P, 1280], BF16, "qkv%d" % i) for i in range(2)]
    qT = k.sb([P, 4, 128], BF16, "qT")
    spre = k.sb([P, 2, 8], BF16, "spre")
    posW = k.sb([P, 2], F32, "posW")
    cmpB = k.sb([P, 2, 4, 128], BF16, "cmpB_t")
    selc = k.sb([P, 2, 64], F32, "selc_t")
    gate = k.sb([P, 24], F32, "gate_t")
    PT = [k.sb([P, 1024], BF16, "PT%d" % i) for i in range(2)]
    ocs = k.sb([P, 2, 4, 127], F32, "ocs")
    osw = k.sb([P, 2, 4, 65], F32, "osw")
    rc = k.sb([P, 8], F32, "rc")
    coef = k.sb([P, 8], F32, "coef")
    imp = k.sb([P, 2, 64], F32, "imp")
    sc2 = k.sb([P, 64], F32, "sc2")
    top = k.sb([P, 16], F32, "top")
    nsel = k.sb([P, 2, 128], F32, "nsel")
    selT = k.sb([P, 2, 4, 128], BF16, "selT")
    oacc = k.sb([P, 8, 64], F32, "oacc")
    oa_bf = k.sb([P, 512], BF16, "oa_bf")
    psT = k.ps([P, 8, 128], BF16, "psTb")
    psZ = [k.ps([P, 1024], F32, "psZb%d" % i) for i in range(2)]
    psV = [k.ps([P, 512], F32, "psVb%d" % i) for i in range(2)]
    psC = k.ps([P, 512], F32, "psCb")

    for j in range(2):
        for idx in range(32):
            k.op("pe", lambda e, j=j, idx=idx: e.matmul(psC[:, j:j + 1], lhsT=W1[:, j * 32 + idx, :], rhs=posv[:, j * 32 + idx:j * 32 + idx + 1],
                                                        start=(idx == 0), stop=(idx == 31)), [W1, posv], [psC])
    k.op("act", lambda e: e.copy(out=posW[:], in_=psC[:, 0:2]), [psC], [posW])
    ck(11)
    zi = [0]
    pi = [0]

    pend = [None]

    cur = []

    def emit():
        blocks = list(cur)
        del cur[:]
        if not blocks:
            return
        zi[0] ^= 1
        S = psZ[zi[0]]
        for j, (g, k_lhsT, biases, v_rhs, acc_ap, ncol, first, ab) in enumerate(blocks):
            pairs = [(k_lhsT, qT[g * 64:(g + 1) * 64, :, :].rearrange("p a b -> p (a b)"))] + biases
            n = len(pairs)
            for i, (l, r) in enumerate(pairs):
                k.op("pe", lambda e, l=l, r=r, i=i, j=j, n=n: e.matmul(S[:, j * 512:(j + 1) * 512], lhsT=l, rhs=r, start=(i == 0), stop=(i == n - 1)),
                     [KS, kvccT, qT, efull, selT, ident_bf, triB, cmpB, pfx, ones_row], [S], tag=("S", l.partition_size(), l.base_partition()))
        if pend[0] is not None:
            f = pend[0]
            pend[0] = None
            f()
        pi[0] ^= 1
        pt = PT[pi[0]]
        nb_ = len(blocks)
        k.act(pt[:, 0:nb_ * 512], S[:, 0:nb_ * 512], AF.Exp, [S], [pt])

        def pv():
            for j, (g, k_lhsT, biases, v_rhs, acc_ap, ncol, first, ab) in enumerate(blocks):
                for hh in range(4):
                    k.op("pe", lambda e, hh=hh, j=j, acc_ap=acc_ap, ncol=ncol, v_rhs=v_rhs, first=first: e.matmul(
                        acc_ap[:, hh, 0:ncol], lhsT=pt[:, j * 512 + hh * 128:j * 512 + (hh + 1) * 128], rhs=v_rhs,
                        start=(first and hh == 0), stop=False, skip_group_check=True), [pt, VX, VcX], [ab], tag=("PV", ncol))
        pend[0] = pv

    def flush():
        emit()
        if pend[0] is not None:
            f = pend[0]
            pend[0] = None
            f()

    def attn_block(g, k_lhsT, biases, v_rhs, acc_ap, ncol, first):
        cur.append((g, k_lhsT, biases, v_rhs, acc_ap, ncol, first, acc_buf[0]))
        if len(cur) == 2:
            emit()

    acc_buf = [None]

    for lt in range(NT - ntp, NT):
        full = lt >= NPRE - 1
        i_own = lt - (NPRE - 1)
        qk = qkv[lt % 2]
        k.dma("sp", qk[:], scr_qk[lt * 128:(lt + 1) * 128, :], reads=[scr_qk], writes=[qk])
        v14 = qk[:, 0:896].rearrange("p (h d) -> p h d", d=64)
        if full:
            for hh in range(4):
                k.op("pe", lambda e, hh=hh: e.transpose(psT[:, hh, :], qk[:, hh * 128:(hh + 1) * 128], ident_bf[:]), [qk, ident_bf], [psT], tag=("T",))
        k.op("pe", lambda e: e.transpose(psT[:, 4, :], qk[:, 512:640], ident_bf[:]), [qk, ident_bf], [psT], tag=("T",))
        k.op("pe", lambda e: e.transpose(psT[:, 5, :], qk[:, 640:768], ident_bf[:]), [qk, ident_bf], [psT], tag=("T",))
        k.op("pe", lambda e: e.transpose(psT[:, 6, :], qk[:, 768:896], ident_bf[:]), [qk, ident_bf], [psT], tag=("T",))
        k.op("pe", lambda e: e.transpose(psT[:, 7, :], qk[:, 896:1024], ident_bf[:]), [qk, ident_bf], [psT], tag=("T",))
        if full:
            k.op("act", lambda e: e.copy(out=qT[:], in_=psT[:, 0:4, :]), [psT], [qT])
        k.op("act", lambda e: e.copy(out=KS[:, :, lt * 128:(lt + 1) * 128], in_=psT[:, 5:7, :]), [psT], [KS])
        k.op("dve", lambda e: e.tensor_copy(out=KC2[:, :, 0:128], in_=KC2[:, :, 128:256]), [KC2], [KC2])
        k.op("dve", lambda e: e.tensor_copy(out=KC2[:, :, 128:256], in_=psT[:, 4:8:3, :]), [psT], [KC2])
        k.op("pool", lambda e: e.tensor_copy(out=VX[:, :, lt, :, 0:64], in_=qk[:, 1024:1280].rearrange("p (a g d) -> p a g d", a=2, g=2)), [qk], [VX])
        c0 = max(8 * lt - 1, 0)
        nb = 8 * lt + 7 - c0
        for j in range(2):
            n = 0
            for r in range(2):
                for i16 in range(16):
                    st_col = 16 * (c0 + r) + i16 - 128 * lt + 128
                    k.op("pe", lambda e, j=j, r=r, i16=i16, st_col=st_col, n=n: e.matmul(
                        psC[:, 8 * j:8 * j + nb], lhsT=W1[:, j * 32 + r * 16 + i16, :], rhs=KC2[:, j, st_col:st_col + 16 * (nb - 1) + 1:16],
                        start=(n == 0), stop=(n == 31)), [W1, KC2], [psC], tag=("CMP", j))
                    n += 1
            k.act(spre[:, j, 0:nb], psC[:, 8 * j:8 * j + nb], AF.Silu, [psC, posW], [spre], bias=posW[:, j:j + 1])
        for j in range(2):
            k.mm(psC[:, 16 + 8 * j:16 + 8 * j + nb], [(W2[:, j, :], spre[:, j, 0:nb])], [W2, spre], [psC])
        k.op("act", lambda e: e.copy(out=kvccT[:, :, c0:c0 + nb], in_=psC[:, 16:32].rearrange("p (j c) -> p j c", j=2)[:, :, 0:nb]), [psC], [kvccT])
        if not full:
            continue
        ck(12)
        k.dma("sp", selc[:], selc_d[i_own].rearrange("p (a b) -> p a b", b=64), writes=[selc])
        k.dma("sp", gate[:], scr_gate[i_own * 128:(i_own + 1) * 128, :], reads=[scr_gate], writes=[gate])
        for hh in range(4):
            k.dma("pool", cmpB[:, :, hh, :], cmpB_d[i_own].rearrange("p (a b) -> p a b", b=128), writes=[cmpB])
        nchv = 1 if lt <= 15 else 2
        for ch in range(nchv):
            k.op("pe", lambda e, ch=ch: e.transpose(psT[:, ch, :], kvccT[:, 1, ch * 128:(ch + 1) * 128], ident_bf[:]), [kvccT, ident_bf], [psT], tag=("T",))
        k.op("act", lambda e: e.copy(out=VcX[:, 0:nchv, :, 0:64], in_=psT[:, 0:nchv, :].rearrange("p c (g d) -> p c g d", g=2)), [psT], [VcX])
        for g in range(2):
            acc_buf[0] = psV[g]
            acc = psV[g][:, 0:508].rearrange("p (h c) -> p h c", c=127)
            for ch in range(nchv):
                attn_block(g, kvccT[g * 64:(g + 1) * 64, 0, ch * 128:(ch + 1) * 128],
                           [(ident_bf[:], cmpB[:, ch, :, :].rearrange("p a b -> p (a b)"))],
                           VcX[:, ch, g, :], acc, 127, ch == 0)
            flush()
            k.op("act", lambda e, g=g, acc=acc: e.copy(out=ocs[:, g, :, :], in_=acc), [psV[g]], [ocs])
        ck(13)
        k.ts("dve", rc[:], ocs[:, :, :, 64].rearrange("p g h -> p (g h)"), 1e-30, None, ALU.max, None, [ocs], [rc])
        k.op("dve", lambda e: e.reciprocal(out=rc[:], in_=rc[:]), [rc], [rc])
        gv = gate[:].rearrange("p (h b) -> p h b", b=3)
        k.tt("dve", coef[:], rc[:], gv[:, :, 0], ALU.mult, [rc, gate], [coef])
        for h in range(8):
            k.ts("dve", oacc[:, h, :], ocs[:, h // 4, h % 4, 0:64], coef[:, h:h + 1], None, ALU.mult, None, [ocs, coef], [oacc])
        for g in range(2):
            k.ts("dve", imp[:, g, 0:62], ocs[:, g, 0, 65:127], rc[:, 4 * g:4 * g + 1], None, ALU.mult, None, [ocs, rc], [imp])
            for hh in range(1, 4):
                k.op("dve", lambda e, g=g, hh=hh: e.scalar_tensor_tensor(out=imp[:, g, 0:62], in0=ocs[:, g, hh, 65:127], scalar=rc[:, 4 * g + hh:4 * g + hh + 1],
                                                                         in1=imp[:, g, 0:62], op0=ALU.mult, op1=ALU.add), [ocs, rc, imp], [imp])
        k.op("dve", lambda e: e.memset(imp[:, :, 62:64], 0.0), [], [imp])
        for g in range(2):
            k.tt("dve", imp[:, g, :], imp[:, g, :], selc[:, 0, :], ALU.mult, [imp, selc], [imp])
            k.tt("dve", imp[:, g, :], imp[:, g, :], selc[:, 1, :], ALU.add, [imp, selc], [imp])
            k.op("dve", lambda e, g=g: e.max(out=top[:, 0:8], in_=imp[:, g, :]), [imp], [top])
            k.op("dve", lambda e, g=g: e.match_replace(out=sc2[:], in_to_replace=top[:, 0:8], in_values=imp[:, g, :], imm_value=-1e30), [imp, top], [sc2])
            k.op("dve", lambda e: e.max(out=top[:, 8:16], in_=sc2[:]), [sc2], [top])
            k.ts("dve", top[:, 15:16], top[:, 15:16], -0.5, None, ALU.max, None, [top], [top])
            k.ts("dve", nsel[:, g, 0:64], imp[:, g, :], top[:, 15:16], None, ALU.is_ge, None, [imp, top], [nsel])
            k.ts("dve", nsel[:, g, 0:64], nsel[:, g, 0:64], 30000.0, -30000.0, ALU.mult, ALU.add, [nsel], [nsel])
            k.op("dve", lambda e, g=g: e.tensor_copy(out=nsel[:, g, 64:128], in_=nsel[:, g, 0:64]), [nsel], [nsel])
            k.mm(psC[:, 64 + 128 * g:64 + 128 * (g + 1)], [(nsel[:, g, :], identf[:])], [nsel, identf], [psC])
        for hh in range(4):
            k.op("act", lambda e, hh=hh: e.copy(out=selT[:, :, hh, :], in_=psC[:, 64:320].rearrange("p (g t) -> p g t", g=2)), [psC], [selT])
        ck(14)
        for g in range(2):
            acc_buf[0] = psV[g]
            acc = psV[g][:, 0:260].rearrange("p (h c) -> p h c", c=65)
            for kb in range(0, lt + 1):
                biases = [(efull[g * 64:(g + 1) * 64, kb * 128:(kb + 1) * 128], selT[g * 64:(g + 1) * 64, g, :, :].rearrange("p a b -> p (a b)"))]
                if kb == lt:
                    biases.append((ident_bf[:], triB[:, 0, :, :].rearrange("p a b -> p (a b)")))
                attn_block(g, KS[g * 64:(g + 1) * 64, 0, kb * 128:(kb + 1) * 128], biases, VX[:, 0, kb, g, :], acc, 65, kb == 0)
            flush()
            k.op("act", lambda e, g=g, acc=acc: e.copy(out=osw[:, g, :, :], in_=acc), [psV[g]], [osw])
        for br in (1, 2):
            if br == 2:
                for g in range(2):
                    acc_buf[0] = psV[g]
                    acc = psV[g][:, 0:260].rearrange("p (h c) -> p h c", c=65)
                    kb0 = max(lt - 4, 0)
                    for kb in range(kb0, lt + 1):
                        biases = []
                        if kb == lt:
                            biases.append((ident_bf[:], triB[:, 0, :, :].rearrange("p a b -> p (a b)")))
                        if kb == lt - 4:
                            biases.append((ident_bf[:], triB[:, 1, :, :].rearrange("p a b -> p (a b)")))
                        if kb < NPRE:
                            biases.append((pfx[:], ones_row[:]))
                        attn_block(g, KS[g * 64:(g + 1) * 64, 1, kb * 128:(kb + 1) * 128], biases, VX[:, 1, kb, g, :], acc, 65, kb == kb0)
                    flush()
                    k.op("act", lambda e, g=g, acc=acc: e.copy(out=osw[:, g, :, :], in_=acc), [psV[g]], [osw])
            k.ts("dve", rc[:], osw[:, :, :, 64].rearrange("p g h -> p (g h)"), 1e-30, None, ALU.max, None, [osw], [rc])
            k.op("dve", lambda e: e.reciprocal(out=rc[:], in_=rc[:]), [rc], [rc])
            k.tt("dve", coef[:], rc[:], gv[:, :, br], ALU.mult, [rc, gate], [coef])
            for h in range(8):
                k.op("dve", lambda e, h=h: e.scalar_tensor_tensor(out=oacc[:, h, :], in0=osw[:, h // 4, h % 4, 0:64], scalar=coef[:, h:h + 1],
                                                                  in1=oacc[:, h, :], op0=ALU.mult, op1=ALU.add), [osw, coef, oacc], [oacc])
        k.op("act", lambda e: e.copy(out=oa_bf[:], in_=oacc[:].rearrange("p h d -> p (h d)")), [oacc], [oa_bf])
        k.dma("pool", scr_oab[i_own * 128:(i_own + 1) * 128, 0:512], oa_bf[:], reads=[oa_bf], writes=[scr_oab])
    k.pop()


def pass2(k, ck, env):
    P = 128
    xloc, xs, w_in, scr_hT, scr_oab = env["xloc"], env["xs"], env["w_in"], env["scr_hT"], env["scr_oab"]
    w_br_d = k.dram("w_branch", [D, D], F32, "ExternalInput")
    w_out_d = k.dram("w_out", [D, D], F32, "ExternalInput")
    identf_d = env["identf_d"]
    scr_x1 = env["scr_x1"]
    k.push()
    wmg = k.sb([P, 8, 2048], BF16, "wmg")
    wbr = k.sb([P, 8, 1024], BF16, "wbr")
    wo = k.sb([P, 8, 1024], BF16, "wo")
    for kc in range(8):
        k.dma("pool", wmg[:, kc, :], w_in[kc * 128:(kc + 1) * 128, NCOL1:5400], writes=[wmg])
        k.dma("pool", wbr[:, kc, :], w_br_d[kc * 128:(kc + 1) * 128, :], writes=[wbr])
        k.dma("pool", wo[:, kc, :], w_out_d[kc * 128:(kc + 1) * 128, :], writes=[wo])
    ident_bf = k.sb([P, 128], BF16, "ident_bf3")
    k.dma("pool", ident_bf[:], identf_d[:], writes=[ident_bf])
    xt = [k.sb([P, D], F32, "x2_%d" % i) for i in range(2)]
    hT = [k.sb([P, 8, 128], BF16, "hT2_%d" % i) for i in range(2)]
    oab = [k.sb([P, 1024], BF16, "oab2_%d" % i) for i in range(2)]
    mab = k.sb([P, 2048], BF16, "mab")
    oT = k.sb([P, 8, 128], BF16, "oT")
    m1 = k.sb([P, 1024], F32, "m1")
    m2 = k.sb([P, 1024], F32, "m2")
    mbf = k.sb([P, 1024], BF16, "mbf")
    mT = k.sb([P, 8, 128], BF16, "mT")
    x1 = k.sb([P, 1024], F32, "x1")
    psT = k.ps([P, 8, 128], BF16, "psT2")
    psZ = [k.ps([P, 512], F32, "psZ2_%d" % i) for i in range(3)]
    zi = [0]

    def nz():
        zi[0] = (zi[0] + 1) % 3
        return psZ[zi[0]]

    for ti in range(NOWN + 2):
        sample = ti == NOWN + 1
        rows = 32 if sample else 128
        rs = slice(0, rows)
        x = xt[ti % 2]
        h = hT[ti % 2]
        ob = oab[ti % 2]
        xsrc = xs[:, :] if sample else xloc[(NPRE - 1 + ti) * 128:(NPRE + ti) * 128, :]
        k.dma("sp", x[rs, :], xsrc, writes=[x])
        k.dma("sp", h[:, :, rs], scr_hT[ti, :, :].rearrange("p (a b) -> p a b", b=128)[:, :, rs], reads=[scr_hT], writes=[h])
        k.dma("sp", ob[rs, :], scr_oab[ti * 128:ti * 128 + rows, :], reads=[scr_oab], writes=[ob])
        for c in range(4):
            z = nz()
            k.mm(z[rs, :], [(h[:, kc, rs], wmg[:, kc, c * 512:(c + 1) * 512]) for kc in range(8)], [h, wmg], [z])
            k.act(mab[rs, c * 512:(c + 1) * 512], z[rs, :], AF.Sigmoid, [z], [mab])
        for kc in range(8):
            k.op("pe", lambda e, kc=kc: e.transpose(psT[:, kc, rs], ob[rs, kc * 128:(kc + 1) * 128], ident_bf[rs, rs]), [ob, ident_bf], [psT], tag=("T",))
        k.op("act", lambda e: e.copy(out=oT[:, :, rs], in_=psT[:, :, rs]), [psT], [oT])
        for c in range(2):
            z = nz()
            k.mm(z[rs, :], [(oT[:, kc, rs], wbr[:, kc, c * 512:(c + 1) * 512]) for kc in range(4)], [oT, wbr], [z])
            k.tt("dve", m1[rs, c * 512:(c + 1) * 512], z[rs, :], mab[rs, c * 512:(c + 1) * 512], ALU.mult, [z, mab], [m1])
            z = nz()
            k.mm(z[rs, :], [(oT[:, kc, rs], wbr[:, kc, c * 512:(c + 1) * 512]) for kc in range(4, 8)], [oT, wbr], [z])
            k.tt("dve", m2[rs, c * 512:(c + 1) * 512], z[rs, :], mab[rs, 1024 + c * 512:1024 + (c + 1) * 512], ALU.mult, [z, mab], [m2])
        k.tt("pool", mbf[rs, :], m1[rs, :], m2[rs, :], ALU.add, [m1, m2], [mbf])
        for kc in range(8):
            k.op("pe", lambda e, kc=kc: e.transpose(psT[:, kc, rs], mbf[rs, kc * 128:(kc + 1) * 128], ident_bf[rs, rs]), [mbf, ident_bf], [psT], tag=("T",))
        k.op("act", lambda e: e.copy(out=mT[:, :, rs], in_=psT[:, :, rs]), [psT], [mT])
        for c in range(2):
            z = nz()
            k.mm(z[rs, :], [(mT[:, kc, rs], wo[:, kc, c * 512:(c + 1) * 512]) for kc in range(8)], [mT, wo], [z])
            k.tt("dve", x1[rs, c * 512:(c + 1) * 512], z[rs, :], x[rs, c * 512:(c + 1) * 512], ALU.add, [z, x], [x1])
        k.dma("pool", scr_x1[ti * 128:ti * 128 + rows, :], x1[rs, :], reads=[x1], writes=[scr_x1])
    k.pop()


def pass3(k, ck, env):
    P = 128
    scr_x1, identf_d = env["scr_x1"], env["identf_d"]
    f1_d = k.dram("ffn_w_in", [D, 5632], F32, "ExternalInput")
    f2_d = k.dram("ffn_w_out", [2816, D], F32, "ExternalInput")
    fvec_d = k.dram("fvec", [1, 1024], F32, "ExternalInput")
    cw_d = k.dram("convw", [128, 22 * 4], F32, "ExternalInput")
    cst_d = k.dram("convst", [128, 22 * 8], F32, "ExternalInput")
    o_y = k.dram("o_y", [NOWN * 128, D], F32, "ExternalOutput")
    o_ys = k.dram("o_ys", [32, D], F32, "ExternalOutput")
    o_cp = k.dram("o_cp", [128, 22 * 2], F32, "ExternalOutput")
    o_cs = k.dram("o_cs", [128, 22 * 8], F32, "ExternalOutput")
    k.push()
    wf1 = k.sb([P, 8, 5632], BF16, "wf1")
    wf2 = k.sb([P, 22, 1024], BF16, "wf2")
    for kc in range(8):
        k.dma("pool", wf1[:, kc, :], f1_d[kc * 128:(kc + 1) * 128, :], writes=[wf1])
    for fc in range(22):
        k.dma("pool", wf2[:, fc, :], f2_d[fc * 128:(fc + 1) * 128, :], writes=[wf2])
    ident_bf = k.sb([P, 128], BF16, "ident_bf4")
    k.dma("pool", ident_bf[:], identf_d[:], writes=[ident_bf])
    g2 = k.sb([P, D], F32, "g2")
    k.dma("sp", g2[:], fvec_d[0:1, :].partition_broadcast(P), writes=[g2])
    cw = k.sb([P, 22, 4], F32, "cw")
    k.dma("sp", cw[:], cw_d[:].rearrange("p (a b) -> p a b", b=4), writes=[cw])
    x1 = [k.sb([P, D], F32, "x3_%d" % i) for i in range(2)]
    junk = k.sb([P, D], BF16, "junk3")
    h2 = k.sb([P, D], BF16, "h2")
    h2T = k.sb([P, 8, 128], BF16, "h2T")
    st4 = k.sb([P, 8], F32, "st4_3")
    aT = k.sb([P, 22, 130], F32, "aT")
    t1 = k.sb([P, 11, 128], F32, "t1")
    t2 = k.sb([P, 11, 128], F32, "t2")
    gT = k.sb([P, 22, 128], BF16, "gT")
    y = k.sb([P, D], F32, "y")
    psT = k.ps([P, 8, 128], BF16, "psT3")
    psZ = [k.ps([P, 512], F32, "psZ3_%d" % i) for i in range(3)]
    zi = [0]

    def nz():
        zi[0] = (zi[0] + 1) % 3
        return psZ[zi[0]]

    k.op("dve", lambda e: e.memset(aT[:], 0.0), [], [aT])
    for ti in range(NOWN + 2):
        sample = ti == NOWN + 1
        rows = 32 if sample else 128
        rs = slice(0, rows)
        nbat, T = (4, 8) if sample else (1, 128)
        x = x1[ti % 2]
        k.dma("sp", x[rs, :], scr_x1[ti * 128:ti * 128 + rows, :], reads=[scr_x1], writes=[x])
        k.act(junk[rs, :], x[rs, :], AF.Square, [x], [junk, st4], accum_out=st4[rs, 0:1])
        k.ts("dve", st4[rs, 1:2], st4[rs, 0:1], 1.0 / D, EPS, ALU.mult, ALU.add, [st4], [st4])
        k.act(st4[rs, 2:3], st4[rs, 1:2], AF.Sqrt, [st4], [st4])
        k.op("dve", lambda e: e.reciprocal(out=st4[rs, 3:4], in_=st4[rs, 2:3]), [st4], [st4])
        k.op("dve", lambda e: e.scalar_tensor_tensor(out=h2[rs, :], in0=x[rs, :], scalar=st4[rs, 3:4], in1=g2[rs, :],
                                                     op0=ALU.mult, op1=ALU.mult), [x, st4, g2], [h2])
        for kc in range(8):
            k.op("pe", lambda e, kc=kc: e.transpose(psT[:, kc, rs], h2[rs, kc * 128:(kc + 1) * 128], ident_bf[rs, rs]), [h2, ident_bf], [psT], tag=("T",))
        k.op("act", lambda e: e.copy(out=h2T[:, :, rs], in_=psT[:, :, rs]), [psT], [h2T])
        av = aT[:, :, 0:nbat * (T + 2)].rearrange("p f (b t) -> p f b t", b=nbat)
        if sample:
            for b in range(4):
                k.dma("sp", av[:, :, b, 0:2], cst_d[:].rearrange("p (f b j) -> p f b j", f=22, b=4)[:, :, b, :], writes=[aT])
        for f0 in range(0, 22, 4):
            n = min(4, 22 - f0)
            z = nz()
            for j in range(n):
                fc = f0 + j
                k.mm(z[:, j * 128:j * 128 + rows], [(wf1[:, kc, fc * 128:(fc + 1) * 128], h2T[:, kc, rs]) for kc in range(8)], [wf1, h2T], [z])
            k.op("act", lambda e, f0=f0, n=n, z=z: e.copy(out=av[:, f0:f0 + n, :, 2:2 + T],
                                                       in_=z[:, 0:n * 128].rearrange("p (f t) -> p f t", t=128)[:, :, 0:rows].rearrange("p f (b t) -> p f b t", b=nbat)),
                 [z], [aT])
        for hf in range(2):
            fs = slice(hf * 11, hf * 11 + 11)
            tv1 = t1[:, :, 0:rows].rearrange("p f (b t) -> p f b t", b=nbat)
            tv2 = t2[:, :, 0:rows].rearrange("p f (b t) -> p f b t", b=nbat)

            def wb(j):
                return cw[:, fs, j:j + 1].unsqueeze(3).to_broadcast([P, 11, nbat, T])
            k.tt("dve", tv1, av[:, fs, :, 2:2 + T], wb(2), ALU.mult, [aT, cw], [t1])
            k.tt("pool", tv2, av[:, fs, :, 1:1 + T], wb(1), ALU.mult, [aT, cw], [t2])
            k.tt("dve", tv1, tv1, tv2, ALU.add, [t1, t2], [t1])
            k.tt("pool", tv2, av[:, fs, :, 0:T], wb(0), ALU.mult, [aT, cw], [t2])
            k.tt("dve", tv1, tv1, tv2, ALU.add, [t1, t2], [t1])
            k.tt("dve", tv1, tv1, wb(3), ALU.add, [t1, cw], [t1])
            k.act(t1[:, :, 0:rows], t1[:, :, 0:rows], AF.Silu, [t1], [t1])
            for f0 in range(hf * 11, hf * 11 + 11, 4):
                n = min(4, hf * 11 + 11 - f0)
                z = nz()
                for j in range(n):
                    fc = f0 + j
                    k.mm(z[:, j * 128:j * 128 + rows], [(wf1[:, kc, 2816 + fc * 128:2816 + (fc + 1) * 128], h2T[:, kc, rs]) for kc in range(8)], [wf1, h2T], [z])
                k.tt("dve", gT[:, f0:f0 + n, 0:rows], t1[:, f0 - hf * 11:f0 - hf * 11 + n, 0:rows],
                     z[:, 0:n * 128].rearrange("p (f t) -> p f t", t=128)[:, :, 0:rows], ALU.mult, [t1, z], [gT])
        for c in range(2):
            z = nz()
            k.mm(z[rs, :], [(gT[:, fc, rs], wf2[:, fc, c * 512:(c + 1) * 512]) for fc in range(22)], [gT, wf2], [z])
            k.tt("dve", y[rs, c * 512:(c + 1) * 512], z[rs, :], x[rs, c * 512:(c + 1) * 512], ALU.add, [z, x], [y])
        if sample:
            k.dma("pool", o_ys[:, :], y[rs, :], reads=[y], writes=[o_ys])
            for b in range(4):
                k.dma("pool", o_cs[:].rearrange("p (f b j) -> p f b j", f=22, b=4)[:, :, b, :], av[:, :, b, T:T + 2], reads=[aT], writes=[o_cs])
        else:
            if ti >= 1:
                k.dma("pool", o_y[(ti - 1) * 128:ti * 128, :], y[rs, :], reads=[y], writes=[o_y])
            if ti == NOWN:
                k.dma("pool", o_cp[:].rearrange("p (f j) -> p f j", j=2), aT[:, :, 128:130], reads=[aT], writes=[o_cp])
            k.op("pool", lambda e: e.tensor_copy(out=aT[:, :, 0:2], in_=aT[:, :, 128:130]), [aT], [aT])
    k.pop()


def pass1c(k, ck, env, npool):
    P = 128
    nc = k.nc
    scr_qk, scr_gate, scr_oab = env["scr_qk"], env["scr_gate"], env["scr_oab"]
    st_win = env["st_win"]
    identf_d = env["identf_d"]
    cpool = k.dram("cache_cmp", [npool, 128, 256], F32, "ExternalInput")
    spool = k.dram("cache_slc", [npool, 128, 256], F32, "ExternalInput")
    ptab_d = k.dram("ptab", [4, 128], I32, "ExternalInput")
    selcS_d = k.dram("selcS", [8, 2 * 384], F32, "ExternalInput")
    selm_d = k.dram("selm", [32, 8 + 16 * 32], F32, "ExternalInput")
    rep_d = k.dram("rep", [8, 32], F32, "ExternalInput")
    ef_d = k.dram("efull128", [128, 8192], F32, "ExternalInput")
    triS_d = k.dram("triBs", [128, 2 * 32], F32, "ExternalInput")
    w1bd_d, w2bd_d, posvec_d = env["w1bd_d"], env["w2bd_d"], env["posvec_d"]
    k.push()
    W1 = k.sb([P, 64, 128], BF16, "W1c")
    for a in range(4):
        k.dma("pool", W1[:, a * 16:(a + 1) * 16, :], w1bd_d[:, a * 2048:(a + 1) * 2048].rearrange("p (a b) -> p a b", b=128), writes=[W1])
    W2 = k.sb([P, 2, 128], BF16, "W2c")
    k.dma("pool", W2[:], w2bd_d[:].rearrange("p (a b) -> p a b", b=128), writes=[W2])
    posv = k.sb([P, 64], BF16, "posvc")
    k.dma("pool", posv[:], posvec_d[:], writes=[posv])
    ef = k.sb([P, 8192], BF16, "ef128")
    k.dma("pool", ef[:], ef_d[:], writes=[ef])
    triS = k.sb([P, 2, 32], BF16, "triS")
    k.dma("pool", triS[:], triS_d[:].rearrange("p (a b) -> p a b", b=32), writes=[triS])
    identf = k.sb([P, 128], F32, "identfc")
    k.dma("sp", identf[:], identf_d[:], writes=[identf])
    ident_bf = k.sb([P, 128], BF16, "identbc")
    k.op("dve", lambda e: e.tensor_copy(out=ident_bf[:], in_=identf[:]), [identf], [ident_bf])
    selcS = k.sb([8, 2, 384], F32, "selcS")
    k.dma("sp", selcS[:], selcS_d[:].rearrange("p (a b) -> p a b", b=384), writes=[selcS])
    selm = k.sb([32, 8 + 512], F32, "selm")
    k.dma("sp", selm[:], selm_d[:], writes=[selm])
    rep = k.sb([8, 32], F32, "rep")
    k.dma("sp", rep[:], rep_d[:], writes=[rep])
    iot_d = k.dram("iot", [128, 1], F32, "ExternalInput")
    ptb = k.sb([P, 512], I32, "ptb")
    for b in range(4):
        k.dma("sp", ptb[:, b * 128:(b + 1) * 128], ptab_d[b:b + 1, :].partition_broadcast(P), writes=[ptb])
    io = k.sb([P, 1], F32, "iotc")
    k.dma("sp", io[:], iot_d[:], writes=[io])
    idxf = k.sb([P, 512], F32, "idxf")
    pidx = k.sb([P, 512], I32, "pidx")
    k.op("dve", lambda e: e.tensor_copy(out=idxf[:], in_=ptb[:]), [ptb], [idxf])
    k.op("dve", lambda e: e.tensor_scalar(out=idxf[:], in0=idxf[:], scalar1=128.0, scalar2=io[:, 0:1], op0=ALU.mult, op1=ALU.add), [idxf, io], [idxf])
    k.op("dve", lambda e: e.tensor_copy(out=pidx[:], in_=idxf[:]), [idxf], [pidx])
    KCb = k.sb([P, 2, 2048 + 16], BF16, "KCb")
    KSb = k.sb([P, 16384 + 128], BF16, "KSb")
    VXb = k.sb([P, 129, 2, 65], BF16, "VXb")
    KWb = k.sb([P, 640], BF16, "KWb")
    VWb = k.sb([P, 5, 2, 65], BF16, "VWb")
    kvc = k.sb([P, 2, 1024], BF16, "kvc")
    vcb = k.sb([P, 8, 128], BF16, "vcb")
    k.op("pool", lambda e: e.memset(KSb[:, 16384:16512], 0.0), [], [KSb])
    k.op("pool", lambda e: e.memset(KWb[:, 512:640], 0.0), [], [KWb])
    k.op("pool", lambda e: e.memset(VXb[:, :, :, 64:65], 1.0), [], [VXb])
    k.op("pool", lambda e: e.memset(VXb[:, 128, :, 0:64], 0.0), [], [VXb])
    k.op("pool", lambda e: e.memset(VWb[:, :, :, 64:65], 1.0), [], [VWb])
    k.op("pool", lambda e: e.memset(VWb[:, 4, :, 0:64], 0.0), [], [VWb])
    k.op("pool", lambda e: e.memset(kvc[:], 0.0), [], [kvc])
    pg = [k.sb([P, 2, 256], F32, "pg%d" % i) for i in range(8)]
    qs = k.sb([8, 1280], BF16, "qs")
    qTs = k.sb([P, 4, 8], BF16, "qTs")
    spre = k.sb([P, 2, 128], BF16, "sprec")
    posW = k.sb([P, 2], F32, "posWc")
    pS = k.sb([32, 1024], F32, "pS")
    pbf = k.sb([32, 1024], BF16, "pbf")
    pTs = k.sb([P, 8, 32], BF16, "pTs")
    st = k.sb([32, 8], F32, "stc")
    impr = k.sb([32, 256], F32, "impr")
    sc = k.sb([8, 2, 384], F32, "scc")
    sc2 = k.sb([8, 384], F32, "sc2c")
    top = k.sb([8, 16], F32, "topc")
    selTs = k.sb([P, 3, 2, 32], BF16, "selTs")
    PTs = [k.sb([P, 512], BF16, "PTs%d" % i) for i in range(2)]
    obr = k.sb([32, 3, 2, 65], F32, "obr")
    gate_r = k.sb([32, 2, 3], F32, "gate_r")
    rcs = k.sb([32, 8], F32, "rcs")
    ofin = k.sb([32, 2, 64], F32, "ofin")
    oas = k.sb([32, 512], BF16, "oas")
    psT = k.ps([P, 8, 128], BF16, "psTc")
    psF = k.ps([P, 4, 128], F32, "psFc")
    psS = k.ps([P, 1024], F32, "psSc")
    psC = k.ps([P, 512], F32, "psCc")
    psA = k.ps([P, 512], F32, "psAc")
    psO = k.ps([P, 512], F32, "psOc")
    psX = k.ps([P, 512], F32, "psXc")
    pendc = [None]
    for j in range(2):
        for idx in range(32):
            k.op("pe", lambda e, j=j, idx=idx: e.matmul(psC[:, j:j + 1], lhsT=W1[:, j * 32 + idx, :], rhs=posv[:, j * 32 + idx:j * 32 + idx + 1],
                                                        start=(idx == 0), stop=(idx == 31)), [W1, posv], [psC])
    k.op("act", lambda e: e.copy(out=posW[:], in_=psC[:, 0:2]), [psC], [posW])
    k.op("dve", lambda e: e.memset(pS[:], 0.0), [], [pS])

    def compress_group(gi):
        c0 = max(128 * gi - 1, 0)
        nb = 128 * gi + 127 - c0
        for j in range(2):
            n = 0
            for r in range(2):
                for i16 in range(16):
                    s0 = 16 * (c0 + r) + i16 - 2048 * gi + 16
                    k.op("pe", lambda e, j=j, r=r, i16=i16, s0=s0, n=n, nb=nb: e.matmul(
                        psC[:, 128 * j:128 * j + nb], lhsT=W1[:, j * 32 + r * 16 + i16, :], rhs=KCb[:, j, s0:s0 + 16 * (nb - 1) + 1:16],
                        start=(n == 0), stop=(n == 31)), [W1, KCb], [psC], tag=("CMPc", j))
                    n += 1
            k.act(spre[:, j, 0:nb], psC[:, 128 * j:128 * j + nb], AF.Silu, [psC, posW], [spre], bias=posW[:, j:j + 1])
        for j in range(2):
            k.mm(psC[:, 256 + 128 * j:256 + 128 * j + nb], [(W2[:, j, :], spre[:, j, 0:nb])], [W2, spre], [psC])
        k.op("act", lambda e, c0=c0, nb=nb: e.copy(out=kvc[:, :, c0:c0 + nb], in_=psC[:, 256:512].rearrange("p (j c) -> p j c", j=2)[:, :, 0:nb]), [psC], [kvc])
        k.op("dve", lambda e: e.tensor_copy(out=KCb[:, :, 0:16], in_=KCb[:, :, 2048:2064]), [KCb], [KCb])

    first_o = [True]
    rn = [0]
    for b in range(4):
        r0 = NT * 128 + 8 * b
        k.dma("sp", qs[:], scr_qk[r0:r0 + 8, :], reads=[scr_qk], writes=[qs])
        for hh in range(4):
            k.dma("sp", gate_r[hh * 8:hh * 8 + 8, :, :],
                  scr_gate[NOWN * 128 + 128 + 8 * b:NOWN * 128 + 128 + 8 * b + 8, :].rearrange("p (g x) -> p g x", g=2)[:, :, hh * 3:hh * 3 + 3],
                  reads=[scr_gate], writes=[gate_r])
        for hh in range(4):
            k.op("pe", lambda e, hh=hh: e.transpose(psT[:, hh, 0:8], qs[:, hh * 128:(hh + 1) * 128], ident_bf[0:8, 0:8]), [qs, ident_bf], [psT], tag=("T",))
        k.op("pe", lambda e: e.transpose(psT[:, 4, 0:8], qs[:, 640:768], ident_bf[0:8, 0:8]), [qs, ident_bf], [psT], tag=("T",))
        k.op("pe", lambda e: e.transpose(psT[:, 5, 0:8], qs[:, 768:896], ident_bf[0:8, 0:8]), [qs, ident_bf], [psT], tag=("T",))
        k.op("act", lambda e: e.copy(out=qTs[:], in_=psT[:, 0:4, 0:8]), [psT], [qTs])
        k.op("act", lambda e: e.copy(out=KSb[:, 16384:16392], in_=psT[:, 4, 0:8]), [psT], [KSb])
        k.op("act", lambda e: e.copy(out=KWb[:, 512:520], in_=psT[:, 5, 0:8]), [psT], [KWb])
        k.dma("sp", VXb[0:8, 128, :, 0:64], scr_qk[r0:r0 + 8, 1024:1152].rearrange("p (g d) -> p g d", g=2), reads=[scr_qk], writes=[VXb])
        k.dma("sp", VWb[0:8, 4, :, 0:64], scr_qk[r0:r0 + 8, 1152:1280].rearrange("p (g d) -> p g d", g=2), reads=[scr_qk], writes=[VWb])
        qflat = qTs[:].rearrange("p a b -> p (a b)")
        for p in range(128):
            t_pg = pg[p % 8]
            for which, pool_d in ((0, cpool), (1, spool)):
                k._sync("pool", [pidx], [t_pg])
                di = k.dnext
                k.dnext = (k.dnext + 1) % NDS
                k._wait("pool", ("d", di), k.dval[di])
                ins = nc.gpsimd.indirect_dma_start(out=t_pg[:, which, :], out_offset=None, in_=pool_d[:].rearrange("n r f -> (n r) f"),
                                                   in_offset=bass.IndirectOffsetOnAxis(ap=pidx[:, b * 128 + p:b * 128 + p + 1], axis=0))
                k.n_ins += 1
                k.dval[di] += 16
                ins.then_inc(k.dsems[di], 16)
                t_pg.w[("d", di)] = k.dval[di]
            k.op("pe", lambda e, t_pg=t_pg: e.transpose(psF[:, 0, :], t_pg[:, 0, 0:128], identf[:]), [t_pg, identf], [psF], tag=("T",))
            k.op("pe", lambda e, t_pg=t_pg: e.transpose(psF[:, 1, :], t_pg[:, 0, 128:256], identf[:]), [t_pg, identf], [psF], tag=("T",))
            k.op("pe", lambda e, t_pg=t_pg: e.transpose(psF[:, 2, :], t_pg[:, 1, 0:128], identf[:]), [t_pg, identf], [psF], tag=("T",))
            k.op("act", lambda e, p=p: e.copy(out=KCb[:, :, 16 + (p % 16) * 128:16 + (p % 16 + 1) * 128], in_=psF[:, 0:2, :]), [psF], [KCb])
            k.op("dve", lambda e, p=p: e.tensor_copy(out=KSb[:, p * 128:(p + 1) * 128], in_=psF[:, 2, :]), [psF], [KSb])
            k.op("dve", lambda e, p=p, t_pg=t_pg: e.tensor_copy(out=VXb[:, p, :, 0:64], in_=t_pg[:, 1, 128:256].rearrange("p (g d) -> p g d", g=2)), [t_pg], [VXb])
            if p % 16 == 15:
                compress_group(p // 16)
        for i in range(4):
            t_pg = pg[i % 8]
            k.dma("sp", t_pg[:, 0, :], st_win[b, i * 128:(i + 1) * 128, :], writes=[t_pg])
            k.op("pe", lambda e, t_pg=t_pg: e.transpose(psF[:, 3, :], t_pg[:, 0, 0:128], identf[:]), [t_pg, identf], [psF], tag=("T",))
            k.op("act", lambda e, i=i: e.copy(out=KWb[:, i * 128:(i + 1) * 128], in_=psF[:, 3, :]), [psF], [KWb])
            k.op("dve", lambda e, i=i, t_pg=t_pg: e.tensor_copy(out=VWb[:, i, :, 0:64], in_=t_pg[:, 0, 128:256].rearrange("p (g d) -> p g d", g=2)), [t_pg], [VWb])
        for ch in range(8):
            k.op("pe", lambda e, ch=ch: e.transpose(psT[:, ch, :], kvc[:, 1, ch * 128:(ch + 1) * 128], ident_bf[:]), [kvc, ident_bf], [psT], tag=("T",))
        k.op("act", lambda e: e.copy(out=vcb[:], in_=psT[:]), [psT], [vcb])
        for g in range(2):
            gs = slice(g * 64, (g + 1) * 64)
            for hf in range(2):
                k.mm(psS[0:32, hf * 512:(hf + 1) * 512], [(qflat[gs, :], kvc[gs, 0, hf * 512:(hf + 1) * 512])], [qTs, kvc], [psS])
            k.op("dve", lambda e: e.tensor_reduce(out=st[:, 0:1], in_=psS[0:32, 0:1023], axis=AX.X, op=ALU.max), [psS], [st])
            k.ts("dve", st[:, 1:2], st[:, 0:1], -1.0, None, ALU.mult, None, [st], [st])
            k.act(pS[:, 0:1023], psS[0:32, 0:1023], AF.Exp, [psS, st], [pS, st], bias=st[:, 1:2], accum_out=st[:, 2:3])
            k.op("dve", lambda e: e.reciprocal(out=st[:, 3:4], in_=st[:, 2:3]), [st], [st])
            k.ts("dve", pS[:, 0:1023], pS[:, 0:1023], st[:, 3:4], None, ALU.mult, None, [pS, st], [pS])
            k.op("act", lambda e: e.copy(out=pbf[:], in_=pS[:]), [pS], [pbf])
            k.op("dve", lambda e: e.tensor_reduce(out=impr[:], in_=pS[:].rearrange("p (s f) -> p s f", f=4), axis=AX.X, op=ALU.add), [pS], [impr])
            k.tt("dve", impr[:, 1:256], impr[:, 1:256], pS[:, 3:1020:4], ALU.add, [impr, pS], [impr])
            k.mm(psA[0:8, 0:256], [(selm[:, 0:8], impr[:])], [selm, impr], [psA])
            k.op("dve", lambda e, g=g: e.memset(sc[:, g, :], 0.0), [], [sc])
            k.op("act", lambda e, g=g: e.copy(out=sc[:, g, 0:256], in_=psA[0:8, 0:256]), [psA], [sc])
            for ch in range(8):
                k.op("pe", lambda e, ch=ch: e.transpose(psT[:, ch, 0:32], pbf[:, ch * 128:(ch + 1) * 128], ident_bf[0:32, 0:32]), [pbf, ident_bf], [psT], tag=("T",))
            k.op("act", lambda e: e.copy(out=pTs[:], in_=psT[:, :, 0:32]), [psT], [pTs])
            k.mm(psA[0:32, 256:320], [(pTs[:, ch, :], vcb[:, ch, g * 64:(g + 1) * 64]) for ch in range(8)], [pTs, vcb], [psA])
            k.op("act", lambda e, g=g: e.copy(out=obr[:, 0, g, 0:64], in_=psA[0:32, 256:320]), [psA], [obr])
            k.op("dve", lambda e, g=g: e.memset(obr[:, 0, g, 64:65], 1.0), [], [obr])
            k.tt("dve", sc[:, g, :], sc[:, g, :], selcS[:, 0, :], ALU.mult, [sc, selcS], [sc])
            k.tt("dve", sc[:, g, :], sc[:, g, :], selcS[:, 1, :], ALU.add, [sc, selcS], [sc])
            k.op("dve", lambda e, g=g: e.max(out=top[:, 0:8], in_=sc[:, g, :]), [sc], [top])
            k.op("dve", lambda e, g=g: e.match_replace(out=sc2[:], in_to_replace=top[:, 0:8], in_values=sc[:, g, :], imm_value=-1e30), [sc, top], [sc2])
            k.op("dve", lambda e: e.max(out=top[:, 8:16], in_=sc2[:]), [sc2], [top])
            k.ts("dve", top[:, 15:16], top[:, 15:16], -0.5, None, ALU.max, None, [top], [top])
            k.ts("dve", sc[:, g, :], sc[:, g, :], top[:, 15:16], None, ALU.is_ge, None, [sc, top], [sc])
            k.ts("dve", sc[:, g, :], sc[:, g, :], 30000.0, -30000.0, ALU.mult, ALU.add, [sc], [sc])
            for c3 in range(3):
                k.mm(psA[:, 320 + 32 * c3:352 + 32 * c3], [(sc[:, g, c3 * 128:(c3 + 1) * 128], rep[:])], [sc, rep], [psA])
            k.op("act", lambda e, g=g: e.copy(out=selTs[:, :, g, :], in_=psA[:, 320:416].rearrange("p (c t) -> p c t", c=3)), [psA], [selTs])
        ck(141 + b)
        pi = 0
        for br in (1, 2):
            for g in range(2):
                gs = slice(g * 64, (g + 1) * 64)
                nkb = 129 if br == 1 else 5
                blocks = []
                for kb in range(nkb):
                    if br == 1:
                        sp_ = (KSb[gs, kb * 128:(kb + 1) * 128], qflat[gs, :])
                        bs_ = [(ef[:, (kb % 64) * 128:(kb % 64) * 128 + 128], selTs[:, (2 * kb) // 128, g, :])]
                        if kb == 128:
                            bs_.append((ident_bf[:], triS[:, 0, :]))
                        vr = VXb[:, kb, g, :]
                    else:
                        sp_ = (KWb[gs, kb * 128:(kb + 1) * 128], qflat[gs, :])
                        bs_ = []
                        if kb == 0:
                            bs_.append((ident_bf[:], triS[:, 1, :]))
                        if kb == 4:
                            bs_.append((ident_bf[:], triS[:, 0, :]))
                        vr = VWb[:, kb, g, :]
                    blocks.append((sp_, bs_, vr, kb))
                gi_ = 0
                for c0_ in range(0, nkb, 16):
                    grp = blocks[c0_:c0_ + 16]
                    ng = len(grp)
                    Sb = (psC, psX)[gi_ % 2]
                    gi_ += 1
                    for j, (sp_, bs_, vr, kb) in enumerate(grp):
                        k.op("pe", lambda e, sp_=sp_, j=j, Sb=Sb: e.matmul(Sb[:, j * 32:(j + 1) * 32], lhsT=sp_[0], rhs=sp_[1], start=(j == 0), stop=False,
                                                                     skip_group_check=True), [KSb, KWb, qTs], [Sb], tag=("S1c", g))
                    for j, (sp_, bs_, vr, kb) in enumerate(grp):
                        for (l, r) in bs_:
                            k.op("pe", lambda e, l=l, r=r, j=j, Sb=Sb: e.matmul(Sb[:, j * 32:(j + 1) * 32], lhsT=l, rhs=r, start=False, stop=False,
                                                                             skip_group_check=True), [ef, selTs, ident_bf, triS], [Sb], tag=("E1c",))
                    if pendc[0] is not None:
                        f = pendc[0]
                        pendc[0] = None
                        f()
                    pi ^= 1
                    pt = PTs[pi]
                    k.act(pt[:, 0:ng * 32], Sb[:, 0:ng * 32], AF.Exp, [Sb], [pt])

                    def pv(pt=pt, grp=grp, nkb=nkb):
                        for j, (sp_, bs_, vr, kb) in enumerate(grp):
                            k.op("pe", lambda e, j=j, vr=vr, kb=kb: e.matmul(psA[0:32, 416 + 0:416 + 65], lhsT=pt[:, j * 32:(j + 1) * 32], rhs=vr,
                                                                          start=(kb == 0), stop=(kb == nkb - 1)), [pt, VXb, VWb], [psA], tag=("PV1c",))
                    pendc[0] = pv
                if pendc[0] is not None:
                    f = pendc[0]
                    pendc[0] = None
                    f()
                k.op("act", lambda e, br=br, g=g: e.copy(out=obr[:, br, g, :], in_=psA[0:32, 416:481]), [psA], [obr])
        k.ts("dve", rcs[:, 0:6], obr[:, :, :, 64].rearrange("p a g -> p (a g)"), 1e-30, None, ALU.max, None, [obr], [rcs])
        k.op("dve", lambda e: e.reciprocal(out=rcs[:, 0:6], in_=rcs[:, 0:6]), [rcs], [rcs])
        for g in range(2):
            for br in range(3):
                k.tt("dve", st[:, 4:5], rcs[:, br * 2 + g:br * 2 + g + 1], gate_r[:, g, br:br + 1], ALU.mult, [rcs, gate_r], [st])
                if br == 0:
                    k.ts("dve", ofin[:, g, :], obr[:, 0, g, 0:64], st[:, 4:5], None, ALU.mult, None, [obr, st], [ofin])
                else:
                    k.op("dve", lambda e, g=g, br=br: e.scalar_tensor_tensor(out=ofin[:, g, :], in0=obr[:, br, g, 0:64], scalar=st[:, 4:5], in1=ofin[:, g, :],
                                                                             op0=ALU.mult, op1=ALU.add), [obr, st, ofin], [ofin])
            for hh in range(4):
                fo = first_o[0]
                first_o[0] = False
                k.op("pe", lambda e, g=g, hh=hh, fo=fo, b=b: e.matmul(psO[0:32, (g * 4 + hh) * 64:(g * 4 + hh + 1) * 64],
                                                                   lhsT=selm[:, 8 + (hh * 4 + b) * 32:8 + (hh * 4 + b + 1) * 32], rhs=ofin[:, g, :],
                                                                   start=fo, stop=False, skip_group_check=True), [selm, ofin], [psO])
    k.op("act", lambda e: e.copy(out=oas[:], in_=psO[0:32, :]), [psO], [oas])
    k.dma("pool", scr_oab[(NOWN + 1) * 128:(NOWN + 1) * 128 + 32, 0:512], oas[:], reads=[oas], writes=[scr_oab])
    k.pop()


def _consts():
    cm = np.zeros((128, 6, 128), np.float32)
    cm[:, 0, :] = np.eye(128, dtype=np.float32)
    s = np.arange(128)[:, None]
    t = np.arange(128)[None, :]
    same_p = (s // 64) == (t // 64)
    cm[:, 1, :] = ((s <= t) & same_p)
    cm[:, 2, :] = ((s > t) & same_p)
    same_s = ((s // 8) == (t // 8)) & (s < 32) & (t < 32)
    cm[:, 3, :] = ((s <= t) & same_s)
    cm[:, 4, :] = ((s > t) & same_s)
    cc = np.zeros((128, 6, 128), np.float32)
    cc[:, 0, 0:64] = 1
    cc[:, 1, 64:128] = 1
    for b in range(4):
        cc[:, 2 + b, 8 * b:8 * b + 8] = 1
    ci = np.zeros((128, 8), np.float32)
    ci[0:64, 0] = 1
    ci[64:128, 1] = 1
    for b in range(4):
        ci[8 * b:8 * b + 8, 2 + b] = 1
    return cm.reshape(128, 768), ci, cc.reshape(128, 768)


def _rope_tab(pos):
    inv = (10000.0 ** (-np.arange(32, dtype=np.float32) / np.float32(32))).astype(np.float32)
    ang = pos.astype(np.float32)[:, None] * inv[None, :]
    return np.concatenate([np.cos(ang), np.sin(ang)], axis=1).astype(np.float32)


def _nsa_consts(inp, half):
    w1 = inp["cmp_w1"][0]
    w2 = inp["cmp_w2"][0]
    pe = inp["cmp_pos_emb"][0]
    w1bd = np.zeros((2, 64, 64, 2, 64), np.float32)
    w1r = w1.reshape(2, 32, 64, 64)
    for g in range(2):
        w1bd[g, :, :, g, :] = w1r.transpose(2, 0, 1, 3).reshape(64, 64, 64)
    w1bd = w1bd.reshape(128, 64 * 128)
    w2bd = np.zeros((2, 64, 2, 2, 64), np.float32)
    for g in range(2):
        w2bd[g, :, :, g, :] = w2.transpose(1, 0, 2)
    w2bd = w2bd.reshape(128, 256)
    posvec = np.tile(pe.transpose(2, 0, 1).reshape(64, 64), (2, 1)).astype(np.float32)
    c = np.arange(256)[:, None]
    sidx = np.arange(62)[None, :]
    cov = ((c >= 4 * sidx - 1) & (c <= 4 * sidx + 3)).astype(np.float32)
    cover = cov.reshape(2, 128, 62).transpose(1, 0, 2).reshape(128, 124)
    NEG = -30000.0
    n_t = NOWN + 1
    cmpB = np.zeros((n_t, 128, 2, 128), np.float32)
    selc = np.zeros((n_t, 128, 2, 64), np.float32)
    cl = np.arange(128)[:, None]
    t = np.arange(128)[None, :]
    soff = 32 if half == 0 else 0
    coff = 128 if half == 0 else 0
    for i in range(n_t):
        lt = NPRE - 1 + i
        for ch in range(2):
            cc = ch * 128 + cl
            vis = (16 * cc + 31 <= 128 * lt + t) & (cc - coff >= 0) & (cc <= 254)
            cmpB[i, :, ch, :] = np.where(vis, 0.0, NEG)
        l = 128 * lt + np.arange(128)[:, None]
        s = np.arange(64)[None, :]
        sg = s - soff
        cur_g = l // 64 - soff
        valid = (64 * s <= l) & (sg >= 0)
        forced = (sg >= 0) & ((sg == 0) | (sg == cur_g) | (sg == cur_g - 1)) & (cur_g >= 0)
        selc[i, :, 0, :] = (valid & ~forced)
        selc[i, :, 1, :] = np.where(forced, 1e4, np.where(valid, 0.0, -1.0))
    key = np.arange(4096)[None, :]
    efull = (key // 64 == np.arange(64)[:, None]).astype(np.float32)
    kk = np.arange(128)[:, None]
    triB = np.zeros((128, 2, 128), np.float32)
    triB[:, 0, :] = np.where(kk > t, NEG, 0.0)
    triB[:, 1, :] = np.where(kk <= t, NEG, 0.0)
    pfx = np.full((1, 128), NEG if half == 0 else 0.0, np.float32)
    return dict(w1bd=w1bd, w2bd=w2bd, posvec=posvec, cover=cover, cmpB=cmpB.reshape(n_t, 128, 256), selc=selc.reshape(n_t, 128, 128),
                efull=efull, triB=triB.reshape(128, 256), pfx=pfx, identf=np.eye(128, dtype=np.float32))


def _sample_consts():
    NEG = -30000.0
    selcS = np.zeros((8, 2, 384), np.float32)
    s = np.arange(384)
    valid = s <= 256
    forced = (s == 0) | (s == 255) | (s == 256)
    selcS[:, 0, :] = (valid & ~forced)[None, :]
    selcS[:, 1, :] = np.where(forced, 1e4, np.where(valid, 0.0, -1.0))[None, :]
    selm = np.zeros((32, 8 + 512), np.float32)
    rep = np.zeros((8, 32), np.float32)
    for hh in range(4):
        for t in range(8):
            selm[hh * 8 + t, t] = 1
            rep[t, hh * 8 + t] = 1
            for b in range(4):
                selm[hh * 8 + t, 8 + (hh * 4 + b) * 32 + b * 8 + t] = 1
    key = np.arange(8192)[None, :]
    ef = (key // 64 == np.arange(128)[:, None]).astype(np.float32)
    r = np.arange(128)[:, None]
    tt = (np.arange(32) % 8)[None, :]
    tri = np.zeros((128, 2, 32), np.float32)
    tri[:, 0, :] = np.where(r > tt, NEG, 0.0)
    tri[:, 1, :] = np.where(r <= tt, NEG, 0.0)
    return dict(selcS=selcS.reshape(8, 768), selm=selm, rep=rep, efull128=ef, triBs=tri.reshape(128, 64),
                iot=np.arange(128, dtype=np.float32)[:, None])


_CACHE = {}


def _get_program(key=(NT, True, 0, 5120)):
    if key not in _CACHE:
        _CACHE[key] = build_program(*key)
    return _CACHE[key]


def make_in_maps(inp, cores, pools=None):
    cmask, cind, ccolmask = _consts()
    vecs = np.concatenate([inp["attn_norm_g"][0], inp["q_norm_g"][0], inp["k_norm_g"][0].reshape(-1),
                           inp["hgrn_lb_logits"].reshape(-1), inp["hgrn_norm_g"][0]]).astype(np.float32)[None, :]
    w_in = np.ascontiguousarray(inp["w_in"][0])
    maps = []
    for c in cores:
        b, half = c // 2, c % 2
        c0 = half * 2048
        xloc = np.zeros((NT * 128, D), np.float32)
        if half == 0:
            xloc[2048:] = inp["x_prompt"][b, 0:2048]
        else:
            xloc[:] = inp["x_prompt"][b, 0:4096]
        pos = np.arange(NT * 128) + c0 - 2048
        cs_tab = np.concatenate([_rope_tab(pos), _rope_tab(16384 + (np.arange(32) % 8))], axis=0)
        maps.append(dict(
            xloc=xloc, xs=np.ascontiguousarray(inp["x_sample"][4 * c:4 * c + 4].reshape(32, D)), cs_tab=cs_tab,
            w_in=w_in, vecs=vecs, cmask=cmask, cind=cind, ccolmask=ccolmask,
            st_win=np.ascontiguousarray(inp["state_win_kv"][0, 4 * c:4 * c + 4].reshape(4, 512, 256)),
            st_hgrn=np.ascontiguousarray(inp["state_hgrn"][0, 4 * c:4 * c + 4]),
        ))
        maps[-1].update(_nsa_consts(inp, half))
        cwv = np.concatenate([inp["ffn_conv_w"][0], inp["ffn_conv_b"]], axis=0)
        convw = np.ascontiguousarray(cwv.reshape(4, 22, 128).transpose(2, 1, 0)).reshape(128, 88)
        cst = inp["state_ffn_conv"][0, 4 * c:4 * c + 4]
        convst = np.ascontiguousarray(cst.reshape(4, 2, 22, 128).transpose(3, 2, 0, 1)).reshape(128, 176)
        maps[-1].update(w_branch=np.ascontiguousarray(inp["w_branch"][0]), w_out=np.ascontiguousarray(inp["w_out"][0]),
                        ffn_w_in=np.ascontiguousarray(inp["ffn_w_in"][0]), ffn_w_out=np.ascontiguousarray(inp["ffn_w_out"][0]),
                        fvec=np.ascontiguousarray(inp["ffn_norm_g"][0][None, :]), convw=convw, convst=convst)
        if pools is None:
            maps[-1].update(cache_cmp=inp["cache_cmp_kv"][0].reshape(-1, 128, 256), cache_slc=inp["cache_slc_kv"][0].reshape(-1, 128, 256),
                            ptab=np.ascontiguousarray(inp["page_table"][4 * c:4 * c + 4]).astype(np.int32))
        else:
            maps[-1].update(pools(c))
        maps[-1].update(_sample_consts())
    return maps


def assemble(res, cores, out):
    for i, c in enumerate(cores):
        r = res[i]
        b, half = c // 2, c % 2
        c0 = half * 2048
        for j, name in enumerate(("cmp_kv_prompt", "slc_kv_prompt")):
            out[name][0, b, c0:c0 + 2048] = r["o_kv"][j].reshape(2048, 2, 2, 64)
        if half == 1:
            out["win_kv_prompt"][0, b] = r["o_kv"][2][2048 - 512:].reshape(512, 2, 2, 64)
            out["hgrn_prompt"][0, b] = r["o_hg_p"]
        out["cmp_kv_sample"][0, 4 * c:4 * c + 4] = r["o_kvs"][0].reshape(4, 8, 2, 2, 64)
        out["slc_kv_sample"][0, 4 * c:4 * c + 4] = r["o_kvs"][1].reshape(4, 8, 2, 2, 64)
        out["win_kv_sample"][0, 4 * c:4 * c + 4] = r["o_win_s"].reshape(4, 512, 2, 2, 64)
        out["hgrn_sample"][0, 4 * c:4 * c + 4] = r["o_hg_s"]
        out["y_prompt"][b, c0:c0 + 2048] = r["o_y"]
        out["y_sample"][4 * c:4 * c + 4] = r["o_ys"].reshape(4, 8, D)
        if half == 1:
            out["ffn_conv_prompt"][0, b] = r["o_cp"].reshape(128, 22, 2).transpose(2, 1, 0).reshape(2, 2816)
        out["ffn_conv_sample"][0, 4 * c:4 * c + 4] = r["o_cs"].reshape(128, 22, 4, 2).transpose(2, 3, 1, 0).reshape(4, 2, 2816)


OUT_SHAPES = dict(
    y_prompt=(4, 4096, 1024), y_sample=(32, 8, 1024),
    cmp_kv_prompt=(1, 4, 4096, 2, 2, 64), cmp_kv_sample=(1, 32, 8, 2, 2, 64),
    slc_kv_prompt=(1, 4, 4096, 2, 2, 64), slc_kv_sample=(1, 32, 8, 2, 2, 64),
    win_kv_prompt=(1, 4, 512, 2, 2, 64), win_kv_sample=(1, 32, 512, 2, 2, 64),
    hgrn_prompt=(1, 4, 4, 128, 128), hgrn_sample=(1, 32, 4, 128, 128),
    ffn_conv_prompt=(1, 4, 2, 2816), ffn_conv_sample=(1, 32, 2, 2816))
OUT_ORDER = ["y_prompt", "y_sample", "cmp_kv_prompt", "cmp_kv_sample", "slc_kv_prompt", "slc_kv_sample",
             "win_kv_prompt", "win_kv_sample", "hgrn_prompt", "hgrn_sample", "ffn_conv_prompt", "ffn_conv_sample"]


def kernel(**inp):
    inp = {n: np.asarray(v) for n, v in inp.items()}
    cores = list(range(8))
    prog = _get_program()
    maps = make_in_maps(inp, cores)
    res = run_bass_kernel_spmd(prog.nc, maps, core_ids=cores)
    out = {n: np.zeros(s, np.float32) for n, s in OUT_SHAPES.items()}
    assemble(res.results, cores, out)
    return tuple(out[n] for n in OUT_ORDER)
```
